# Optimizing a Trainium2 kernel written in Bass

```python
import math
import jax, jax.numpy as jnp
from jax import lax
import numpy as np

D_MODEL = 2048
BATCH = 4
SEQ = 8192
DEPTH = 1
DEC_BATCH = 16
DEC_SEQ = 2048
PAST_LEN = 128

GRID_W = 64
N_MEM = 256
RWKV_HEADS = 12
RWKV_HEAD_DIM = 64
RWKV_W = RWKV_HEADS * RWKV_HEAD_DIM
DECAY_LORA = 64
AAA_LORA = 64
GN_EPS = 64e-5
ATT_HEADS = 6
ATT_KV_HEADS = 2
ATT_HEAD_DIM = 128
ATT_W = ATT_HEADS * ATT_HEAD_DIM
ATT_KV_W = ATT_KV_HEADS * ATT_HEAD_DIM
ROPE_THETA = 10000.0
Q_BLOCK = 128
MEM_HEADS = 4
MEM_HEAD_DIM = 128
MEM_W = MEM_HEADS * MEM_HEAD_DIM
MIX_W = RWKV_W + ATT_W + MEM_W
RWKV_SHIFT_W = 3 * RWKV_W + 2 * DECAY_LORA + 2 * AAA_LORA
SPLIT_RWKV_GATE = RWKV_SHIFT_W
SPLIT_ATT_Q = SPLIT_RWKV_GATE + RWKV_W
SPLIT_ATT_K = SPLIT_ATT_Q + ATT_W
SPLIT_ATT_V = SPLIT_ATT_K + ATT_KV_W
SPLIT_ATT_GATE = SPLIT_ATT_V + ATT_KV_W
SPLIT_MEM_Q = SPLIT_ATT_GATE + ATT_W
SPLIT_MEM_GATE = SPLIT_MEM_Q + MEM_W
IN_W = SPLIT_MEM_GATE + MEM_W
NORM_EPS = 1e-6

kernel_name = "hymba_rwkv7_axial_gqa_memory_encoder"


def rms_norm(x, g):
    xf = x.astype(jnp.float32)
    y = xf * lax.rsqrt(jnp.mean(xf * xf, axis=-1, keepdims=True) + NORM_EPS)
    return (y * g.astype(jnp.float32)).astype(x.dtype)


def centred_shift(h, mu_prev, mu_next):
    z = jnp.zeros_like(h[:, :1])
    prev = jnp.concatenate([z, h[:, :-1]], axis=1)
    nxt = jnp.concatenate([h[:, 1:], z], axis=1)
    return h + mu_prev * (prev - h) + mu_next * (nxt - h)


def wkv7_scan(r, w, k, v, kk, a, reverse):
    B, T, H, N = r.shape

    def step(S, inp):
        r_t, w_t, k_t, v_t, kk_t, a_t = inp
        sa = jnp.einsum('bhij,bhj->bhi', S, -kk_t)
        S = (S * w_t[:, :, None, :]
             + sa[..., None] * (kk_t * a_t)[:, :, None, :]
             + v_t[..., None] * k_t[:, :, None, :])
        y_t = jnp.einsum('bhij,bhj->bhi', S, r_t)
        return S, y_t

    xs = (jnp.swapaxes(r, 0, 1), jnp.swapaxes(w, 0, 1), jnp.swapaxes(k, 0, 1),
          jnp.swapaxes(v, 0, 1), jnp.swapaxes(kk, 0, 1), jnp.swapaxes(a, 0, 1))
    S0 = jnp.zeros((B, H, N, N), jnp.float32)
    _, y = lax.scan(step, S0, xs, reverse=reverse)
    return jnp.swapaxes(y, 0, 1)


def rwkv7_branch(hs_in, mu_prev, mu_next, w0, w_up, a0, a_up, k_k, k_a, r_k, gn_g, gn_b):
    B, T, _ = hs_in.shape
    hs = centred_shift(hs_in, mu_prev, mu_next).astype(jnp.float32)
    r, k, v, wl, al = jnp.split(hs, [RWKV_W, 2 * RWKV_W, 3 * RWKV_W, 3 * RWKV_W + 2 * DECAY_LORA], axis=-1)
    wl = jnp.tanh(wl.reshape(B, T, 2, DECAY_LORA))
    al = al.reshape(B, T, 2, AAA_LORA)
    w_raw = w0.astype(jnp.float32) + jnp.einsum('btdr,drc->btdc', wl, w_up.astype(jnp.float32))
    decay = jnp.exp(-jnp.exp(-jax.nn.softplus(-w_raw) - 0.5))
    a = jax.nn.sigmoid(a0.astype(jnp.float32) + jnp.einsum('btdr,drc->btdc', al, a_up.astype(jnp.float32)))

    def heads(t):
        return t.reshape(B, T, RWKV_HEADS, RWKV_HEAD_DIM)

    k_k32 = k_k.astype(jnp.float32)
    k_a32 = k_a.astype(jnp.float32)
    kk = heads(k * k_k32)
    kk = kk / jnp.maximum(jnp.sqrt(jnp.sum(kk * kk, axis=-1, keepdims=True)), 1e-12)
    k_f = heads(k * (1.0 + (a[:, :, 0] - 1.0) * k_a32))
    k_b = heads(k * (1.0 + (a[:, :, 1] - 1.0) * k_a32))
    r_h, v_h = heads(r), heads(v)
    y = (wkv7_scan(r_h, heads(decay[:, :, 0]), k_f, v_h, kk, heads(a[:, :, 0]), False)
         + wkv7_scan(r_h, heads(decay[:, :, 1]), k_b, v_h, kk, heads(a[:, :, 1]), True))
    mean = jnp.mean(y, axis=-1, keepdims=True)
    var = jnp.mean(jnp.square(y - mean), axis=-1, keepdims=True)
    y = (y - mean) * lax.rsqrt(var + GN_EPS)
    y = y * gn_g.astype(jnp.float32).reshape(RWKV_HEADS, RWKV_HEAD_DIM) + gn_b.astype(jnp.float32).reshape(RWKV_HEADS, RWKV_HEAD_DIM)
    bonus = jnp.sum(r_h * heads(k) * r_k.astype(jnp.float32), axis=-1, keepdims=True) * v_h
    return (y + bonus).reshape(B, T, RWKV_W)


def axial_rope_tables(T):
    rows = T // GRID_W
    row = jnp.repeat(jnp.arange(rows, dtype=jnp.float32), GRID_W)
    col = jnp.tile(jnp.arange(GRID_W, dtype=jnp.float32), rows)
    axis_dim = ATT_HEAD_DIM // 2
    freqs = ROPE_THETA ** (-jnp.arange(0, axis_dim, 2, dtype=jnp.float32) / axis_dim)
    ang = jnp.concatenate([row[:, None] * freqs, col[:, None] * freqs], axis=-1)
    return jnp.cos(ang), jnp.sin(ang)


def apply_axial_rope(x, cos, sin):
    B, T, H, D = x.shape
    xf = x.astype(jnp.float32).reshape(B, T, H, D // 2, 2)
    x0, x1 = xf[..., 0], xf[..., 1]
    c = cos[None, :, None, :]
    s = sin[None, :, None, :]
    out = jnp.stack([x0 * c - x1 * s, x0 * s + x1 * c], axis=-1).reshape(B, T, H, D)
    return out.astype(x.dtype)


def block_gqa_attention(q, k, v):
    B, T, Hq, D = q.shape
    G = Hq // ATT_KV_HEADS
    nb = T // Q_BLOCK
    qb = q.reshape(B, nb, Q_BLOCK, ATT_KV_HEADS, G, D).transpose(1, 0, 2, 3, 4, 5)
    scale = D ** -0.5

    def one_block(q_blk):
        s = jnp.einsum('bqhgd,bkhd->bhgqk', q_blk, k, preferred_element_type=jnp.float32) * scale
        p = jax.nn.softmax(s, axis=-1).astype(v.dtype)
        return jnp.einsum('bhgqk,bkhd->bqhgd', p, v)

    o = lax.map(one_block, qb)
    return o.transpose(1, 0, 2, 3, 4, 5).reshape(B, T, Hq * D)


def memory_attention(q, mem_n, w_mem_kv):
    B, T, _ = q.shape
    M = mem_n.shape[1]
    kv = mem_n @ w_mem_kv
    mk = kv[..., :MEM_W].reshape(B, M, MEM_HEADS, MEM_HEAD_DIM)
    mv = kv[..., MEM_W:].reshape(B, M, MEM_HEADS, MEM_HEAD_DIM)
    qh = q.reshape(B, T, MEM_HEADS, MEM_HEAD_DIM)
    s = jnp.einsum('bthd,bmhd->bhtm', qh, mk, preferred_element_type=jnp.float32) * (MEM_HEAD_DIM ** -0.5)
    p = jax.nn.softmax(s, axis=-1).astype(mv.dtype)
    return jnp.einsum('bhtm,bmhd->bthd', p, mv).reshape(B, T, MEM_W)


def hybrid_layer(x, mem, norm_g, w_in, mu_prev, mu_next, w0, w_up, a0, a_up, k_k, k_a, r_k,
                 gn_g, gn_b, q_norm_g, k_norm_g, mem_norm_g, w_mem_kv, w_out):
    B, T, _ = x.shape
    h = rms_norm(x, norm_g)
    p = h @ w_in
    rw_in = p[..., :SPLIT_RWKV_GATE]
    rw_gate = p[..., SPLIT_RWKV_GATE:SPLIT_ATT_Q]
    aq = p[..., SPLIT_ATT_Q:SPLIT_ATT_K]
    ak = p[..., SPLIT_ATT_K:SPLIT_ATT_V]
    av = p[..., SPLIT_ATT_V:SPLIT_ATT_GATE]
    a_gate = p[..., SPLIT_ATT_GATE:SPLIT_MEM_Q]
    mq = p[..., SPLIT_MEM_Q:SPLIT_MEM_GATE]
    m_gate = p[..., SPLIT_MEM_GATE:]

    y_rwkv = rwkv7_branch(rw_in, mu_prev, mu_next, w0, w_up, a0, a_up, k_k, k_a, r_k, gn_g, gn_b)
    y_rwkv = (y_rwkv * jax.nn.silu(rw_gate.astype(jnp.float32))).astype(x.dtype)

    cos, sin = axial_rope_tables(T)
    q = rms_norm(aq.reshape(B, T, ATT_HEADS, ATT_HEAD_DIM), q_norm_g)
    k = rms_norm(ak.reshape(B, T, ATT_KV_HEADS, ATT_HEAD_DIM), k_norm_g)
    q = apply_axial_rope(q, cos, sin)
    k = apply_axial_rope(k, cos, sin)
    y_att = block_gqa_attention(q, k, av.reshape(B, T, ATT_KV_HEADS, ATT_HEAD_DIM)) * jax.nn.silu(a_gate)

    mem_n = rms_norm(mem, mem_norm_g)
    y_mem = memory_attention(mq, mem_n, w_mem_kv) * jax.nn.silu(m_gate)

    y = jnp.concatenate([y_rwkv, y_att.astype(x.dtype), y_mem.astype(x.dtype)], axis=-1) @ w_out
    return x + y


def trunk(x, mem, norm_g, w_in, mu_prev, mu_next, w0, w_up, a0, a_up, k_k, k_a, r_k,
          gn_g, gn_b, q_norm_g, k_norm_g, mem_norm_g, w_mem_kv, w_out, final_g):
    for l in range(DEPTH):
        x = hybrid_layer(x, mem, norm_g[l], w_in[l], mu_prev[l], mu_next[l], w0[l], w_up[l], a0[l], a_up[l],
                         k_k[l], k_a[l], r_k[l], gn_g[l], gn_b[l], q_norm_g[l], k_norm_g[l],
                         mem_norm_g[l], w_mem_kv[l], w_out[l])
    return rms_norm(x, final_g)


def setup_inputs(seed: int = 0) -> dict:
    key = jax.random.key(seed)
    ks = jax.random.split(key, 24)
    f32 = jnp.float32
    nrm = lambda k, shape, s: jax.random.normal(k, shape, f32) * s
    return {
        "x_prompt": nrm(ks[0], (BATCH, SEQ, D_MODEL), 1.0),
        "x_sample": nrm(ks[1], (DEC_BATCH, DEC_SEQ, D_MODEL), 1.0),
        "mem_prompt": nrm(ks[2], (BATCH, N_MEM, D_MODEL), 1.0),
        "mem_sample": nrm(ks[3], (DEC_BATCH, N_MEM, D_MODEL), 1.0),
        "norm_g": 1.0 + nrm(ks[4], (DEPTH, D_MODEL), 0.02),
        "w_in": nrm(ks[5], (DEPTH, D_MODEL, IN_W), D_MODEL ** -0.5),
        "mu_prev": jax.random.uniform(ks[6], (DEPTH, RWKV_SHIFT_W), f32, 0.0, 0.5),
        "mu_next": jax.random.uniform(ks[7], (DEPTH, RWKV_SHIFT_W), f32, 0.0, 0.5),
        "w0": jax.random.uniform(ks[8], (DEPTH, 2, RWKV_W), f32, -6.5, -1.5),
        "w_up": nrm(ks[9], (DEPTH, 2, DECAY_LORA, RWKV_W), 0.1),
        "a0": nrm(ks[10], (DEPTH, 2, RWKV_W), 0.1),
        "a_up": nrm(ks[11], (DEPTH, 2, AAA_LORA, RWKV_W), 0.5 * AAA_LORA ** -0.5),
        "k_k": 0.85 + nrm(ks[12], (DEPTH, RWKV_W), 0.02),
        "k_a": 1.0 + nrm(ks[13], (DEPTH, RWKV_W), 0.02),
        "r_k": nrm(ks[14], (DEPTH, RWKV_HEADS, RWKV_HEAD_DIM), 0.1),
        "gn_g": 1.0 + nrm(ks[15], (DEPTH, RWKV_W), 0.02),
        "gn_b": nrm(ks[16], (DEPTH, RWKV_W), 0.02),
        "q_norm_g": 1.0 + nrm(ks[17], (DEPTH, ATT_HEAD_DIM), 0.02),
        "k_norm_g": 1.0 + nrm(ks[18], (DEPTH, ATT_HEAD_DIM), 0.02),
        "mem_norm_g": 1.0 + nrm(ks[19], (DEPTH, D_MODEL), 0.02),
        "w_mem_kv": nrm(ks[20], (DEPTH, D_MODEL, 2 * MEM_W), D_MODEL ** -0.5),
        "w_out": nrm(ks[21], (DEPTH, MIX_W, D_MODEL), MIX_W ** -0.5),
        "final_g": 1.0 + nrm(ks[22], (D_MODEL,), 0.02),
    }


def reference(x_prompt, x_sample, mem_prompt, mem_sample, norm_g, w_in, mu_prev, mu_next, w0, w_up, a0, a_up,
              k_k, k_a, r_k, gn_g, gn_b, q_norm_g, k_norm_g, mem_norm_g, w_mem_kv, w_out, final_g):
    y_prompt = trunk(x_prompt, mem_prompt, norm_g, w_in, mu_prev, mu_next, w0, w_up, a0, a_up, k_k, k_a, r_k,
                     gn_g, gn_b, q_norm_g, k_norm_g, mem_norm_g, w_mem_kv, w_out, final_g)
    y_sample = trunk(x_sample, mem_sample, norm_g, w_in, mu_prev, mu_next, w0, w_up, a0, a_up, k_k, k_a, r_k,
                     gn_g, gn_b, q_norm_g, k_norm_g, mem_norm_g, w_mem_kv, w_out, final_g)
    return (y_prompt, y_sample)
```

```python
import contextlib
import numpy as np
import ml_dtypes
import concourse.bass as bass
import concourse.mybir as mybir
from concourse.bass_utils import run_bass_kernel_spmd

F32 = mybir.dt.float32
BF16 = mybir.dt.bfloat16
ALU = mybir.AluOpType
AF = mybir.ActivationFunctionType

D = 2048
IN_W = 6400
NCB = 50
N_MEM = 256
NORM_EPS = 1e-6
GN_EPS = 64e-5
DECAY_K = float(np.exp(-0.5))
ATT_SCALE = 128 ** -0.5
NEG = -30000.0

CB_PERM = list(range(0, 20)) + list(range(20, 26)) + list(range(36, 42)) + list(range(46, 50)) + \
    list(range(42, 46)) + list(range(26, 32)) + [32, 33] + [34, 35]

ENGS = ("pe", "dve", "act", "pool", "sp")
SEM_EPOCH = 20000
import os as _osg
SERIAL = int(_osg.environ.get("K_SERIAL", "3"))


class Op:
    __slots__ = ("eng", "fn", "deps", "is_dma", "seq", "signal", "sig_idx", "dma_sem", "dma_val", "waits", "barrier")

    def __init__(self, eng, fn, is_dma):
        self.eng = eng
        self.fn = fn
        self.is_dma = is_dma
        self.deps = set()
        self.signal = False
        self.sig_idx = 0
        self.dma_sem = None
        self.dma_val = 0
        self.waits = []
        self.barrier = False


class Prog:
    def __init__(self, nc, n_dma_sems=8):
        self.nc = nc
        self.ops = []
        self.by_eng = {e: [] for e in ENGS}
        self.last_w = {}
        self.readers = {}
        self.n_dma_sems = n_dma_sems
        self.last_comp = None

    def _add(self, eng, fn, reads, writes, is_dma):
        op = Op(eng, fn, is_dma)
        op.seq = len(self.ops)
        for k in reads:
            w = self.last_w.get(k)
            if w is not None:
                op.deps.add(w)
        for k in writes:
            w = self.last_w.get(k)
            if w is not None:
                op.deps.add(w)
            rl = self.readers.get(k)
            if rl:
                op.deps.update(rl)
        for k in reads:
            self.readers.setdefault(k, []).append(op)
        for k in writes:
            self.last_w[k] = op
            self.readers[k] = []
        op.deps.discard(op)
        if SERIAL == 1 and self.ops and not self.ops[-1].barrier:
            op.deps.add(self.ops[-1])
        elif SERIAL == 2 and not is_dma:
            if self.last_comp is not None:
                op.deps.add(self.last_comp)
            self.last_comp = op
        elif SERIAL == 3 and not is_dma and eng != "pe":
            if self.last_comp is not None:
                op.deps.add(self.last_comp)
            self.last_comp = op
        self.ops.append(op)
        self.by_eng[eng].append(op)
        return op

    def op(self, eng, fn, reads=(), writes=()):
        return self._add(eng, fn, reads, writes, False)

    def dma(self, eng, fn, reads=(), writes=()):
        return self._add(eng, fn, reads, writes, True)

    def barrier(self):
        tails = []
        for e in ENGS:
            comp = [o for o in self.by_eng[e] if not o.is_dma and not o.barrier]
            if comp:
                tails.append(comp[-1])
            dm = [o for o in self.by_eng[e] if o.is_dma]
            tails.extend(dm[-self.n_dma_sems:])
        for e in ENGS:
            op = Op(e, lambda eng: None, False)
            op.barrier = True
            op.seq = len(self.ops)
            op.deps = set(tails)
            self.ops.append(op)
            self.by_eng[e].append(op)
        self.last_w = {}
        self.readers = {}
        self.last_comp = None

    def finalize(self):
        for op in self.ops:
            for d in op.deps:
                if d.is_dma:
                    continue
                if d.eng == "pe" and op.eng == "pe" and not op.is_dma and not op.barrier:
                    continue
                d.signal = True
        self.n_sig = {}
        for e in ENGS:
            c = 0
            for op in self.by_eng[e]:
                if (not op.is_dma) and op.signal:
                    c += 1
                    op.sig_idx = c
            self.n_sig[e] = c
        for e in ENGS:
            k = 0
            slots = [None] * self.n_dma_sems
            counts = [0] * self.n_dma_sems
            for op in self.by_eng[e]:
                if op.is_dma:
                    s = k % self.n_dma_sems
                    prev = slots[s]
                    if prev is not None:
                        op.deps.add(prev)
                    counts[s] += 16
                    op.dma_sem = (e, s)
                    op.dma_val = counts[s]
                    slots[s] = op
                    k += 1
        for e in ENGS:
            wd = {}
            dma_waited = {}
            for op in self.by_eng[e]:
                need = {}
                for d in op.deps:
                    if d.is_dma:
                        key = d.dma_sem
                        if dma_waited.get(key, 0) < d.dma_val:
                            dma_waited[key] = d.dma_val
                            op.waits.append(("dma", key, d.dma_val))
                    else:
                        if d.eng == "pe" and op.eng == "pe" and not op.is_dma and not op.barrier:
                            continue
                        if d.sig_idx > need.get(d.eng, 0):
                            need[d.eng] = d.sig_idx
                for src, idx in need.items():
                    if wd.get(src, 0) < idx:
                        op.waits.append(("eng", src, idx))
                        wd[src] = idx

    def emit(self):
        nc = self.nc
        with contextlib.ExitStack() as st:
            sems = {}
            for e in ENGS:
                n_ep = (self.n_sig[e] + SEM_EPOCH - 1) // SEM_EPOCH
                for i in range(n_ep):
                    sems[("eng", e, i)] = st.enter_context(nc.semaphore(f"s_{e}_{i}"))
                used = sorted(set(op.dma_sem for op in self.by_eng[e] if op.is_dma))
                for key in used:
                    sems[("dma",) + key] = st.enter_context(nc.semaphore(f"d_{key[0]}_{key[1]}"))
            block = st.enter_context(nc.Block())
            engmap = {"pe": "tensor", "dve": "vector", "act": "scalar", "pool": "gpsimd", "sp": "sync"}

            def make(e):
                ops = self.by_eng[e]

                def body(engine):
                    for op in ops:
                        for w in op.waits:
                            if w[0] == "dma":
                                engine.wait_ge(sems[("dma",) + w[1]], w[2])
                            else:
                                idx = w[2]
                                ep = (idx - 1) // SEM_EPOCH
                                engine.wait_ge(sems[("eng", w[1], ep)], idx - ep * SEM_EPOCH)
                        ins = op.fn(engine)
                        if ins is None:
                            continue
                        if op.is_dma:
                            ins.then_inc(sems[("dma",) + op.dma_sem], 16)
                        elif op.signal:
                            ep = (op.sig_idx - 1) // SEM_EPOCH
                            ins.then_inc(sems[("eng", e, ep)], 1)
                return body

            for e in ENGS:
                if self.by_eng[e]:
                    getattr(block, engmap[e])(make(e))


class Arena:
    def __init__(self, tile, nbytes):
        self.tile = tile
        self.nbytes = nbytes
        self.off = 0
        self.cnt = 0

    def alloc(self, shape, dtype):
        esz = 4 if dtype == F32 else 2
        n = int(np.prod(shape))
        nb = n * esz
        self.off = (self.off + 63) // 64 * 64
        assert self.off + nb <= self.nbytes, f"arena overflow {self.off}+{nb}>{self.nbytes}"
        a = self.tile[:, self.off // 2:(self.off + nb) // 2]
        if dtype == F32:
            a = a.bitcast(F32)
        self.off += nb
        self.cnt += 1
        key = f"A{self.cnt}"
        if len(shape) == 2:
            a = a.rearrange("p (a b) -> p a b", a=shape[0])
        elif len(shape) == 3:
            a = a.rearrange("p (a b c) -> p a b c", a=shape[0], b=shape[1])
        elif len(shape) == 4:
            a = a.rearrange("p (a b c d) -> p a b c d", a=shape[0], b=shape[1], c=shape[2])
        return a, key

    def mark(self):
        return self.off

    def release(self, m):
        self.off = m


def build(NSEG, SEG, debug=False, passes=(1, 2, 3, 4)):
    T = NSEG * SEG
    NT = T // 128
    GRP = 512
    NG = T // GRP
    assert SEG % GRP == 0
    nc = bass.Bass("TRN2", target_bir_lowering=False)
    dt_in = lambda n, s, d=F32: nc.dram_tensor(n, list(s), d, kind="ExternalInput").ap()
    okind = "ExternalOutput" if debug else "Internal"
    dt_scr = lambda n, s, d: nc.dram_tensor(n, list(s), d, kind=okind).ap()

    xs = dt_in("xs", [T, D])
    mem = dt_in("mem", [NSEG, N_MEM, D])
    w_in_t = dt_in("w_in_t", [NCB * 128, D])
    w_out_t = dt_in("w_out_t", [128 * 16, D])
    w_kv_t = dt_in("w_kv_t", [128 * 16, 1024])
    rowv = dt_in("rowv", [128, 3, D])
    qkg = dt_in("qkg", [128, 2, 128])
    colv = dt_in("colv", [128, 96])
    gnv = dt_in("gnv", [128, 2, 768])
    lora_up = dt_in("lora_up", [128, 2, 768])
    cs_tab = dt_in("cs_tab", [T, 2, 64])
    consts = dt_in("consts", [128, 1024])
    flags = dt_in("flags", [128, 32])
    y_out = nc.dram_tensor("y", [T, D], F32, kind="ExternalOutput").ap()

    W1 = dt_scr("W1", [NCB * 128, D], BF16)
    WO = dt_scr("WO", [128 * 16, D], BF16)
    WKV = dt_scr("WKV", [128 * 16, 1024], BF16)
    PR = dt_scr("PR", [20, 128, T + 2], F32)
    G = dt_scr("G", [12, 128, T], BF16)
    MIX = dt_scr("MIX", [16, 128, T], BF16)
    QT = dt_scr("QT", [6, 128, T], BF16)
    KTd = dt_scr("KTd", [2, 128, T], BF16)
    Vd = dt_scr("Vd", [T, 256], BF16)
    YF = dt_scr("YF", [T, 768], F32)

    with contextlib.ExitStack() as top:
        ARENA_BYTES = 200 * 1024
        arena_t = top.enter_context(nc.sbuf_tensor("arena", [128, ARENA_BYTES // 2], BF16))
        AR = Arena(arena_t, ARENA_BYTES)
        psb = [top.enter_context(nc.psum_tensor(f"psb{i}", [128, 512], F32)) for i in range(8)]
        P = Prog(nc)
        ps_rr = [0]

        import os as _os0
        _NB = int(_os0.environ.get("K_NB", "8"))

        nb_cfg = [_NB]

        def bank():
            i = ps_rr[0] % nb_cfg[0]
            ps_rr[0] += 1
            return psb[i], f"ps{i}"

        def bank2():
            if ps_rr[0] % 2:
                ps_rr[0] += 1
            i = ps_rr[0] % 8
            ps_rr[0] += 2
            return psb[i], psb[i + 1], f"ps{i}", f"ps{i + 1}"

        ev_rr = [0]

        def evac_eng():
            ev_rr[0] += 1
            return "act" if ev_rr[0] % 2 else "dve"

        def copy_op(eng, out, in_, reads, writes):
            if eng == "act":
                P.op("act", lambda e: e.activation(out=out, in_=in_, func=AF.Copy), reads, writes)
            else:
                P.op(eng, lambda e: e.tensor_copy(out=out, in_=in_), reads, writes)

        def MM(out, lhsT, rhs, start, stop, reads, writes):
            P.op("pe", lambda e: e.matmul(out, lhsT=lhsT, rhs=rhs, start=start, stop=stop), reads, writes)

        def TR(out, in_, ident, reads, writes):
            P.op("pe", lambda e: e.transpose(out, in_, ident), reads, writes)

        def ACT(out, in_, func, reads, writes, scale=None, bias=None, accum_out=None):
            kw = {}
            if scale is not None:
                kw["scale"] = scale
            if bias is not None:
                kw["bias"] = bias
            if accum_out is not None:
                kw["accum_out"] = accum_out
            P.op("act", lambda e: e.activation(out=out, in_=in_, func=func, **kw), reads, writes)

        def TT(eng, out, in0, in1, op, reads, writes):
            P.op(eng, lambda e: e.tensor_tensor(out=out, in0=in0, in1=in1, op=op), reads, writes)

        def TS(eng, out, in0, s1, op0, reads, writes, s2=None, op1=None):
            if op1 is None:
                P.op(eng, lambda e: e.tensor_scalar(out=out, in0=in0, scalar1=s1, scalar2=None, op0=op0), reads, writes)
            else:
                P.op(eng, lambda e: e.tensor_scalar(out=out, in0=in0, scalar1=s1, scalar2=s2, op0=op0, op1=op1), reads, writes)

        def STT(out, in0, scalar, in1, op0, op1, reads, writes):
            P.op("dve", lambda e: e.scalar_tensor_tensor(out=out, in0=in0, scalar=scalar, in1=in1, op0=op0, op1=op1), reads, writes)

        def DMA(eng, out, in_, reads, writes, slow=False):
            if slow:
                P.dma(eng, lambda e: e.dma_start(out=out, in_=in_, allow_slow_non_contiguous=True), reads, writes)
            else:
                P.dma(eng, lambda e: e.dma_start(out=out, in_=in_), reads, writes)

        def CP(eng, out, in_, reads, writes):
            if eng == "act":
                P.op("act", lambda e: e.activation(out=out, in_=in_, func=AF.Copy), reads, writes)
            else:
                P.op(eng, lambda e: e.tensor_copy(out=out, in_=in_), reads, writes)

        def MEMSET(eng, out, val, reads, writes):
            P.op(eng, lambda e: e.memset(out, val), reads, writes)

        c_f32, k_cf = AR.alloc((1024,), F32)
        c_flag, k_flag = AR.alloc((32,), F32)
        c_colv, k_colv = AR.alloc((96,), F32)
        identb, k_idb = AR.alloc((128,), BF16)
        onesb, k_onesb = AR.alloc((128,), BF16)
        c_eps_t, k_ceps = AR.alloc((4,), F32)
        DMA("sp", c_f32, consts, [], [k_cf])
        DMA("sp", c_flag, flags, [], [k_flag])
        DMA("sp", c_colv, colv, [], [k_colv])
        ident_f = c_f32[:, 0:128]
        CP("dve", identb, ident_f, [k_cf], [k_idb])
        MEMSET("dve", onesb, 1.0, [], [k_onesb])
        MEMSET("dve", c_eps_t[:, 0:1], NORM_EPS, [], [k_ceps])
        MEMSET("dve", c_eps_t[:, 1:2], GN_EPS, [k_ceps], [k_ceps])
        MEMSET("dve", c_eps_t[:, 2:3], 1e-30, [k_ceps], [k_ceps])
        c_eps = c_eps_t[:, 0:1]
        c_gneps = c_eps_t[:, 1:2]
        c_tiny = c_eps_t[:, 2:3]

        import os as _os
        _skip = _os.environ.get("K_SKIP", "")
        if "cast" not in _skip:
            for i in range(NCB):
                DMA("pool", W1[i * 128:(i + 1) * 128, :], w_in_t[i * 128:(i + 1) * 128, :], [], [f"W1_{i}"])
            for i in range(16):
                DMA("pool", WO[i * 128:(i + 1) * 128, :], w_out_t[i * 128:(i + 1) * 128, :], [], [f"WO{i}"])
            for i in range(16):
                DMA("pool", WKV[i * 128:(i + 1) * 128, :], w_kv_t[i * 128:(i + 1) * 128, :], [], [f"WKV{i}"])
        zt, k_zt = AR.alloc((20, 1), F32)
        MEMSET("pool", zt, 0.0, [], [k_zt])
        if "pad" not in _skip:
            DMA("sp", PR[:, :, 0:1].rearrange("b p o -> p b o"), zt, [k_zt], ["PRpad0"], slow=True)
            DMA("sp", PR[:, :, T + 1:T + 2].rearrange("b p o -> p b o"), zt, [k_zt], ["PRpad1"], slow=True)
        if "mixz" not in _skip:
            zb, k_zb = AR.alloc((2048,), BF16)
            MEMSET("pool", zb, 0.0, [], [k_zb])
            for cbz in range(12):
                for tz in range(0, T, 2048):
                    nz = min(2048, T - tz)
                    DMA("sp", MIX[cbz, :, tz:tz + nz], zb[:, 0:nz], [k_zb], [f"MIXz{cbz}_{tz}"])
        P.barrier()

        def rmsnorm_tile(x_ap, k_x, g_ap, k_g, h_ap, k_h, junk, k_junk, ssq, k_ssq, rstd, k_rstd):
            ACT(junk, x_ap, AF.Square, [k_x], [k_junk, k_ssq], accum_out=ssq)
            ACT(rstd, ssq, AF.Ln, [k_ssq, k_ceps], [k_rstd], scale=1.0 / D, bias=c_eps)
            ACT(rstd, rstd, AF.Exp, [k_rstd], [k_rstd], scale=-0.5)
            STT(h_ap, x_ap, rstd, g_ap, ALU.mult, ALU.mult, [k_x, k_rstd, k_g], [k_h])

        if 1 in passes:
            m1 = AR.mark()
            ng_b, k_ng = AR.alloc((D,), F32)
            qkg_b, k_qkg = AR.alloc((2, 128), F32)
            DMA("sp", ng_b, rowv[:, 0, :], [], [k_ng])
            DMA("sp", qkg_b, qkg, [], [k_qkg])
            mkT, k_mkT = AR.alloc((NSEG, 4, 256), BF16)
            mv, k_mv = AR.alloc((NSEG, 2, 512), BF16)

            mkv_mark = AR.mark()
            mng_b, k_mng = AR.alloc((D,), F32)
            DMA("sp", mng_b, rowv[:, 2, :], [], [k_mng])
            wkv_sb, k_wkv = AR.alloc((16, 1024), BF16)
            DMA("sp", wkv_sb, WKV.rearrange("(p k) n -> p k n", k=16), [f"WKV{i}" for i in range(16)], [k_wkv])
            mx, k_mx = AR.alloc((D,), F32)
            mh, k_mh = AR.alloc((D,), BF16)
            mjunk, k_mjunk = AR.alloc((D,), BF16)
            mss, k_mss = AR.alloc((2,), F32)
            mhT, k_mhT = AR.alloc((16, 256), BF16)
            for s in range(NSEG):
                for mt in range(2):
                    DMA("sp", mx, mem[s, mt * 128:(mt + 1) * 128, :], [], [k_mx])
                    if "mkvn" in _skip:
                        continue
                    rmsnorm_tile(mx, k_mx, mng_b, k_mng, mh, k_mh, mjunk, k_mjunk, mss[:, 0:1], k_mss, mss[:, 1:2], k_mss + "b")
                    for half in range(2):
                        if "mkvt" in _skip:
                            continue
                        pb, kpb = bank()
                        pbb = pb[:].bitcast(BF16)
                        for j in range(8):
                            kc = half * 8 + j
                            TR(pbb[:, j * 128:(j + 1) * 128], mh[:, kc * 128:(kc + 1) * 128], identb, [k_mh, k_idb], [kpb])
                        CP(evac_eng(), mhT[:, half * 8:(half + 1) * 8, mt * 128:(mt + 1) * 128],
                           pbb[:, 0:1024].rearrange("p (a b) -> p a b", a=8), [kpb], [k_mhT])
                if "mkvm" in _skip:
                    continue
                for hd in range(4):
                    if "mkva" in _skip:
                        continue
                    pb, kpb = bank()
                    for kc in range(16):
                        MM(pb[:, 0:256], wkv_sb[:, kc, hd * 128:(hd + 1) * 128], mhT[:, kc, :], kc == 0, kc == 15, [k_wkv, k_mhT], [kpb])
                    CP(evac_eng(), mkT[:, s, hd, :], pb[:, 0:256], [kpb], [k_mkT])
                P.barrier()
                for mt in range(2):
                    if "mkvb" in _skip:
                        continue
                    pb, kpb = bank()
                    for kc in range(16):
                        MM(pb[:, 0:512], mhT[:, kc, mt * 128:(mt + 1) * 128], wkv_sb[:, kc, 512:1024], kc == 0, kc == 15, [k_wkv, k_mhT], [kpb])
                    CP(evac_eng(), mv[:, s, mt, :], pb[:, 0:512], [kpb], [k_mv])
            P.barrier()
            AR.release(mkv_mark)

            xt = [AR.alloc((D,), F32) for _ in range(2)]
            ht = [AR.alloc((D,), BF16) for _ in range(2)]
            junk, k_junk = AR.alloc((D,), BF16)
            st_small, k_sts = AR.alloc((2, 2), F32)
            hT = [AR.alloc((16, GRP), BF16) for _ in range(2)]
            wblk = [AR.alloc((4, 16, 128), BF16) for _ in range(3)]
            stg32 = [AR.alloc((GRP,), F32) for _ in range(3)]
            stg16 = [AR.alloc((GRP,), BF16) for _ in range(3)]
            mqT, k_mqT = AR.alloc((4, GRP), BF16)
            mg, k_mg = AR.alloc((4, GRP), BF16)
            PTm = [AR.alloc((2, GRP), BF16) for _ in range(2)]
            rs_t, k_rs = AR.alloc((GRP,), F32)
            ym_t, k_ym = AR.alloc((GRP,), F32)
            qn, k_qn = AR.alloc((8, 128), F32)
            qsq, k_qsq = AR.alloc((8, 128), F32)
            qss, k_qss = AR.alloc((8,), F32)
            qrs, k_qrs = AR.alloc((8,), F32)
            rp1, k_rp1 = AR.alloc((8, 64), F32)
            rp2, k_rp2 = AR.alloc((8, 64), F32)
            qr, k_qr = AR.alloc((8, 128), BF16)
            cst = [AR.alloc((4, 2, 64), F32) for _ in range(1)]
            qTs = [AR.alloc((6, GRP), BF16) for _ in range(1)]
            kTs = [AR.alloc((2, GRP), BF16) for _ in range(1)]
            vst = [AR.alloc((4, 256), BF16) for _ in range(1)]
            wl_rr = [0]
            s32_rr = [0]
            s16_rr = [0]
            tile_ctr = [0]
            loads = [(b, min(4, NCB - b)) for b in range(0, NCB, 4)]

            for g in range(NG):
                if "main" in _skip:
                    break
                t0 = g * GRP
                seg = t0 // SEG
                hTg, k_hT = hT[g % 2]
                for ti in range(4):
                    tt = tile_ctr[0]
                    tile_ctr[0] += 1
                    x_ap, k_x = xt[tt % 2]
                    h_ap, k_h = ht[tt % 2]
                    tok = t0 + ti * 128
                    DMA("sp", x_ap, xs[tok:tok + 128, :], [], [k_x])
                    rmsnorm_tile(x_ap, k_x, ng_b, k_ng, h_ap, k_h, junk, k_junk,
                                 st_small[:, tt % 2, 0:1], k_sts + f"a{tt % 2}", st_small[:, tt % 2, 1:2], k_sts + f"b{tt % 2}")
                    for half in range(2):
                        pb, kpb = bank()
                        pbb = pb[:].bitcast(BF16)
                        for j in range(8):
                            kc = half * 8 + j
                            TR(pbb[:, j * 128:(j + 1) * 128], h_ap[:, kc * 128:(kc + 1) * 128], identb, [k_h, k_idb], [kpb])
                        CP(evac_eng(), hTg[:, half * 8:(half + 1) * 8, ti * 128:(ti + 1) * 128],
                           pbb[:, 0:1024].rearrange("p (a b) -> p a b", a=8), [kpb], [k_hT])
                cs_ap, k_cs = cst[0]
                qTs_ap, k_qTs = qTs[0]
                kTs_ap, k_kTs = kTs[0]
                vst_ap, k_vst = vst[0]
                for (b0, nb) in loads:
                    if "fm" in _skip and b0 < 40:
                        continue
                    if "tm" in _skip and b0 >= 40:
                        continue
                    w_ap, k_w = wblk[wl_rr[0] % 3]
                    wl_rr[0] += 1
                    DMA("sp", w_ap[:, 0:nb], W1[b0 * 128:(b0 + nb) * 128, :].rearrange("(b p) (k j) -> p b k j", p=128, k=16),
                        [f"W1_{j}" for j in range(b0, b0 + nb)], [k_w])
                    if b0 < 40:
                        for bi in range(nb):
                            cb = b0 + bi
                            pb, kpb = bank()
                            for kc in range(16):
                                MM(pb[:, 0:GRP], w_ap[:, bi, kc, :], hTg[:, kc, :], kc == 0, kc == 15, [k_w, k_hT], [kpb])
                            if cb < 20:
                                s_ap, k_s = stg32[s32_rr[0] % 3]
                                s32_rr[0] += 1
                                CP(evac_eng(), s_ap, pb[:, 0:GRP], [kpb], [k_s])
                                DMA("sp", PR[cb, :, 1 + t0:1 + t0 + GRP], s_ap, [k_s], [f"PR{cb}_{g}"])
                            elif cb < 32:
                                s_ap, k_s = stg16[s16_rr[0] % 3]
                                s16_rr[0] += 1
                                ACT(s_ap, pb[:, 0:GRP], AF.Silu, [kpb], [k_s])
                                DMA("sp", G[cb - 20, :, t0:t0 + GRP], s_ap, [k_s], [f"G{cb}_{g}"])
                            elif cb < 36:
                                ACT(mg[:, cb - 32, :], pb[:, 0:GRP], AF.Silu, [kpb], [k_mg])
                            else:
                                CP("dve", mqT[:, cb - 36, :], pb[:, 0:GRP], [kpb], [k_mqT])
                        if b0 == 36 and "mat" not in _skip:
                            for hd in range(4):
                                PT_ap, k_PT = PTm[hd % 2]
                                for mc in range(2):
                                    pb, kpb = bank()
                                    MM(pb[:, 0:GRP], mkT[:, seg, hd, mc * 128:(mc + 1) * 128], mqT[:, hd, :], True, True, [k_mkT, k_mqT], [kpb])
                                    ACT(PT_ap[:, mc, :], pb[:, 0:GRP], AF.Exp, [kpb], [k_PT], scale=ATT_SCALE)
                                pby, kpby = bank()
                                pbs, kpbs = bank()
                                for mc in range(2):
                                    MM(pby[:, 0:GRP], mv[:, seg, mc, hd * 128:(hd + 1) * 128], PT_ap[:, mc, :], mc == 0, mc == 1, [k_mv, k_PT], [kpby])
                                for mc in range(2):
                                    MM(pbs[:, 0:GRP], onesb, PT_ap[:, mc, :], mc == 0, mc == 1, [k_onesb, k_PT], [kpbs])
                                P.op("dve", (lambda o, i: (lambda e: e.reciprocal(out=o, in_=i)))(rs_t, pbs[:, 0:GRP]), [kpbs], [k_rs])
                                TT("dve", ym_t, pby[:, 0:GRP], rs_t, ALU.mult, [kpby, k_rs], [k_ym])
                                s_ap, k_s = stg16[s16_rr[0] % 3]
                                s16_rr[0] += 1
                                TT("pool", s_ap, ym_t, mg[:, hd, :], ALU.mult, [k_ym, k_mg], [k_s])
                                DMA("sp", MIX[12 + hd, :, t0:t0 + GRP], s_ap, [k_s], [f"MIX{12 + hd}_{g}"])
                    else:
                        if b0 == 40:
                            P.barrier()
                            DMA("sp", cs_ap, cs_tab[t0:t0 + GRP].rearrange("(t p) c f -> p t c f", p=128), [], [k_cs])
                        for ti in range(4):
                            pb, kpb = bank()
                            for kc in range(16):
                                MM(pb[:, 0:nb * 128].rearrange("p (b j) -> p b j", b=nb), hTg[:, kc, ti * 128:(ti + 1) * 128], w_ap[:, 0:nb, kc, :],
                                   kc == 0, kc == 15, [k_w, k_hT], [kpb])
                            if b0 == 48:
                                CP(evac_eng(), vst_ap[:, ti, :], pb[:, 0:256], [kpb], [k_vst])
                                continue
                            if "qk" in _skip:
                                continue
                            hoff = 0 if b0 == 40 else 4
                            pv = pb[:, 0:512].rearrange("p (h d) -> p h d", h=4)
                            ACT(qsq[:, hoff:hoff + 4, :], pv, AF.Square, [kpb], [k_qsq] + [k_qsq + f"h{hoff + i}" for i in range(4)])
                            P.op("dve", (lambda o, i: (lambda e: e.tensor_reduce(out=o, in_=i, op=ALU.add, axis=mybir.AxisListType.X)))(
                                qss[:, hoff:hoff + 4], qsq[:, hoff:hoff + 4, :]), [k_qsq], [k_qss])
                            ACT(qrs[:, hoff:hoff + 4], qss[:, hoff:hoff + 4], AF.Ln, [k_qss, k_ceps], [k_qrs], scale=1.0 / 128, bias=c_eps)
                            ACT(qrs[:, hoff:hoff + 4], qrs[:, hoff:hoff + 4], AF.Exp, [k_qrs], [k_qrs], scale=-0.5)
                            for hh in range(4):
                                h8 = hoff + hh
                                gsel = 0 if h8 < 6 else 1
                                STT(qn[:, h8, :], pv[:, hh, :], qrs[:, h8:h8 + 1], qkg_b[:, gsel, :], ALU.mult, ALU.mult, [kpb, k_qrs, k_qkg], [k_qn])
                            if "rope" in _skip:
                                continue
                            for hh in range(4):
                                h8 = hoff + hh
                                qv = qn[:, h8, :].rearrange("p (f two) -> p f two", two=2)
                                x0, x1 = qv[:, :, 0], qv[:, :, 1]
                                cosb = cs_ap[:, ti, 0, :]
                                sinb = cs_ap[:, ti, 1, :]
                                ov = qsq[:, h8, :].rearrange("p (f two) -> p f two", two=2)
                                r1 = rp1[:, h8, :]
                                r2 = rp2[:, h8, :]
                                kq = k_qsq + f"h{h8}"
                                k1 = k_rp1 + f"h{h8}"
                                k2 = k_rp2 + f"h{h8}"
                                TT("dve", r1, x0, cosb, ALU.mult, [k_qn, k_cs], [k1])
                                TT("dve", r2, x1, sinb, ALU.mult, [k_qn, k_cs], [k2])
                                TT("dve", ov[:, :, 0], r1, r2, ALU.subtract, [k1, k2, k_qss], [kq])
                                TT("dve", r1, x0, sinb, ALU.mult, [k_qn, k_cs, kq], [k1])
                                TT("dve", r2, x1, cosb, ALU.mult, [k_qn, k_cs, kq], [k2])
                                TT("dve", ov[:, :, 1], r1, r2, ALU.add, [k1, k2, kq], [kq])
                                CP("act", qr[:, h8, :], qsq[:, h8, :], [kq], [k_qr])
                            if "notr" in _skip:
                                continue
                            if "trbar" in _skip:
                                P.barrier()
                            pbt, kpbt = bank()
                            pbtb = pbt[:].bitcast(BF16)
                            for hh in range(4):
                                TR(pbtb[:, hh * 128:(hh + 1) * 128], qr[:, hoff + hh, :], identb, [k_qr, k_idb], [kpbt])
                            if "nocp" in _skip:
                                continue
                            for hh in range(4):
                                h8 = hoff + hh
                                if h8 < 6:
                                    CP(evac_eng(), qTs_ap[:, h8, ti * 128:(ti + 1) * 128], pbtb[:, hh * 128:(hh + 1) * 128], [kpbt], [k_qTs + f"h{h8}"])
                                else:
                                    CP(evac_eng(), kTs_ap[:, h8 - 6, ti * 128:(ti + 1) * 128], pbtb[:, hh * 128:(hh + 1) * 128], [kpbt], [k_kTs + f"h{h8 - 6}"])
                if "tm" in _skip or "qk" in _skip or "rope" in _skip or "notr" in _skip or "nocp" in _skip:
                    DMA("sp", Vd[t0:t0 + GRP, :].rearrange("(t p) c -> p t c", p=128), vst_ap, [k_vst], [f"V_{g}"])
                    P.barrier()
                    continue
                if "qst" not in _skip:
                    for hq in range(6):
                        DMA("sp", QT[hq, :, t0:t0 + GRP], qTs_ap[:, hq, :], [k_qTs + f"h{hq}"], [f"QT_{g}_{hq}"])
                    for hk in range(2):
                        DMA("sp", KTd[hk, :, t0:t0 + GRP], kTs_ap[:, hk, :], [k_kTs + f"h{hk}"], [f"KT_{g}_{hk}"])
                DMA("sp", Vd[t0:t0 + GRP, :].rearrange("(t p) c -> p t c", p=128), vst_ap, [k_vst], [f"V_{g}"])
                P.barrier()
            P.barrier()
            AR.release(m1)

        if 2 in passes:
            m2 = AR.mark()
            nb_cfg[0] = 5
            C = 128
            NCH = T // C
            CPS = SEG // C
            KD = DECAY_K
            pY0, kY0 = psb[5], "ps5"
            pY1, kY1 = psb[6], "ps6"
            pZ, kZ = psb[7], "ps7"
            gnv_b, k_gnv = AR.alloc((2, 768), F32)
            DMA("sp", gnv_b, gnv, [], [k_gnv])
            lup, k_lup = AR.alloc((2, 768), F32)
            DMA("sp", lup, lora_up, [], [k_lup])
            lupb, k_lupb = AR.alloc((2, 768), BF16)
            CP("dve", lupb, lup, [k_lup], [k_lupb])
            blk1b, k_blk1 = AR.alloc((128,), BF16)
            CP("dve", blk1b, c_f32[:, 706:834], [k_cf], [k_blk1])
            bselb, k_bsel = AR.alloc((2,), BF16)
            CP("dve", bselb, c_f32[:, 704:706], [k_cf], [k_bsel])
            ones_f, k_onesf = AR.alloc((128,), F32)
            MEMSET("dve", ones_f, 1.0, [], [k_onesf])
            c0v, k_c0v = AR.alloc((20,), F32)
            TT("dve", c0v, c_colv[:, 0:20], c_colv[:, 20:40], ALU.add, [k_colv], [k_c0v])
            TS("dve", c0v, c0v, -1.0, ALU.mult, [k_c0v], [k_c0v], s2=1.0, op1=ALU.add)
            m4 = []
            mLs = []
            for d_ in range(2):
                mt_, k_mt = AR.alloc((4, 128), F32)
                s_off, i_off = (128, 256) if d_ == 0 else (384, 512)
                for q_ in range(4):
                    off = s_off if q_ % 2 == 0 else i_off
                    CP("dve", mt_[:, q_, :], c_f32[:, off:off + 128], [k_cf, k_mt], [k_mt])
                m4.append((mt_, k_mt))
                mLs.append(c_f32[:, 384:512] if d_ == 0 else c_f32[:, 128:256])
            mu_p = c_colv[:, 0:20].unsqueeze(2).to_broadcast([128, 20, 128])
            mu_n = c_colv[:, 20:40].unsqueeze(2).to_broadcast([128, 20, 128])
            c0_b = c0v.unsqueeze(2).to_broadcast([128, 20, 128])
            kk_b = c_colv[:, 64:70].unsqueeze(2).to_broadcast([128, 6, 128])
            ka_b = c_colv[:, 70:76].unsqueeze(2).to_broadcast([128, 6, 128])
            rk_b = c_colv[:, 76:82].unsqueeze(2).to_broadcast([128, 6, 128])
            flag_ap = c_flag[:, 0:1]
            praw, k_praw = AR.alloc((20, 130), F32)
            hs, k_hs = AR.alloc((20, 128), F32)
            tmp20, k_tmp20 = AR.alloc((20, 128), F32)
            f6 = lambda: AR.alloc((6, 128), F32)
            kkraw, k_kkraw = f6()
            kk, k_kk = f6()
            rn, k_rn = f6()
            sg, k_sg = f6()
            a_t, k_at = f6()
            cs, k_cs = f6()
            ex, k_ex = f6()
            eL, k_eL = f6()
            emL, k_emL = f6()
            eLx, k_eLx = f6()
            b_t, k_bt = f6()
            kp, k_kp = f6()
            t1, k_t1 = f6()
            wtot, k_wtot = AR.alloc((6,), F32)
            b6 = lambda: AR.alloc((6, 128), BF16)
            sqb, k_sqb = b6()
            aq, k_aq = AR.alloc((6, 2, 128), BF16)
            btl, k_btl = b6()
            ktl, k_ktl = b6()
            vb, k_vb = b6()
            atm, k_atm = b6()
            btm, k_btm = b6()
            ktm, k_ktm = b6()
            vtm, k_vtm = b6()
            prodb, k_prodb = b6()
            twl, k_twl = AR.alloc((128,), BF16)
            alb, k_alb = AR.alloc((128,), BF16)
            AT, k_AT = AR.alloc((12, 4, 128), BF16)
            Pp = [AR.alloc((12, 2, 128), BF16) for _ in range(2)]
            Rb = [AR.alloc((12, 128), BF16) for _ in range(2)]
            nU, k_nU = AR.alloc((12, 64), BF16)
            IXb, k_IXb = AR.alloc((6, 64), BF16)
            QhT, k_QhT = b6()
            Gp, k_Gp = AR.alloc((6, 64), F32)
            ST, k_ST = AR.alloc((6, 64), F32)
            STb, k_STb = AR.alloc((6, 64), BF16)
            ztmp, k_ztmp = AR.alloc((6, 64), F32)
            ysb, k_ysb = AR.alloc((768,), F32)
            yf_t, k_yf = AR.alloc((768,), F32)
            yc, k_yc = AR.alloc((768,), F32)
            ysq, k_ysq = AR.alloc((768,), F32)
            st12, k_st12 = AR.alloc((4, 12), F32)
            bon, k_bon = AR.alloc((12,), F32)
            gate2, k_gate2 = b6()
            mixo, k_mixo = b6()

            def exp_op(out, k_out, in_, k_in, scale):
                ACT(out, in_, AF.Exp, [k_in], [k_out], scale=scale)

            for d_ in range(2):
                MEMSET("dve", ST, 0.0, [], [k_ST])
                MEMSET("dve", STb, 0.0, [], [k_STb])
                order = list(range(NCH)) if d_ == 0 else list(range(NCH - 1, -1, -1))
                m4t, k_m4 = m4[d_]
                mL = mLs[d_]
                for ci, c in enumerate(order):
                    t0 = c * C
                    cross = (c % CPS == 0 and c > 0) if d_ == 0 else ((c + 1) % CPS == 0 and c < NCH - 1)
                    if cross:
                        TS("dve", ST, ST, flag_ap, ALU.mult, [k_ST, k_flag], [k_ST])
                        CP("act", STb, ST, [k_ST], [k_STb])
                    DMA("sp", praw, PR[:, :, t0:t0 + 130].rearrange("b p t -> p b t"), [], [k_praw])
                    if c % CPS == 0 and c > 0:
                        TS("dve", praw[:, :, 0:1], praw[:, :, 0:1], flag_ap, ALU.mult, [k_praw, k_flag], [k_praw])
                    if (c + 1) % CPS == 0 and c < NCH - 1:
                        TS("dve", praw[:, :, 129:130], praw[:, :, 129:130], flag_ap, ALU.mult, [k_praw, k_flag], [k_praw])
                    TT("dve", hs, praw[:, :, 1:129], c0_b, ALU.mult, [k_praw, k_c0v], [k_hs])
                    TT("dve", tmp20, praw[:, :, 0:128], mu_p, ALU.mult, [k_praw, k_colv], [k_tmp20])
                    TT("dve", hs, hs, tmp20, ALU.add, [k_hs, k_tmp20], [k_hs])
                    TT("dve", tmp20, praw[:, :, 2:130], mu_n, ALU.mult, [k_praw, k_colv, k_hs], [k_tmp20])
                    TT("dve", hs, hs, tmp20, ALU.add, [k_hs, k_tmp20], [k_hs])
                    r_ = hs[:, 0:6, :]
                    k_ = hs[:, 6:12, :]
                    v_ = hs[:, 12:18, :]
                    TT("dve", kkraw, k_, kk_b, ALU.mult, [k_hs, k_colv], [k_kkraw])
                    ACT(sqb, kkraw, AF.Square, [k_kkraw], [k_sqb])
                    pb, kpb = bank()
                    MM(pb[:, 0:512], blk1b, sqb[:, 0:4, :], True, True, [k_blk1, k_sqb], [kpb])
                    pb2, kpb2 = bank()
                    MM(pb2[:, 0:256], blk1b, sqb[:, 4:6, :], True, True, [k_blk1, k_sqb], [kpb2])
                    ACT(rn[:, 0:4, :], pb[:, 0:512].rearrange("p (a b) -> p a b", a=4), AF.Ln, [kpb, k_ceps], [k_rn], bias=c_tiny)
                    ACT(rn[:, 4:6, :], pb2[:, 0:256].rearrange("p (a b) -> p a b", a=2), AF.Ln, [kpb2, k_ceps, k_rn], [k_rn], bias=c_tiny)
                    ACT(rn, rn, AF.Exp, [k_rn], [k_rn], scale=-0.5)
                    TT("dve", kk, kkraw, rn, ALU.mult, [k_kkraw, k_rn], [k_kk])
                    ACT(twl, hs[:, 18, :], AF.Tanh, [k_hs], [k_twl])
                    CP("dve", alb, hs[:, 19, :], [k_hs], [k_alb])
                    hsl = slice(d_ * 64, (d_ + 1) * 64)
                    for which in range(2):
                        src = twl if which == 0 else alb
                        ksrc = k_twl if which == 0 else k_alb
                        dst, kdst = (sg, k_sg) if which == 0 else (a_t, k_at)
                        cbase = 40 if which == 0 else 52
                        pb, kpb = bank()
                        pb2, kpb2 = bank()
                        for blk in range(6):
                            tgt, ktgt = (pb, kpb) if blk < 4 else (pb2, kpb2)
                            cc = (blk % 4) * 128
                            MM(tgt[:, cc:cc + 128], lupb[hsl, which, blk * 128:(blk + 1) * 128], src[hsl, :], True, True, [k_lupb, ksrc], [ktgt])
                        for blk in range(6):
                            tgt, ktgt = (pb, kpb) if blk < 4 else (pb2, kpb2)
                            cc = (blk % 4) * 128
                            ACT(dst[:, blk, :], tgt[:, cc:cc + 128], AF.Sigmoid, [ktgt, k_colv, kdst], [kdst],
                                bias=c_colv[:, cbase + d_ * 6 + blk:cbase + d_ * 6 + blk + 1])
                    for blk in range(6):
                        P.op("dve", (lambda o, d1: (lambda e: e.tensor_tensor_scan(out=o, data0=ones_f, data1=d1, initial=0.0, op0=ALU.mult, op1=ALU.add)))(
                            cs[:, blk, :], sg[:, blk, :]), [k_sg, k_onesf, k_cs], [k_cs])
                    ACT(wtot.unsqueeze(2), cs[:, :, 127:128], AF.Exp, [k_cs], [k_wtot], scale=-KD)
                    if d_ == 1:
                        TT("dve", ex, sg, cs, ALU.subtract, [k_sg, k_cs], [k_ex])
                        TT("dve", cs, ex, cs[:, :, 127:128].to_broadcast([128, 6, 128]), ALU.add, [k_ex, k_cs], [k_cs])
                    TT("dve", ex, cs, sg, ALU.subtract, [k_cs, k_sg], [k_ex])
                    exp_op(eL, k_eL, cs, k_cs, -KD)
                    exp_op(emL, k_emL, cs, k_cs, KD)
                    exp_op(eLx, k_eLx, ex, k_ex, -KD)
                    TT("dve", b_t, kk, a_t, ALU.mult, [k_kk, k_at], [k_bt])
                    STT(t1, a_t, -1.0, ka_b, ALU.add, ALU.mult, [k_at, k_colv], [k_t1])
                    STT(kp, t1, 1.0, k_, ALU.add, ALU.mult, [k_t1, k_hs], [k_kp])
                    TT("dve", aq[:, :, 1, :], r_, eL, ALU.mult, [k_hs, k_eL], [k_aq])
                    TT("dve", aq[:, :, 0, :], kk, eLx, ALU.mult, [k_kk, k_eLx, k_aq], [k_aq])
                    TT("dve", btl, b_t, emL, ALU.mult, [k_bt, k_emL], [k_btl])
                    TT("dve", ktl, kp, emL, ALU.mult, [k_kp, k_emL], [k_ktl])
                    CP("act", vb, v_, [k_hs], [k_vb])
                    for (src3, ksrc, dst3, kdst, sel) in ((aq, k_aq, atm, k_atm, 0), (btl, k_btl, btm, k_btm, None), (ktl, k_ktl, ktm, k_ktm, None), (vb, k_vb, vtm, k_vtm, None)):
                        pb, kpb = bank()
                        pbb = pb[:].bitcast(BF16)
                        for blk in range(6):
                            sin = src3[:, blk, 0, :] if sel is not None else src3[:, blk, :]
                            TR(pbb[:, blk * 128:(blk + 1) * 128], sin, identb, [ksrc, k_idb], [kpb])
                        CP("dve", dst3, pbb[:, 0:768].rearrange("p (a b) -> p a b", a=6), [kpb], [kdst])
                    def hsl_(hd):
                        return hd // 2, slice((hd % 2) * 64, (hd % 2) * 64 + 64)
                    P0, k_P0 = Pp[0]
                    R0, k_R0 = Rb[0]
                    for hd in range(12):
                        blk, hp = hsl_(hd)
                        pA, kpA = bank()
                        MM(pA[:, 0:256].rearrange("p (a b) -> p a b", a=2), btl[hp, blk, :], aq[hp, blk, :, :], True, True, [k_btl, k_aq], [kpA])
                        MM(pA[:, 256:512].rearrange("p (a b) -> p a b", a=2), ktl[hp, blk, :], aq[hp, blk, :, :], True, True, [k_ktl, k_aq], [kpA])
                        TT("dve", AT[:, hd, :, :], pA[:, 0:512].rearrange("p (a b) -> p a b", a=4), m4t, ALU.mult, [kpA, k_m4], [k_AT + f"{hd}"])
                        STT(P0[:, hd, 1, :], pA[:, 0:128], -1.0, m4t[:, 0, :], ALU.mult, ALU.mult, [kpA, k_m4], [k_P0 + f"t{hd}"])
                        pL, kpL = bank()
                        MM(pL[:, 0:128], aq[hp, blk, 0, :], btl[hp, blk, :], True, True, [k_aq, k_btl], [kpL])
                        STT(P0[:, hd, 0, :], pL[:, 0:128], -1.0, mL, ALU.mult, ALU.mult, [kpL, k_cf], [k_P0 + f"n{hd}"])
                    for hd in range(12):
                        blk, hp = hsl_(hd)
                        pR, kpR = bank()
                        MM(pR[:, 0:64], AT[:, hd, 2, :], vtm[:, blk, hp], True, True, [k_AT + f"{hd}", k_vtm], [kpR])
                        CP("act", R0[:, hd, 0:64], atm[:, blk, hp], [k_atm], [k_R0 + f"a{hd}"])
                        CP("dve", R0[:, hd, 64:128], pR[:, 0:64], [kpR], [k_R0 + f"b{hd}"])
                    for lv in range(7):
                        Pc, k_Pc = Pp[lv % 2]
                        Pn_, k_Pn = Pp[(lv + 1) % 2]
                        Rc, k_Rc = Rb[lv % 2]
                        Rn, k_Rn = Rb[(lv + 1) % 2]
                        for hd in range(12):
                            kP = [k_Pc + f"t{hd}", k_Pc + f"n{hd}"]
                            kR = [k_Rc + f"a{hd}", k_Rc + f"b{hd}"]
                            pD, kpD = bank()
                            MM(pD[:, 0:128], Pc[:, hd, 1, :], Rc[:, hd, :], True, True, kP + kR, [kpD])
                            if lv < 6:
                                MM(pD[:, 128:256], Pc[:, hd, 1, :], Pc[:, hd, 0, :], True, True, kP, [kpD])
                                MM(pD[:, 256:384], Pc[:, hd, 0, :], Pc[:, hd, 1, :], True, True, kP, [kpD])
                            TT("dve", Rn[:, hd, :], pD[:, 0:128], Rc[:, hd, :], ALU.add, [kpD] + kR, [k_Rn + f"a{hd}", k_Rn + f"b{hd}"])
                            if lv < 6:
                                CP("act", Pn_[:, hd, :, :], pD[:, 128:384].rearrange("p (a b) -> p a b", a=2), [kpD], [k_Pn + f"n{hd}", k_Pn + f"t{hd}"])
                    Rf, k_Rf = Rb[1]
                    for hd in range(12):
                        blk, hp = hsl_(hd)
                        kRf = [k_Rf + f"a{hd}", k_Rf + f"b{hd}"]
                        TS("dve", nU[:, hd, :], Rf[:, hd, 64:128], -1.0, ALU.mult, kRf, [k_nU + f"{hd}"])
                        pE, kpE = bank()
                        MM(pE[hp, 0:64], Rf[:, hd, 0:64], btm[:, blk, hp], True, True, kRf + [k_btm], [kpE])
                        MM(pE[hp, 128:256], Rf[:, hd, 0:64], AT[:, hd, 1, :], True, True, kRf + [k_AT + f"{hd}"], [kpE])
                        MM(pE[hp, 64:128], ktm[:, blk, hp], vtm[:, blk, hp], True, False, [k_ktm, k_vtm], [kpE])
                        MM(pE[hp, 64:128], btm[:, blk, hp], nU[:, hd, :], False, True, [k_btm, k_nU + f"{hd}"], [kpE])
                        TT("dve", IXb[hp, blk, :], c_f32[hp, 640:704], pE[hp, 0:64], ALU.subtract, [k_cf, kpE], [k_IXb + f"{hd}"])
                        TT("dve", QhT[hp, blk, :], aq[hp, blk, 1, :], pE[hp, 128:256], ALU.subtract, [k_aq, kpE], [k_QhT + f"{hd}"])
                        CP("act", Gp[hp, blk, :], pE[hp, 64:128], [kpE], [k_Gp + f"{hd}"])
                    for hd in range(12):
                        blk, hp = hsl_(hd)
                        pY, kY = (pY0, kY0) if hd < 8 else (pY1, kY1)
                        yc0 = (hd % 8) * 64
                        MM(pY[:, yc0:yc0 + 64], AT[:, hd, 3, :], vtm[:, blk, hp], True, False, [k_AT + f"{hd}", k_vtm], [kY])
                        MM(pY[:, yc0:yc0 + 64], AT[:, hd, 1, :], nU[:, hd, :], False, False, [k_AT + f"{hd}", k_nU + f"{hd}"], [kY])
                        MM(pY[:, yc0:yc0 + 64], QhT[hp, blk, :], STb[hp, blk, :], False, True, [k_QhT + f"{hd}", k_STb], [kY])
                        MM(pZ[hp, blk * 64:(blk + 1) * 64], IXb[hp, blk, :], STb[hp, blk, :], True, True, [k_IXb + f"{hd}", k_STb], [kZ])
                    kGp_all = [k_Gp + f"{i}" for i in range(12)]
                    TT("dve", ztmp, pZ[:, 0:384].rearrange("p (a b) -> p a b", a=6), Gp, ALU.add, [kZ] + kGp_all, [k_ztmp])
                    TT("dve", ST, ztmp, wtot.unsqueeze(2).to_broadcast([128, 6, 64]), ALU.mult, [k_ztmp, k_wtot], [k_ST])
                    CP("act", STb, ST, [k_ST], [k_STb])
                    CP("act", ysb[:, 0:512], pY0[:, 0:512], [kY0], [k_ysb + "0"])
                    CP("dve", ysb[:, 512:768], pY1[:, 0:256], [kY1], [k_ysb + "1"])
                    k_ysb_all = [k_ysb + "0", k_ysb + "1"]
                    if d_ == 0:
                        DMA("sp", YF[t0:t0 + C, :], ysb, k_ysb_all, [f"YF_{c}"])
                        continue
                    DMA("sp", yf_t, YF[t0:t0 + C, :], [f"YF_{c}"], [k_yf])
                    DMA("sp", gate2, G[0:6, :, t0:t0 + C].rearrange("b p t -> p b t"), [], [k_gate2])
                    TT("dve", ysb, ysb, yf_t, ALU.add, k_ysb_all + [k_yf], k_ysb_all)
                    y3 = ysb.rearrange("p (h n) -> p h n", h=12)
                    yc3 = yc.rearrange("p (h n) -> p h n", h=12)
                    ysq3 = ysq.rearrange("p (h n) -> p h n", h=12)
                    mu = st12[:, 0, :]
                    var = st12[:, 1, :]
                    rstd = st12[:, 2, :]
                    P.op("dve", (lambda o, i: (lambda e: e.tensor_reduce(out=o, in_=i, op=ALU.add, axis=mybir.AxisListType.X)))(mu, y3), k_ysb_all, [k_st12 + "m"])
                    TS("dve", mu, mu, 1.0 / 64, ALU.mult, [k_st12 + "m"], [k_st12 + "m"])
                    TT("dve", yc3, y3, mu.unsqueeze(2).to_broadcast([128, 12, 64]), ALU.subtract, k_ysb_all + [k_st12 + "m"], [k_yc])
                    TT("dve", ysq, yc, yc, ALU.mult, [k_yc], [k_ysq])
                    P.op("dve", (lambda o, i: (lambda e: e.tensor_reduce(out=o, in_=i, op=ALU.add, axis=mybir.AxisListType.X)))(var, ysq3), [k_ysq], [k_st12 + "v"])
                    ACT(rstd, var, AF.Ln, [k_st12 + "v", k_ceps], [k_st12 + "r"], scale=1.0 / 64, bias=c_gneps)
                    ACT(rstd, rstd, AF.Exp, [k_st12 + "r"], [k_st12 + "r"], scale=-0.5)
                    TT("dve", yc3, yc3, rstd.unsqueeze(2).to_broadcast([128, 12, 64]), ALU.mult, [k_yc, k_st12 + "r"], [k_yc])
                    TT("dve", yc, yc, gnv_b[:, 0, :], ALU.mult, [k_yc, k_gnv], [k_yc])
                    TT("dve", yc, yc, gnv_b[:, 1, :], ALU.add, [k_yc, k_gnv], [k_yc])
                    TT("dve", t1, r_, k_, ALU.mult, [k_hs], [k_t1])
                    TT("dve", prodb, t1, rk_b, ALU.mult, [k_t1, k_colv], [k_prodb])
                    pB, kpB = bank()
                    for blk in range(6):
                        MM(pB[:, blk * 2:(blk + 1) * 2], prodb[:, blk, :], bselb, True, True, [k_prodb, k_bsel], [kpB])
                    CP("dve", bon, pB[:, 0:12], [kpB], [k_bon])
                    TT("dve", ysq3, vtm.rearrange("p b (h n) -> p (b h) n", h=2), bon.unsqueeze(2).to_broadcast([128, 12, 64]), ALU.mult, [k_vtm, k_bon], [k_ysq])
                    TT("dve", yc, yc, ysq, ALU.add, [k_yc, k_ysq], [k_yc])
                    pT0, kpT0 = bank()
                    pT1, kpT1 = bank()
                    for blk in range(6):
                        tgt, ktgt = (pT0, kpT0) if blk < 4 else (pT1, kpT1)
                        cc = (blk % 4) * 128
                        TR(tgt[:, cc:cc + 128], yc[:, blk * 128:(blk + 1) * 128], ident_f, [k_yc, k_cf], [ktgt])
                    TT("dve", mixo[:, 0:4, :], pT0[:, 0:512].rearrange("p (a b) -> p a b", a=4), gate2[:, 0:4, :], ALU.mult, [kpT0, k_gate2], [k_mixo + "0"])
                    TT("dve", mixo[:, 4:6, :], pT1[:, 0:256].rearrange("p (a b) -> p a b", a=2), gate2[:, 4:6, :], ALU.mult, [kpT1, k_gate2], [k_mixo + "1"])
                    DMA("sp", MIX[0:6, :, t0:t0 + C].rearrange("b p t -> p b t"), mixo, [k_mixo + "0", k_mixo + "1"], [f"MIXr_{c}"])
            P.barrier()
            nb_cfg[0] = 8
            AR.release(m2)

        if 3 in passes:
            m3 = AR.mark()
            nb_cfg[0] = 6
            KT_sb, k_KT = AR.alloc((2, T), BF16)
            V_sb, k_V = AR.alloc((NT, 256), BF16)
            for hk in range(2):
                for tq in range(0, T, 2048):
                    nq = min(2048, T - tq)
                    DMA("sp", KT_sb[:, hk, tq:tq + nq], KTd[hk, :, tq:tq + nq], [], [k_KT + f"_{hk}_{tq}"])
            k_KT_all = [k_KT + f"_{hk}_{tq}" for hk in range(2) for tq in range(0, T, 2048)]
            for tq in range(0, NT, 16):
                nq = min(16, NT - tq)
                DMA("sp", V_sb[:, tq:tq + nq, :], Vd[tq * 128:(tq + nq) * 128, :].rearrange("(t p) c -> p t c", p=128), [], [k_V + f"_{tq}"])
            k_V_all = [k_V + f"_{tq}" for tq in range(0, NT, 16)]
            qT3, k_qT3 = AR.alloc((6, GRP), BF16)
            g3, k_g3 = AR.alloc((6, GRP), BF16)
            PT3 = [AR.alloc((GRP,), BF16) for _ in range(2)]
            rs3, k_rs3 = AR.alloc((GRP,), F32)
            y3, k_y3 = AR.alloc((GRP,), F32)
            o3 = [AR.alloc((GRP,), BF16) for _ in range(2)]
            pO, kpO = psb[6], "ps6"
            pS, kpS = psb[7], "ps7"
            for qg in range(NG):
                t0 = qg * GRP
                qseg = t0 // SEG
                for hq in range(6):
                    DMA("sp", qT3[:, hq, :], QT[hq, :, t0:t0 + GRP], [], [k_qT3 + f"h{hq}"])
                    DMA("sp", g3[:, hq, :], G[6 + hq, :, t0:t0 + GRP], [], [k_g3 + f"h{hq}"])
                for hq in range(6):
                    kvh = hq // 3
                    for kt in range(NT):
                        kseg = (kt * 128) // SEG
                        PT_ap, k_PT = PT3[kt % 2]
                        pb, kpb = bank()
                        MM(pb[:, 0:GRP], KT_sb[:, kvh, kt * 128:(kt + 1) * 128], qT3[:, hq, :], True, True, k_KT_all + [k_qT3 + f"h{hq}"], [kpb])
                        ACT(PT_ap, pb[:, 0:GRP], AF.Exp, [kpb, k_flag], [k_PT], scale=ATT_SCALE,
                            bias=c_flag[:, 8 + qseg * NSEG + kseg:8 + qseg * NSEG + kseg + 1])
                        MM(pO[:, 0:GRP], V_sb[:, kt, kvh * 128:(kvh + 1) * 128], PT_ap, kt == 0, kt == NT - 1, k_V_all + [k_PT], [kpO])
                        MM(pS[:, 0:GRP], onesb, PT_ap, kt == 0, kt == NT - 1, [k_onesb, k_PT], [kpS])
                    P.op("dve", (lambda o, i: (lambda e: e.reciprocal(out=o, in_=i)))(rs3, pS[:, 0:GRP]), [kpS], [k_rs3])
                    TT("dve", y3, pO[:, 0:GRP], rs3, ALU.mult, [kpO, k_rs3], [k_y3])
                    o_ap, k_o = o3[hq % 2]
                    TT("dve", o_ap, y3, g3[:, hq, :], ALU.mult, [k_y3, k_g3 + f"h{hq}"], [k_o])
                    DMA("sp", MIX[6 + hq, :, t0:t0 + GRP], o_ap, [k_o], [f"MIXa{hq}_{qg}"])
            P.barrier()
            nb_cfg[0] = 8
            AR.release(m3)

        if 4 in passes:
            m4 = AR.mark()
            fg_b, k_fg = AR.alloc((D,), F32)
            DMA("sp", fg_b, rowv[:, 1, :], [], [k_fg])
            wo_sb, k_wo = AR.alloc((16, D), BF16)
            for i in range(4):
                DMA("sp", wo_sb[:, i * 4:(i + 1) * 4, :], WO.rearrange("(p k) n -> p k n", k=16)[:, i * 4:(i + 1) * 4, :], [], [k_wo + f"_{i}"])
            k_wo_all = [k_wo + f"_{i}" for i in range(4)]
            mixt = [AR.alloc((16, 128), BF16) for _ in range(2)]
            x4 = [AR.alloc((D,), F32) for _ in range(2)]
            r4 = [AR.alloc((D,), F32) for _ in range(2)]
            y4 = [AR.alloc((D,), F32) for _ in range(2)]
            junk4, k_junk4 = AR.alloc((D,), BF16)
            st4, k_st4 = AR.alloc((2, 2), F32)
            for tt in range(NT):
                tok = tt * 128
                m_ap, k_m = mixt[tt % 2]
                x_ap, k_x = x4[tt % 2]
                r_ap, k_r = r4[tt % 2]
                y_ap, k_y = y4[tt % 2]
                DMA("sp", m_ap, MIX[:, :, tok:tok + 128].rearrange("c p t -> p c t"), [], [k_m])
                DMA("sp", x_ap, xs[tok:tok + 128, :], [], [k_x])
                for ng in range(4):
                    pb, kpb = bank()
                    for kc in range(16):
                        MM(pb[:, 0:512], m_ap[:, kc, :], wo_sb[:, kc, ng * 512:(ng + 1) * 512], kc == 0, kc == 15, [k_m] + k_wo_all, [kpb])
                    TT("dve", r_ap[:, ng * 512:(ng + 1) * 512], pb[:, 0:512], x_ap[:, ng * 512:(ng + 1) * 512], ALU.add, [kpb, k_x], [k_r + f"_{ng}"])
                kr_all = [k_r + f"_{ng}" for ng in range(4)]
                ssq = st4[:, tt % 2, 0:1]
                rstd = st4[:, tt % 2, 1:2]
                ks1 = k_st4 + f"a{tt % 2}"
                ks2 = k_st4 + f"b{tt % 2}"
                ACT(junk4, r_ap, AF.Square, kr_all, [k_junk4, ks1], accum_out=ssq)
                ACT(rstd, ssq, AF.Ln, [ks1, k_ceps], [ks2], scale=1.0 / D, bias=c_eps)
                ACT(rstd, rstd, AF.Exp, [ks2], [ks2], scale=-0.5)
                STT(y_ap, r_ap, rstd, fg_b, ALU.mult, ALU.mult, kr_all + [ks2, k_fg], [k_y])
                DMA("sp", y_out[tok:tok + 128, :], y_ap, [k_y], [f"y_{tt}"])
            P.barrier()
            AR.release(m4)

        P.barrier()
        P.finalize()
        P.emit()
    return nc


def _rope_tab(nseg, seg, carry):
    if carry:
        pos = np.arange(nseg * seg)
    else:
        pos = np.tile(np.arange(seg), nseg)
    row = (pos // 64).astype(np.float32)
    col = (pos % 64).astype(np.float32)
    freqs = (np.float32(10000.0) ** (-np.arange(0, 64, 2, dtype=np.float32) / np.float32(64))).astype(np.float32)
    ang = np.concatenate([row[:, None] * freqs, col[:, None] * freqs], axis=-1).astype(np.float32)
    return np.stack([np.cos(ang), np.sin(ang)], axis=1).astype(np.float32)


def _consts():
    c = np.zeros((128, 1024), np.float32)
    r = np.arange(128)[:, None]
    q = np.arange(128)[None, :]
    c[:, 0:128] = (r == q)
    c[:, 128:256] = (q > r)
    c[:, 256:384] = (q >= r)
    c[:, 384:512] = (q < r)
    c[:, 512:640] = (q <= r)
    c[:, 640:704] = (np.arange(64)[None, :] == (r % 64))
    c[:, 704:706] = (np.arange(2)[None, :] == (r // 64))
    c[:, 706:834] = ((r // 64) == (q // 64))
    return c


def shared_inputs(norm_g, w_in, mu_prev, mu_next, w0, w_up, a0, a_up, k_k, k_a, r_k, gn_g, gn_b,
                  q_norm_g, k_norm_g, mem_norm_g, w_mem_kv, w_out, final_g):
    f = lambda a: np.ascontiguousarray(np.asarray(a, dtype=np.float32))
    w_in = f(w_in)[0]
    wt = w_in.reshape(16, 128, NCB, 128)[:, :, CB_PERM, :]
    w_in_t = np.ascontiguousarray(wt.transpose(2, 1, 0, 3)).reshape(NCB * 128, D)
    w_out_t = np.ascontiguousarray(f(w_out)[0].reshape(16, 128, D).transpose(1, 0, 2)).reshape(128 * 16, D)
    w_kv_t = np.ascontiguousarray(f(w_mem_kv)[0].reshape(16, 128, 1024).transpose(1, 0, 2)).reshape(128 * 16, 1024)
    rowv = np.ascontiguousarray(np.broadcast_to(np.stack([f(norm_g)[0], f(final_g), f(mem_norm_g)[0]])[None], (128, 3, D)))
    qkg = np.ascontiguousarray(np.broadcast_to(np.stack([f(q_norm_g)[0], f(k_norm_g)[0]])[None], (128, 2, 128)))
    gnv = np.ascontiguousarray(np.broadcast_to(np.stack([f(gn_g)[0], f(gn_b)[0]])[None], (128, 2, 768)))
    colv = np.zeros((128, 96), np.float32)
    colv[:, 0:20] = f(mu_prev)[0].reshape(20, 128).T
    colv[:, 20:40] = f(mu_next)[0].reshape(20, 128).T
    colv[:, 40:52] = f(w0)[0].reshape(12, 128).T
    colv[:, 52:64] = f(a0)[0].reshape(12, 128).T
    colv[:, 64:70] = f(k_k)[0].reshape(6, 128).T
    colv[:, 70:76] = f(k_a)[0].reshape(6, 128).T
    colv[:, 76:82] = f(r_k)[0].reshape(6, 128).T
    lora_up = np.ascontiguousarray(np.stack([f(w_up)[0].reshape(128, 768), f(a_up)[0].reshape(128, 768)], axis=1))
    return dict(w_in_t=w_in_t, w_out_t=w_out_t, w_kv_t=w_kv_t, rowv=rowv, qkg=qkg, gnv=gnv, colv=colv,
                lora_up=lora_up, consts=_consts())


def core_inputs(shared, x_core, mem_core, nseg, seg, carry):
    flags = np.zeros((128, 32), np.float32)
    flags[:, 0] = 1.0 if carry else 0.0
    for qs in range(nseg):
        for ks in range(nseg):
            flags[:, 8 + qs * nseg + ks] = 0.0 if (carry or qs == ks) else NEG
    d = dict(shared)
    d["xs"] = np.ascontiguousarray(x_core, dtype=np.float32)
    d["mem"] = np.ascontiguousarray(mem_core, dtype=np.float32)
    d["cs_tab"] = _rope_tab(nseg, seg, carry)
    d["flags"] = flags
    return d


_NC_CACHE = {}


def kernel(x_prompt, x_sample, mem_prompt, mem_sample, norm_g, w_in, mu_prev, mu_next, w0, w_up, a0, a_up,
           k_k, k_a, r_k, gn_g, gn_b, q_norm_g, k_norm_g, mem_norm_g, w_mem_kv, w_out, final_g):
    NSEG, SEG = 4, 2048
    x_prompt = np.asarray(x_prompt, dtype=np.float32)
    x_sample = np.asarray(x_sample, dtype=np.float32)
    mem_prompt = np.asarray(mem_prompt, dtype=np.float32)
    mem_sample = np.asarray(mem_sample, dtype=np.float32)
    shared = shared_inputs(norm_g, w_in, mu_prev, mu_next, w0, w_up, a0, a_up, k_k, k_a, r_k, gn_g, gn_b,
                           q_norm_g, k_norm_g, mem_norm_g, w_mem_kv, w_out, final_g)
    in_maps = []
    for c in range(4):
        in_maps.append(core_inputs(shared, x_prompt[c], np.broadcast_to(mem_prompt[c][None], (NSEG, N_MEM, D)), NSEG, SEG, True))
    for c in range(4):
        in_maps.append(core_inputs(shared, x_sample[4 * c:4 * c + 4].reshape(NSEG * SEG, D), mem_sample[4 * c:4 * c + 4], NSEG, SEG, False))
    key = (NSEG, SEG)
    if key not in _NC_CACHE:
        _NC_CACHE[key] = build(NSEG, SEG)
    nc = _NC_CACHE[key]
    res = run_bass_kernel_spmd(nc, in_maps, core_ids=list(range(8)))
    yp = np.stack([np.asarray(res.results[c]["y"], dtype=np.float32) for c in range(4)])
    ysm = np.concatenate([np.asarray(res.results[4 + c]["y"], dtype=np.float32).reshape(4, SEG, D) for c in range(4)], axis=0)
    return (yp, ysm)
```

```python
import contextlib
import numpy as np
import ml_dtypes
import concourse.bass as bass
import concourse.mybir as mybir
from concourse.bass_utils import run_bass_kernel_spmd

F32 = mybir.dt.float32
BF16 = mybir.dt.bfloat16
ALU = mybir.AluOpType
AF = mybir.ActivationFunctionType

D = 2048
IN_W = 6400
NCB = 50
N_MEM = 256
NORM_EPS = 1e-6
GN_EPS = 64e-5
DECAY_K = float(np.exp(-0.5))
ATT_SCALE = 128 ** -0.5
NEG = -30000.0

CB_PERM = list(range(0, 20)) + list(range(20, 26)) + list(range(36, 42)) + list(range(46, 50)) + \
    list(range(42, 46)) + list(range(26, 32)) + [32, 33] + [34, 35]

ENGS = ("pe", "dve", "act", "pool", "sp")
SEM_EPOCH = 20000
import os as _osg
SERIAL = int(_osg.environ.get("K_SERIAL", "3"))


class Op:
    __slots__ = ("eng", "fn", "deps", "is_dma", "seq", "signal", "sig_idx", "dma_sem", "dma_val", "waits", "barrier")

    def __init__(self, eng, fn, is_dma):
        self.eng = eng
        self.fn = fn
        self.is_dma = is_dma
        self.deps = set()
        self.signal = False
        self.sig_idx = 0
        self.dma_sem = None
        self.dma_val = 0
        self.waits = []
        self.barrier = False


class Prog:
    def __init__(self, nc, n_dma_sems=8):
        self.nc = nc
        self.ops = []
        self.by_eng = {e: [] for e in ENGS}
        self.last_w = {}
        self.readers = {}
        self.n_dma_sems = n_dma_sems
        self.last_comp = None

    def _add(self, eng, fn, reads, writes, is_dma):
        op = Op(eng, fn, is_dma)
        op.seq = len(self.ops)
        for k in reads:
            w = self.last_w.get(k)
            if w is not None:
                op.deps.add(w)
        for k in writes:
            w = self.last_w.get(k)
            if w is not None:
                op.deps.add(w)
            rl = self.readers.get(k)
            if rl:
                op.deps.update(rl)
        for k in reads:
            self.readers.setdefault(k, []).append(op)
        for k in writes:
            self.last_w[k] = op
            self.readers[k] = []
        op.deps.discard(op)
        if SERIAL == 1 and self.ops and not self.ops[-1].barrier:
            op.deps.add(self.ops[-1])
        elif SERIAL == 2 and not is_dma:
            if self.last_comp is not None:
                op.deps.add(self.last_comp)
            self.last_comp = op
        elif SERIAL == 3 and not is_dma and eng != "pe":
            if self.last_comp is not None and self.last_comp.eng != eng:
                op.deps.add(self.last_comp)
            self.last_comp = op
        self.ops.append(op)
        self.by_eng[eng].append(op)
        return op

    def op(self, eng, fn, reads=(), writes=()):
        return self._add(eng, fn, reads, writes, False)

    def dma(self, eng, fn, reads=(), writes=()):
        return self._add(eng, fn, reads, writes, True)

    def barrier(self):
        tails = []
        for e in ENGS:
            comp = [o for o in self.by_eng[e] if not o.is_dma and not o.barrier]
            if comp:
                tails.append(comp[-1])
            dm = [o for o in self.by_eng[e] if o.is_dma]
            tails.extend(dm[-self.n_dma_sems:])
        for e in ENGS:
            op = Op(e, lambda eng: None, False)
            op.barrier = True
            op.seq = len(self.ops)
            op.deps = set(tails)
            self.ops.append(op)
            self.by_eng[e].append(op)
        self.last_w = {}
        self.readers = {}
        self.last_comp = None

    def finalize(self):
        for op in self.ops:
            for d in op.deps:
                if d.is_dma:
                    continue
                if d.eng == "pe" and op.eng == "pe" and not op.is_dma and not op.barrier:
                    continue
                d.signal = True
        self.n_sig = {}
        for e in ENGS:
            c = 0
            for op in self.by_eng[e]:
                if (not op.is_dma) and op.signal:
                    c += 1
                    op.sig_idx = c
            self.n_sig[e] = c
        for e in ENGS:
            k = 0
            slots = [None] * self.n_dma_sems
            counts = [0] * self.n_dma_sems
            for op in self.by_eng[e]:
                if op.is_dma:
                    s = k % self.n_dma_sems
                    prev = slots[s]
                    if prev is not None:
                        op.deps.add(prev)
                    counts[s] += 16
                    op.dma_sem = (e, s)
                    op.dma_val = counts[s]
                    slots[s] = op
                    k += 1
        for e in ENGS:
            wd = {}
            dma_waited = {}
            for op in self.by_eng[e]:
                need = {}
                for d in op.deps:
                    if d.is_dma:
                        key = d.dma_sem
                        if dma_waited.get(key, 0) < d.dma_val:
                            dma_waited[key] = d.dma_val
                            op.waits.append(("dma", key, d.dma_val))
                    else:
                        if d.eng == "pe" and op.eng == "pe" and not op.is_dma and not op.barrier:
                            continue
                        if d.sig_idx > need.get(d.eng, 0):
                            need[d.eng] = d.sig_idx
                for src, idx in need.items():
                    if wd.get(src, 0) < idx:
                        op.waits.append(("eng", src, idx))
                        wd[src] = idx

    def emit(self):
        nc = self.nc
        with contextlib.ExitStack() as st:
            sems = {}
            for e in ENGS:
                n_ep = (self.n_sig[e] + SEM_EPOCH - 1) // SEM_EPOCH
                for i in range(n_ep):
                    sems[("eng", e, i)] = st.enter_context(nc.semaphore(f"s_{e}_{i}"))
                used = sorted(set(op.dma_sem for op in self.by_eng[e] if op.is_dma))
                for key in used:
                    sems[("dma",) + key] = st.enter_context(nc.semaphore(f"d_{key[0]}_{key[1]}"))
            block = st.enter_context(nc.Block())
            engmap = {"pe": "tensor", "dve": "vector", "act": "scalar", "pool": "gpsimd", "sp": "sync"}

            def make(e):
                ops = self.by_eng[e]

                def body(engine):
                    for op in ops:
                        for w in op.waits:
                            if w[0] == "dma":
                                engine.wait_ge(sems[("dma",) + w[1]], w[2])
                            else:
                                idx = w[2]
                                ep = (idx - 1) // SEM_EPOCH
                                engine.wait_ge(sems[("eng", w[1], ep)], idx - ep * SEM_EPOCH)
                        ins = op.fn(engine)
                        if ins is None:
                            continue
                        if op.is_dma:
                            ins.then_inc(sems[("dma",) + op.dma_sem], 16)
                        elif op.signal:
                            ep = (op.sig_idx - 1) // SEM_EPOCH
                            ins.then_inc(sems[("eng", e, ep)], 1)
                return body

            for e in ENGS:
                if self.by_eng[e]:
                    getattr(block, engmap[e])(make(e))


class Arena:
    def __init__(self, tile, nbytes):
        self.tile = tile
        self.nbytes = nbytes
        self.off = 0
        self.cnt = 0

    def alloc(self, shape, dtype):
        esz = 4 if dtype == F32 else 2
        n = int(np.prod(shape))
        nb = n * esz
        self.off = (self.off + 63) // 64 * 64
        assert self.off + nb <= self.nbytes, f"arena overflow {self.off}+{nb}>{self.nbytes}"
        a = self.tile[:, self.off // 2:(self.off + nb) // 2]
        if dtype == F32:
            a = a.bitcast(F32)
        self.off += nb
        self.cnt += 1
        key = f"A{self.cnt}"
        if len(shape) == 2:
            a = a.rearrange("p (a b) -> p a b", a=shape[0])
        elif len(shape) == 3:
            a = a.rearrange("p (a b c) -> p a b c", a=shape[0], b=shape[1])
        elif len(shape) == 4:
            a = a.rearrange("p (a b c d) -> p a b c d", a=shape[0], b=shape[1], c=shape[2])
        return a, key

    def mark(self):
        return self.off

    def release(self, m):
        self.off = m


def build(NSEG, SEG, debug=False, passes=(1, 2, 3, 4)):
    T = NSEG * SEG
    NT = T // 128
    GRP = 512
    NG = T // GRP
    assert SEG % GRP == 0
    nc = bass.Bass("TRN2", target_bir_lowering=False)
    dt_in = lambda n, s, d=F32: nc.dram_tensor(n, list(s), d, kind="ExternalInput").ap()
    okind = "ExternalOutput" if debug else "Internal"
    dt_scr = lambda n, s, d: nc.dram_tensor(n, list(s), d, kind=okind).ap()

    xs = dt_in("xs", [T, D])
    mem = dt_in("mem", [NSEG, N_MEM, D])
    w_in_t = dt_in("w_in_t", [NCB * 128, D])
    w_out_t = dt_in("w_out_t", [128 * 16, D])
    w_kv_t = dt_in("w_kv_t", [128 * 16, 1024])
    rowv = dt_in("rowv", [128, 3, D])
    qkg = dt_in("qkg", [128, 2, 128])
    colv = dt_in("colv", [128, 96])
    gnv = dt_in("gnv", [128, 2, 768])
    lora_up = dt_in("lora_up", [128, 2, 768])
    cs_tab = dt_in("cs_tab", [T, 2, 64])
    consts = dt_in("consts", [128, 1024])
    flags = dt_in("flags", [128, 32])
    y_out = nc.dram_tensor("y", [T, D], F32, kind="ExternalOutput").ap()

    W1 = dt_scr("W1", [NCB * 128, D], BF16)
    WO = dt_scr("WO", [128 * 16, D], BF16)
    WKV = dt_scr("WKV", [128 * 16, 1024], BF16)
    PR = dt_scr("PR", [20, 128, T + 2], F32)
    G = dt_scr("G", [12, 128, T], BF16)
    MIX = dt_scr("MIX", [16, 128, T], BF16)
    QT = dt_scr("QT", [6, 128, T], BF16)
    KTd = dt_scr("KTd", [2, 128, T], BF16)
    Vd = dt_scr("Vd", [T, 256], BF16)
    YF = dt_scr("YF", [T, 768], F32)

    with contextlib.ExitStack() as top:
        ARENA_BYTES = 200 * 1024
        arena_t = top.enter_context(nc.sbuf_tensor("arena", [128, ARENA_BYTES // 2], BF16))
        AR = Arena(arena_t, ARENA_BYTES)
        psb = [top.enter_context(nc.psum_tensor(f"psb{i}", [128, 512], F32)) for i in range(8)]
        P = Prog(nc)
        ps_rr = [0]

        import os as _os0
        _NB = int(_os0.environ.get("K_NB", "8"))

        nb_cfg = [_NB]

        def bank():
            i = ps_rr[0] % nb_cfg[0]
            ps_rr[0] += 1
            return psb[i], f"ps{i}"

        def bank2():
            if ps_rr[0] % 2:
                ps_rr[0] += 1
            i = ps_rr[0] % 8
            ps_rr[0] += 2
            return psb[i], psb[i + 1], f"ps{i}", f"ps{i + 1}"

        ev_rr = [0]

        def evac_eng():
            ev_rr[0] += 1
            return "act" if ev_rr[0] % 2 else "dve"

        def copy_op(eng, out, in_, reads, writes):
            if eng == "act":
                P.op("act", lambda e: e.activation(out=out, in_=in_, func=AF.Copy), reads, writes)
            else:
                P.op(eng, lambda e: e.tensor_copy(out=out, in_=in_), reads, writes)

        def MM(out, lhsT, rhs, start, stop, reads, writes):
            P.op("pe", lambda e: e.matmul(out, lhsT=lhsT, rhs=rhs, start=start, stop=stop), reads, writes)

        def TR(out, in_, ident, reads, writes):
            P.op("pe", lambda e: e.transpose(out, in_, ident), reads, writes)

        def ACT(out, in_, func, reads, writes, scale=None, bias=None, accum_out=None):
            kw = {}
            if scale is not None:
                kw["scale"] = scale
            if bias is not None:
                kw["bias"] = bias
            if accum_out is not None:
                kw["accum_out"] = accum_out
            P.op("act", lambda e: e.activation(out=out, in_=in_, func=func, **kw), reads, writes)

        def TT(eng, out, in0, in1, op, reads, writes):
            P.op(eng, lambda e: e.tensor_tensor(out=out, in0=in0, in1=in1, op=op), reads, writes)

        def TS(eng, out, in0, s1, op0, reads, writes, s2=None, op1=None):
            if op1 is None:
                P.op(eng, lambda e: e.tensor_scalar(out=out, in0=in0, scalar1=s1, scalar2=None, op0=op0), reads, writes)
            else:
                P.op(eng, lambda e: e.tensor_scalar(out=out, in0=in0, scalar1=s1, scalar2=s2, op0=op0, op1=op1), reads, writes)

        def STT(out, in0, scalar, in1, op0, op1, reads, writes):
            P.op("dve", lambda e: e.scalar_tensor_tensor(out=out, in0=in0, scalar=scalar, in1=in1, op0=op0, op1=op1), reads, writes)

        def DMA(eng, out, in_, reads, writes, slow=False):
            if slow:
                P.dma(eng, lambda e: e.dma_start(out=out, in_=in_, allow_slow_non_contiguous=True), reads, writes)
            else:
                P.dma(eng, lambda e: e.dma_start(out=out, in_=in_), reads, writes)

        def CP(eng, out, in_, reads, writes):
            if eng == "act":
                P.op("act", lambda e: e.activation(out=out, in_=in_, func=AF.Copy), reads, writes)
            else:
                P.op(eng, lambda e: e.tensor_copy(out=out, in_=in_), reads, writes)

        def MEMSET(eng, out, val, reads, writes):
            P.op(eng, lambda e: e.memset(out, val), reads, writes)

        c_f32, k_cf = AR.alloc((1024,), F32)
        c_flag, k_flag = AR.alloc((32,), F32)
        c_colv, k_colv = AR.alloc((96,), F32)
        identb, k_idb = AR.alloc((128,), BF16)
        onesb, k_onesb = AR.alloc((128,), BF16)
        c_eps_t, k_ceps = AR.alloc((4,), F32)
        DMA("sp", c_f32, consts, [], [k_cf])
        DMA("sp", c_flag, flags, [], [k_flag])
        DMA("sp", c_colv, colv, [], [k_colv])
        ident_f = c_f32[:, 0:128]
        CP("dve", identb, ident_f, [k_cf], [k_idb])
        MEMSET("dve", onesb, 1.0, [], [k_onesb])
        MEMSET("dve", c_eps_t[:, 0:1], NORM_EPS, [], [k_ceps])
        MEMSET("dve", c_eps_t[:, 1:2], GN_EPS, [k_ceps], [k_ceps])
        MEMSET("dve", c_eps_t[:, 2:3], 1e-30, [k_ceps], [k_ceps])
        c_eps = c_eps_t[:, 0:1]
        c_gneps = c_eps_t[:, 1:2]
        c_tiny = c_eps_t[:, 2:3]

        import os as _os
        _skip = _os.environ.get("K_SKIP", "")
        if "cast" not in _skip:
            for i in range(NCB):
                DMA("pool", W1[i * 128:(i + 1) * 128, :], w_in_t[i * 128:(i + 1) * 128, :], [], [f"W1_{i}"])
            for i in range(16):
                DMA("pool", WO[i * 128:(i + 1) * 128, :], w_out_t[i * 128:(i + 1) * 128, :], [], [f"WO{i}"])
            for i in range(16):
                DMA("pool", WKV[i * 128:(i + 1) * 128, :], w_kv_t[i * 128:(i + 1) * 128, :], [], [f"WKV{i}"])
        zt, k_zt = AR.alloc((20, 1), F32)
        MEMSET("pool", zt, 0.0, [], [k_zt])
        if "pad" not in _skip:
            DMA("sp", PR[:, :, 0:1].rearrange("b p o -> p b o"), zt, [k_zt], ["PRpad0"], slow=True)
            DMA("sp", PR[:, :, T + 1:T + 2].rearrange("b p o -> p b o"), zt, [k_zt], ["PRpad1"], slow=True)
        if "mixz" not in _skip:
            zb, k_zb = AR.alloc((2048,), BF16)
            MEMSET("pool", zb, 0.0, [], [k_zb])
            for cbz in range(12):
                for tz in range(0, T, 2048):
                    nz = min(2048, T - tz)
                    DMA("sp", MIX[cbz, :, tz:tz + nz], zb[:, 0:nz], [k_zb], [f"MIXz{cbz}_{tz}"])
        P.barrier()

        def rmsnorm_tile(x_ap, k_x, g_ap, k_g, h_ap, k_h, junk, k_junk, ssq, k_ssq, rstd, k_rstd):
            ACT(junk, x_ap, AF.Square, [k_x], [k_junk, k_ssq], accum_out=ssq)
            ACT(rstd, ssq, AF.Ln, [k_ssq, k_ceps], [k_rstd], scale=1.0 / D, bias=c_eps)
            ACT(rstd, rstd, AF.Exp, [k_rstd], [k_rstd], scale=-0.5)
            STT(h_ap, x_ap, rstd, g_ap, ALU.mult, ALU.mult, [k_x, k_rstd, k_g], [k_h])

        if 1 in passes:
            m1 = AR.mark()
            ng_b, k_ng = AR.alloc((D,), F32)
            qkg_b, k_qkg = AR.alloc((2, 128), F32)
            DMA("sp", ng_b, rowv[:, 0, :], [], [k_ng])
            DMA("sp", qkg_b, qkg, [], [k_qkg])
            mkT, k_mkT = AR.alloc((NSEG, 4, 256), BF16)
            mv, k_mv = AR.alloc((NSEG, 2, 512), BF16)

            mkv_mark = AR.mark()
            mng_b, k_mng = AR.alloc((D,), F32)
            DMA("sp", mng_b, rowv[:, 2, :], [], [k_mng])
            wkv_sb, k_wkv = AR.alloc((16, 1024), BF16)
            DMA("sp", wkv_sb, WKV.rearrange("(p k) n -> p k n", k=16), [f"WKV{i}" for i in range(16)], [k_wkv])
            mx, k_mx = AR.alloc((D,), F32)
            mh, k_mh = AR.alloc((D,), BF16)
            mjunk, k_mjunk = AR.alloc((D,), BF16)
            mss, k_mss = AR.alloc((2,), F32)
            mhT, k_mhT = AR.alloc((16, 256), BF16)
            for s in range(NSEG):
                for mt in range(2):
                    DMA("sp", mx, mem[s, mt * 128:(mt + 1) * 128, :], [], [k_mx])
                    if "mkvn" in _skip:
                        continue
                    rmsnorm_tile(mx, k_mx, mng_b, k_mng, mh, k_mh, mjunk, k_mjunk, mss[:, 0:1], k_mss, mss[:, 1:2], k_mss + "b")
                    for half in range(2):
                        if "mkvt" in _skip:
                            continue
                        pb, kpb = bank()
                        pbb = pb[:].bitcast(BF16)
                        for j in range(8):
                            kc = half * 8 + j
                            TR(pbb[:, j * 128:(j + 1) * 128], mh[:, kc * 128:(kc + 1) * 128], identb, [k_mh, k_idb], [kpb])
                        CP(evac_eng(), mhT[:, half * 8:(half + 1) * 8, mt * 128:(mt + 1) * 128],
                           pbb[:, 0:1024].rearrange("p (a b) -> p a b", a=8), [kpb], [k_mhT])
                if "mkvm" in _skip:
                    continue
                for hd in range(4):
                    if "mkva" in _skip:
                        continue
                    pb, kpb = bank()
                    for kc in range(16):
                        MM(pb[:, 0:256], wkv_sb[:, kc, hd * 128:(hd + 1) * 128], mhT[:, kc, :], kc == 0, kc == 15, [k_wkv, k_mhT], [kpb])
                    CP(evac_eng(), mkT[:, s, hd, :], pb[:, 0:256], [kpb], [k_mkT])
                P.barrier()
                for mt in range(2):
                    if "mkvb" in _skip:
                        continue
                    pb, kpb = bank()
                    for kc in range(16):
                        MM(pb[:, 0:512], mhT[:, kc, mt * 128:(mt + 1) * 128], wkv_sb[:, kc, 512:1024], kc == 0, kc == 15, [k_wkv, k_mhT], [kpb])
                    CP(evac_eng(), mv[:, s, mt, :], pb[:, 0:512], [kpb], [k_mv])
            P.barrier()
            AR.release(mkv_mark)

            xt = [AR.alloc((D,), F32) for _ in range(2)]
            ht = [AR.alloc((D,), BF16) for _ in range(2)]
            junk, k_junk = AR.alloc((D,), BF16)
            st_small, k_sts = AR.alloc((2, 2), F32)
            hT = [AR.alloc((16, GRP), BF16) for _ in range(2)]
            wblk = [AR.alloc((4, 16, 128), BF16) for _ in range(3)]
            stg32 = [AR.alloc((GRP,), F32) for _ in range(3)]
            stg16 = [AR.alloc((GRP,), BF16) for _ in range(3)]
            mqT, k_mqT = AR.alloc((4, GRP), BF16)
            mg, k_mg = AR.alloc((4, GRP), BF16)
            PTm = [AR.alloc((2, GRP), BF16) for _ in range(2)]
            rs_t, k_rs = AR.alloc((GRP,), F32)
            ym_t, k_ym = AR.alloc((GRP,), F32)
            qn, k_qn = AR.alloc((8, 128), F32)
            qsq, k_qsq = AR.alloc((8, 128), F32)
            qss, k_qss = AR.alloc((8,), F32)
            qrs, k_qrs = AR.alloc((8,), F32)
            rp1, k_rp1 = AR.alloc((8, 64), F32)
            rp2, k_rp2 = AR.alloc((8, 64), F32)
            qr, k_qr = AR.alloc((8, 128), BF16)
            cst = [AR.alloc((4, 2, 64), F32) for _ in range(1)]
            qTs = [AR.alloc((6, GRP), BF16) for _ in range(1)]
            kTs = [AR.alloc((2, GRP), BF16) for _ in range(1)]
            vst = [AR.alloc((4, 256), BF16) for _ in range(1)]
            wl_rr = [0]
            s32_rr = [0]
            s16_rr = [0]
            tile_ctr = [0]
            loads = [(b, min(4, NCB - b)) for b in range(0, NCB, 4)]

            for g in range(NG):
                if "main" in _skip:
                    break
                t0 = g * GRP
                seg = t0 // SEG
                hTg, k_hT = hT[g % 2]
                for ti in range(4):
                    tt = tile_ctr[0]
                    tile_ctr[0] += 1
                    x_ap, k_x = xt[tt % 2]
                    h_ap, k_h = ht[tt % 2]
                    tok = t0 + ti * 128
                    DMA("sp", x_ap, xs[tok:tok + 128, :], [], [k_x])
                    rmsnorm_tile(x_ap, k_x, ng_b, k_ng, h_ap, k_h, junk, k_junk,
                                 st_small[:, tt % 2, 0:1], k_sts + f"a{tt % 2}", st_small[:, tt % 2, 1:2], k_sts + f"b{tt % 2}")
                    for half in range(2):
                        pb, kpb = bank()
                        pbb = pb[:].bitcast(BF16)
                        for j in range(8):
                            kc = half * 8 + j
                            TR(pbb[:, j * 128:(j + 1) * 128], h_ap[:, kc * 128:(kc + 1) * 128], identb, [k_h, k_idb], [kpb])
                        CP(evac_eng(), hTg[:, half * 8:(half + 1) * 8, ti * 128:(ti + 1) * 128],
                           pbb[:, 0:1024].rearrange("p (a b) -> p a b", a=8), [kpb], [k_hT])
                cs_ap, k_cs = cst[0]
                qTs_ap, k_qTs = qTs[0]
                kTs_ap, k_kTs = kTs[0]
                vst_ap, k_vst = vst[0]
                for (b0, nb) in loads:
                    if "fm" in _skip and b0 < 40:
                        continue
                    if "tm" in _skip and b0 >= 40:
                        continue
                    w_ap, k_w = wblk[wl_rr[0] % 3]
                    wl_rr[0] += 1
                    DMA("sp", w_ap[:, 0:nb], W1[b0 * 128:(b0 + nb) * 128, :].rearrange("(b p) (k j) -> p b k j", p=128, k=16),
                        [f"W1_{j}" for j in range(b0, b0 + nb)], [k_w])
                    if b0 < 40:
                        for bi in range(nb):
                            cb = b0 + bi
                            pb, kpb = bank()
                            for kc in range(16):
                                MM(pb[:, 0:GRP], w_ap[:, bi, kc, :], hTg[:, kc, :], kc == 0, kc == 15, [k_w, k_hT], [kpb])
                            if cb < 20:
                                s_ap, k_s = stg32[s32_rr[0] % 3]
                                s32_rr[0] += 1
                                CP(evac_eng(), s_ap, pb[:, 0:GRP], [kpb], [k_s])
                                DMA("sp", PR[cb, :, 1 + t0:1 + t0 + GRP], s_ap, [k_s], [f"PR{cb}_{g}"])
                            elif cb < 32:
                                s_ap, k_s = stg16[s16_rr[0] % 3]
                                s16_rr[0] += 1
                                ACT(s_ap, pb[:, 0:GRP], AF.Silu, [kpb], [k_s])
                                DMA("sp", G[cb - 20, :, t0:t0 + GRP], s_ap, [k_s], [f"G{cb}_{g}"])
                            elif cb < 36:
                                ACT(mg[:, cb - 32, :], pb[:, 0:GRP], AF.Silu, [kpb], [k_mg])
                            else:
                                CP("dve", mqT[:, cb - 36, :], pb[:, 0:GRP], [kpb], [k_mqT])
                        if b0 == 36 and "mat" not in _skip:
                            for hd in range(4):
                                PT_ap, k_PT = PTm[hd % 2]
                                for mc in range(2):
                                    pb, kpb = bank()
                                    MM(pb[:, 0:GRP], mkT[:, seg, hd, mc * 128:(mc + 1) * 128], mqT[:, hd, :], True, True, [k_mkT, k_mqT], [kpb])
                                    ACT(PT_ap[:, mc, :], pb[:, 0:GRP], AF.Exp, [kpb], [k_PT], scale=ATT_SCALE)
                                pby, kpby = bank()
                                pbs, kpbs = bank()
                                for mc in range(2):
                                    MM(pby[:, 0:GRP], mv[:, seg, mc, hd * 128:(hd + 1) * 128], PT_ap[:, mc, :], mc == 0, mc == 1, [k_mv, k_PT], [kpby])
                                for mc in range(2):
                                    MM(pbs[:, 0:GRP], onesb, PT_ap[:, mc, :], mc == 0, mc == 1, [k_onesb, k_PT], [kpbs])
                                P.op("dve", (lambda o, i: (lambda e: e.reciprocal(out=o, in_=i)))(rs_t, pbs[:, 0:GRP]), [kpbs], [k_rs])
                                TT("dve", ym_t, pby[:, 0:GRP], rs_t, ALU.mult, [kpby, k_rs], [k_ym])
                                s_ap, k_s = stg16[s16_rr[0] % 3]
                                s16_rr[0] += 1
                                TT("pool", s_ap, ym_t, mg[:, hd, :], ALU.mult, [k_ym, k_mg], [k_s])
                                DMA("sp", MIX[12 + hd, :, t0:t0 + GRP], s_ap, [k_s], [f"MIX{12 + hd}_{g}"])
                    else:
                        if b0 == 40:
                            P.barrier()
                            DMA("sp", cs_ap, cs_tab[t0:t0 + GRP].rearrange("(t p) c f -> p t c f", p=128), [], [k_cs])
                        for ti in range(4):
                            pb, kpb = bank()
                            for kc in range(16):
                                MM(pb[:, 0:nb * 128].rearrange("p (b j) -> p b j", b=nb), hTg[:, kc, ti * 128:(ti + 1) * 128], w_ap[:, 0:nb, kc, :],
                                   kc == 0, kc == 15, [k_w, k_hT], [kpb])
                            if b0 == 48:
                                CP(evac_eng(), vst_ap[:, ti, :], pb[:, 0:256], [kpb], [k_vst])
                                continue
                            if "qk" in _skip:
                                continue
                            hoff = 0 if b0 == 40 else 4
                            pv = pb[:, 0:512].rearrange("p (h d) -> p h d", h=4)
                            ACT(qsq[:, hoff:hoff + 4, :], pv, AF.Square, [kpb], [k_qsq] + [k_qsq + f"h{hoff + i}" for i in range(4)])
                            P.op("dve", (lambda o, i: (lambda e: e.tensor_reduce(out=o, in_=i, op=ALU.add, axis=mybir.AxisListType.X)))(
                                qss[:, hoff:hoff + 4], qsq[:, hoff:hoff + 4, :]), [k_qsq], [k_qss])
                            ACT(qrs[:, hoff:hoff + 4], qss[:, hoff:hoff + 4], AF.Ln, [k_qss, k_ceps], [k_qrs], scale=1.0 / 128, bias=c_eps)
                            ACT(qrs[:, hoff:hoff + 4], qrs[:, hoff:hoff + 4], AF.Exp, [k_qrs], [k_qrs], scale=-0.5)
                            for hh in range(4):
                                h8 = hoff + hh
                                gsel = 0 if h8 < 6 else 1
                                STT(qn[:, h8, :], pv[:, hh, :], qrs[:, h8:h8 + 1], qkg_b[:, gsel, :], ALU.mult, ALU.mult, [kpb, k_qrs, k_qkg], [k_qn])
                            if "rope" in _skip:
                                continue
                            for hh in range(4):
                                h8 = hoff + hh
                                qv = qn[:, h8, :].rearrange("p (f two) -> p f two", two=2)
                                x0, x1 = qv[:, :, 0], qv[:, :, 1]
                                cosb = cs_ap[:, ti, 0, :]
                                sinb = cs_ap[:, ti, 1, :]
                                ov = qsq[:, h8, :].rearrange("p (f two) -> p f two", two=2)
                                r1 = rp1[:, h8, :]
                                r2 = rp2[:, h8, :]
                                kq = k_qsq + f"h{h8}"
                                k1 = k_rp1 + f"h{h8}"
                                k2 = k_rp2 + f"h{h8}"
                                TT("dve", r1, x0, cosb, ALU.mult, [k_qn, k_cs], [k1])
                                TT("dve", r2, x1, sinb, ALU.mult, [k_qn, k_cs], [k2])
                                TT("dve", ov[:, :, 0], r1, r2, ALU.subtract, [k1, k2, k_qss], [kq])
                                TT("dve", r1, x0, sinb, ALU.mult, [k_qn, k_cs, kq], [k1])
                                TT("dve", r2, x1, cosb, ALU.mult, [k_qn, k_cs, kq], [k2])
                                TT("dve", ov[:, :, 1], r1, r2, ALU.add, [k1, k2, kq], [kq])
                                CP("act", qr[:, h8, :], qsq[:, h8, :], [kq], [k_qr])
                            if "notr" in _skip:
                                continue
                            if "trbar" in _skip:
                                P.barrier()
                            pbt, kpbt = bank()
                            pbtb = pbt[:].bitcast(BF16)
                            for hh in range(4):
                                TR(pbtb[:, hh * 128:(hh + 1) * 128], qr[:, hoff + hh, :], identb, [k_qr, k_idb], [kpbt])
                            if "nocp" in _skip:
                                continue
                            for hh in range(4):
                                h8 = hoff + hh
                                if h8 < 6:
                                    CP(evac_eng(), qTs_ap[:, h8, ti * 128:(ti + 1) * 128], pbtb[:, hh * 128:(hh + 1) * 128], [kpbt], [k_qTs + f"h{h8}"])
                                else:
                                    CP(evac_eng(), kTs_ap[:, h8 - 6, ti * 128:(ti + 1) * 128], pbtb[:, hh * 128:(hh + 1) * 128], [kpbt], [k_kTs + f"h{h8 - 6}"])
                if "tm" in _skip or "qk" in _skip or "rope" in _skip or "notr" in _skip or "nocp" in _skip:
                    DMA("sp", Vd[t0:t0 + GRP, :].rearrange("(t p) c -> p t c", p=128), vst_ap, [k_vst], [f"V_{g}"])
                    P.barrier()
                    continue
                if "qst" not in _skip:
                    for hq in range(6):
                        DMA("sp", QT[hq, :, t0:t0 + GRP], qTs_ap[:, hq, :], [k_qTs + f"h{hq}"], [f"QT_{g}_{hq}"])
                    for hk in range(2):
                        DMA("sp", KTd[hk, :, t0:t0 + GRP], kTs_ap[:, hk, :], [k_kTs + f"h{hk}"], [f"KT_{g}_{hk}"])
                DMA("sp", Vd[t0:t0 + GRP, :].rearrange("(t p) c -> p t c", p=128), vst_ap, [k_vst], [f"V_{g}"])
                P.barrier()
            P.barrier()
            AR.release(m1)

        if 2 in passes:
            m2 = AR.mark()
            nb_cfg[0] = 5
            C = 128
            NCH = T // C
            CPS = SEG // C
            KD = DECAY_K
            pY0, kY0 = psb[5], "ps5"
            pY1, kY1 = psb[6], "ps6"
            pZ, kZ = psb[7], "ps7"
            gnv_b, k_gnv = AR.alloc((2, 768), F32)
            DMA("sp", gnv_b, gnv, [], [k_gnv])
            lup, k_lup = AR.alloc((2, 768), F32)
            DMA("sp", lup, lora_up, [], [k_lup])
            lupb, k_lupb = AR.alloc((2, 768), BF16)
            CP("dve", lupb, lup, [k_lup], [k_lupb])
            blk1b, k_blk1 = AR.alloc((128,), BF16)
            CP("dve", blk1b, c_f32[:, 706:834], [k_cf], [k_blk1])
            bselb, k_bsel = AR.alloc((2,), BF16)
            CP("dve", bselb, c_f32[:, 704:706], [k_cf], [k_bsel])
            ones_f, k_onesf = AR.alloc((128,), F32)
            MEMSET("dve", ones_f, 1.0, [], [k_onesf])
            c0v, k_c0v = AR.alloc((20,), F32)
            TT("dve", c0v, c_colv[:, 0:20], c_colv[:, 20:40], ALU.add, [k_colv], [k_c0v])
            TS("dve", c0v, c0v, -1.0, ALU.mult, [k_c0v], [k_c0v], s2=1.0, op1=ALU.add)
            m4 = []
            mLs = []
            for d_ in range(2):
                mt_, k_mt = AR.alloc((4, 128), F32)
                s_off, i_off = (128, 256) if d_ == 0 else (384, 512)
                for q_ in range(4):
                    off = s_off if q_ % 2 == 0 else i_off
                    CP("dve", mt_[:, q_, :], c_f32[:, off:off + 128], [k_cf, k_mt], [k_mt])
                m4.append((mt_, k_mt))
                mLs.append(c_f32[:, 384:512] if d_ == 0 else c_f32[:, 128:256])
            mu_p = c_colv[:, 0:20].unsqueeze(2).to_broadcast([128, 20, 128])
            mu_n = c_colv[:, 20:40].unsqueeze(2).to_broadcast([128, 20, 128])
            c0_b = c0v.unsqueeze(2).to_broadcast([128, 20, 128])
            kk_b = c_colv[:, 64:70].unsqueeze(2).to_broadcast([128, 6, 128])
            ka_b = c_colv[:, 70:76].unsqueeze(2).to_broadcast([128, 6, 128])
            rk_b = c_colv[:, 76:82].unsqueeze(2).to_broadcast([128, 6, 128])
            flag_ap = c_flag[:, 0:1]
            praw, k_praw = AR.alloc((20, 130), F32)
            hs, k_hs = AR.alloc((20, 128), F32)
            tmp20, k_tmp20 = AR.alloc((20, 128), F32)
            f6 = lambda: AR.alloc((6, 128), F32)
            kkraw, k_kkraw = f6()
            kk, k_kk = f6()
            rn, k_rn = f6()
            sg, k_sg = f6()
            a_t, k_at = f6()
            cs, k_cs = f6()
            ex, k_ex = f6()
            eL, k_eL = f6()
            emL, k_emL = f6()
            eLx, k_eLx = f6()
            b_t, k_bt = f6()
            kp, k_kp = f6()
            t1, k_t1 = f6()
            wtot, k_wtot = AR.alloc((6,), F32)
            b6 = lambda: AR.alloc((6, 128), BF16)
            sqb, k_sqb = b6()
            aq, k_aq = AR.alloc((6, 2, 128), BF16)
            btl, k_btl = b6()
            ktl, k_ktl = b6()
            vb, k_vb = b6()
            atm, k_atm = b6()
            btm, k_btm = b6()
            ktm, k_ktm = b6()
            vtm, k_vtm = b6()
            prodb, k_prodb = b6()
            twl, k_twl = AR.alloc((128,), BF16)
            alb, k_alb = AR.alloc((128,), BF16)
            AT, k_AT = AR.alloc((12, 4, 128), BF16)
            Pp = [AR.alloc((12, 2, 128), BF16) for _ in range(2)]
            Rb = [AR.alloc((12, 128), BF16) for _ in range(2)]
            nU, k_nU = AR.alloc((12, 64), BF16)
            IXb, k_IXb = AR.alloc((6, 64), BF16)
            QhT, k_QhT = b6()
            Gp, k_Gp = AR.alloc((6, 64), F32)
            ST, k_ST = AR.alloc((6, 64), F32)
            STb, k_STb = AR.alloc((6, 64), BF16)
            ztmp, k_ztmp = AR.alloc((6, 64), F32)
            ysb, k_ysb = AR.alloc((768,), F32)
            yf_t, k_yf = AR.alloc((768,), F32)
            yc, k_yc = AR.alloc((768,), F32)
            ysq, k_ysq = AR.alloc((768,), F32)
            st12, k_st12 = AR.alloc((4, 12), F32)
            bon, k_bon = AR.alloc((12,), F32)
            gate2, k_gate2 = b6()
            mixo, k_mixo = b6()

            def exp_op(out, k_out, in_, k_in, scale):
                ACT(out, in_, AF.Exp, [k_in], [k_out], scale=scale)

            for d_ in range(2):
                MEMSET("dve", ST, 0.0, [], [k_ST])
                MEMSET("dve", STb, 0.0, [], [k_STb])
                order = list(range(NCH)) if d_ == 0 else list(range(NCH - 1, -1, -1))
                m4t, k_m4 = m4[d_]
                mL = mLs[d_]
                for ci, c in enumerate(order):
                    t0 = c * C
                    cross = (c % CPS == 0 and c > 0) if d_ == 0 else ((c + 1) % CPS == 0 and c < NCH - 1)
                    if cross:
                        TS("dve", ST, ST, flag_ap, ALU.mult, [k_ST, k_flag], [k_ST])
                        CP("act", STb, ST, [k_ST], [k_STb])
                    DMA("sp", praw, PR[:, :, t0:t0 + 130].rearrange("b p t -> p b t"), [], [k_praw])
                    if c % CPS == 0 and c > 0:
                        TS("dve", praw[:, :, 0:1], praw[:, :, 0:1], flag_ap, ALU.mult, [k_praw, k_flag], [k_praw])
                    if (c + 1) % CPS == 0 and c < NCH - 1:
                        TS("dve", praw[:, :, 129:130], praw[:, :, 129:130], flag_ap, ALU.mult, [k_praw, k_flag], [k_praw])
                    TT("dve", hs, praw[:, :, 1:129], c0_b, ALU.mult, [k_praw, k_c0v], [k_hs])
                    TT("dve", tmp20, praw[:, :, 0:128], mu_p, ALU.mult, [k_praw, k_colv], [k_tmp20])
                    TT("dve", hs, hs, tmp20, ALU.add, [k_hs, k_tmp20], [k_hs])
                    TT("dve", tmp20, praw[:, :, 2:130], mu_n, ALU.mult, [k_praw, k_colv, k_hs], [k_tmp20])
                    TT("dve", hs, hs, tmp20, ALU.add, [k_hs, k_tmp20], [k_hs])
                    r_ = hs[:, 0:6, :]
                    k_ = hs[:, 6:12, :]
                    v_ = hs[:, 12:18, :]
                    TT("dve", kkraw, k_, kk_b, ALU.mult, [k_hs, k_colv], [k_kkraw])
                    ACT(sqb, kkraw, AF.Square, [k_kkraw], [k_sqb])
                    pb, kpb = bank()
                    MM(pb[:, 0:512], blk1b, sqb[:, 0:4, :], True, True, [k_blk1, k_sqb], [kpb])
                    pb2, kpb2 = bank()
                    MM(pb2[:, 0:256], blk1b, sqb[:, 4:6, :], True, True, [k_blk1, k_sqb], [kpb2])
                    ACT(rn[:, 0:4, :], pb[:, 0:512].rearrange("p (a b) -> p a b", a=4), AF.Ln, [kpb, k_ceps], [k_rn], bias=c_tiny)
                    ACT(rn[:, 4:6, :], pb2[:, 0:256].rearrange("p (a b) -> p a b", a=2), AF.Ln, [kpb2, k_ceps, k_rn], [k_rn], bias=c_tiny)
                    ACT(rn, rn, AF.Exp, [k_rn], [k_rn], scale=-0.5)
                    TT("dve", kk, kkraw, rn, ALU.mult, [k_kkraw, k_rn], [k_kk])
                    ACT(twl, hs[:, 18, :], AF.Tanh, [k_hs], [k_twl])
                    CP("dve", alb, hs[:, 19, :], [k_hs], [k_alb])
                    hsl = slice(d_ * 64, (d_ + 1) * 64)
                    for which in range(2):
                        src = twl if which == 0 else alb
                        ksrc = k_twl if which == 0 else k_alb
                        dst, kdst = (sg, k_sg) if which == 0 else (a_t, k_at)
                        cbase = 40 if which == 0 else 52
                        pb, kpb = bank()
                        pb2, kpb2 = bank()
                        for blk in range(6):
                            tgt, ktgt = (pb, kpb) if blk < 4 else (pb2, kpb2)
                            cc = (blk % 4) * 128
                            MM(tgt[:, cc:cc + 128], lupb[hsl, which, blk * 128:(blk + 1) * 128], src[hsl, :], True, True, [k_lupb, ksrc], [ktgt])
                        for blk in range(6):
                            tgt, ktgt = (pb, kpb) if blk < 4 else (pb2, kpb2)
                            cc = (blk % 4) * 128
                            ACT(dst[:, blk, :], tgt[:, cc:cc + 128], AF.Sigmoid, [ktgt, k_colv, kdst], [kdst],
                                bias=c_colv[:, cbase + d_ * 6 + blk:cbase + d_ * 6 + blk + 1])
                    for blk in range(6):
                        P.op("dve", (lambda o, d1: (lambda e: e.tensor_tensor_scan(out=o, data0=ones_f, data1=d1, initial=0.0, op0=ALU.mult, op1=ALU.add)))(
                            cs[:, blk, :], sg[:, blk, :]), [k_sg, k_onesf, k_cs], [k_cs])
                    ACT(wtot.unsqueeze(2), cs[:, :, 127:128], AF.Exp, [k_cs], [k_wtot], scale=-KD)
                    if d_ == 1:
                        TT("dve", ex, sg, cs, ALU.subtract, [k_sg, k_cs], [k_ex])
                        TT("dve", cs, ex, cs[:, :, 127:128].to_broadcast([128, 6, 128]), ALU.add, [k_ex, k_cs], [k_cs])
                    TT("dve", ex, cs, sg, ALU.subtract, [k_cs, k_sg], [k_ex])
                    exp_op(eL, k_eL, cs, k_cs, -KD)
                    exp_op(emL, k_emL, cs, k_cs, KD)
                    exp_op(eLx, k_eLx, ex, k_ex, -KD)
                    TT("dve", b_t, kk, a_t, ALU.mult, [k_kk, k_at], [k_bt])
                    STT(t1, a_t, -1.0, ka_b, ALU.add, ALU.mult, [k_at, k_colv], [k_t1])
                    STT(kp, t1, 1.0, k_, ALU.add, ALU.mult, [k_t1, k_hs], [k_kp])
                    TT("dve", aq[:, :, 1, :], r_, eL, ALU.mult, [k_hs, k_eL], [k_aq])
                    TT("dve", aq[:, :, 0, :], kk, eLx, ALU.mult, [k_kk, k_eLx, k_aq], [k_aq])
                    TT("dve", btl, b_t, emL, ALU.mult, [k_bt, k_emL], [k_btl])
                    TT("dve", ktl, kp, emL, ALU.mult, [k_kp, k_emL], [k_ktl])
                    CP("act", vb, v_, [k_hs], [k_vb])
                    for (src3, ksrc, dst3, kdst, sel) in ((aq, k_aq, atm, k_atm, 0), (btl, k_btl, btm, k_btm, None), (ktl, k_ktl, ktm, k_ktm, None), (vb, k_vb, vtm, k_vtm, None)):
                        pb, kpb = bank()
                        pbb = pb[:].bitcast(BF16)
                        for blk in range(6):
                            sin = src3[:, blk, 0, :] if sel is not None else src3[:, blk, :]
                            TR(pbb[:, blk * 128:(blk + 1) * 128], sin, identb, [ksrc, k_idb], [kpb])
                        CP("dve", dst3, pbb[:, 0:768].rearrange("p (a b) -> p a b", a=6), [kpb], [kdst])
                    def hsl_(hd):
                        return hd // 2, slice((hd % 2) * 64, (hd % 2) * 64 + 64)
                    P0, k_P0 = Pp[0]
                    R0, k_R0 = Rb[0]
                    for hd in range(12):
                        blk, hp = hsl_(hd)
                        pA, kpA = bank()
                        MM(pA[:, 0:256].rearrange("p (a b) -> p a b", a=2), btl[hp, blk, :], aq[hp, blk, :, :], True, True, [k_btl, k_aq], [kpA])
                        MM(pA[:, 256:512].rearrange("p (a b) -> p a b", a=2), ktl[hp, blk, :], aq[hp, blk, :, :], True, True, [k_ktl, k_aq], [kpA])
                        TT("dve", AT[:, hd, :, :], pA[:, 0:512].rearrange("p (a b) -> p a b", a=4), m4t, ALU.mult, [kpA, k_m4], [k_AT + f"{hd}"])
                        STT(P0[:, hd, 1, :], pA[:, 0:128], -1.0, m4t[:, 0, :], ALU.mult, ALU.mult, [kpA, k_m4], [k_P0 + f"t{hd}"])
                        pL, kpL = bank()
                        MM(pL[:, 0:128], aq[hp, blk, 0, :], btl[hp, blk, :], True, True, [k_aq, k_btl], [kpL])
                        STT(P0[:, hd, 0, :], pL[:, 0:128], -1.0, mL, ALU.mult, ALU.mult, [kpL, k_cf], [k_P0 + f"n{hd}"])
                    for hd in range(12):
                        blk, hp = hsl_(hd)
                        pR, kpR = bank()
                        MM(pR[:, 0:64], AT[:, hd, 2, :], vtm[:, blk, hp], True, True, [k_AT + f"{hd}", k_vtm], [kpR])
                        CP("dve", R0[:, hd, 0:64], atm[:, blk, hp], [k_atm], [k_R0 + f"a{hd}"])
                        CP("dve", R0[:, hd, 64:128], pR[:, 0:64], [kpR], [k_R0 + f"b{hd}"])
                    for lv in range(7):
                        Pc, k_Pc = Pp[lv % 2]
                        Pn_, k_Pn = Pp[(lv + 1) % 2]
                        Rc, k_Rc = Rb[lv % 2]
                        Rn, k_Rn = Rb[(lv + 1) % 2]
                        for hd in range(12):
                            kP = [k_Pc + f"t{hd}", k_Pc + f"n{hd}"]
                            kR = [k_Rc + f"a{hd}", k_Rc + f"b{hd}"]
                            pD, kpD = bank()
                            MM(pD[:, 0:128], Pc[:, hd, 1, :], Rc[:, hd, :], True, True, kP + kR, [kpD])
                            if lv < 6:
                                MM(pD[:, 128:256], Pc[:, hd, 1, :], Pc[:, hd, 0, :], True, True, kP, [kpD])
                                MM(pD[:, 256:384], Pc[:, hd, 0, :], Pc[:, hd, 1, :], True, True, kP, [kpD])
                            TT("dve", Rn[:, hd, :], pD[:, 0:128], Rc[:, hd, :], ALU.add, [kpD] + kR, [k_Rn + f"a{hd}", k_Rn + f"b{hd}"])
                            if lv < 6:
                                CP("dve", Pn_[:, hd, :, :], pD[:, 128:384].rearrange("p (a b) -> p a b", a=2), [kpD], [k_Pn + f"n{hd}", k_Pn + f"t{hd}"])
                    Rf, k_Rf = Rb[1]
                    for hd in range(12):
                        blk, hp = hsl_(hd)
                        kRf = [k_Rf + f"a{hd}", k_Rf + f"b{hd}"]
                        TS("dve", nU[:, hd, :], Rf[:, hd, 64:128], -1.0, ALU.mult, kRf, [k_nU + f"{hd}"])
                        pE, kpE = bank()
                        MM(pE[hp, 0:64], Rf[:, hd, 0:64], btm[:, blk, hp], True, True, kRf + [k_btm], [kpE])
                        MM(pE[hp, 128:256], Rf[:, hd, 0:64], AT[:, hd, 1, :], True, True, kRf + [k_AT + f"{hd}"], [kpE])
                        MM(pE[hp, 64:128], ktm[:, blk, hp], vtm[:, blk, hp], True, False, [k_ktm, k_vtm], [kpE])
                        MM(pE[hp, 64:128], btm[:, blk, hp], nU[:, hd, :], False, True, [k_btm, k_nU + f"{hd}"], [kpE])
                        TT("dve", IXb[hp, blk, :], c_f32[hp, 640:704], pE[hp, 0:64], ALU.subtract, [k_cf, kpE], [k_IXb + f"{hd}"])
                        TT("dve", QhT[hp, blk, :], aq[hp, blk, 1, :], pE[hp, 128:256], ALU.subtract, [k_aq, kpE], [k_QhT + f"{hd}"])
                        CP("dve", Gp[hp, blk, :], pE[hp, 64:128], [kpE], [k_Gp + f"{hd}"])
                    for hd in range(12):
                        blk, hp = hsl_(hd)
                        pY, kY = (pY0, kY0) if hd < 8 else (pY1, kY1)
                        yc0 = (hd % 8) * 64
                        MM(pY[:, yc0:yc0 + 64], AT[:, hd, 3, :], vtm[:, blk, hp], True, False, [k_AT + f"{hd}", k_vtm], [kY])
                        MM(pY[:, yc0:yc0 + 64], AT[:, hd, 1, :], nU[:, hd, :], False, False, [k_AT + f"{hd}", k_nU + f"{hd}"], [kY])
                        MM(pY[:, yc0:yc0 + 64], QhT[hp, blk, :], STb[hp, blk, :], False, True, [k_QhT + f"{hd}", k_STb], [kY])
                        MM(pZ[hp, blk * 64:(blk + 1) * 64], IXb[hp, blk, :], STb[hp, blk, :], True, True, [k_IXb + f"{hd}", k_STb], [kZ])
                    kGp_all = [k_Gp + f"{i}" for i in range(12)]
                    TT("dve", ztmp, pZ[:, 0:384].rearrange("p (a b) -> p a b", a=6), Gp, ALU.add, [kZ] + kGp_all, [k_ztmp])
                    TT("dve", ST, ztmp, wtot.unsqueeze(2).to_broadcast([128, 6, 64]), ALU.mult, [k_ztmp, k_wtot], [k_ST])
                    CP("act", STb, ST, [k_ST], [k_STb])
                    CP("act", ysb[:, 0:512], pY0[:, 0:512], [kY0], [k_ysb + "0"])
                    CP("dve", ysb[:, 512:768], pY1[:, 0:256], [kY1], [k_ysb + "1"])
                    k_ysb_all = [k_ysb + "0", k_ysb + "1"]
                    if d_ == 0:
                        DMA("sp", YF[t0:t0 + C, :], ysb, k_ysb_all, [f"YF_{c}"])
                        continue
                    DMA("sp", yf_t, YF[t0:t0 + C, :], [f"YF_{c}"], [k_yf])
                    DMA("sp", gate2, G[0:6, :, t0:t0 + C].rearrange("b p t -> p b t"), [], [k_gate2])
                    TT("dve", ysb, ysb, yf_t, ALU.add, k_ysb_all + [k_yf], k_ysb_all)
                    y3 = ysb.rearrange("p (h n) -> p h n", h=12)
                    yc3 = yc.rearrange("p (h n) -> p h n", h=12)
                    ysq3 = ysq.rearrange("p (h n) -> p h n", h=12)
                    mu = st12[:, 0, :]
                    var = st12[:, 1, :]
                    rstd = st12[:, 2, :]
                    P.op("dve", (lambda o, i: (lambda e: e.tensor_reduce(out=o, in_=i, op=ALU.add, axis=mybir.AxisListType.X)))(mu, y3), k_ysb_all, [k_st12 + "m"])
                    TS("dve", mu, mu, 1.0 / 64, ALU.mult, [k_st12 + "m"], [k_st12 + "m"])
                    TT("dve", yc3, y3, mu.unsqueeze(2).to_broadcast([128, 12, 64]), ALU.subtract, k_ysb_all + [k_st12 + "m"], [k_yc])
                    TT("dve", ysq, yc, yc, ALU.mult, [k_yc], [k_ysq])
                    P.op("dve", (lambda o, i: (lambda e: e.tensor_reduce(out=o, in_=i, op=ALU.add, axis=mybir.AxisListType.X)))(var, ysq3), [k_ysq], [k_st12 + "v"])
                    ACT(rstd, var, AF.Ln, [k_st12 + "v", k_ceps], [k_st12 + "r"], scale=1.0 / 64, bias=c_gneps)
                    ACT(rstd, rstd, AF.Exp, [k_st12 + "r"], [k_st12 + "r"], scale=-0.5)
                    TT("dve", yc3, yc3, rstd.unsqueeze(2).to_broadcast([128, 12, 64]), ALU.mult, [k_yc, k_st12 + "r"], [k_yc])
                    TT("dve", yc, yc, gnv_b[:, 0, :], ALU.mult, [k_yc, k_gnv], [k_yc])
                    TT("dve", yc, yc, gnv_b[:, 1, :], ALU.add, [k_yc, k_gnv], [k_yc])
                    TT("dve", t1, r_, k_, ALU.mult, [k_hs], [k_t1])
                    TT("dve", prodb, t1, rk_b, ALU.mult, [k_t1, k_colv], [k_prodb])
                    pB, kpB = bank()
                    for blk in range(6):
                        MM(pB[:, blk * 2:(blk + 1) * 2], prodb[:, blk, :], bselb, True, True, [k_prodb, k_bsel], [kpB])
                    CP("dve", bon, pB[:, 0:12], [kpB], [k_bon])
                    TT("dve", ysq3, vtm.rearrange("p b (h n) -> p (b h) n", h=2), bon.unsqueeze(2).to_broadcast([128, 12, 64]), ALU.mult, [k_vtm, k_bon], [k_ysq])
                    TT("dve", yc, yc, ysq, ALU.add, [k_yc, k_ysq], [k_yc])
                    pT0, kpT0 = bank()
                    pT1, kpT1 = bank()
                    for blk in range(6):
                        tgt, ktgt = (pT0, kpT0) if blk < 4 else (pT1, kpT1)
                        cc = (blk % 4) * 128
                        TR(tgt[:, cc:cc + 128], yc[:, blk * 128:(blk + 1) * 128], ident_f, [k_yc, k_cf], [ktgt])
                    TT("dve", mixo[:, 0:4, :], pT0[:, 0:512].rearrange("p (a b) -> p a b", a=4), gate2[:, 0:4, :], ALU.mult, [kpT0, k_gate2], [k_mixo + "0"])
                    TT("dve", mixo[:, 4:6, :], pT1[:, 0:256].rearrange("p (a b) -> p a b", a=2), gate2[:, 4:6, :], ALU.mult, [kpT1, k_gate2], [k_mixo + "1"])
                    DMA("sp", MIX[0:6, :, t0:t0 + C].rearrange("b p t -> p b t"), mixo, [k_mixo + "0", k_mixo + "1"], [f"MIXr_{c}"])
            P.barrier()
            nb_cfg[0] = 8
            AR.release(m2)

        if 3 in passes:
            m3 = AR.mark()
            nb_cfg[0] = 6
            KT_sb, k_KT = AR.alloc((2, T), BF16)
            V_sb, k_V = AR.alloc((NT, 256), BF16)
            for hk in range(2):
                for tq in range(0, T, 2048):
                    nq = min(2048, T - tq)
                    DMA("sp", KT_sb[:, hk, tq:tq + nq], KTd[hk, :, tq:tq + nq], [], [k_KT + f"_{hk}_{tq}"])
            k_KT_all = [k_KT + f"_{hk}_{tq}" for hk in range(2) for tq in range(0, T, 2048)]
            for tq in range(0, NT, 16):
                nq = min(16, NT - tq)
                DMA("sp", V_sb[:, tq:tq + nq, :], Vd[tq * 128:(tq + nq) * 128, :].rearrange("(t p) c -> p t c", p=128), [], [k_V + f"_{tq}"])
            k_V_all = [k_V + f"_{tq}" for tq in range(0, NT, 16)]
            qT3, k_qT3 = AR.alloc((6, GRP), BF16)
            g3, k_g3 = AR.alloc((6, GRP), BF16)
            PT3 = [AR.alloc((GRP,), BF16) for _ in range(2)]
            rs3, k_rs3 = AR.alloc((GRP,), F32)
            y3, k_y3 = AR.alloc((GRP,), F32)
            o3 = [AR.alloc((GRP,), BF16) for _ in range(2)]
            pO, kpO = psb[6], "ps6"
            pS, kpS = psb[7], "ps7"
            for qg in range(NG):
                t0 = qg * GRP
                qseg = t0 // SEG
                for hq in range(6):
                    DMA("sp", qT3[:, hq, :], QT[hq, :, t0:t0 + GRP], [], [k_qT3 + f"h{hq}"])
                    DMA("sp", g3[:, hq, :], G[6 + hq, :, t0:t0 + GRP], [], [k_g3 + f"h{hq}"])
                for hq in range(6):
                    kvh = hq // 3
                    for kt in range(NT):
                        kseg = (kt * 128) // SEG
                        PT_ap, k_PT = PT3[kt % 2]
                        pb, kpb = bank()
                        MM(pb[:, 0:GRP], KT_sb[:, kvh, kt * 128:(kt + 1) * 128], qT3[:, hq, :], True, True, k_KT_all + [k_qT3 + f"h{hq}"], [kpb])
                        ACT(PT_ap, pb[:, 0:GRP], AF.Exp, [kpb, k_flag], [k_PT], scale=ATT_SCALE,
                            bias=c_flag[:, 8 + qseg * NSEG + kseg:8 + qseg * NSEG + kseg + 1])
                        MM(pO[:, 0:GRP], V_sb[:, kt, kvh * 128:(kvh + 1) * 128], PT_ap, kt == 0, kt == NT - 1, k_V_all + [k_PT], [kpO])
                        MM(pS[:, 0:GRP], onesb, PT_ap, kt == 0, kt == NT - 1, [k_onesb, k_PT], [kpS])
                    P.op("dve", (lambda o, i: (lambda e: e.reciprocal(out=o, in_=i)))(rs3, pS[:, 0:GRP]), [kpS], [k_rs3])
                    TT("dve", y3, pO[:, 0:GRP], rs3, ALU.mult, [kpO, k_rs3], [k_y3])
                    o_ap, k_o = o3[hq % 2]
                    TT("dve", o_ap, y3, g3[:, hq, :], ALU.mult, [k_y3, k_g3 + f"h{hq}"], [k_o])
                    DMA("sp", MIX[6 + hq, :, t0:t0 + GRP], o_ap, [k_o], [f"MIXa{hq}_{qg}"])
            P.barrier()
            nb_cfg[0] = 8
            AR.release(m3)

        if 4 in passes:
            m4 = AR.mark()
            fg_b, k_fg = AR.alloc((D,), F32)
            DMA("sp", fg_b, rowv[:, 1, :], [], [k_fg])
            wo_sb, k_wo = AR.alloc((16, D), BF16)
            for i in range(4):
                DMA("sp", wo_sb[:, i * 4:(i + 1) * 4, :], WO.rearrange("(p k) n -> p k n", k=16)[:, i * 4:(i + 1) * 4, :], [], [k_wo + f"_{i}"])
            k_wo_all = [k_wo + f"_{i}" for i in range(4)]
            mixt = [AR.alloc((16, 128), BF16) for _ in range(2)]
            x4 = [AR.alloc((D,), F32) for _ in range(2)]
            r4 = [AR.alloc((D,), F32) for _ in range(2)]
            y4 = [AR.alloc((D,), F32) for _ in range(2)]
            junk4, k_junk4 = AR.alloc((D,), BF16)
            st4, k_st4 = AR.alloc((2, 2), F32)
            for tt in range(NT):
                tok = tt * 128
                m_ap, k_m = mixt[tt % 2]
                x_ap, k_x = x4[tt % 2]
                r_ap, k_r = r4[tt % 2]
                y_ap, k_y = y4[tt % 2]
                DMA("sp", m_ap, MIX[:, :, tok:tok + 128].rearrange("c p t -> p c t"), [], [k_m])
                DMA("sp", x_ap, xs[tok:tok + 128, :], [], [k_x])
                for ng in range(4):
                    pb, kpb = bank()
                    for kc in range(16):
                        MM(pb[:, 0:512], m_ap[:, kc, :], wo_sb[:, kc, ng * 512:(ng + 1) * 512], kc == 0, kc == 15, [k_m] + k_wo_all, [kpb])
                    TT("dve", r_ap[:, ng * 512:(ng + 1) * 512], pb[:, 0:512], x_ap[:, ng * 512:(ng + 1) * 512], ALU.add, [kpb, k_x], [k_r + f"_{ng}"])
                kr_all = [k_r + f"_{ng}" for ng in range(4)]
                ssq = st4[:, tt % 2, 0:1]
                rstd = st4[:, tt % 2, 1:2]
                ks1 = k_st4 + f"a{tt % 2}"
                ks2 = k_st4 + f"b{tt % 2}"
                ACT(junk4, r_ap, AF.Square, kr_all, [k_junk4, ks1], accum_out=ssq)
                ACT(rstd, ssq, AF.Ln, [ks1, k_ceps], [ks2], scale=1.0 / D, bias=c_eps)
                ACT(rstd, rstd, AF.Exp, [ks2], [ks2], scale=-0.5)
                STT(y_ap, r_ap, rstd, fg_b, ALU.mult, ALU.mult, kr_all + [ks2, k_fg], [k_y])
                DMA("sp", y_out[tok:tok + 128, :], y_ap, [k_y], [f"y_{tt}"])
            P.barrier()
            AR.release(m4)

        P.barrier()
        P.finalize()
        P.emit()
    return nc


def _rope_tab(nseg, seg, carry):
    if carry:
        pos = np.arange(nseg * seg)
    else:
        pos = np.tile(np.arange(seg), nseg)
    row = (pos // 64).astype(np.float32)
    col = (pos % 64).astype(np.float32)
    freqs = (np.float32(10000.0) ** (-np.arange(0, 64, 2, dtype=np.float32) / np.float32(64))).astype(np.float32)
    ang = np.concatenate([row[:, None] * freqs, col[:, None] * freqs], axis=-1).astype(np.float32)
    return np.stack([np.cos(ang), np.sin(ang)], axis=1).astype(np.float32)


def _consts():
    c = np.zeros((128, 1024), np.float32)
    r = np.arange(128)[:, None]
    q = np.arange(128)[None, :]
    c[:, 0:128] = (r == q)
    c[:, 128:256] = (q > r)
    c[:, 256:384] = (q >= r)
    c[:, 384:512] = (q < r)
    c[:, 512:640] = (q <= r)
    c[:, 640:704] = (np.arange(64)[None, :] == (r % 64))
    c[:, 704:706] = (np.arange(2)[None, :] == (r // 64))
    c[:, 706:834] = ((r // 64) == (q // 64))
    return c


def shared_inputs(norm_g, w_in, mu_prev, mu_next, w0, w_up, a0, a_up, k_k, k_a, r_k, gn_g, gn_b,
                  q_norm_g, k_norm_g, mem_norm_g, w_mem_kv, w_out, final_g):
    f = lambda a: np.ascontiguousarray(np.asarray(a, dtype=np.float32))
    w_in = f(w_in)[0]
    wt = w_in.reshape(16, 128, NCB, 128)[:, :, CB_PERM, :]
    w_in_t = np.ascontiguousarray(wt.transpose(2, 1, 0, 3)).reshape(NCB * 128, D)
    w_out_t = np.ascontiguousarray(f(w_out)[0].reshape(16, 128, D).transpose(1, 0, 2)).reshape(128 * 16, D)
    w_kv_t = np.ascontiguousarray(f(w_mem_kv)[0].reshape(16, 128, 1024).transpose(1, 0, 2)).reshape(128 * 16, 1024)
    rowv = np.ascontiguousarray(np.broadcast_to(np.stack([f(norm_g)[0], f(final_g), f(mem_norm_g)[0]])[None], (128, 3, D)))
    qkg = np.ascontiguousarray(np.broadcast_to(np.stack([f(q_norm_g)[0], f(k_norm_g)[0]])[None], (128, 2, 128)))
    gnv = np.ascontiguousarray(np.broadcast_to(np.stack([f(gn_g)[0], f(gn_b)[0]])[None], (128, 2, 768)))
    colv = np.zeros((128, 96), np.float32)
    colv[:, 0:20] = f(mu_prev)[0].reshape(20, 128).T
    colv[:, 20:40] = f(mu_next)[0].reshape(20, 128).T
    colv[:, 40:52] = f(w0)[0].reshape(12, 128).T
    colv[:, 52:64] = f(a0)[0].reshape(12, 128).T
    colv[:, 64:70] = f(k_k)[0].reshape(6, 128).T
    colv[:, 70:76] = f(k_a)[0].reshape(6, 128).T
    colv[:, 76:82] = f(r_k)[0].reshape(6, 128).T
    lora_up = np.ascontiguousarray(np.stack([f(w_up)[0].reshape(128, 768), f(a_up)[0].reshape(128, 768)], axis=1))
    return dict(w_in_t=w_in_t, w_out_t=w_out_t, w_kv_t=w_kv_t, rowv=rowv, qkg=qkg, gnv=gnv, colv=colv,
                lora_up=lora_up, consts=_consts())


def core_inputs(shared, x_core, mem_core, nseg, seg, carry):
    flags = np.zeros((128, 32), np.float32)
    flags[:, 0] = 1.0 if carry else 0.0
    for qs in range(nseg):
        for ks in range(nseg):
            flags[:, 8 + qs * nseg + ks] = 0.0 if (carry or qs == ks) else NEG
    d = dict(shared)
    d["xs"] = np.ascontiguousarray(x_core, dtype=np.float32)
    d["mem"] = np.ascontiguousarray(mem_core, dtype=np.float32)
    d["cs_tab"] = _rope_tab(nseg, seg, carry)
    d["flags"] = flags
    return d


_NC_CACHE = {}


def kernel(x_prompt, x_sample, mem_prompt, mem_sample, norm_g, w_in, mu_prev, mu_next, w0, w_up, a0, a_up,
           k_k, k_a, r_k, gn_g, gn_b, q_norm_g, k_norm_g, mem_norm_g, w_mem_kv, w_out, final_g):
    NSEG, SEG = 4, 2048
    x_prompt = np.asarray(x_prompt, dtype=np.float32)
    x_sample = np.asarray(x_sample, dtype=np.float32)
    mem_prompt = np.asarray(mem_prompt, dtype=np.float32)
    mem_sample = np.asarray(mem_sample, dtype=np.float32)
    shared = shared_inputs(norm_g, w_in, mu_prev, mu_next, w0, w_up, a0, a_up, k_k, k_a, r_k, gn_g, gn_b,
                           q_norm_g, k_norm_g, mem_norm_g, w_mem_kv, w_out, final_g)
    in_maps = []
    for c in range(4):
        in_maps.append(core_inputs(shared, x_prompt[c], np.broadcast_to(mem_prompt[c][None], (NSEG, N_MEM, D)), NSEG, SEG, True))
    for c in range(4):
        in_maps.append(core_inputs(shared, x_sample[4 * c:4 * c + 4].reshape(NSEG * SEG, D), mem_sample[4 * c:4 * c + 4], NSEG, SEG, False))
    key = (NSEG, SEG)
    if key not in _NC_CACHE:
        _NC_CACHE[key] = build(NSEG, SEG)
    nc = _NC_CACHE[key]
    res = run_bass_kernel_spmd(nc, in_maps, core_ids=list(range(8)))
    yp = np.stack([np.asarray(res.results[c]["y"], dtype=np.float32) for c in range(4)])
    ysm = np.concatenate([np.asarray(res.results[4 + c]["y"], dtype=np.float32).reshape(4, SEG, D) for c in range(4)], axis=0)
    return (yp, ysm)
```

```python
import contextlib
import numpy as np
import ml_dtypes
import concourse.bass as bass
import concourse.mybir as mybir
from concourse.bass_utils import run_bass_kernel_spmd

F32 = mybir.dt.float32
BF16 = mybir.dt.bfloat16
ALU = mybir.AluOpType
AF = mybir.ActivationFunctionType

D = 2048
IN_W = 6400
NCB = 50
N_MEM = 256
NORM_EPS = 1e-6
GN_EPS = 64e-5
DECAY_K = float(np.exp(-0.5))
ATT_SCALE = 128 ** -0.5
NEG = -30000.0

CB_PERM = list(range(0, 20)) + list(range(20, 26)) + list(range(36, 42)) + list(range(46, 50)) + \
    list(range(42, 46)) + list(range(26, 32)) + [32, 33] + [34, 35]

ENGS = ("pe", "dve", "act", "pool", "sp")
SEM_EPOCH = 20000
import os as _osg
SERIAL = int(_osg.environ.get("K_SERIAL", "3"))


class Op:
    __slots__ = ("eng", "fn", "deps", "is_dma", "seq", "signal", "sig_idx", "dma_sem", "dma_val", "waits", "barrier")

    def __init__(self, eng, fn, is_dma):
        self.eng = eng
        self.fn = fn
        self.is_dma = is_dma
        self.deps = set()
        self.signal = False
        self.sig_idx = 0
        self.dma_sem = None
        self.dma_val = 0
        self.waits = []
        self.barrier = False


class Prog:
    def __init__(self, nc, n_dma_sems=8):
        self.nc = nc
        self.ops = []
        self.by_eng = {e: [] for e in ENGS}
        self.last_w = {}
        self.readers = {}
        self.n_dma_sems = n_dma_sems
        self.last_comp = None

    def _add(self, eng, fn, reads, writes, is_dma):
        op = Op(eng, fn, is_dma)
        op.seq = len(self.ops)
        for k in reads:
            w = self.last_w.get(k)
            if w is not None:
                op.deps.add(w)
        for k in writes:
            w = self.last_w.get(k)
            if w is not None:
                op.deps.add(w)
            rl = self.readers.get(k)
            if rl:
                op.deps.update(rl)
        for k in reads:
            self.readers.setdefault(k, []).append(op)
        for k in writes:
            self.last_w[k] = op
            self.readers[k] = []
        op.deps.discard(op)
        if SERIAL == 1 and self.ops and not self.ops[-1].barrier:
            op.deps.add(self.ops[-1])
        elif SERIAL == 2 and not is_dma:
            if self.last_comp is not None:
                op.deps.add(self.last_comp)
            self.last_comp = op
        elif SERIAL == 3 and not is_dma and eng != "pe":
            if self.last_comp is not None and self.last_comp.eng != eng:
                op.deps.add(self.last_comp)
            self.last_comp = op
        self.ops.append(op)
        self.by_eng[eng].append(op)
        return op

    def op(self, eng, fn, reads=(), writes=()):
        return self._add(eng, fn, reads, writes, False)

    def dma(self, eng, fn, reads=(), writes=()):
        return self._add(eng, fn, reads, writes, True)

    def barrier(self):
        tails = []
        for e in ENGS:
            comp = [o for o in self.by_eng[e] if not o.is_dma and not o.barrier]
            if comp:
                tails.append(comp[-1])
            dm = [o for o in self.by_eng[e] if o.is_dma]
            tails.extend(dm[-self.n_dma_sems:])
        for e in ENGS:
            op = Op(e, lambda eng: None, False)
            op.barrier = True
            op.seq = len(self.ops)
            op.deps = set(tails)
            self.ops.append(op)
            self.by_eng[e].append(op)
        self.last_w = {}
        self.readers = {}
        self.last_comp = None

    def finalize(self):
        for op in self.ops:
            for d in op.deps:
                if d.is_dma:
                    continue
                if d.eng == "pe" and op.eng == "pe" and not op.is_dma and not op.barrier:
                    continue
                d.signal = True
        self.n_sig = {}
        for e in ENGS:
            c = 0
            for op in self.by_eng[e]:
                if (not op.is_dma) and op.signal:
                    c += 1
                    op.sig_idx = c
            self.n_sig[e] = c
        for e in ENGS:
            k = 0
            slots = [None] * self.n_dma_sems
            counts = [0] * self.n_dma_sems
            for op in self.by_eng[e]:
                if op.is_dma:
                    s = k % self.n_dma_sems
                    prev = slots[s]
                    if prev is not None:
                        op.deps.add(prev)
                    counts[s] += 16
                    op.dma_sem = (e, s)
                    op.dma_val = counts[s]
                    slots[s] = op
                    k += 1
        for e in ENGS:
            wd = {}
            dma_waited = {}
            for op in self.by_eng[e]:
                need = {}
                for d in op.deps:
                    if d.is_dma:
                        key = d.dma_sem
                        if dma_waited.get(key, 0) < d.dma_val:
                            dma_waited[key] = d.dma_val
                            op.waits.append(("dma", key, d.dma_val))
                    else:
                        if d.eng == "pe" and op.eng == "pe" and not op.is_dma and not op.barrier:
                            continue
                        if d.sig_idx > need.get(d.eng, 0):
                            need[d.eng] = d.sig_idx
                for src, idx in need.items():
                    if wd.get(src, 0) < idx:
                        op.waits.append(("eng", src, idx))
                        wd[src] = idx

    def emit(self):
        nc = self.nc
        with contextlib.ExitStack() as st:
            sems = {}
            for e in ENGS:
                n_ep = (self.n_sig[e] + SEM_EPOCH - 1) // SEM_EPOCH
                for i in range(n_ep):
                    sems[("eng", e, i)] = st.enter_context(nc.semaphore(f"s_{e}_{i}"))
                used = sorted(set(op.dma_sem for op in self.by_eng[e] if op.is_dma))
                for key in used:
                    sems[("dma",) + key] = st.enter_context(nc.semaphore(f"d_{key[0]}_{key[1]}"))
            block = st.enter_context(nc.Block())
            engmap = {"pe": "tensor", "dve": "vector", "act": "scalar", "pool": "gpsimd", "sp": "sync"}

            def make(e):
                ops = self.by_eng[e]

                def body(engine):
                    for op in ops:
                        for w in op.waits:
                            if w[0] == "dma":
                                engine.wait_ge(sems[("dma",) + w[1]], w[2])
                            else:
                                idx = w[2]
                                ep = (idx - 1) // SEM_EPOCH
                                engine.wait_ge(sems[("eng", w[1], ep)], idx - ep * SEM_EPOCH)
                        ins = op.fn(engine)
                        if ins is None:
                            continue
                        if op.is_dma:
                            ins.then_inc(sems[("dma",) + op.dma_sem], 16)
                        elif op.signal:
                            ep = (op.sig_idx - 1) // SEM_EPOCH
                            ins.then_inc(sems[("eng", e, ep)], 1)
                return body

            for e in ENGS:
                if self.by_eng[e]:
                    getattr(block, engmap[e])(make(e))


class Arena:
    def __init__(self, tile, nbytes):
        self.tile = tile
        self.nbytes = nbytes
        self.off = 0
        self.cnt = 0

    def alloc(self, shape, dtype):
        esz = 4 if dtype == F32 else 2
        n = int(np.prod(shape))
        nb = n * esz
        self.off = (self.off + 63) // 64 * 64
        assert self.off + nb <= self.nbytes, f"arena overflow {self.off}+{nb}>{self.nbytes}"
        a = self.tile[:, self.off // 2:(self.off + nb) // 2]
        if dtype == F32:
            a = a.bitcast(F32)
        self.off += nb
        self.cnt += 1
        key = f"A{self.cnt}"
        if len(shape) == 2:
            a = a.rearrange("p (a b) -> p a b", a=shape[0])
        elif len(shape) == 3:
            a = a.rearrange("p (a b c) -> p a b c", a=shape[0], b=shape[1])
        elif len(shape) == 4:
            a = a.rearrange("p (a b c d) -> p a b c d", a=shape[0], b=shape[1], c=shape[2])
        return a, key

    def mark(self):
        return self.off

    def release(self, m):
        self.off = m


def build(NSEG, SEG, debug=False, passes=(1, 2, 3, 4)):
    T = NSEG * SEG
    NT = T // 128
    GRP = 512
    NG = T // GRP
    assert SEG % GRP == 0
    nc = bass.Bass("TRN2", target_bir_lowering=False)
    dt_in = lambda n, s, d=F32: nc.dram_tensor(n, list(s), d, kind="ExternalInput").ap()
    okind = "ExternalOutput" if debug else "Internal"
    dt_scr = lambda n, s, d: nc.dram_tensor(n, list(s), d, kind=okind).ap()

    xs = dt_in("xs", [T, D])
    mem = dt_in("mem", [NSEG, N_MEM, D])
    w_in_t = dt_in("w_in_t", [NCB * 128, D])
    w_out_t = dt_in("w_out_t", [128 * 16, D])
    w_kv_t = dt_in("w_kv_t", [128 * 16, 1024])
    rowv = dt_in("rowv", [128, 3, D])
    qkg = dt_in("qkg", [128, 2, 128])
    colv = dt_in("colv", [128, 96])
    gnv = dt_in("gnv", [128, 2, 768])
    lora_up = dt_in("lora_up", [128, 2, 768])
    cs_tab = dt_in("cs_tab", [T, 2, 64])
    consts = dt_in("consts", [128, 1024])
    flags = dt_in("flags", [128, 32])
    y_out = nc.dram_tensor("y", [T, D], F32, kind="ExternalOutput").ap()

    W1 = dt_scr("W1", [NCB * 128, D], BF16)
    WO = dt_scr("WO", [128 * 16, D], BF16)
    WKV = dt_scr("WKV", [128 * 16, 1024], BF16)
    PR = dt_scr("PR", [20, 128, T + 2], F32)
    G = dt_scr("G", [12, 128, T], BF16)
    MIX = dt_scr("MIX", [16, 128, T], BF16)
    QT = dt_scr("QT", [6, 128, T], BF16)
    KTd = dt_scr("KTd", [2, 128, T], BF16)
    Vd = dt_scr("Vd", [T, 256], BF16)
    YF = dt_scr("YF", [T, 768], F32)

    with contextlib.ExitStack() as top:
        ARENA_BYTES = 200 * 1024
        arena_t = top.enter_context(nc.sbuf_tensor("arena", [128, ARENA_BYTES // 2], BF16))
        AR = Arena(arena_t, ARENA_BYTES)
        psb = [top.enter_context(nc.psum_tensor(f"psb{i}", [128, 512], F32)) for i in range(8)]
        P = Prog(nc)
        ps_rr = [0]

        import os as _os0
        _NB = int(_os0.environ.get("K_NB", "8"))

        nb_cfg = [_NB]

        def bank():
            i = ps_rr[0] % nb_cfg[0]
            ps_rr[0] += 1
            return psb[i], f"ps{i}"

        def bank2():
            if ps_rr[0] % 2:
                ps_rr[0] += 1
            i = ps_rr[0] % 8
            ps_rr[0] += 2
            return psb[i], psb[i + 1], f"ps{i}", f"ps{i + 1}"

        ev_rr = [0]

        def evac_eng():
            return "dve"

        def copy_op(eng, out, in_, reads, writes):
            if eng == "act":
                P.op("act", lambda e: e.activation(out=out, in_=in_, func=AF.Copy), reads, writes)
            else:
                P.op(eng, lambda e: e.tensor_copy(out=out, in_=in_), reads, writes)

        def MM(out, lhsT, rhs, start, stop, reads, writes):
            P.op("pe", lambda e: e.matmul(out, lhsT=lhsT, rhs=rhs, start=start, stop=stop), reads, writes)

        def TR(out, in_, ident, reads, writes):
            P.op("pe", lambda e: e.transpose(out, in_, ident), reads, writes)

        def ACT(out, in_, func, reads, writes, scale=None, bias=None, accum_out=None):
            kw = {}
            if scale is not None:
                kw["scale"] = scale
            if bias is not None:
                kw["bias"] = bias
            if accum_out is not None:
                kw["accum_out"] = accum_out
            P.op("act", lambda e: e.activation(out=out, in_=in_, func=func, **kw), reads, writes)

        def TT(eng, out, in0, in1, op, reads, writes):
            P.op(eng, lambda e: e.tensor_tensor(out=out, in0=in0, in1=in1, op=op), reads, writes)

        def TS(eng, out, in0, s1, op0, reads, writes, s2=None, op1=None):
            if op1 is None:
                P.op(eng, lambda e: e.tensor_scalar(out=out, in0=in0, scalar1=s1, scalar2=None, op0=op0), reads, writes)
            else:
                P.op(eng, lambda e: e.tensor_scalar(out=out, in0=in0, scalar1=s1, scalar2=s2, op0=op0, op1=op1), reads, writes)

        def STT(out, in0, scalar, in1, op0, op1, reads, writes):
            P.op("dve", lambda e: e.scalar_tensor_tensor(out=out, in0=in0, scalar=scalar, in1=in1, op0=op0, op1=op1), reads, writes)

        def DMA(eng, out, in_, reads, writes, slow=False):
            if slow:
                P.dma(eng, lambda e: e.dma_start(out=out, in_=in_, allow_slow_non_contiguous=True), reads, writes)
            else:
                P.dma(eng, lambda e: e.dma_start(out=out, in_=in_), reads, writes)

        def CP(eng, out, in_, reads, writes):
            if eng == "act":
                P.op("act", lambda e: e.activation(out=out, in_=in_, func=AF.Copy), reads, writes)
            else:
                P.op(eng, lambda e: e.tensor_copy(out=out, in_=in_), reads, writes)

        def MEMSET(eng, out, val, reads, writes):
            P.op(eng, lambda e: e.memset(out, val), reads, writes)

        c_f32, k_cf = AR.alloc((1024,), F32)
        c_flag, k_flag = AR.alloc((32,), F32)
        c_colv, k_colv = AR.alloc((96,), F32)
        identb, k_idb = AR.alloc((128,), BF16)
        onesb, k_onesb = AR.alloc((128,), BF16)
        c_eps_t, k_ceps = AR.alloc((4,), F32)
        DMA("sp", c_f32, consts, [], [k_cf])
        DMA("sp", c_flag, flags, [], [k_flag])
        DMA("sp", c_colv, colv, [], [k_colv])
        ident_f = c_f32[:, 0:128]
        CP("dve", identb, ident_f, [k_cf], [k_idb])
        MEMSET("dve", onesb, 1.0, [], [k_onesb])
        MEMSET("dve", c_eps_t[:, 0:1], NORM_EPS, [], [k_ceps])
        MEMSET("dve", c_eps_t[:, 1:2], GN_EPS, [k_ceps], [k_ceps])
        MEMSET("dve", c_eps_t[:, 2:3], 1e-30, [k_ceps], [k_ceps])
        c_eps = c_eps_t[:, 0:1]
        c_gneps = c_eps_t[:, 1:2]
        c_tiny = c_eps_t[:, 2:3]

        import os as _os
        _skip = _os.environ.get("K_SKIP", "")
        if "cast" not in _skip:
            for i in range(NCB):
                DMA("pool", W1[i * 128:(i + 1) * 128, :], w_in_t[i * 128:(i + 1) * 128, :], [], [f"W1_{i}"])
            for i in range(16):
                DMA("pool", WO[i * 128:(i + 1) * 128, :], w_out_t[i * 128:(i + 1) * 128, :], [], [f"WO{i}"])
            for i in range(16):
                DMA("pool", WKV[i * 128:(i + 1) * 128, :], w_kv_t[i * 128:(i + 1) * 128, :], [], [f"WKV{i}"])
        zt, k_zt = AR.alloc((20, 1), F32)
        MEMSET("pool", zt, 0.0, [], [k_zt])
        if "pad" not in _skip:
            DMA("sp", PR[:, :, 0:1].rearrange("b p o -> p b o"), zt, [k_zt], ["PRpad0"], slow=True)
            DMA("sp", PR[:, :, T + 1:T + 2].rearrange("b p o -> p b o"), zt, [k_zt], ["PRpad1"], slow=True)
        if "mixz" not in _skip:
            zb, k_zb = AR.alloc((2048,), BF16)
            MEMSET("pool", zb, 0.0, [], [k_zb])
            for cbz in range(12):
                for tz in range(0, T, 2048):
                    nz = min(2048, T - tz)
                    DMA("sp", MIX[cbz, :, tz:tz + nz], zb[:, 0:nz], [k_zb], [f"MIXz{cbz}_{tz}"])
        P.barrier()

        def rmsnorm_tile(x_ap, k_x, g_ap, k_g, h_ap, k_h, junk, k_junk, ssq, k_ssq, rstd, k_rstd):
            ACT(junk, x_ap, AF.Square, [k_x], [k_junk, k_ssq], accum_out=ssq)
            ACT(rstd, ssq, AF.Ln, [k_ssq, k_ceps], [k_rstd], scale=1.0 / D, bias=c_eps)
            ACT(rstd, rstd, AF.Exp, [k_rstd], [k_rstd], scale=-0.5)
            STT(h_ap, x_ap, rstd, g_ap, ALU.mult, ALU.mult, [k_x, k_rstd, k_g], [k_h])

        if 1 in passes:
            m1 = AR.mark()
            ng_b, k_ng = AR.alloc((D,), F32)
            qkg_b, k_qkg = AR.alloc((2, 128), F32)
            DMA("sp", ng_b, rowv[:, 0, :], [], [k_ng])
            DMA("sp", qkg_b, qkg, [], [k_qkg])
            mkT, k_mkT = AR.alloc((NSEG, 4, 256), BF16)
            mv, k_mv = AR.alloc((NSEG, 2, 512), BF16)

            mkv_mark = AR.mark()
            mng_b, k_mng = AR.alloc((D,), F32)
            DMA("sp", mng_b, rowv[:, 2, :], [], [k_mng])
            wkv_sb, k_wkv = AR.alloc((16, 1024), BF16)
            DMA("sp", wkv_sb, WKV.rearrange("(p k) n -> p k n", k=16), [f"WKV{i}" for i in range(16)], [k_wkv])
            mx, k_mx = AR.alloc((D,), F32)
            mh, k_mh = AR.alloc((D,), BF16)
            mjunk, k_mjunk = AR.alloc((D,), BF16)
            mss, k_mss = AR.alloc((2,), F32)
            mhT, k_mhT = AR.alloc((16, 256), BF16)
            for s in range(NSEG):
                for mt in range(2):
                    DMA("sp", mx, mem[s, mt * 128:(mt + 1) * 128, :], [], [k_mx])
                    if "mkvn" in _skip:
                        continue
                    rmsnorm_tile(mx, k_mx, mng_b, k_mng, mh, k_mh, mjunk, k_mjunk, mss[:, 0:1], k_mss, mss[:, 1:2], k_mss + "b")
                    for half in range(2):
                        if "mkvt" in _skip:
                            continue
                        pb, kpb = bank()
                        pbb = pb[:].bitcast(BF16)
                        for j in range(8):
                            kc = half * 8 + j
                            TR(pbb[:, j * 128:(j + 1) * 128], mh[:, kc * 128:(kc + 1) * 128], identb, [k_mh, k_idb], [kpb])
                        CP(evac_eng(), mhT[:, half * 8:(half + 1) * 8, mt * 128:(mt + 1) * 128],
                           pbb[:, 0:1024].rearrange("p (a b) -> p a b", a=8), [kpb], [k_mhT])
                if "mkvm" in _skip:
                    continue
                for hd in range(4):
                    if "mkva" in _skip:
                        continue
                    pb, kpb = bank()
                    for kc in range(16):
                        MM(pb[:, 0:256], wkv_sb[:, kc, hd * 128:(hd + 1) * 128], mhT[:, kc, :], kc == 0, kc == 15, [k_wkv, k_mhT], [kpb])
                    CP(evac_eng(), mkT[:, s, hd, :], pb[:, 0:256], [kpb], [k_mkT])
                P.barrier()
                for mt in range(2):
                    if "mkvb" in _skip:
                        continue
                    pb, kpb = bank()
                    for kc in range(16):
                        MM(pb[:, 0:512], mhT[:, kc, mt * 128:(mt + 1) * 128], wkv_sb[:, kc, 512:1024], kc == 0, kc == 15, [k_wkv, k_mhT], [kpb])
                    CP(evac_eng(), mv[:, s, mt, :], pb[:, 0:512], [kpb], [k_mv])
            P.barrier()
            AR.release(mkv_mark)

            xt = [AR.alloc((D,), F32) for _ in range(2)]
            ht = [AR.alloc((D,), BF16) for _ in range(2)]
            junk, k_junk = AR.alloc((D,), BF16)
            st_small, k_sts = AR.alloc((2, 2), F32)
            hT = [AR.alloc((16, GRP), BF16) for _ in range(2)]
            wblk = [AR.alloc((4, 16, 128), BF16) for _ in range(3)]
            stg32 = [AR.alloc((GRP,), F32) for _ in range(3)]
            stg16 = [AR.alloc((GRP,), BF16) for _ in range(3)]
            mqT, k_mqT = AR.alloc((4, GRP), BF16)
            mg, k_mg = AR.alloc((4, GRP), BF16)
            PTm = [AR.alloc((2, GRP), BF16) for _ in range(2)]
            rs_t, k_rs = AR.alloc((GRP,), F32)
            ym_t, k_ym = AR.alloc((GRP,), F32)
            qn, k_qn = AR.alloc((8, 128), F32)
            qsq, k_qsq = AR.alloc((8, 128), F32)
            qss, k_qss = AR.alloc((8,), F32)
            qrs, k_qrs = AR.alloc((8,), F32)
            rp1, k_rp1 = AR.alloc((8, 64), F32)
            rp2, k_rp2 = AR.alloc((8, 64), F32)
            qr, k_qr = AR.alloc((8, 128), BF16)
            cst = [AR.alloc((4, 2, 64), F32) for _ in range(1)]
            qTs = [AR.alloc((6, GRP), BF16) for _ in range(1)]
            kTs = [AR.alloc((2, GRP), BF16) for _ in range(1)]
            vst = [AR.alloc((4, 256), BF16) for _ in range(1)]
            wl_rr = [0]
            s32_rr = [0]
            s16_rr = [0]
            tile_ctr = [0]
            loads = [(b, min(4, NCB - b)) for b in range(0, NCB, 4)]

            for g in range(NG):
                if "main" in _skip:
                    break
                t0 = g * GRP
                seg = t0 // SEG
                hTg, k_hT = hT[g % 2]
                for ti in range(4):
                    tt = tile_ctr[0]
                    tile_ctr[0] += 1
                    x_ap, k_x = xt[tt % 2]
                    h_ap, k_h = ht[tt % 2]
                    tok = t0 + ti * 128
                    DMA("sp", x_ap, xs[tok:tok + 128, :], [], [k_x])
                    rmsnorm_tile(x_ap, k_x, ng_b, k_ng, h_ap, k_h, junk, k_junk,
                                 st_small[:, tt % 2, 0:1], k_sts + f"a{tt % 2}", st_small[:, tt % 2, 1:2], k_sts + f"b{tt % 2}")
                    for half in range(2):
                        pb, kpb = bank()
                        pbb = pb[:].bitcast(BF16)
                        for j in range(8):
                            kc = half * 8 + j
                            TR(pbb[:, j * 128:(j + 1) * 128], h_ap[:, kc * 128:(kc + 1) * 128], identb, [k_h, k_idb], [kpb])
                        CP(evac_eng(), hTg[:, half * 8:(half + 1) * 8, ti * 128:(ti + 1) * 128],
                           pbb[:, 0:1024].rearrange("p (a b) -> p a b", a=8), [kpb], [k_hT])
                cs_ap, k_cs = cst[0]
                qTs_ap, k_qTs = qTs[0]
                kTs_ap, k_kTs = kTs[0]
                vst_ap, k_vst = vst[0]
                for (b0, nb) in loads:
                    if "fm" in _skip and b0 < 40:
                        continue
                    if "tm" in _skip and b0 >= 40:
                        continue
                    w_ap, k_w = wblk[wl_rr[0] % 3]
                    wl_rr[0] += 1
                    DMA("sp", w_ap[:, 0:nb], W1[b0 * 128:(b0 + nb) * 128, :].rearrange("(b p) (k j) -> p b k j", p=128, k=16),
                        [f"W1_{j}" for j in range(b0, b0 + nb)], [k_w])
                    if b0 < 40:
                        for bi in range(nb):
                            cb = b0 + bi
                            pb, kpb = bank()
                            for kc in range(16):
                                MM(pb[:, 0:GRP], w_ap[:, bi, kc, :], hTg[:, kc, :], kc == 0, kc == 15, [k_w, k_hT], [kpb])
                            if cb < 20:
                                s_ap, k_s = stg32[s32_rr[0] % 3]
                                s32_rr[0] += 1
                                CP(evac_eng(), s_ap, pb[:, 0:GRP], [kpb], [k_s])
                                DMA("sp", PR[cb, :, 1 + t0:1 + t0 + GRP], s_ap, [k_s], [f"PR{cb}_{g}"])
                            elif cb < 32:
                                s_ap, k_s = stg16[s16_rr[0] % 3]
                                s16_rr[0] += 1
                                ACT(s_ap, pb[:, 0:GRP], AF.Silu, [kpb], [k_s])
                                DMA("sp", G[cb - 20, :, t0:t0 + GRP], s_ap, [k_s], [f"G{cb}_{g}"])
                            elif cb < 36:
                                ACT(mg[:, cb - 32, :], pb[:, 0:GRP], AF.Silu, [kpb], [k_mg])
                            else:
                                CP("dve", mqT[:, cb - 36, :], pb[:, 0:GRP], [kpb], [k_mqT])
                        if b0 == 36 and "mat" not in _skip:
                            for hd in range(4):
                                PT_ap, k_PT = PTm[hd % 2]
                                for mc in range(2):
                                    pb, kpb = bank()
                                    MM(pb[:, 0:GRP], mkT[:, seg, hd, mc * 128:(mc + 1) * 128], mqT[:, hd, :], True, True, [k_mkT, k_mqT], [kpb])
                                    ACT(PT_ap[:, mc, :], pb[:, 0:GRP], AF.Exp, [kpb], [k_PT], scale=ATT_SCALE)
                                pby, kpby = bank()
                                pbs, kpbs = bank()
                                for mc in range(2):
                                    MM(pby[:, 0:GRP], mv[:, seg, mc, hd * 128:(hd + 1) * 128], PT_ap[:, mc, :], mc == 0, mc == 1, [k_mv, k_PT], [kpby])
                                for mc in range(2):
                                    MM(pbs[:, 0:GRP], onesb, PT_ap[:, mc, :], mc == 0, mc == 1, [k_onesb, k_PT], [kpbs])
                                P.op("dve", (lambda o, i: (lambda e: e.reciprocal(out=o, in_=i)))(rs_t, pbs[:, 0:GRP]), [kpbs], [k_rs])
                                TT("dve", ym_t, pby[:, 0:GRP], rs_t, ALU.mult, [kpby, k_rs], [k_ym])
                                s_ap, k_s = stg16[s16_rr[0] % 3]
                                s16_rr[0] += 1
                                TT("pool", s_ap, ym_t, mg[:, hd, :], ALU.mult, [k_ym, k_mg], [k_s])
                                DMA("sp", MIX[12 + hd, :, t0:t0 + GRP], s_ap, [k_s], [f"MIX{12 + hd}_{g}"])
                    else:
                        if b0 == 40:
                            P.barrier()
                            DMA("sp", cs_ap, cs_tab[t0:t0 + GRP].rearrange("(t p) c f -> p t c f", p=128), [], [k_cs])
                        for ti in range(4):
                            pb, kpb = bank()
                            for kc in range(16):
                                MM(pb[:, 0:nb * 128].rearrange("p (b j) -> p b j", b=nb), hTg[:, kc, ti * 128:(ti + 1) * 128], w_ap[:, 0:nb, kc, :],
                                   kc == 0, kc == 15, [k_w, k_hT], [kpb])
                            if b0 == 48:
                                CP(evac_eng(), vst_ap[:, ti, :], pb[:, 0:256], [kpb], [k_vst])
                                continue
                            if "qk" in _skip:
                                continue
                            hoff = 0 if b0 == 40 else 4
                            pv = pb[:, 0:512].rearrange("p (h d) -> p h d", h=4)
                            ACT(qsq[:, hoff:hoff + 4, :], pv, AF.Square, [kpb], [k_qsq] + [k_qsq + f"h{hoff + i}" for i in range(4)])
                            P.op("dve", (lambda o, i: (lambda e: e.tensor_reduce(out=o, in_=i, op=ALU.add, axis=mybir.AxisListType.X)))(
                                qss[:, hoff:hoff + 4], qsq[:, hoff:hoff + 4, :]), [k_qsq], [k_qss])
                            ACT(qrs[:, hoff:hoff + 4], qss[:, hoff:hoff + 4], AF.Ln, [k_qss, k_ceps], [k_qrs], scale=1.0 / 128, bias=c_eps)
                            ACT(qrs[:, hoff:hoff + 4], qrs[:, hoff:hoff + 4], AF.Exp, [k_qrs], [k_qrs], scale=-0.5)
                            for hh in range(4):
                                h8 = hoff + hh
                                gsel = 0 if h8 < 6 else 1
                                STT(qn[:, h8, :], pv[:, hh, :], qrs[:, h8:h8 + 1], qkg_b[:, gsel, :], ALU.mult, ALU.mult, [kpb, k_qrs, k_qkg], [k_qn])
                            if "rope" in _skip:
                                continue
                            for hh in range(4):
                                h8 = hoff + hh
                                qv = qn[:, h8, :].rearrange("p (f two) -> p f two", two=2)
                                x0, x1 = qv[:, :, 0], qv[:, :, 1]
                                cosb = cs_ap[:, ti, 0, :]
                                sinb = cs_ap[:, ti, 1, :]
                                ov = qsq[:, h8, :].rearrange("p (f two) -> p f two", two=2)
                                r1 = rp1[:, h8, :]
                                r2 = rp2[:, h8, :]
                                kq = k_qsq + f"h{h8}"
                                k1 = k_rp1 + f"h{h8}"
                                k2 = k_rp2 + f"h{h8}"
                                TT("dve", r1, x0, cosb, ALU.mult, [k_qn, k_cs], [k1])
                                TT("dve", r2, x1, sinb, ALU.mult, [k_qn, k_cs], [k2])
                                TT("dve", ov[:, :, 0], r1, r2, ALU.subtract, [k1, k2, k_qss], [kq])
                                TT("dve", r1, x0, sinb, ALU.mult, [k_qn, k_cs, kq], [k1])
                                TT("dve", r2, x1, cosb, ALU.mult, [k_qn, k_cs, kq], [k2])
                                TT("dve", ov[:, :, 1], r1, r2, ALU.add, [k1, k2, kq], [kq])
                                CP("dve", qr[:, h8, :], qsq[:, h8, :], [kq], [k_qr])
                            if "notr" in _skip:
                                continue
                            if "trbar" in _skip:
                                P.barrier()
                            pbt, kpbt = bank()
                            pbtb = pbt[:].bitcast(BF16)
                            for hh in range(4):
                                TR(pbtb[:, hh * 128:(hh + 1) * 128], qr[:, hoff + hh, :], identb, [k_qr, k_idb], [kpbt])
                            if "nocp" in _skip:
                                continue
                            for hh in range(4):
                                h8 = hoff + hh
                                if h8 < 6:
                                    CP(evac_eng(), qTs_ap[:, h8, ti * 128:(ti + 1) * 128], pbtb[:, hh * 128:(hh + 1) * 128], [kpbt], [k_qTs + f"h{h8}"])
                                else:
                                    CP(evac_eng(), kTs_ap[:, h8 - 6, ti * 128:(ti + 1) * 128], pbtb[:, hh * 128:(hh + 1) * 128], [kpbt], [k_kTs + f"h{h8 - 6}"])
                if "tm" in _skip or "qk" in _skip or "rope" in _skip or "notr" in _skip or "nocp" in _skip:
                    DMA("sp", Vd[t0:t0 + GRP, :].rearrange("(t p) c -> p t c", p=128), vst_ap, [k_vst], [f"V_{g}"])
                    P.barrier()
                    continue
                if "qst" not in _skip:
                    for hq in range(6):
                        DMA("sp", QT[hq, :, t0:t0 + GRP], qTs_ap[:, hq, :], [k_qTs + f"h{hq}"], [f"QT_{g}_{hq}"])
                    for hk in range(2):
                        DMA("sp", KTd[hk, :, t0:t0 + GRP], kTs_ap[:, hk, :], [k_kTs + f"h{hk}"], [f"KT_{g}_{hk}"])
                DMA("sp", Vd[t0:t0 + GRP, :].rearrange("(t p) c -> p t c", p=128), vst_ap, [k_vst], [f"V_{g}"])
                P.barrier()
            P.barrier()
            AR.release(m1)

        if 2 in passes:
            m2 = AR.mark()
            nb_cfg[0] = 5
            C = 128
            NCH = T // C
            CPS = SEG // C
            KD = DECAY_K
            pY0, kY0 = psb[5], "ps5"
            pY1, kY1 = psb[6], "ps6"
            pZ, kZ = psb[7], "ps7"
            gnv_b, k_gnv = AR.alloc((2, 768), F32)
            DMA("sp", gnv_b, gnv, [], [k_gnv])
            lup, k_lup = AR.alloc((2, 768), F32)
            DMA("sp", lup, lora_up, [], [k_lup])
            lupb, k_lupb = AR.alloc((2, 768), BF16)
            CP("dve", lupb, lup, [k_lup], [k_lupb])
            blk1b, k_blk1 = AR.alloc((128,), BF16)
            CP("dve", blk1b, c_f32[:, 706:834], [k_cf], [k_blk1])
            bselb, k_bsel = AR.alloc((2,), BF16)
            CP("dve", bselb, c_f32[:, 704:706], [k_cf], [k_bsel])
            ones_f, k_onesf = AR.alloc((128,), F32)
            MEMSET("dve", ones_f, 1.0, [], [k_onesf])
            c0v, k_c0v = AR.alloc((20,), F32)
            TT("dve", c0v, c_colv[:, 0:20], c_colv[:, 20:40], ALU.add, [k_colv], [k_c0v])
            TS("dve", c0v, c0v, -1.0, ALU.mult, [k_c0v], [k_c0v], s2=1.0, op1=ALU.add)
            m4 = []
            mLs = []
            for d_ in range(2):
                mt_, k_mt = AR.alloc((4, 128), F32)
                s_off, i_off = (128, 256) if d_ == 0 else (384, 512)
                for q_ in range(4):
                    off = s_off if q_ % 2 == 0 else i_off
                    CP("dve", mt_[:, q_, :], c_f32[:, off:off + 128], [k_cf, k_mt], [k_mt])
                m4.append((mt_, k_mt))
                mLs.append(c_f32[:, 384:512] if d_ == 0 else c_f32[:, 128:256])
            mu_p = c_colv[:, 0:20].unsqueeze(2).to_broadcast([128, 20, 128])
            mu_n = c_colv[:, 20:40].unsqueeze(2).to_broadcast([128, 20, 128])
            c0_b = c0v.unsqueeze(2).to_broadcast([128, 20, 128])
            kk_b = c_colv[:, 64:70].unsqueeze(2).to_broadcast([128, 6, 128])
            ka_b = c_colv[:, 70:76].unsqueeze(2).to_broadcast([128, 6, 128])
            rk_b = c_colv[:, 76:82].unsqueeze(2).to_broadcast([128, 6, 128])
            flag_ap = c_flag[:, 0:1]
            praw, k_praw = AR.alloc((20, 130), F32)
            hs, k_hs = AR.alloc((20, 128), F32)
            tmp20, k_tmp20 = AR.alloc((20, 128), F32)
            f6 = lambda: AR.alloc((6, 128), F32)
            kkraw, k_kkraw = f6()
            kk, k_kk = f6()
            rn, k_rn = f6()
            sg, k_sg = f6()
            a_t, k_at = f6()
            cs, k_cs = f6()
            ex, k_ex = f6()
            eL, k_eL = f6()
            emL, k_emL = f6()
            eLx, k_eLx = f6()
            b_t, k_bt = f6()
            kp, k_kp = f6()
            t1, k_t1 = f6()
            wtot, k_wtot = AR.alloc((6,), F32)
            b6 = lambda: AR.alloc((6, 128), BF16)
            sqb, k_sqb = b6()
            aq, k_aq = AR.alloc((6, 2, 128), BF16)
            btl, k_btl = b6()
            ktl, k_ktl = b6()
            vb, k_vb = b6()
            atm, k_atm = b6()
            btm, k_btm = b6()
            ktm, k_ktm = b6()
            vtm, k_vtm = b6()
            prodb, k_prodb = b6()
            twl, k_twl = AR.alloc((128,), BF16)
            alb, k_alb = AR.alloc((128,), BF16)
            AT, k_AT = AR.alloc((12, 4, 128), BF16)
            Pp = [AR.alloc((12, 2, 128), BF16) for _ in range(2)]
            Rb = [AR.alloc((12, 128), BF16) for _ in range(2)]
            nU, k_nU = AR.alloc((12, 64), BF16)
            IXb, k_IXb = AR.alloc((6, 64), BF16)
            QhT, k_QhT = b6()
            Gp, k_Gp = AR.alloc((6, 64), F32)
            ST, k_ST = AR.alloc((6, 64), F32)
            STb, k_STb = AR.alloc((6, 64), BF16)
            ztmp, k_ztmp = AR.alloc((6, 64), F32)
            ysb, k_ysb = AR.alloc((768,), F32)
            yf_t, k_yf = AR.alloc((768,), F32)
            yc, k_yc = AR.alloc((768,), F32)
            ysq, k_ysq = AR.alloc((768,), F32)
            st12, k_st12 = AR.alloc((4, 12), F32)
            bon, k_bon = AR.alloc((12,), F32)
            gate2, k_gate2 = b6()
            mixo, k_mixo = b6()

            def exp_op(out, k_out, in_, k_in, scale):
                ACT(out, in_, AF.Exp, [k_in], [k_out], scale=scale)

            for d_ in range(2):
                MEMSET("dve", ST, 0.0, [], [k_ST])
                MEMSET("dve", STb, 0.0, [], [k_STb])
                order = list(range(NCH)) if d_ == 0 else list(range(NCH - 1, -1, -1))
                m4t, k_m4 = m4[d_]
                mL = mLs[d_]
                for ci, c in enumerate(order):
                    t0 = c * C
                    cross = (c % CPS == 0 and c > 0) if d_ == 0 else ((c + 1) % CPS == 0 and c < NCH - 1)
                    if cross:
                        TS("dve", ST, ST, flag_ap, ALU.mult, [k_ST, k_flag], [k_ST])
                        CP("dve", STb, ST, [k_ST], [k_STb])
                    DMA("sp", praw, PR[:, :, t0:t0 + 130].rearrange("b p t -> p b t"), [], [k_praw])
                    if c % CPS == 0 and c > 0:
                        TS("dve", praw[:, :, 0:1], praw[:, :, 0:1], flag_ap, ALU.mult, [k_praw, k_flag], [k_praw])
                    if (c + 1) % CPS == 0 and c < NCH - 1:
                        TS("dve", praw[:, :, 129:130], praw[:, :, 129:130], flag_ap, ALU.mult, [k_praw, k_flag], [k_praw])
                    TT("dve", hs, praw[:, :, 1:129], c0_b, ALU.mult, [k_praw, k_c0v], [k_hs])
                    TT("dve", tmp20, praw[:, :, 0:128], mu_p, ALU.mult, [k_praw, k_colv], [k_tmp20])
                    TT("dve", hs, hs, tmp20, ALU.add, [k_hs, k_tmp20], [k_hs])
                    TT("dve", tmp20, praw[:, :, 2:130], mu_n, ALU.mult, [k_praw, k_colv, k_hs], [k_tmp20])
                    TT("dve", hs, hs, tmp20, ALU.add, [k_hs, k_tmp20], [k_hs])
                    r_ = hs[:, 0:6, :]
                    k_ = hs[:, 6:12, :]
                    v_ = hs[:, 12:18, :]
                    TT("dve", kkraw, k_, kk_b, ALU.mult, [k_hs, k_colv], [k_kkraw])
                    ACT(sqb, kkraw, AF.Square, [k_kkraw], [k_sqb])
                    pb, kpb = bank()
                    MM(pb[:, 0:512], blk1b, sqb[:, 0:4, :], True, True, [k_blk1, k_sqb], [kpb])
                    pb2, kpb2 = bank()
                    MM(pb2[:, 0:256], blk1b, sqb[:, 4:6, :], True, True, [k_blk1, k_sqb], [kpb2])
                    ACT(rn[:, 0:4, :], pb[:, 0:512].rearrange("p (a b) -> p a b", a=4), AF.Ln, [kpb, k_ceps], [k_rn], bias=c_tiny)
                    ACT(rn[:, 4:6, :], pb2[:, 0:256].rearrange("p (a b) -> p a b", a=2), AF.Ln, [kpb2, k_ceps, k_rn], [k_rn], bias=c_tiny)
                    ACT(rn, rn, AF.Exp, [k_rn], [k_rn], scale=-0.5)
                    TT("dve", kk, kkraw, rn, ALU.mult, [k_kkraw, k_rn], [k_kk])
                    ACT(twl, hs[:, 18, :], AF.Tanh, [k_hs], [k_twl])
                    CP("dve", alb, hs[:, 19, :], [k_hs], [k_alb])
                    hsl = slice(d_ * 64, (d_ + 1) * 64)
                    for which in range(2):
                        src = twl if which == 0 else alb
                        ksrc = k_twl if which == 0 else k_alb
                        dst, kdst = (sg, k_sg) if which == 0 else (a_t, k_at)
                        cbase = 40 if which == 0 else 52
                        pb, kpb = bank()
                        pb2, kpb2 = bank()
                        for blk in range(6):
                            tgt, ktgt = (pb, kpb) if blk < 4 else (pb2, kpb2)
                            cc = (blk % 4) * 128
                            MM(tgt[:, cc:cc + 128], lupb[hsl, which, blk * 128:(blk + 1) * 128], src[hsl, :], True, True, [k_lupb, ksrc], [ktgt])
                        for blk in range(6):
                            tgt, ktgt = (pb, kpb) if blk < 4 else (pb2, kpb2)
                            cc = (blk % 4) * 128
                            ACT(dst[:, blk, :], tgt[:, cc:cc + 128], AF.Sigmoid, [ktgt, k_colv, kdst], [kdst],
                                bias=c_colv[:, cbase + d_ * 6 + blk:cbase + d_ * 6 + blk + 1])
                    for blk in range(6):
                        P.op("dve", (lambda o, d1: (lambda e: e.tensor_tensor_scan(out=o, data0=ones_f, data1=d1, initial=0.0, op0=ALU.mult, op1=ALU.add)))(
                            cs[:, blk, :], sg[:, blk, :]), [k_sg, k_onesf, k_cs], [k_cs])
                    ACT(wtot.unsqueeze(2), cs[:, :, 127:128], AF.Exp, [k_cs], [k_wtot], scale=-KD)
                    if d_ == 1:
                        TT("dve", ex, sg, cs, ALU.subtract, [k_sg, k_cs], [k_ex])
                        TT("dve", cs, ex, cs[:, :, 127:128].to_broadcast([128, 6, 128]), ALU.add, [k_ex, k_cs], [k_cs])
                    TT("dve", ex, cs, sg, ALU.subtract, [k_cs, k_sg], [k_ex])
                    exp_op(eL, k_eL, cs, k_cs, -KD)
                    exp_op(emL, k_emL, cs, k_cs, KD)
                    exp_op(eLx, k_eLx, ex, k_ex, -KD)
                    TT("dve", b_t, kk, a_t, ALU.mult, [k_kk, k_at], [k_bt])
                    STT(t1, a_t, -1.0, ka_b, ALU.add, ALU.mult, [k_at, k_colv], [k_t1])
                    STT(kp, t1, 1.0, k_, ALU.add, ALU.mult, [k_t1, k_hs], [k_kp])
                    TT("dve", aq[:, :, 1, :], r_, eL, ALU.mult, [k_hs, k_eL], [k_aq])
                    TT("dve", aq[:, :, 0, :], kk, eLx, ALU.mult, [k_kk, k_eLx, k_aq], [k_aq])
                    TT("dve", btl, b_t, emL, ALU.mult, [k_bt, k_emL], [k_btl])
                    TT("dve", ktl, kp, emL, ALU.mult, [k_kp, k_emL], [k_ktl])
                    CP("dve", vb, v_, [k_hs], [k_vb])
                    for (src3, ksrc, dst3, kdst, sel) in ((aq, k_aq, atm, k_atm, 0), (btl, k_btl, btm, k_btm, None), (ktl, k_ktl, ktm, k_ktm, None), (vb, k_vb, vtm, k_vtm, None)):
                        pb, kpb = bank()
                        pbb = pb[:].bitcast(BF16)
                        for blk in range(6):
                            sin = src3[:, blk, 0, :] if sel is not None else src3[:, blk, :]
                            TR(pbb[:, blk * 128:(blk + 1) * 128], sin, identb, [ksrc, k_idb], [kpb])
                        CP("dve", dst3, pbb[:, 0:768].rearrange("p (a b) -> p a b", a=6), [kpb], [kdst])
                    def hsl_(hd):
                        return hd // 2, slice((hd % 2) * 64, (hd % 2) * 64 + 64)
                    P0, k_P0 = Pp[0]
                    R0, k_R0 = Rb[0]
                    for hd in range(12):
                        blk, hp = hsl_(hd)
                        pA, kpA = bank()
                        MM(pA[:, 0:256].rearrange("p (a b) -> p a b", a=2), btl[hp, blk, :], aq[hp, blk, :, :], True, True, [k_btl, k_aq], [kpA])
                        MM(pA[:, 256:512].rearrange("p (a b) -> p a b", a=2), ktl[hp, blk, :], aq[hp, blk, :, :], True, True, [k_ktl, k_aq], [kpA])
                        TT("dve", AT[:, hd, :, :], pA[:, 0:512].rearrange("p (a b) -> p a b", a=4), m4t, ALU.mult, [kpA, k_m4], [k_AT + f"{hd}"])
                        STT(P0[:, hd, 1, :], pA[:, 0:128], -1.0, m4t[:, 0, :], ALU.mult, ALU.mult, [kpA, k_m4], [k_P0 + f"t{hd}"])
                        pL, kpL = bank()
                        MM(pL[:, 0:128], aq[hp, blk, 0, :], btl[hp, blk, :], True, True, [k_aq, k_btl], [kpL])
                        STT(P0[:, hd, 0, :], pL[:, 0:128], -1.0, mL, ALU.mult, ALU.mult, [kpL, k_cf], [k_P0 + f"n{hd}"])
                    for hd in range(12):
                        blk, hp = hsl_(hd)
                        pR, kpR = bank()
                        MM(pR[:, 0:64], AT[:, hd, 2, :], vtm[:, blk, hp], True, True, [k_AT + f"{hd}", k_vtm], [kpR])
                        CP("dve", R0[:, hd, 0:64], atm[:, blk, hp], [k_atm], [k_R0 + f"a{hd}"])
                        CP("dve", R0[:, hd, 64:128], pR[:, 0:64], [kpR], [k_R0 + f"b{hd}"])
                    for lv in range(7):
                        Pc, k_Pc = Pp[lv % 2]
                        Pn_, k_Pn = Pp[(lv + 1) % 2]
                        Rc, k_Rc = Rb[lv % 2]
                        Rn, k_Rn = Rb[(lv + 1) % 2]
                        for hd in range(12):
                            kP = [k_Pc + f"t{hd}", k_Pc + f"n{hd}"]
                            kR = [k_Rc + f"a{hd}", k_Rc + f"b{hd}"]
                            pD, kpD = bank()
                            MM(pD[:, 0:128], Pc[:, hd, 1, :], Rc[:, hd, :], True, True, kP + kR, [kpD])
                            if lv < 6:
                                MM(pD[:, 128:256], Pc[:, hd, 1, :], Pc[:, hd, 0, :], True, True, kP, [kpD])
                                MM(pD[:, 256:384], Pc[:, hd, 0, :], Pc[:, hd, 1, :], True, True, kP, [kpD])
                            TT("dve", Rn[:, hd, :], pD[:, 0:128], Rc[:, hd, :], ALU.add, [kpD] + kR, [k_Rn + f"a{hd}", k_Rn + f"b{hd}"])
                            if lv < 6:
                                CP("dve", Pn_[:, hd, :, :], pD[:, 128:384].rearrange("p (a b) -> p a b", a=2), [kpD], [k_Pn + f"n{hd}", k_Pn + f"t{hd}"])
                    Rf, k_Rf = Rb[1]
                    for hd in range(12):
                        blk, hp = hsl_(hd)
                        kRf = [k_Rf + f"a{hd}", k_Rf + f"b{hd}"]
                        TS("dve", nU[:, hd, :], Rf[:, hd, 64:128], -1.0, ALU.mult, kRf, [k_nU + f"{hd}"])
                        pE, kpE = bank()
                        MM(pE[hp, 0:64], Rf[:, hd, 0:64], btm[:, blk, hp], True, True, kRf + [k_btm], [kpE])
                        MM(pE[hp, 128:256], Rf[:, hd, 0:64], AT[:, hd, 1, :], True, True, kRf + [k_AT + f"{hd}"], [kpE])
                        MM(pE[hp, 64:128], ktm[:, blk, hp], vtm[:, blk, hp], True, False, [k_ktm, k_vtm], [kpE])
                        MM(pE[hp, 64:128], btm[:, blk, hp], nU[:, hd, :], False, True, [k_btm, k_nU + f"{hd}"], [kpE])
                        TT("dve", IXb[hp, blk, :], c_f32[hp, 640:704], pE[hp, 0:64], ALU.subtract, [k_cf, kpE], [k_IXb + f"{hd}"])
                        TT("dve", QhT[hp, blk, :], aq[hp, blk, 1, :], pE[hp, 128:256], ALU.subtract, [k_aq, kpE], [k_QhT + f"{hd}"])
                        CP("dve", Gp[hp, blk, :], pE[hp, 64:128], [kpE], [k_Gp + f"{hd}"])
                    for hd in range(12):
                        blk, hp = hsl_(hd)
                        pY, kY = (pY0, kY0) if hd < 8 else (pY1, kY1)
                        yc0 = (hd % 8) * 64
                        MM(pY[:, yc0:yc0 + 64], AT[:, hd, 3, :], vtm[:, blk, hp], True, False, [k_AT + f"{hd}", k_vtm], [kY])
                        MM(pY[:, yc0:yc0 + 64], AT[:, hd, 1, :], nU[:, hd, :], False, False, [k_AT + f"{hd}", k_nU + f"{hd}"], [kY])
                        MM(pY[:, yc0:yc0 + 64], QhT[hp, blk, :], STb[hp, blk, :], False, True, [k_QhT + f"{hd}", k_STb], [kY])
                        MM(pZ[hp, blk * 64:(blk + 1) * 64], IXb[hp, blk, :], STb[hp, blk, :], True, True, [k_IXb + f"{hd}", k_STb], [kZ])
                    kGp_all = [k_Gp + f"{i}" for i in range(12)]
                    TT("dve", ztmp, pZ[:, 0:384].rearrange("p (a b) -> p a b", a=6), Gp, ALU.add, [kZ] + kGp_all, [k_ztmp])
                    TT("dve", ST, ztmp, wtot.unsqueeze(2).to_broadcast([128, 6, 64]), ALU.mult, [k_ztmp, k_wtot], [k_ST])
                    CP("dve", STb, ST, [k_ST], [k_STb])
                    CP("dve", ysb[:, 0:512], pY0[:, 0:512], [kY0], [k_ysb + "0"])
                    CP("dve", ysb[:, 512:768], pY1[:, 0:256], [kY1], [k_ysb + "1"])
                    k_ysb_all = [k_ysb + "0", k_ysb + "1"]
                    if d_ == 0:
                        DMA("sp", YF[t0:t0 + C, :], ysb, k_ysb_all, [f"YF_{c}"])
                        continue
                    DMA("sp", yf_t, YF[t0:t0 + C, :], [f"YF_{c}"], [k_yf])
                    DMA("sp", gate2, G[0:6, :, t0:t0 + C].rearrange("b p t -> p b t"), [], [k_gate2])
                    TT("dve", ysb, ysb, yf_t, ALU.add, k_ysb_all + [k_yf], k_ysb_all)
                    y3 = ysb.rearrange("p (h n) -> p h n", h=12)
                    yc3 = yc.rearrange("p (h n) -> p h n", h=12)
                    ysq3 = ysq.rearrange("p (h n) -> p h n", h=12)
                    mu = st12[:, 0, :]
                    var = st12[:, 1, :]
                    rstd = st12[:, 2, :]
                    P.op("dve", (lambda o, i: (lambda e: e.tensor_reduce(out=o, in_=i, op=ALU.add, axis=mybir.AxisListType.X)))(mu, y3), k_ysb_all, [k_st12 + "m"])
                    TS("dve", mu, mu, 1.0 / 64, ALU.mult, [k_st12 + "m"], [k_st12 + "m"])
                    TT("dve", yc3, y3, mu.unsqueeze(2).to_broadcast([128, 12, 64]), ALU.subtract, k_ysb_all + [k_st12 + "m"], [k_yc])
                    TT("dve", ysq, yc, yc, ALU.mult, [k_yc], [k_ysq])
                    P.op("dve", (lambda o, i: (lambda e: e.tensor_reduce(out=o, in_=i, op=ALU.add, axis=mybir.AxisListType.X)))(var, ysq3), [k_ysq], [k_st12 + "v"])
                    ACT(rstd, var, AF.Ln, [k_st12 + "v", k_ceps], [k_st12 + "r"], scale=1.0 / 64, bias=c_gneps)
                    ACT(rstd, rstd, AF.Exp, [k_st12 + "r"], [k_st12 + "r"], scale=-0.5)
                    TT("dve", yc3, yc3, rstd.unsqueeze(2).to_broadcast([128, 12, 64]), ALU.mult, [k_yc, k_st12 + "r"], [k_yc])
                    TT("dve", yc, yc, gnv_b[:, 0, :], ALU.mult, [k_yc, k_gnv], [k_yc])
                    TT("dve", yc, yc, gnv_b[:, 1, :], ALU.add, [k_yc, k_gnv], [k_yc])
                    TT("dve", t1, r_, k_, ALU.mult, [k_hs], [k_t1])
                    TT("dve", prodb, t1, rk_b, ALU.mult, [k_t1, k_colv], [k_prodb])
                    pB, kpB = bank()
                    for blk in range(6):
                        MM(pB[:, blk * 2:(blk + 1) * 2], prodb[:, blk, :], bselb, True, True, [k_prodb, k_bsel], [kpB])
                    CP("dve", bon, pB[:, 0:12], [kpB], [k_bon])
                    TT("dve", ysq3, vtm.rearrange("p b (h n) -> p (b h) n", h=2), bon.unsqueeze(2).to_broadcast([128, 12, 64]), ALU.mult, [k_vtm, k_bon], [k_ysq])
                    TT("dve", yc, yc, ysq, ALU.add, [k_yc, k_ysq], [k_yc])
                    pT0, kpT0 = bank()
                    pT1, kpT1 = bank()
                    for blk in range(6):
                        tgt, ktgt = (pT0, kpT0) if blk < 4 else (pT1, kpT1)
                        cc = (blk % 4) * 128
                        TR(tgt[:, cc:cc + 128], yc[:, blk * 128:(blk + 1) * 128], ident_f, [k_yc, k_cf], [ktgt])
                    TT("dve", mixo[:, 0:4, :], pT0[:, 0:512].rearrange("p (a b) -> p a b", a=4), gate2[:, 0:4, :], ALU.mult, [kpT0, k_gate2], [k_mixo + "0"])
                    TT("dve", mixo[:, 4:6, :], pT1[:, 0:256].rearrange("p (a b) -> p a b", a=2), gate2[:, 4:6, :], ALU.mult, [kpT1, k_gate2], [k_mixo + "1"])
                    DMA("sp", MIX[0:6, :, t0:t0 + C].rearrange("b p t -> p b t"), mixo, [k_mixo + "0", k_mixo + "1"], [f"MIXr_{c}"])
            P.barrier()
            nb_cfg[0] = 8
            AR.release(m2)

        if 3 in passes:
            m3 = AR.mark()
            nb_cfg[0] = 6
            KT_sb, k_KT = AR.alloc((2, T), BF16)
            V_sb, k_V = AR.alloc((NT, 256), BF16)
            for hk in range(2):
                for tq in range(0, T, 2048):
                    nq = min(2048, T - tq)
                    DMA("sp", KT_sb[:, hk, tq:tq + nq], KTd[hk, :, tq:tq + nq], [], [k_KT + f"_{hk}_{tq}"])
            k_KT_all = [k_KT + f"_{hk}_{tq}" for hk in range(2) for tq in range(0, T, 2048)]
            for tq in range(0, NT, 16):
                nq = min(16, NT - tq)
                DMA("sp", V_sb[:, tq:tq + nq, :], Vd[tq * 128:(tq + nq) * 128, :].rearrange("(t p) c -> p t c", p=128), [], [k_V + f"_{tq}"])
            k_V_all = [k_V + f"_{tq}" for tq in range(0, NT, 16)]
            qT3, k_qT3 = AR.alloc((6, GRP), BF16)
            g3, k_g3 = AR.alloc((6, GRP), BF16)
            PT3 = [AR.alloc((GRP,), BF16) for _ in range(2)]
            rs3, k_rs3 = AR.alloc((GRP,), F32)
            y3, k_y3 = AR.alloc((GRP,), F32)
            o3 = [AR.alloc((GRP,), BF16) for _ in range(2)]
            pO, kpO = psb[6], "ps6"
            pS, kpS = psb[7], "ps7"
            for qg in range(NG):
                t0 = qg * GRP
                qseg = t0 // SEG
                for hq in range(6):
                    DMA("sp", qT3[:, hq, :], QT[hq, :, t0:t0 + GRP], [], [k_qT3 + f"h{hq}"])
                    DMA("sp", g3[:, hq, :], G[6 + hq, :, t0:t0 + GRP], [], [k_g3 + f"h{hq}"])
                for hq in range(6):
                    kvh = hq // 3
                    for kt in range(NT):
                        kseg = (kt * 128) // SEG
                        PT_ap, k_PT = PT3[kt % 2]
                        pb, kpb = bank()
                        MM(pb[:, 0:GRP], KT_sb[:, kvh, kt * 128:(kt + 1) * 128], qT3[:, hq, :], True, True, k_KT_all + [k_qT3 + f"h{hq}"], [kpb])
                        ACT(PT_ap, pb[:, 0:GRP], AF.Exp, [kpb, k_flag], [k_PT], scale=ATT_SCALE,
                            bias=c_flag[:, 8 + qseg * NSEG + kseg:8 + qseg * NSEG + kseg + 1])
                        MM(pO[:, 0:GRP], V_sb[:, kt, kvh * 128:(kvh + 1) * 128], PT_ap, kt == 0, kt == NT - 1, k_V_all + [k_PT], [kpO])
                        MM(pS[:, 0:GRP], onesb, PT_ap, kt == 0, kt == NT - 1, [k_onesb, k_PT], [kpS])
                    P.op("dve", (lambda o, i: (lambda e: e.reciprocal(out=o, in_=i)))(rs3, pS[:, 0:GRP]), [kpS], [k_rs3])
                    TT("dve", y3, pO[:, 0:GRP], rs3, ALU.mult, [kpO, k_rs3], [k_y3])
                    o_ap, k_o = o3[hq % 2]
                    TT("dve", o_ap, y3, g3[:, hq, :], ALU.mult, [k_y3, k_g3 + f"h{hq}"], [k_o])
                    DMA("sp", MIX[6 + hq, :, t0:t0 + GRP], o_ap, [k_o], [f"MIXa{hq}_{qg}"])
            P.barrier()
            nb_cfg[0] = 8
            AR.release(m3)

        if 4 in passes:
            m4 = AR.mark()
            fg_b, k_fg = AR.alloc((D,), F32)
            DMA("sp", fg_b, rowv[:, 1, :], [], [k_fg])
            wo_sb, k_wo = AR.alloc((16, D), BF16)
            for i in range(4):
                DMA("sp", wo_sb[:, i * 4:(i + 1) * 4, :], WO.rearrange("(p k) n -> p k n", k=16)[:, i * 4:(i + 1) * 4, :], [], [k_wo + f"_{i}"])
            k_wo_all = [k_wo + f"_{i}" for i in range(4)]
            mixt = [AR.alloc((16, 128), BF16) for _ in range(2)]
            x4 = [AR.alloc((D,), F32) for _ in range(2)]
            r4 = [AR.alloc((D,), F32) for _ in range(2)]
            y4 = [AR.alloc((D,), F32) for _ in range(2)]
            junk4, k_junk4 = AR.alloc((D,), BF16)
            st4, k_st4 = AR.alloc((2, 2), F32)
            for tt in range(NT):
                tok = tt * 128
                m_ap, k_m = mixt[tt % 2]
                x_ap, k_x = x4[tt % 2]
                r_ap, k_r = r4[tt % 2]
                y_ap, k_y = y4[tt % 2]
                DMA("sp", m_ap, MIX[:, :, tok:tok + 128].rearrange("c p t -> p c t"), [], [k_m])
                DMA("sp", x_ap, xs[tok:tok + 128, :], [], [k_x])
                for ng in range(4):
                    pb, kpb = bank()
                    for kc in range(16):
                        MM(pb[:, 0:512], m_ap[:, kc, :], wo_sb[:, kc, ng * 512:(ng + 1) * 512], kc == 0, kc == 15, [k_m] + k_wo_all, [kpb])
                    TT("dve", r_ap[:, ng * 512:(ng + 1) * 512], pb[:, 0:512], x_ap[:, ng * 512:(ng + 1) * 512], ALU.add, [kpb, k_x], [k_r + f"_{ng}"])
                kr_all = [k_r + f"_{ng}" for ng in range(4)]
                ssq = st4[:, tt % 2, 0:1]
                rstd = st4[:, tt % 2, 1:2]
                ks1 = k_st4 + f"a{tt % 2}"
                ks2 = k_st4 + f"b{tt % 2}"
                ACT(junk4, r_ap, AF.Square, kr_all, [k_junk4, ks1], accum_out=ssq)
                ACT(rstd, ssq, AF.Ln, [ks1, k_ceps], [ks2], scale=1.0 / D, bias=c_eps)
                ACT(rstd, rstd, AF.Exp, [ks2], [ks2], scale=-0.5)
                STT(y_ap, r_ap, rstd, fg_b, ALU.mult, ALU.mult, kr_all + [ks2, k_fg], [k_y])
                DMA("sp", y_out[tok:tok + 128, :], y_ap, [k_y], [f"y_{tt}"])
            P.barrier()
            AR.release(m4)

        P.barrier()
        P.finalize()
        P.emit()
    return nc


def _rope_tab(nseg, seg, carry):
    if carry:
        pos = np.arange(nseg * seg)
    else:
        pos = np.tile(np.arange(seg), nseg)
    row = (pos // 64).astype(np.float32)
    col = (pos % 64).astype(np.float32)
    freqs = (np.float32(10000.0) ** (-np.arange(0, 64, 2, dtype=np.float32) / np.float32(64))).astype(np.float32)
    ang = np.concatenate([row[:, None] * freqs, col[:, None] * freqs], axis=-1).astype(np.float32)
    return np.stack([np.cos(ang), np.sin(ang)], axis=1).astype(np.float32)


def _consts():
    c = np.zeros((128, 1024), np.float32)
    r = np.arange(128)[:, None]
    q = np.arange(128)[None, :]
    c[:, 0:128] = (r == q)
    c[:, 128:256] = (q > r)
    c[:, 256:384] = (q >= r)
    c[:, 384:512] = (q < r)
    c[:, 512:640] = (q <= r)
    c[:, 640:704] = (np.arange(64)[None, :] == (r % 64))
    c[:, 704:706] = (np.arange(2)[None, :] == (r // 64))
    c[:, 706:834] = ((r // 64) == (q // 64))
    return c


def shared_inputs(norm_g, w_in, mu_prev, mu_next, w0, w_up, a0, a_up, k_k, k_a, r_k, gn_g, gn_b,
                  q_norm_g, k_norm_g, mem_norm_g, w_mem_kv, w_out, final_g):
    f = lambda a: np.ascontiguousarray(np.asarray(a, dtype=np.float32))
    w_in = f(w_in)[0]
    wt = w_in.reshape(16, 128, NCB, 128)[:, :, CB_PERM, :]
    w_in_t = np.ascontiguousarray(wt.transpose(2, 1, 0, 3)).reshape(NCB * 128, D)
    w_out_t = np.ascontiguousarray(f(w_out)[0].reshape(16, 128, D).transpose(1, 0, 2)).reshape(128 * 16, D)
    w_kv_t = np.ascontiguousarray(f(w_mem_kv)[0].reshape(16, 128, 1024).transpose(1, 0, 2)).reshape(128 * 16, 1024)
    rowv = np.ascontiguousarray(np.broadcast_to(np.stack([f(norm_g)[0], f(final_g), f(mem_norm_g)[0]])[None], (128, 3, D)))
    qkg = np.ascontiguousarray(np.broadcast_to(np.stack([f(q_norm_g)[0], f(k_norm_g)[0]])[None], (128, 2, 128)))
    gnv = np.ascontiguousarray(np.broadcast_to(np.stack([f(gn_g)[0], f(gn_b)[0]])[None], (128, 2, 768)))
    colv = np.zeros((128, 96), np.float32)
    colv[:, 0:20] = f(mu_prev)[0].reshape(20, 128).T
    colv[:, 20:40] = f(mu_next)[0].reshape(20, 128).T
    colv[:, 40:52] = f(w0)[0].reshape(12, 128).T
    colv[:, 52:64] = f(a0)[0].reshape(12, 128).T
    colv[:, 64:70] = f(k_k)[0].reshape(6, 128).T
    colv[:, 70:76] = f(k_a)[0].reshape(6, 128).T
    colv[:, 76:82] = f(r_k)[0].reshape(6, 128).T
    lora_up = np.ascontiguousarray(np.stack([f(w_up)[0].reshape(128, 768), f(a_up)[0].reshape(128, 768)], axis=1))
    return dict(w_in_t=w_in_t, w_out_t=w_out_t, w_kv_t=w_kv_t, rowv=rowv, qkg=qkg, gnv=gnv, colv=colv,
                lora_up=lora_up, consts=_consts())


def core_inputs(shared, x_core, mem_core, nseg, seg, carry):
    flags = np.zeros((128, 32), np.float32)
    flags[:, 0] = 1.0 if carry else 0.0
    for qs in range(nseg):
        for ks in range(nseg):
            flags[:, 8 + qs * nseg + ks] = 0.0 if (carry or qs == ks) else NEG
    d = dict(shared)
    d["xs"] = np.ascontiguousarray(x_core, dtype=np.float32)
    d["mem"] = np.ascontiguousarray(mem_core, dtype=np.float32)
    d["cs_tab"] = _rope_tab(nseg, seg, carry)
    d["flags"] = flags
    return d


_NC_CACHE = {}


def kernel(x_prompt, x_sample, mem_prompt, mem_sample, norm_g, w_in, mu_prev, mu_next, w0, w_up, a0, a_up,
           k_k, k_a, r_k, gn_g, gn_b, q_norm_g, k_norm_g, mem_norm_g, w_mem_kv, w_out, final_g):
    NSEG, SEG = 4, 2048
    x_prompt = np.asarray(x_prompt, dtype=np.float32)
    x_sample = np.asarray(x_sample, dtype=np.float32)
    mem_prompt = np.asarray(mem_prompt, dtype=np.float32)
    mem_sample = np.asarray(mem_sample, dtype=np.float32)
    shared = shared_inputs(norm_g, w_in, mu_prev, mu_next, w0, w_up, a0, a_up, k_k, k_a, r_k, gn_g, gn_b,
                           q_norm_g, k_norm_g, mem_norm_g, w_mem_kv, w_out, final_g)
    in_maps = []
    for c in range(4):
        in_maps.append(core_inputs(shared, x_prompt[c], np.broadcast_to(mem_prompt[c][None], (NSEG, N_MEM, D)), NSEG, SEG, True))
    for c in range(4):
        in_maps.append(core_inputs(shared, x_sample[4 * c:4 * c + 4].reshape(NSEG * SEG, D), mem_sample[4 * c:4 * c + 4], NSEG, SEG, False))
    key = (NSEG, SEG)
    if key not in _NC_CACHE:
        _NC_CACHE[key] = build(NSEG, SEG)
    nc = _NC_CACHE[key]
    res = run_bass_kernel_spmd(nc, in_maps, core_ids=list(range(8)))
    yp = np.stack([np.asarray(res.results[c]["y"], dtype=np.float32) for c in range(4)])
    ysm = np.concatenate([np.asarray(res.results[4 + c]["y"], dtype=np.float32).reshape(4, SEG, D) for c in range(4)], axis=0)
    return (yp, ysm)
```

```python
import contextlib
import numpy as np
import ml_dtypes
import concourse.bass as bass
import concourse.mybir as mybir
from concourse.bass_utils import run_bass_kernel_spmd

F32 = mybir.dt.float32
BF16 = mybir.dt.bfloat16
ALU = mybir.AluOpType
AF = mybir.ActivationFunctionType

D = 2048
IN_W = 6400
NCB = 50
N_MEM = 256
NORM_EPS = 1e-6
GN_EPS = 64e-5
DECAY_K = float(np.exp(-0.5))
ATT_SCALE = 128 ** -0.5
NEG = -30000.0

CB_PERM = list(range(0, 20)) + list(range(20, 26)) + list(range(36, 42)) + list(range(46, 50)) + \
    list(range(42, 46)) + list(range(26, 32)) + [32, 33] + [34, 35]

ENGS = ("pe", "dve", "act", "pool", "sp")
SEM_EPOCH = 20000
import os as _osg
SERIAL = int(_osg.environ.get("K_SERIAL", "3"))


class Op:
    __slots__ = ("eng", "fn", "deps", "is_dma", "seq", "signal", "sig_idx", "dma_sem", "dma_val", "waits", "barrier")

    def __init__(self, eng, fn, is_dma):
        self.eng = eng
        self.fn = fn
        self.is_dma = is_dma
        self.deps = set()
        self.signal = False
        self.sig_idx = 0
        self.dma_sem = None
        self.dma_val = 0
        self.waits = []
        self.barrier = False


class Prog:
    def __init__(self, nc, n_dma_sems=8):
        self.nc = nc
        self.ops = []
        self.by_eng = {e: [] for e in ENGS}
        self.last_w = {}
        self.readers = {}
        self.n_dma_sems = n_dma_sems
        self.last_comp = None

    def _add(self, eng, fn, reads, writes, is_dma):
        op = Op(eng, fn, is_dma)
        op.seq = len(self.ops)
        for k in reads:
            w = self.last_w.get(k)
            if w is not None:
                op.deps.add(w)
        for k in writes:
            w = self.last_w.get(k)
            if w is not None:
                op.deps.add(w)
            rl = self.readers.get(k)
            if rl:
                op.deps.update(rl)
        for k in reads:
            self.readers.setdefault(k, []).append(op)
        for k in writes:
            self.last_w[k] = op
            self.readers[k] = []
        op.deps.discard(op)
        if SERIAL == 1 and self.ops and not self.ops[-1].barrier:
            op.deps.add(self.ops[-1])
        elif SERIAL == 2 and not is_dma:
            if self.last_comp is not None:
                op.deps.add(self.last_comp)
            self.last_comp = op
        elif SERIAL == 3 and not is_dma and eng != "pe":
            if self.last_comp is not None and self.last_comp.eng != eng:
                op.deps.add(self.last_comp)
            self.last_comp = op
        self.ops.append(op)
        self.by_eng[eng].append(op)
        return op

    def op(self, eng, fn, reads=(), writes=()):
        return self._add(eng, fn, reads, writes, False)

    def dma(self, eng, fn, reads=(), writes=()):
        return self._add(eng, fn, reads, writes, True)

    def barrier(self):
        tails = []
        for e in ENGS:
            comp = [o for o in self.by_eng[e] if not o.is_dma and not o.barrier]
            if comp:
                tails.append(comp[-1])
            dm = [o for o in self.by_eng[e] if o.is_dma]
            tails.extend(dm[-self.n_dma_sems:])
        for e in ENGS:
            op = Op(e, lambda eng: None, False)
            op.barrier = True
            op.seq = len(self.ops)
            op.deps = set(tails)
            self.ops.append(op)
            self.by_eng[e].append(op)
        self.last_w = {}
        self.readers = {}
        self.last_comp = None

    def finalize(self):
        for op in self.ops:
            for d in op.deps:
                if d.is_dma:
                    continue
                if d.eng == "pe" and op.eng == "pe" and not op.is_dma and not op.barrier:
                    continue
                d.signal = True
        self.n_sig = {}
        for e in ENGS:
            c = 0
            for op in self.by_eng[e]:
                if (not op.is_dma) and op.signal:
                    c += 1
                    op.sig_idx = c
            self.n_sig[e] = c
        for e in ENGS:
            k = 0
            slots = [None] * self.n_dma_sems
            counts = [0] * self.n_dma_sems
            for op in self.by_eng[e]:
                if op.is_dma:
                    s = k % self.n_dma_sems
                    prev = slots[s]
                    if prev is not None:
                        op.deps.add(prev)
                    counts[s] += 16
                    op.dma_sem = (e, s)
                    op.dma_val = counts[s]
                    slots[s] = op
                    k += 1
        for e in ENGS:
            wd = {}
            dma_waited = {}
            for op in self.by_eng[e]:
                need = {}
                for d in op.deps:
                    if d.is_dma:
                        key = d.dma_sem
                        if dma_waited.get(key, 0) < d.dma_val:
                            dma_waited[key] = d.dma_val
                            op.waits.append(("dma", key, d.dma_val))
                    else:
                        if d.eng == "pe" and op.eng == "pe" and not op.is_dma and not op.barrier:
                            continue
                        if d.sig_idx > need.get(d.eng, 0):
                            need[d.eng] = d.sig_idx
                for src, idx in need.items():
                    if wd.get(src, 0) < idx:
                        op.waits.append(("eng", src, idx))
                        wd[src] = idx

    def emit(self):
        nc = self.nc
        with contextlib.ExitStack() as st:
            sems = {}
            for e in ENGS:
                n_ep = (self.n_sig[e] + SEM_EPOCH - 1) // SEM_EPOCH
                for i in range(n_ep):
                    sems[("eng", e, i)] = st.enter_context(nc.semaphore(f"s_{e}_{i}"))
                used = sorted(set(op.dma_sem for op in self.by_eng[e] if op.is_dma))
                for key in used:
                    sems[("dma",) + key] = st.enter_context(nc.semaphore(f"d_{key[0]}_{key[1]}"))
            block = st.enter_context(nc.Block())
            engmap = {"pe": "tensor", "dve": "vector", "act": "scalar", "pool": "gpsimd", "sp": "sync"}

            def make(e):
                ops = self.by_eng[e]

                def body(engine):
                    for op in ops:
                        for w in op.waits:
                            if w[0] == "dma":
                                engine.wait_ge(sems[("dma",) + w[1]], w[2])
                            else:
                                idx = w[2]
                                ep = (idx - 1) // SEM_EPOCH
                                engine.wait_ge(sems[("eng", w[1], ep)], idx - ep * SEM_EPOCH)
                        ins = op.fn(engine)
                        if ins is None:
                            continue
                        if op.is_dma:
                            ins.then_inc(sems[("dma",) + op.dma_sem], 16)
                        elif op.signal:
                            ep = (op.sig_idx - 1) // SEM_EPOCH
                            ins.then_inc(sems[("eng", e, ep)], 1)
                return body

            for e in ENGS:
                if self.by_eng[e]:
                    getattr(block, engmap[e])(make(e))


class Arena:
    def __init__(self, tile, nbytes):
        self.tile = tile
        self.nbytes = nbytes
        self.off = 0
        self.cnt = 0

    def alloc(self, shape, dtype):
        esz = 4 if dtype == F32 else 2
        n = int(np.prod(shape))
        nb = n * esz
        self.off = (self.off + 63) // 64 * 64
        assert self.off + nb <= self.nbytes, f"arena overflow {self.off}+{nb}>{self.nbytes}"
        a = self.tile[:, self.off // 2:(self.off + nb) // 2]
        if dtype == F32:
            a = a.bitcast(F32)
        self.off += nb
        self.cnt += 1
        key = f"A{self.cnt}"
        if len(shape) == 2:
            a = a.rearrange("p (a b) -> p a b", a=shape[0])
        elif len(shape) == 3:
            a = a.rearrange("p (a b c) -> p a b c", a=shape[0], b=shape[1])
        elif len(shape) == 4:
            a = a.rearrange("p (a b c d) -> p a b c d", a=shape[0], b=shape[1], c=shape[2])
        return a, key

    def mark(self):
        return self.off

    def release(self, m):
        self.off = m


def build(NSEG, SEG, debug=False, passes=(1, 2, 3, 4)):
    T = NSEG * SEG
    NT = T // 128
    GRP = 512
    NG = T // GRP
    assert SEG % GRP == 0
    nc = bass.Bass("TRN2", target_bir_lowering=False)
    dt_in = lambda n, s, d=F32: nc.dram_tensor(n, list(s), d, kind="ExternalInput").ap()
    okind = "ExternalOutput" if debug else "Internal"
    dt_scr = lambda n, s, d: nc.dram_tensor(n, list(s), d, kind=okind).ap()

    xs = dt_in("xs", [T, D])
    mem = dt_in("mem", [NSEG, N_MEM, D])
    w_in_t = dt_in("w_in_t", [NCB * 128, D])
    w_out_t = dt_in("w_out_t", [128 * 16, D])
    w_kv_t = dt_in("w_kv_t", [128 * 16, 1024])
    rowv = dt_in("rowv", [128, 3, D])
    qkg = dt_in("qkg", [128, 2, 128])
    colv = dt_in("colv", [128, 96])
    gnv = dt_in("gnv", [128, 2, 768])
    lora_up = dt_in("lora_up", [128, 2, 768])
    cs_tab = dt_in("cs_tab", [T, 2, 64])
    consts = dt_in("consts", [128, 1024])
    flags = dt_in("flags", [128, 32])
    y_out = nc.dram_tensor("y", [T, D], F32, kind="ExternalOutput").ap()

    W1 = dt_scr("W1", [NCB * 128, D], BF16)
    WO = dt_scr("WO", [128 * 16, D], BF16)
    WKV = dt_scr("WKV", [128 * 16, 1024], BF16)
    PR = dt_scr("PR", [20, 128, T + 2], F32)
    G = dt_scr("G", [12, 128, T], BF16)
    MIX = dt_scr("MIX", [16, 128, T], BF16)
    QT = dt_scr("QT", [6, 128, T], BF16)
    KTd = dt_scr("KTd", [2, 128, T], BF16)
    Vd = dt_scr("Vd", [T, 256], BF16)
    YF = dt_scr("YF", [T, 768], F32)
    HSd = dt_scr("HSd", [T // 128, 128, 26 * 128], F32)

    with contextlib.ExitStack() as top:
        ARENA_BYTES = 200 * 1024
        arena_t = top.enter_context(nc.sbuf_tensor("arena", [128, ARENA_BYTES // 2], BF16))
        AR = Arena(arena_t, ARENA_BYTES)
        psb = [top.enter_context(nc.psum_tensor(f"psb{i}", [128, 512], F32)) for i in range(8)]
        P = Prog(nc)
        ps_rr = [0]

        import os as _os0
        _NB = int(_os0.environ.get("K_NB", "8"))

        nb_cfg = [_NB]

        def bank():
            i = ps_rr[0] % nb_cfg[0]
            ps_rr[0] += 1
            return psb[i], f"ps{i}"

        def bank2():
            if ps_rr[0] % 2:
                ps_rr[0] += 1
            i = ps_rr[0] % 8
            ps_rr[0] += 2
            return psb[i], psb[i + 1], f"ps{i}", f"ps{i + 1}"

        ev_rr = [0]

        def evac_eng():
            return "dve"

        def copy_op(eng, out, in_, reads, writes):
            if eng == "act":
                P.op("act", lambda e: e.activation(out=out, in_=in_, func=AF.Copy), reads, writes)
            else:
                P.op(eng, lambda e: e.tensor_copy(out=out, in_=in_), reads, writes)

        def MM(out, lhsT, rhs, start, stop, reads, writes):
            P.op("pe", lambda e: e.matmul(out, lhsT=lhsT, rhs=rhs, start=start, stop=stop), reads, writes)

        def TR(out, in_, ident, reads, writes):
            P.op("pe", lambda e: e.transpose(out, in_, ident), reads, writes)

        def ACT(out, in_, func, reads, writes, scale=None, bias=None, accum_out=None):
            kw = {}
            if scale is not None:
                kw["scale"] = scale
            if bias is not None:
                kw["bias"] = bias
            if accum_out is not None:
                kw["accum_out"] = accum_out
            P.op("act", lambda e: e.activation(out=out, in_=in_, func=func, **kw), reads, writes)

        def TT(eng, out, in0, in1, op, reads, writes):
            P.op(eng, lambda e: e.tensor_tensor(out=out, in0=in0, in1=in1, op=op), reads, writes)

        def TS(eng, out, in0, s1, op0, reads, writes, s2=None, op1=None):
            if op1 is None:
                P.op(eng, lambda e: e.tensor_scalar(out=out, in0=in0, scalar1=s1, scalar2=None, op0=op0), reads, writes)
            else:
                P.op(eng, lambda e: e.tensor_scalar(out=out, in0=in0, scalar1=s1, scalar2=s2, op0=op0, op1=op1), reads, writes)

        def STT(out, in0, scalar, in1, op0, op1, reads, writes):
            P.op("dve", lambda e: e.scalar_tensor_tensor(out=out, in0=in0, scalar=scalar, in1=in1, op0=op0, op1=op1), reads, writes)

        def DMA(eng, out, in_, reads, writes, slow=False):
            if slow:
                P.dma(eng, lambda e: e.dma_start(out=out, in_=in_, allow_slow_non_contiguous=True), reads, writes)
            else:
                P.dma(eng, lambda e: e.dma_start(out=out, in_=in_), reads, writes)

        def CP(eng, out, in_, reads, writes):
            if eng == "act":
                P.op("act", lambda e: e.activation(out=out, in_=in_, func=AF.Copy), reads, writes)
            else:
                P.op(eng, lambda e: e.tensor_copy(out=out, in_=in_), reads, writes)

        def MEMSET(eng, out, val, reads, writes):
            P.op(eng, lambda e: e.memset(out, val), reads, writes)

        c_f32, k_cf = AR.alloc((1024,), F32)
        c_flag, k_flag = AR.alloc((32,), F32)
        c_colv, k_colv = AR.alloc((96,), F32)
        identb, k_idb = AR.alloc((128,), BF16)
        onesb, k_onesb = AR.alloc((128,), BF16)
        c_eps_t, k_ceps = AR.alloc((4,), F32)
        DMA("sp", c_f32, consts, [], [k_cf])
        DMA("sp", c_flag, flags, [], [k_flag])
        DMA("sp", c_colv, colv, [], [k_colv])
        ident_f = c_f32[:, 0:128]
        CP("dve", identb, ident_f, [k_cf], [k_idb])
        MEMSET("dve", onesb, 1.0, [], [k_onesb])
        MEMSET("dve", c_eps_t[:, 0:1], NORM_EPS, [], [k_ceps])
        MEMSET("dve", c_eps_t[:, 1:2], GN_EPS, [k_ceps], [k_ceps])
        MEMSET("dve", c_eps_t[:, 2:3], 1e-30, [k_ceps], [k_ceps])
        c_eps = c_eps_t[:, 0:1]
        c_gneps = c_eps_t[:, 1:2]
        c_tiny = c_eps_t[:, 2:3]

        import os as _os
        _skip = _os.environ.get("K_SKIP", "")
        if "cast" not in _skip:
            for i in range(NCB):
                DMA("pool", W1[i * 128:(i + 1) * 128, :], w_in_t[i * 128:(i + 1) * 128, :], [], [f"W1_{i}"])
            for i in range(16):
                DMA("pool", WO[i * 128:(i + 1) * 128, :], w_out_t[i * 128:(i + 1) * 128, :], [], [f"WO{i}"])
            for i in range(16):
                DMA("pool", WKV[i * 128:(i + 1) * 128, :], w_kv_t[i * 128:(i + 1) * 128, :], [], [f"WKV{i}"])
        zt, k_zt = AR.alloc((20, 1), F32)
        MEMSET("pool", zt, 0.0, [], [k_zt])
        if "pad" not in _skip:
            DMA("sp", PR[:, :, 0:1].rearrange("b p o -> p b o"), zt, [k_zt], ["PRpad0"], slow=True)
            DMA("sp", PR[:, :, T + 1:T + 2].rearrange("b p o -> p b o"), zt, [k_zt], ["PRpad1"], slow=True)
        if "mixz" not in _skip:
            zb, k_zb = AR.alloc((2048,), BF16)
            MEMSET("pool", zb, 0.0, [], [k_zb])
            for cbz in range(12):
                for tz in range(0, T, 2048):
                    nz = min(2048, T - tz)
                    DMA("sp", MIX[cbz, :, tz:tz + nz], zb[:, 0:nz], [k_zb], [f"MIXz{cbz}_{tz}"])
        P.barrier()

        def rmsnorm_tile(x_ap, k_x, g_ap, k_g, h_ap, k_h, junk, k_junk, ssq, k_ssq, rstd, k_rstd):
            ACT(junk, x_ap, AF.Square, [k_x], [k_junk, k_ssq], accum_out=ssq)
            ACT(rstd, ssq, AF.Ln, [k_ssq, k_ceps], [k_rstd], scale=1.0 / D, bias=c_eps)
            ACT(rstd, rstd, AF.Exp, [k_rstd], [k_rstd], scale=-0.5)
            STT(h_ap, x_ap, rstd, g_ap, ALU.mult, ALU.mult, [k_x, k_rstd, k_g], [k_h])

        if 1 in passes:
            m1 = AR.mark()
            ng_b, k_ng = AR.alloc((D,), F32)
            qkg_b, k_qkg = AR.alloc((2, 128), F32)
            DMA("sp", ng_b, rowv[:, 0, :], [], [k_ng])
            DMA("sp", qkg_b, qkg, [], [k_qkg])
            mkT, k_mkT = AR.alloc((NSEG, 4, 256), BF16)
            mv, k_mv = AR.alloc((NSEG, 2, 512), BF16)

            mkv_mark = AR.mark()
            mng_b, k_mng = AR.alloc((D,), F32)
            DMA("sp", mng_b, rowv[:, 2, :], [], [k_mng])
            wkv_sb, k_wkv = AR.alloc((16, 1024), BF16)
            DMA("sp", wkv_sb, WKV.rearrange("(p k) n -> p k n", k=16), [f"WKV{i}" for i in range(16)], [k_wkv])
            mx, k_mx = AR.alloc((D,), F32)
            mh, k_mh = AR.alloc((D,), BF16)
            mjunk, k_mjunk = AR.alloc((D,), BF16)
            mss, k_mss = AR.alloc((2,), F32)
            mhT, k_mhT = AR.alloc((16, 256), BF16)
            for s in range(NSEG):
                for mt in range(2):
                    DMA("sp", mx, mem[s, mt * 128:(mt + 1) * 128, :], [], [k_mx])
                    if "mkvn" in _skip:
                        continue
                    rmsnorm_tile(mx, k_mx, mng_b, k_mng, mh, k_mh, mjunk, k_mjunk, mss[:, 0:1], k_mss, mss[:, 1:2], k_mss + "b")
                    for half in range(2):
                        if "mkvt" in _skip:
                            continue
                        pb, kpb = bank()
                        pbb = pb[:].bitcast(BF16)
                        for j in range(8):
                            kc = half * 8 + j
                            TR(pbb[:, j * 128:(j + 1) * 128], mh[:, kc * 128:(kc + 1) * 128], identb, [k_mh, k_idb], [kpb])
                        CP(evac_eng(), mhT[:, half * 8:(half + 1) * 8, mt * 128:(mt + 1) * 128],
                           pbb[:, 0:1024].rearrange("p (a b) -> p a b", a=8), [kpb], [k_mhT])
                if "mkvm" in _skip:
                    continue
                for hd in range(4):
                    if "mkva" in _skip:
                        continue
                    pb, kpb = bank()
                    for kc in range(16):
                        MM(pb[:, 0:256], wkv_sb[:, kc, hd * 128:(hd + 1) * 128], mhT[:, kc, :], kc == 0, kc == 15, [k_wkv, k_mhT], [kpb])
                    CP(evac_eng(), mkT[:, s, hd, :], pb[:, 0:256], [kpb], [k_mkT])
                P.barrier()
                for mt in range(2):
                    if "mkvb" in _skip:
                        continue
                    pb, kpb = bank()
                    for kc in range(16):
                        MM(pb[:, 0:512], mhT[:, kc, mt * 128:(mt + 1) * 128], wkv_sb[:, kc, 512:1024], kc == 0, kc == 15, [k_wkv, k_mhT], [kpb])
                    CP(evac_eng(), mv[:, s, mt, :], pb[:, 0:512], [kpb], [k_mv])
            P.barrier()
            AR.release(mkv_mark)

            xt = [AR.alloc((D,), F32) for _ in range(2)]
            ht = [AR.alloc((D,), BF16) for _ in range(2)]
            junk, k_junk = AR.alloc((D,), BF16)
            st_small, k_sts = AR.alloc((2, 2), F32)
            hT = [AR.alloc((16, GRP), BF16) for _ in range(2)]
            wblk = [AR.alloc((4, 16, 128), BF16) for _ in range(3)]
            stg32 = [AR.alloc((GRP,), F32) for _ in range(3)]
            stg16 = [AR.alloc((GRP,), BF16) for _ in range(3)]
            mqT, k_mqT = AR.alloc((4, GRP), BF16)
            mg, k_mg = AR.alloc((4, GRP), BF16)
            PTm = [AR.alloc((2, GRP), BF16) for _ in range(2)]
            rs_t, k_rs = AR.alloc((GRP,), F32)
            ym_t, k_ym = AR.alloc((GRP,), F32)
            qn, k_qn = AR.alloc((8, 128), F32)
            qsq, k_qsq = AR.alloc((8, 128), F32)
            qss, k_qss = AR.alloc((8,), F32)
            qrs, k_qrs = AR.alloc((8,), F32)
            rp1, k_rp1 = AR.alloc((8, 64), F32)
            rp2, k_rp2 = AR.alloc((8, 64), F32)
            qr, k_qr = AR.alloc((8, 128), BF16)
            cst = [AR.alloc((4, 2, 64), F32) for _ in range(1)]
            qTs = [AR.alloc((6, GRP), BF16) for _ in range(1)]
            kTs = [AR.alloc((2, GRP), BF16) for _ in range(1)]
            vst = [AR.alloc((4, 256), BF16) for _ in range(1)]
            wl_rr = [0]
            s32_rr = [0]
            s16_rr = [0]
            tile_ctr = [0]
            loads = [(b, min(4, NCB - b)) for b in range(0, NCB, 4)]

            for g in range(NG):
                if "main" in _skip:
                    break
                t0 = g * GRP
                seg = t0 // SEG
                hTg, k_hT = hT[g % 2]
                for ti in range(4):
                    tt = tile_ctr[0]
                    tile_ctr[0] += 1
                    x_ap, k_x = xt[tt % 2]
                    h_ap, k_h = ht[tt % 2]
                    tok = t0 + ti * 128
                    DMA("sp", x_ap, xs[tok:tok + 128, :], [], [k_x])
                    rmsnorm_tile(x_ap, k_x, ng_b, k_ng, h_ap, k_h, junk, k_junk,
                                 st_small[:, tt % 2, 0:1], k_sts + f"a{tt % 2}", st_small[:, tt % 2, 1:2], k_sts + f"b{tt % 2}")
                    for half in range(2):
                        pb, kpb = bank()
                        pbb = pb[:].bitcast(BF16)
                        for j in range(8):
                            kc = half * 8 + j
                            TR(pbb[:, j * 128:(j + 1) * 128], h_ap[:, kc * 128:(kc + 1) * 128], identb, [k_h, k_idb], [kpb])
                        CP(evac_eng(), hTg[:, half * 8:(half + 1) * 8, ti * 128:(ti + 1) * 128],
                           pbb[:, 0:1024].rearrange("p (a b) -> p a b", a=8), [kpb], [k_hT])
                cs_ap, k_cs = cst[0]
                qTs_ap, k_qTs = qTs[0]
                kTs_ap, k_kTs = kTs[0]
                vst_ap, k_vst = vst[0]
                for (b0, nb) in loads:
                    if "fm" in _skip and b0 < 40:
                        continue
                    if "tm" in _skip and b0 >= 40:
                        continue
                    w_ap, k_w = wblk[wl_rr[0] % 3]
                    wl_rr[0] += 1
                    DMA("sp", w_ap[:, 0:nb], W1[b0 * 128:(b0 + nb) * 128, :].rearrange("(b p) (k j) -> p b k j", p=128, k=16),
                        [f"W1_{j}" for j in range(b0, b0 + nb)], [k_w])
                    if b0 < 40:
                        for bi in range(nb):
                            cb = b0 + bi
                            pb, kpb = bank()
                            for kc in range(16):
                                MM(pb[:, 0:GRP], w_ap[:, bi, kc, :], hTg[:, kc, :], kc == 0, kc == 15, [k_w, k_hT], [kpb])
                            if cb < 20:
                                s_ap, k_s = stg32[s32_rr[0] % 3]
                                s32_rr[0] += 1
                                CP(evac_eng(), s_ap, pb[:, 0:GRP], [kpb], [k_s])
                                DMA("sp", PR[cb, :, 1 + t0:1 + t0 + GRP], s_ap, [k_s], [f"PR{cb}_{g}"])
                            elif cb < 32:
                                s_ap, k_s = stg16[s16_rr[0] % 3]
                                s16_rr[0] += 1
                                ACT(s_ap, pb[:, 0:GRP], AF.Silu, [kpb], [k_s])
                                DMA("sp", G[cb - 20, :, t0:t0 + GRP], s_ap, [k_s], [f"G{cb}_{g}"])
                            elif cb < 36:
                                ACT(mg[:, cb - 32, :], pb[:, 0:GRP], AF.Silu, [kpb], [k_mg])
                            else:
                                CP("dve", mqT[:, cb - 36, :], pb[:, 0:GRP], [kpb], [k_mqT])
                        if b0 == 36 and "mat" not in _skip:
                            for hd in range(4):
                                PT_ap, k_PT = PTm[hd % 2]
                                for mc in range(2):
                                    pb, kpb = bank()
                                    MM(pb[:, 0:GRP], mkT[:, seg, hd, mc * 128:(mc + 1) * 128], mqT[:, hd, :], True, True, [k_mkT, k_mqT], [kpb])
                                    ACT(PT_ap[:, mc, :], pb[:, 0:GRP], AF.Exp, [kpb], [k_PT], scale=ATT_SCALE)
                                pby, kpby = bank()
                                pbs, kpbs = bank()
                                for mc in range(2):
                                    MM(pby[:, 0:GRP], mv[:, seg, mc, hd * 128:(hd + 1) * 128], PT_ap[:, mc, :], mc == 0, mc == 1, [k_mv, k_PT], [kpby])
                                for mc in range(2):
                                    MM(pbs[:, 0:GRP], onesb, PT_ap[:, mc, :], mc == 0, mc == 1, [k_onesb, k_PT], [kpbs])
                                P.op("dve", (lambda o, i: (lambda e: e.reciprocal(out=o, in_=i)))(rs_t, pbs[:, 0:GRP]), [kpbs], [k_rs])
                                TT("dve", ym_t, pby[:, 0:GRP], rs_t, ALU.mult, [kpby, k_rs], [k_ym])
                                s_ap, k_s = stg16[s16_rr[0] % 3]
                                s16_rr[0] += 1
                                TT("pool", s_ap, ym_t, mg[:, hd, :], ALU.mult, [k_ym, k_mg], [k_s])
                                DMA("sp", MIX[12 + hd, :, t0:t0 + GRP], s_ap, [k_s], [f"MIX{12 + hd}_{g}"])
                    else:
                        if b0 == 40:
                            P.barrier()
                            DMA("sp", cs_ap, cs_tab[t0:t0 + GRP].rearrange("(t p) c f -> p t c f", p=128), [], [k_cs])
                        for ti in range(4):
                            pb, kpb = bank()
                            for kc in range(16):
                                MM(pb[:, 0:nb * 128].rearrange("p (b j) -> p b j", b=nb), hTg[:, kc, ti * 128:(ti + 1) * 128], w_ap[:, 0:nb, kc, :],
                                   kc == 0, kc == 15, [k_w, k_hT], [kpb])
                            if b0 == 48:
                                CP(evac_eng(), vst_ap[:, ti, :], pb[:, 0:256], [kpb], [k_vst])
                                continue
                            if "qk" in _skip:
                                continue
                            hoff = 0 if b0 == 40 else 4
                            pv = pb[:, 0:512].rearrange("p (h d) -> p h d", h=4)
                            ACT(qsq[:, hoff:hoff + 4, :], pv, AF.Square, [kpb], [k_qsq] + [k_qsq + f"h{hoff + i}" for i in range(4)])
                            P.op("dve", (lambda o, i: (lambda e: e.tensor_reduce(out=o, in_=i, op=ALU.add, axis=mybir.AxisListType.X)))(
                                qss[:, hoff:hoff + 4], qsq[:, hoff:hoff + 4, :]), [k_qsq], [k_qss])
                            ACT(qrs[:, hoff:hoff + 4], qss[:, hoff:hoff + 4], AF.Ln, [k_qss, k_ceps], [k_qrs], scale=1.0 / 128, bias=c_eps)
                            ACT(qrs[:, hoff:hoff + 4], qrs[:, hoff:hoff + 4], AF.Exp, [k_qrs], [k_qrs], scale=-0.5)
                            for hh in range(4):
                                h8 = hoff + hh
                                gsel = 0 if h8 < 6 else 1
                                STT(qn[:, h8, :], pv[:, hh, :], qrs[:, h8:h8 + 1], qkg_b[:, gsel, :], ALU.mult, ALU.mult, [kpb, k_qrs, k_qkg], [k_qn])
                            if "rope" in _skip:
                                continue
                            for hh in range(4):
                                h8 = hoff + hh
                                qv = qn[:, h8, :].rearrange("p (f two) -> p f two", two=2)
                                x0, x1 = qv[:, :, 0], qv[:, :, 1]
                                cosb = cs_ap[:, ti, 0, :]
                                sinb = cs_ap[:, ti, 1, :]
                                ov = qsq[:, h8, :].rearrange("p (f two) -> p f two", two=2)
                                r1 = rp1[:, h8, :]
                                r2 = rp2[:, h8, :]
                                kq = k_qsq + f"h{h8}"
                                k1 = k_rp1 + f"h{h8}"
                                k2 = k_rp2 + f"h{h8}"
                                TT("dve", r1, x0, cosb, ALU.mult, [k_qn, k_cs], [k1])
                                TT("dve", r2, x1, sinb, ALU.mult, [k_qn, k_cs], [k2])
                                TT("dve", ov[:, :, 0], r1, r2, ALU.subtract, [k1, k2, k_qss], [kq])
                                TT("dve", r1, x0, sinb, ALU.mult, [k_qn, k_cs, kq], [k1])
                                TT("dve", r2, x1, cosb, ALU.mult, [k_qn, k_cs, kq], [k2])
                                TT("dve", ov[:, :, 1], r1, r2, ALU.add, [k1, k2, kq], [kq])
                                CP("dve", qr[:, h8, :], qsq[:, h8, :], [kq], [k_qr])
                            if "notr" in _skip:
                                continue
                            if "trbar" in _skip:
                                P.barrier()
                            pbt, kpbt = bank()
                            pbtb = pbt[:].bitcast(BF16)
                            for hh in range(4):
                                TR(pbtb[:, hh * 128:(hh + 1) * 128], qr[:, hoff + hh, :], identb, [k_qr, k_idb], [kpbt])
                            if "nocp" in _skip:
                                continue
                            for hh in range(4):
                                h8 = hoff + hh
                                if h8 < 6:
                                    CP(evac_eng(), qTs_ap[:, h8, ti * 128:(ti + 1) * 128], pbtb[:, hh * 128:(hh + 1) * 128], [kpbt], [k_qTs + f"h{h8}"])
                                else:
                                    CP(evac_eng(), kTs_ap[:, h8 - 6, ti * 128:(ti + 1) * 128], pbtb[:, hh * 128:(hh + 1) * 128], [kpbt], [k_kTs + f"h{h8 - 6}"])
                if "tm" in _skip or "qk" in _skip or "rope" in _skip or "notr" in _skip or "nocp" in _skip:
                    DMA("sp", Vd[t0:t0 + GRP, :].rearrange("(t p) c -> p t c", p=128), vst_ap, [k_vst], [f"V_{g}"])
                    P.barrier()
                    continue
                if "qst" not in _skip:
                    for hq in range(6):
                        DMA("sp", QT[hq, :, t0:t0 + GRP], qTs_ap[:, hq, :], [k_qTs + f"h{hq}"], [f"QT_{g}_{hq}"])
                    for hk in range(2):
                        DMA("sp", KTd[hk, :, t0:t0 + GRP], kTs_ap[:, hk, :], [k_kTs + f"h{hk}"], [f"KT_{g}_{hk}"])
                DMA("sp", Vd[t0:t0 + GRP, :].rearrange("(t p) c -> p t c", p=128), vst_ap, [k_vst], [f"V_{g}"])
                P.barrier()
            P.barrier()
            AR.release(m1)

        if 2 in passes:
            m2 = AR.mark()
            nb_cfg[0] = 5
            C = 128
            NCH = T // C
            CPS = SEG // C
            KD = DECAY_K
            pY0, kY0 = psb[5], "ps5"
            pY1, kY1 = psb[6], "ps6"
            pZ, kZ = psb[7], "ps7"
            gnv_b, k_gnv = AR.alloc((2, 768), F32)
            DMA("sp", gnv_b, gnv, [], [k_gnv])
            lup, k_lup = AR.alloc((2, 768), F32)
            DMA("sp", lup, lora_up, [], [k_lup])
            lupb, k_lupb = AR.alloc((2, 768), BF16)
            CP("dve", lupb, lup, [k_lup], [k_lupb])
            blk1b, k_blk1 = AR.alloc((128,), BF16)
            CP("dve", blk1b, c_f32[:, 706:834], [k_cf], [k_blk1])
            bselb, k_bsel = AR.alloc((2,), BF16)
            CP("dve", bselb, c_f32[:, 704:706], [k_cf], [k_bsel])
            ones_f, k_onesf = AR.alloc((128,), F32)
            MEMSET("dve", ones_f, 1.0, [], [k_onesf])
            c0v, k_c0v = AR.alloc((20,), F32)
            TT("dve", c0v, c_colv[:, 0:20], c_colv[:, 20:40], ALU.add, [k_colv], [k_c0v])
            TS("dve", c0v, c0v, -1.0, ALU.mult, [k_c0v], [k_c0v], s2=1.0, op1=ALU.add)
            m4 = []
            mLs = []
            for d_ in range(2):
                mt_, k_mt = AR.alloc((4, 128), F32)
                s_off, i_off = (128, 256) if d_ == 0 else (384, 512)
                for q_ in range(4):
                    off = s_off if q_ % 2 == 0 else i_off
                    CP("dve", mt_[:, q_, :], c_f32[:, off:off + 128], [k_cf, k_mt], [k_mt])
                m4.append((mt_, k_mt))
                mLs.append(c_f32[:, 384:512] if d_ == 0 else c_f32[:, 128:256])
            mu_p = c_colv[:, 0:20].unsqueeze(2).to_broadcast([128, 20, 128])
            mu_n = c_colv[:, 20:40].unsqueeze(2).to_broadcast([128, 20, 128])
            c0_b = c0v.unsqueeze(2).to_broadcast([128, 20, 128])
            kk_b = c_colv[:, 64:70].unsqueeze(2).to_broadcast([128, 6, 128])
            ka_b = c_colv[:, 70:76].unsqueeze(2).to_broadcast([128, 6, 128])
            rk_b = c_colv[:, 76:82].unsqueeze(2).to_broadcast([128, 6, 128])
            flag_ap = c_flag[:, 0:1]
            praw, k_praw = AR.alloc((20, 130), F32)
            hs_bufs = [AR.alloc((26, 128), F32) for _ in range(2)]
            tmp20, k_tmp20 = AR.alloc((20, 128), F32)
            f6 = lambda: AR.alloc((6, 128), F32)
            kkraw, k_kkraw = f6()
            rn, k_rn = f6()
            sg, k_sg = f6()
            a_t, k_at = f6()
            cs, k_cs = f6()
            ex, k_ex = f6()
            eL, k_eL = f6()
            emL, k_emL = f6()
            eLx, k_eLx = f6()
            b_t, k_bt = f6()
            kp, k_kp = f6()
            t1, k_t1 = f6()
            wtot, k_wtot = AR.alloc((6,), F32)
            b6 = lambda: AR.alloc((6, 128), BF16)
            sqb, k_sqb = b6()
            aq, k_aq = AR.alloc((6, 2, 128), BF16)
            btl, k_btl = b6()
            ktl, k_ktl = b6()
            vb, k_vb = b6()
            atm, k_atm = b6()
            btm, k_btm = b6()
            ktm, k_ktm = b6()
            vtm, k_vtm = b6()
            prodb, k_prodb = b6()
            twl, k_twl = AR.alloc((128,), BF16)
            alb, k_alb = AR.alloc((128,), BF16)
            AT, k_AT = AR.alloc((12, 4, 128), BF16)
            Pp = [AR.alloc((12, 2, 128), BF16) for _ in range(2)]
            Rb = [AR.alloc((12, 128), BF16) for _ in range(2)]
            nU, k_nU = AR.alloc((12, 64), BF16)
            IXb, k_IXb = AR.alloc((6, 64), BF16)
            QhT, k_QhT = b6()
            Gp, k_Gp = AR.alloc((6, 64), F32)
            ST, k_ST = AR.alloc((6, 64), F32)
            STb, k_STb = AR.alloc((6, 64), BF16)
            ztmp, k_ztmp = AR.alloc((6, 64), F32)
            ysb, k_ysb = AR.alloc((768,), F32)
            yf_t, k_yf = AR.alloc((768,), F32)
            yc, k_yc = AR.alloc((768,), F32)
            ysq, k_ysq = AR.alloc((768,), F32)
            st12, k_st12 = AR.alloc((4, 12), F32)
            bon, k_bon = AR.alloc((12,), F32)
            gate2, k_gate2 = b6()
            mixo, k_mixo = b6()

            def exp_op(out, k_out, in_, k_in, scale):
                ACT(out, in_, AF.Exp, [k_in], [k_out], scale=scale)

            for d_ in range(2):
                MEMSET("dve", ST, 0.0, [], [k_ST])
                MEMSET("dve", STb, 0.0, [], [k_STb])
                order = list(range(NCH)) if d_ == 0 else list(range(NCH - 1, -1, -1))
                m4t, k_m4 = m4[d_]
                mL = mLs[d_]
                for ci, c in enumerate(order):
                    t0 = c * C
                    cross = (c % CPS == 0 and c > 0) if d_ == 0 else ((c + 1) % CPS == 0 and c < NCH - 1)
                    if cross:
                        TS("dve", ST, ST, flag_ap, ALU.mult, [k_ST, k_flag], [k_ST])
                        CP("dve", STb, ST, [k_ST], [k_STb])
                    hsb, k_hs = hs_bufs[ci % 2]
                    hs = hsb[:, 0:20, :]
                    kk = hsb[:, 20:26, :]
                    k_kk = k_hs + "kk"
                    r_ = hs[:, 0:6, :]
                    k_ = hs[:, 6:12, :]
                    v_ = hs[:, 12:18, :]
                    if d_ == 1:
                        DMA("sp", hsb, HSd[c].rearrange("p (a b) -> p a b", a=26), [f"HS_{c}"], [k_hs, k_kk])
                    else:
                        DMA("sp", praw, PR[:, :, t0:t0 + 130].rearrange("b p t -> p b t"), [], [k_praw])
                        if c % CPS == 0 and c > 0:
                            TS("dve", praw[:, :, 0:1], praw[:, :, 0:1], flag_ap, ALU.mult, [k_praw, k_flag], [k_praw])
                        if (c + 1) % CPS == 0 and c < NCH - 1:
                            TS("dve", praw[:, :, 129:130], praw[:, :, 129:130], flag_ap, ALU.mult, [k_praw, k_flag], [k_praw])
                        TT("dve", hs, praw[:, :, 1:129], c0_b, ALU.mult, [k_praw, k_c0v], [k_hs])
                        TT("dve", tmp20, praw[:, :, 0:128], mu_p, ALU.mult, [k_praw, k_colv], [k_tmp20])
                        TT("dve", hs, hs, tmp20, ALU.add, [k_hs, k_tmp20], [k_hs])
                        TT("dve", tmp20, praw[:, :, 2:130], mu_n, ALU.mult, [k_praw, k_colv, k_hs], [k_tmp20])
                        TT("dve", hs, hs, tmp20, ALU.add, [k_hs, k_tmp20], [k_hs])
                        r_ = hs[:, 0:6, :]
                        k_ = hs[:, 6:12, :]
                        v_ = hs[:, 12:18, :]
                        TT("dve", kkraw, k_, kk_b, ALU.mult, [k_hs, k_colv], [k_kkraw])
                        ACT(sqb, kkraw, AF.Square, [k_kkraw], [k_sqb])
                        pb, kpb = bank()
                        MM(pb[:, 0:512], blk1b, sqb[:, 0:4, :], True, True, [k_blk1, k_sqb], [kpb])
                        pb2, kpb2 = bank()
                        MM(pb2[:, 0:256], blk1b, sqb[:, 4:6, :], True, True, [k_blk1, k_sqb], [kpb2])
                        ACT(rn[:, 0:4, :], pb[:, 0:512].rearrange("p (a b) -> p a b", a=4), AF.Ln, [kpb, k_ceps], [k_rn], bias=c_tiny)
                        ACT(rn[:, 4:6, :], pb2[:, 0:256].rearrange("p (a b) -> p a b", a=2), AF.Ln, [kpb2, k_ceps, k_rn], [k_rn], bias=c_tiny)
                        ACT(rn, rn, AF.Exp, [k_rn], [k_rn], scale=-0.5)
                        TT("dve", kk, kkraw, rn, ALU.mult, [k_kkraw, k_rn], [k_kk])
                        DMA("sp", HSd[c].rearrange("p (a b) -> p a b", a=26), hsb, [k_hs, k_kk], [f"HS_{c}"])
                    ACT(twl, hs[:, 18, :], AF.Tanh, [k_hs], [k_twl])
                    CP("dve", alb, hs[:, 19, :], [k_hs], [k_alb])
                    hsl = slice(d_ * 64, (d_ + 1) * 64)
                    for which in range(2):
                        src = twl if which == 0 else alb
                        ksrc = k_twl if which == 0 else k_alb
                        dst, kdst = (sg, k_sg) if which == 0 else (a_t, k_at)
                        cbase = 40 if which == 0 else 52
                        pb, kpb = bank()
                        pb2, kpb2 = bank()
                        for blk in range(6):
                            tgt, ktgt = (pb, kpb) if blk < 4 else (pb2, kpb2)
                            cc = (blk % 4) * 128
                            MM(tgt[:, cc:cc + 128], lupb[hsl, which, blk * 128:(blk + 1) * 128], src[hsl, :], True, True, [k_lupb, ksrc], [ktgt])
                        for blk in range(6):
                            tgt, ktgt = (pb, kpb) if blk < 4 else (pb2, kpb2)
                            cc = (blk % 4) * 128
                            ACT(dst[:, blk, :], tgt[:, cc:cc + 128], AF.Sigmoid, [ktgt, k_colv, kdst], [kdst],
                                bias=c_colv[:, cbase + d_ * 6 + blk:cbase + d_ * 6 + blk + 1])
                    for blk in range(6):
                        P.op("dve", (lambda o, d1: (lambda e: e.tensor_tensor_scan(out=o, data0=ones_f, data1=d1, initial=0.0, op0=ALU.mult, op1=ALU.add)))(
                            cs[:, blk, :], sg[:, blk, :]), [k_sg, k_onesf, k_cs], [k_cs])
                    ACT(wtot.unsqueeze(2), cs[:, :, 127:128], AF.Exp, [k_cs], [k_wtot], scale=-KD)
                    if d_ == 1:
                        TT("dve", ex, sg, cs, ALU.subtract, [k_sg, k_cs], [k_ex])
                        TT("dve", cs, ex, cs[:, :, 127:128].to_broadcast([128, 6, 128]), ALU.add, [k_ex, k_cs], [k_cs])
                    TT("dve", ex, cs, sg, ALU.subtract, [k_cs, k_sg], [k_ex])
                    exp_op(eL, k_eL, cs, k_cs, -KD)
                    exp_op(emL, k_emL, cs, k_cs, KD)
                    exp_op(eLx, k_eLx, ex, k_ex, -KD)
                    TT("dve", b_t, kk, a_t, ALU.mult, [k_kk, k_at], [k_bt])
                    STT(t1, a_t, -1.0, ka_b, ALU.add, ALU.mult, [k_at, k_colv], [k_t1])
                    STT(kp, t1, 1.0, k_, ALU.add, ALU.mult, [k_t1, k_hs], [k_kp])
                    TT("dve", aq[:, :, 1, :], r_, eL, ALU.mult, [k_hs, k_eL], [k_aq])
                    TT("dve", aq[:, :, 0, :], kk, eLx, ALU.mult, [k_kk, k_eLx, k_aq], [k_aq])
                    TT("dve", btl, b_t, emL, ALU.mult, [k_bt, k_emL], [k_btl])
                    TT("dve", ktl, kp, emL, ALU.mult, [k_kp, k_emL], [k_ktl])
                    CP("dve", vb, v_, [k_hs], [k_vb])
                    for (src3, ksrc, dst3, kdst, sel) in ((aq, k_aq, atm, k_atm, 0), (btl, k_btl, btm, k_btm, None), (ktl, k_ktl, ktm, k_ktm, None), (vb, k_vb, vtm, k_vtm, None)):
                        pb, kpb = bank()
                        pbb = pb[:].bitcast(BF16)
                        for blk in range(6):
                            sin = src3[:, blk, 0, :] if sel is not None else src3[:, blk, :]
                            TR(pbb[:, blk * 128:(blk + 1) * 128], sin, identb, [ksrc, k_idb], [kpb])
                        CP("dve", dst3, pbb[:, 0:768].rearrange("p (a b) -> p a b", a=6), [kpb], [kdst])
                    def hsl_(hd):
                        return hd // 2, slice((hd % 2) * 64, (hd % 2) * 64 + 64)
                    P0, k_P0 = Pp[0]
                    R0, k_R0 = Rb[0]
                    for hd in range(12):
                        blk, hp = hsl_(hd)
                        pA, kpA = bank()
                        MM(pA[:, 0:256].rearrange("p (a b) -> p a b", a=2), btl[hp, blk, :], aq[hp, blk, :, :], True, True, [k_btl, k_aq], [kpA])
                        MM(pA[:, 256:512].rearrange("p (a b) -> p a b", a=2), ktl[hp, blk, :], aq[hp, blk, :, :], True, True, [k_ktl, k_aq], [kpA])
                        TT("dve", AT[:, hd, :, :], pA[:, 0:512].rearrange("p (a b) -> p a b", a=4), m4t, ALU.mult, [kpA, k_m4], [k_AT + f"{hd}"])
                        STT(P0[:, hd, 1, :], pA[:, 0:128], -1.0, m4t[:, 0, :], ALU.mult, ALU.mult, [kpA, k_m4], [k_P0 + f"t{hd}"])
                        pL, kpL = bank()
                        MM(pL[:, 0:128], aq[hp, blk, 0, :], btl[hp, blk, :], True, True, [k_aq, k_btl], [kpL])
                        STT(P0[:, hd, 0, :], pL[:, 0:128], -1.0, mL, ALU.mult, ALU.mult, [kpL, k_cf], [k_P0 + f"n{hd}"])
                    for hd in range(12):
                        blk, hp = hsl_(hd)
                        pR, kpR = bank()
                        MM(pR[:, 0:64], AT[:, hd, 2, :], vtm[:, blk, hp], True, True, [k_AT + f"{hd}", k_vtm], [kpR])
                        CP("dve", R0[:, hd, 0:64], atm[:, blk, hp], [k_atm], [k_R0 + f"a{hd}"])
                        CP("dve", R0[:, hd, 64:128], pR[:, 0:64], [kpR], [k_R0 + f"b{hd}"])
                    for lv in range(7):
                        Pc, k_Pc = Pp[lv % 2]
                        Pn_, k_Pn = Pp[(lv + 1) % 2]
                        Rc, k_Rc = Rb[lv % 2]
                        Rn, k_Rn = Rb[(lv + 1) % 2]
                        for hd in range(12):
                            kP = [k_Pc + f"t{hd}", k_Pc + f"n{hd}"]
                            kR = [k_Rc + f"a{hd}", k_Rc + f"b{hd}"]
                            pD, kpD = bank()
                            MM(pD[:, 0:128], Pc[:, hd, 1, :], Rc[:, hd, :], True, True, kP + kR, [kpD])
                            if lv < 6:
                                MM(pD[:, 128:256], Pc[:, hd, 1, :], Pc[:, hd, 0, :], True, True, kP, [kpD])
                                MM(pD[:, 256:384], Pc[:, hd, 0, :], Pc[:, hd, 1, :], True, True, kP, [kpD])
                            TT("dve", Rn[:, hd, :], pD[:, 0:128], Rc[:, hd, :], ALU.add, [kpD] + kR, [k_Rn + f"a{hd}", k_Rn + f"b{hd}"])
                            if lv < 6:
                                CP("dve", Pn_[:, hd, :, :], pD[:, 128:384].rearrange("p (a b) -> p a b", a=2), [kpD], [k_Pn + f"n{hd}", k_Pn + f"t{hd}"])
                    Rf, k_Rf = Rb[1]
                    for hd in range(12):
                        blk, hp = hsl_(hd)
                        kRf = [k_Rf + f"a{hd}", k_Rf + f"b{hd}"]
                        TS("dve", nU[:, hd, :], Rf[:, hd, 64:128], -1.0, ALU.mult, kRf, [k_nU + f"{hd}"])
                        pE, kpE = bank()
                        MM(pE[hp, 0:64], Rf[:, hd, 0:64], btm[:, blk, hp], True, True, kRf + [k_btm], [kpE])
                        MM(pE[hp, 128:256], Rf[:, hd, 0:64], AT[:, hd, 1, :], True, True, kRf + [k_AT + f"{hd}"], [kpE])
                        MM(pE[hp, 64:128], ktm[:, blk, hp], vtm[:, blk, hp], True, False, [k_ktm, k_vtm], [kpE])
                        MM(pE[hp, 64:128], btm[:, blk, hp], nU[:, hd, :], False, True, [k_btm, k_nU + f"{hd}"], [kpE])
                        TT("dve", IXb[hp, blk, :], c_f32[hp, 640:704], pE[hp, 0:64], ALU.subtract, [k_cf, kpE], [k_IXb + f"{hd}"])
                        TT("dve", QhT[hp, blk, :], aq[hp, blk, 1, :], pE[hp, 128:256], ALU.subtract, [k_aq, kpE], [k_QhT + f"{hd}"])
                        CP("dve", Gp[hp, blk, :], pE[hp, 64:128], [kpE], [k_Gp + f"{hd}"])
                    for hd in range(12):
                        blk, hp = hsl_(hd)
                        pY, kY = (pY0, kY0) if hd < 8 else (pY1, kY1)
                        yc0 = (hd % 8) * 64
                        MM(pY[:, yc0:yc0 + 64], AT[:, hd, 3, :], vtm[:, blk, hp], True, False, [k_AT + f"{hd}", k_vtm], [kY])
                        MM(pY[:, yc0:yc0 + 64], AT[:, hd, 1, :], nU[:, hd, :], False, False, [k_AT + f"{hd}", k_nU + f"{hd}"], [kY])
                        MM(pY[:, yc0:yc0 + 64], QhT[hp, blk, :], STb[hp, blk, :], False, True, [k_QhT + f"{hd}", k_STb], [kY])
                        MM(pZ[hp, blk * 64:(blk + 1) * 64], IXb[hp, blk, :], STb[hp, blk, :], True, True, [k_IXb + f"{hd}", k_STb], [kZ])
                    kGp_all = [k_Gp + f"{i}" for i in range(12)]
                    TT("dve", ztmp, pZ[:, 0:384].rearrange("p (a b) -> p a b", a=6), Gp, ALU.add, [kZ] + kGp_all, [k_ztmp])
                    TT("dve", ST, ztmp, wtot.unsqueeze(2).to_broadcast([128, 6, 64]), ALU.mult, [k_ztmp, k_wtot], [k_ST])
                    CP("dve", STb, ST, [k_ST], [k_STb])
                    CP("dve", ysb[:, 0:512], pY0[:, 0:512], [kY0], [k_ysb + "0"])
                    CP("dve", ysb[:, 512:768], pY1[:, 0:256], [kY1], [k_ysb + "1"])
                    k_ysb_all = [k_ysb + "0", k_ysb + "1"]
                    if d_ == 0:
                        DMA("sp", YF[t0:t0 + C, :], ysb, k_ysb_all, [f"YF_{c}"])
                        continue
                    DMA("sp", yf_t, YF[t0:t0 + C, :], [f"YF_{c}"], [k_yf])
                    DMA("sp", gate2, G[0:6, :, t0:t0 + C].rearrange("b p t -> p b t"), [], [k_gate2])
                    TT("dve", ysb, ysb, yf_t, ALU.add, k_ysb_all + [k_yf], k_ysb_all)
                    y3 = ysb.rearrange("p (h n) -> p h n", h=12)
                    yc3 = yc.rearrange("p (h n) -> p h n", h=12)
                    ysq3 = ysq.rearrange("p (h n) -> p h n", h=12)
                    mu = st12[:, 0, :]
                    var = st12[:, 1, :]
                    rstd = st12[:, 2, :]
                    P.op("dve", (lambda o, i: (lambda e: e.tensor_reduce(out=o, in_=i, op=ALU.add, axis=mybir.AxisListType.X)))(mu, y3), k_ysb_all, [k_st12 + "m"])
                    TS("dve", mu, mu, 1.0 / 64, ALU.mult, [k_st12 + "m"], [k_st12 + "m"])
                    TT("dve", yc3, y3, mu.unsqueeze(2).to_broadcast([128, 12, 64]), ALU.subtract, k_ysb_all + [k_st12 + "m"], [k_yc])
                    TT("dve", ysq, yc, yc, ALU.mult, [k_yc], [k_ysq])
                    P.op("dve", (lambda o, i: (lambda e: e.tensor_reduce(out=o, in_=i, op=ALU.add, axis=mybir.AxisListType.X)))(var, ysq3), [k_ysq], [k_st12 + "v"])
                    ACT(rstd, var, AF.Ln, [k_st12 + "v", k_ceps], [k_st12 + "r"], scale=1.0 / 64, bias=c_gneps)
                    ACT(rstd, rstd, AF.Exp, [k_st12 + "r"], [k_st12 + "r"], scale=-0.5)
                    TT("dve", yc3, yc3, rstd.unsqueeze(2).to_broadcast([128, 12, 64]), ALU.mult, [k_yc, k_st12 + "r"], [k_yc])
                    TT("dve", yc, yc, gnv_b[:, 0, :], ALU.mult, [k_yc, k_gnv], [k_yc])
                    TT("dve", yc, yc, gnv_b[:, 1, :], ALU.add, [k_yc, k_gnv], [k_yc])
                    TT("dve", t1, r_, k_, ALU.mult, [k_hs], [k_t1])
                    TT("dve", prodb, t1, rk_b, ALU.mult, [k_t1, k_colv], [k_prodb])
                    pB, kpB = bank()
                    for blk in range(6):
                        MM(pB[:, blk * 2:(blk + 1) * 2], prodb[:, blk, :], bselb, True, True, [k_prodb, k_bsel], [kpB])
                    CP("dve", bon, pB[:, 0:12], [kpB], [k_bon])
                    TT("dve", ysq3, vtm.rearrange("p b (h n) -> p (b h) n", h=2), bon.unsqueeze(2).to_broadcast([128, 12, 64]), ALU.mult, [k_vtm, k_bon], [k_ysq])
                    TT("dve", yc, yc, ysq, ALU.add, [k_yc, k_ysq], [k_yc])
                    pT0, kpT0 = bank()
                    pT1, kpT1 = bank()
                    for blk in range(6):
                        tgt, ktgt = (pT0, kpT0) if blk < 4 else (pT1, kpT1)
                        cc = (blk % 4) * 128
                        TR(tgt[:, cc:cc + 128], yc[:, blk * 128:(blk + 1) * 128], ident_f, [k_yc, k_cf], [ktgt])
                    TT("dve", mixo[:, 0:4, :], pT0[:, 0:512].rearrange("p (a b) -> p a b", a=4), gate2[:, 0:4, :], ALU.mult, [kpT0, k_gate2], [k_mixo + "0"])
                    TT("dve", mixo[:, 4:6, :], pT1[:, 0:256].rearrange("p (a b) -> p a b", a=2), gate2[:, 4:6, :], ALU.mult, [kpT1, k_gate2], [k_mixo + "1"])
                    DMA("sp", MIX[0:6, :, t0:t0 + C].rearrange("b p t -> p b t"), mixo, [k_mixo + "0", k_mixo + "1"], [f"MIXr_{c}"])
            P.barrier()
            nb_cfg[0] = 8
            AR.release(m2)

        if 3 in passes:
            m3 = AR.mark()
            nb_cfg[0] = 6
            KT_sb, k_KT = AR.alloc((2, T), BF16)
            V_sb, k_V = AR.alloc((NT, 256), BF16)
            for hk in range(2):
                for tq in range(0, T, 2048):
                    nq = min(2048, T - tq)
                    DMA("sp", KT_sb[:, hk, tq:tq + nq], KTd[hk, :, tq:tq + nq], [], [k_KT + f"_{hk}_{tq}"])
            k_KT_all = [k_KT + f"_{hk}_{tq}" for hk in range(2) for tq in range(0, T, 2048)]
            for tq in range(0, NT, 16):
                nq = min(16, NT - tq)
                DMA("sp", V_sb[:, tq:tq + nq, :], Vd[tq * 128:(tq + nq) * 128, :].rearrange("(t p) c -> p t c", p=128), [], [k_V + f"_{tq}"])
            k_V_all = [k_V + f"_{tq}" for tq in range(0, NT, 16)]
            qT3, k_qT3 = AR.alloc((6, GRP), BF16)
            g3, k_g3 = AR.alloc((6, GRP), BF16)
            PT3 = [AR.alloc((GRP,), BF16) for _ in range(2)]
            rs3, k_rs3 = AR.alloc((GRP,), F32)
            y3, k_y3 = AR.alloc((GRP,), F32)
            o3 = [AR.alloc((GRP,), BF16) for _ in range(2)]
            pO, kpO = psb[6], "ps6"
            pS, kpS = psb[7], "ps7"
            for qg in range(NG):
                t0 = qg * GRP
                qseg = t0 // SEG
                for hq in range(6):
                    DMA("sp", qT3[:, hq, :], QT[hq, :, t0:t0 + GRP], [], [k_qT3 + f"h{hq}"])
                    DMA("sp", g3[:, hq, :], G[6 + hq, :, t0:t0 + GRP], [], [k_g3 + f"h{hq}"])
                for hq in range(6):
                    kvh = hq // 3
                    for kt in range(NT):
                        kseg = (kt * 128) // SEG
                        PT_ap, k_PT = PT3[kt % 2]
                        pb, kpb = bank()
                        MM(pb[:, 0:GRP], KT_sb[:, kvh, kt * 128:(kt + 1) * 128], qT3[:, hq, :], True, True, k_KT_all + [k_qT3 + f"h{hq}"], [kpb])
                        ACT(PT_ap, pb[:, 0:GRP], AF.Exp, [kpb, k_flag], [k_PT], scale=ATT_SCALE,
                            bias=c_flag[:, 8 + qseg * NSEG + kseg:8 + qseg * NSEG + kseg + 1])
                        MM(pO[:, 0:GRP], V_sb[:, kt, kvh * 128:(kvh + 1) * 128], PT_ap, kt == 0, kt == NT - 1, k_V_all + [k_PT], [kpO])
                        MM(pS[:, 0:GRP], onesb, PT_ap, kt == 0, kt == NT - 1, [k_onesb, k_PT], [kpS])
                    P.op("dve", (lambda o, i: (lambda e: e.reciprocal(out=o, in_=i)))(rs3, pS[:, 0:GRP]), [kpS], [k_rs3])
                    TT("dve", y3, pO[:, 0:GRP], rs3, ALU.mult, [kpO, k_rs3], [k_y3])
                    o_ap, k_o = o3[hq % 2]
                    TT("dve", o_ap, y3, g3[:, hq, :], ALU.mult, [k_y3, k_g3 + f"h{hq}"], [k_o])
                    DMA("sp", MIX[6 + hq, :, t0:t0 + GRP], o_ap, [k_o], [f"MIXa{hq}_{qg}"])
            P.barrier()
            nb_cfg[0] = 8
            AR.release(m3)

        if 4 in passes:
            m4 = AR.mark()
            fg_b, k_fg = AR.alloc((D,), F32)
            DMA("sp", fg_b, rowv[:, 1, :], [], [k_fg])
            wo_sb, k_wo = AR.alloc((16, D), BF16)
            for i in range(4):
                DMA("sp", wo_sb[:, i * 4:(i + 1) * 4, :], WO.rearrange("(p k) n -> p k n", k=16)[:, i * 4:(i + 1) * 4, :], [], [k_wo + f"_{i}"])
            k_wo_all = [k_wo + f"_{i}" for i in range(4)]
            mixt = [AR.alloc((16, 128), BF16) for _ in range(2)]
            x4 = [AR.alloc((D,), F32) for _ in range(2)]
            r4 = [AR.alloc((D,), F32) for _ in range(2)]
            y4 = [AR.alloc((D,), F32) for _ in range(2)]
            junk4, k_junk4 = AR.alloc((D,), BF16)
            st4, k_st4 = AR.alloc((2, 2), F32)
            for tt in range(NT):
                tok = tt * 128
                m_ap, k_m = mixt[tt % 2]
                x_ap, k_x = x4[tt % 2]
                r_ap, k_r = r4[tt % 2]
                y_ap, k_y = y4[tt % 2]
                DMA("sp", m_ap, MIX[:, :, tok:tok + 128].rearrange("c p t -> p c t"), [], [k_m])
                DMA("sp", x_ap, xs[tok:tok + 128, :], [], [k_x])
                for ng in range(4):
                    pb, kpb = bank()
                    for kc in range(16):
                        MM(pb[:, 0:512], m_ap[:, kc, :], wo_sb[:, kc, ng * 512:(ng + 1) * 512], kc == 0, kc == 15, [k_m] + k_wo_all, [kpb])
                    TT("dve", r_ap[:, ng * 512:(ng + 1) * 512], pb[:, 0:512], x_ap[:, ng * 512:(ng + 1) * 512], ALU.add, [kpb, k_x], [k_r + f"_{ng}"])
                kr_all = [k_r + f"_{ng}" for ng in range(4)]
                ssq = st4[:, tt % 2, 0:1]
                rstd = st4[:, tt % 2, 1:2]
                ks1 = k_st4 + f"a{tt % 2}"
                ks2 = k_st4 + f"b{tt % 2}"
                ACT(junk4, r_ap, AF.Square, kr_all, [k_junk4, ks1], accum_out=ssq)
                ACT(rstd, ssq, AF.Ln, [ks1, k_ceps], [ks2], scale=1.0 / D, bias=c_eps)
                ACT(rstd, rstd, AF.Exp, [ks2], [ks2], scale=-0.5)
                STT(y_ap, r_ap, rstd, fg_b, ALU.mult, ALU.mult, kr_all + [ks2, k_fg], [k_y])
                DMA("sp", y_out[tok:tok + 128, :], y_ap, [k_y], [f"y_{tt}"])
            P.barrier()
            AR.release(m4)

        P.barrier()
        P.finalize()
        P.emit()
    return nc


def _rope_tab(nseg, seg, carry):
    if carry:
        pos = np.arange(nseg * seg)
    else:
        pos = np.tile(np.arange(seg), nseg)
    row = (pos // 64).astype(np.float32)
    col = (pos % 64).astype(np.float32)
    freqs = (np.float32(10000.0) ** (-np.arange(0, 64, 2, dtype=np.float32) / np.float32(64))).astype(np.float32)
    ang = np.concatenate([row[:, None] * freqs, col[:, None] * freqs], axis=-1).astype(np.float32)
    return np.stack([np.cos(ang), np.sin(ang)], axis=1).astype(np.float32)


def _consts():
    c = np.zeros((128, 1024), np.float32)
    r = np.arange(128)[:, None]
    q = np.arange(128)[None, :]
    c[:, 0:128] = (r == q)
    c[:, 128:256] = (q > r)
    c[:, 256:384] = (q >= r)
    c[:, 384:512] = (q < r)
    c[:, 512:640] = (q <= r)
    c[:, 640:704] = (np.arange(64)[None, :] == (r % 64))
    c[:, 704:706] = (np.arange(2)[None, :] == (r // 64))
    c[:, 706:834] = ((r // 64) == (q // 64))
    return c


def shared_inputs(norm_g, w_in, mu_prev, mu_next, w0, w_up, a0, a_up, k_k, k_a, r_k, gn_g, gn_b,
                  q_norm_g, k_norm_g, mem_norm_g, w_mem_kv, w_out, final_g):
    f = lambda a: np.ascontiguousarray(np.asarray(a, dtype=np.float32))
    w_in = f(w_in)[0]
    wt = w_in.reshape(16, 128, NCB, 128)[:, :, CB_PERM, :]
    w_in_t = np.ascontiguousarray(wt.transpose(2, 1, 0, 3)).reshape(NCB * 128, D)
    w_out_t = np.ascontiguousarray(f(w_out)[0].reshape(16, 128, D).transpose(1, 0, 2)).reshape(128 * 16, D)
    w_kv_t = np.ascontiguousarray(f(w_mem_kv)[0].reshape(16, 128, 1024).transpose(1, 0, 2)).reshape(128 * 16, 1024)
    rowv = np.ascontiguousarray(np.broadcast_to(np.stack([f(norm_g)[0], f(final_g), f(mem_norm_g)[0]])[None], (128, 3, D)))
    qkg = np.ascontiguousarray(np.broadcast_to(np.stack([f(q_norm_g)[0], f(k_norm_g)[0]])[None], (128, 2, 128)))
    gnv = np.ascontiguousarray(np.broadcast_to(np.stack([f(gn_g)[0], f(gn_b)[0]])[None], (128, 2, 768)))
    colv = np.zeros((128, 96), np.float32)
    colv[:, 0:20] = f(mu_prev)[0].reshape(20, 128).T
    colv[:, 20:40] = f(mu_next)[0].reshape(20, 128).T
    colv[:, 40:52] = f(w0)[0].reshape(12, 128).T
    colv[:, 52:64] = f(a0)[0].reshape(12, 128).T
    colv[:, 64:70] = f(k_k)[0].reshape(6, 128).T
    colv[:, 70:76] = f(k_a)[0].reshape(6, 128).T
    colv[:, 76:82] = f(r_k)[0].reshape(6, 128).T
    lora_up = np.ascontiguousarray(np.stack([f(w_up)[0].reshape(128, 768), f(a_up)[0].reshape(128, 768)], axis=1))
    return dict(w_in_t=w_in_t, w_out_t=w_out_t, w_kv_t=w_kv_t, rowv=rowv, qkg=qkg, gnv=gnv, colv=colv,
                lora_up=lora_up, consts=_consts())


def core_inputs(shared, x_core, mem_core, nseg, seg, carry):
    flags = np.zeros((128, 32), np.float32)
    flags[:, 0] = 1.0 if carry else 0.0
    for qs in range(nseg):
        for ks in range(nseg):
            flags[:, 8 + qs * nseg + ks] = 0.0 if (carry or qs == ks) else NEG
    d = dict(shared)
    d["xs"] = np.ascontiguousarray(x_core, dtype=np.float32)
    d["mem"] = np.ascontiguousarray(mem_core, dtype=np.float32)
    d["cs_tab"] = _rope_tab(nseg, seg, carry)
    d["flags"] = flags
    return d


_NC_CACHE = {}


def kernel(x_prompt, x_sample, mem_prompt, mem_sample, norm_g, w_in, mu_prev, mu_next, w0, w_up, a0, a_up,
           k_k, k_a, r_k, gn_g, gn_b, q_norm_g, k_norm_g, mem_norm_g, w_mem_kv, w_out, final_g):
    NSEG, SEG = 4, 2048
    x_prompt = np.asarray(x_prompt, dtype=np.float32)
    x_sample = np.asarray(x_sample, dtype=np.float32)
    mem_prompt = np.asarray(mem_prompt, dtype=np.float32)
    mem_sample = np.asarray(mem_sample, dtype=np.float32)
    shared = shared_inputs(norm_g, w_in, mu_prev, mu_next, w0, w_up, a0, a_up, k_k, k_a, r_k, gn_g, gn_b,
                           q_norm_g, k_norm_g, mem_norm_g, w_mem_kv, w_out, final_g)
    in_maps = []
    for c in range(4):
        in_maps.append(core_inputs(shared, x_prompt[c], np.broadcast_to(mem_prompt[c][None], (NSEG, N_MEM, D)), NSEG, SEG, True))
    for c in range(4):
        in_maps.append(core_inputs(shared, x_sample[4 * c:4 * c + 4].reshape(NSEG * SEG, D), mem_sample[4 * c:4 * c + 4], NSEG, SEG, False))
    key = (NSEG, SEG)
    if key not in _NC_CACHE:
        _NC_CACHE[key] = build(NSEG, SEG)
    nc = _NC_CACHE[key]
    res = run_bass_kernel_spmd(nc, in_maps, core_ids=list(range(8)))
    yp = np.stack([np.asarray(res.results[c]["y"], dtype=np.float32) for c in range(4)])
    ysm = np.concatenate([np.asarray(res.results[4 + c]["y"], dtype=np.float32).reshape(4, SEG, D) for c in range(4)], axis=0)
    return (yp, ysm)
```

```python
import contextlib
import numpy as np
import ml_dtypes
import concourse.bass as bass
import concourse.mybir as mybir
from concourse.bass_utils import run_bass_kernel_spmd

F32 = mybir.dt.float32
BF16 = mybir.dt.bfloat16
ALU = mybir.AluOpType
AF = mybir.ActivationFunctionType

D = 2048
IN_W = 6400
NCB = 50
N_MEM = 256
NORM_EPS = 1e-6
GN_EPS = 64e-5
DECAY_K = float(np.exp(-0.5))
ATT_SCALE = 128 ** -0.5
NEG = -30000.0

CB_PERM = list(range(0, 20)) + list(range(20, 26)) + list(range(36, 42)) + list(range(46, 50)) + \
    list(range(42, 46)) + list(range(26, 32)) + [32, 33] + [34, 35]

ENGS = ("pe", "dve", "act", "pool", "sp")
SEM_EPOCH = 20000
import os as _osg
SERIAL = int(_osg.environ.get("K_SERIAL", "0"))


class Op:
    __slots__ = ("eng", "fn", "deps", "is_dma", "seq", "signal", "sig_idx", "dma_sem", "dma_val", "waits", "barrier")

    def __init__(self, eng, fn, is_dma):
        self.eng = eng
        self.fn = fn
        self.is_dma = is_dma
        self.deps = set()
        self.signal = False
        self.sig_idx = 0
        self.dma_sem = None
        self.dma_val = 0
        self.waits = []
        self.barrier = False


class Prog:
    def __init__(self, nc, n_dma_sems=8):
        self.nc = nc
        self.ops = []
        self.by_eng = {e: [] for e in ENGS}
        self.last_w = {}
        self.readers = {}
        self.n_dma_sems = n_dma_sems
        self.last_comp = None

    def _add(self, eng, fn, reads, writes, is_dma):
        op = Op(eng, fn, is_dma)
        op.seq = len(self.ops)
        for k in reads:
            w = self.last_w.get(k)
            if w is not None:
                op.deps.add(w)
        for k in writes:
            w = self.last_w.get(k)
            if w is not None:
                op.deps.add(w)
            rl = self.readers.get(k)
            if rl:
                op.deps.update(rl)
        for k in reads:
            self.readers.setdefault(k, []).append(op)
        for k in writes:
            self.last_w[k] = op
            self.readers[k] = []
        op.deps.discard(op)
        if SERIAL == 1 and self.ops and not self.ops[-1].barrier:
            op.deps.add(self.ops[-1])
        elif SERIAL == 2 and not is_dma:
            if self.last_comp is not None:
                op.deps.add(self.last_comp)
            self.last_comp = op
        elif SERIAL == 3 and not is_dma and eng != "pe":
            if self.last_comp is not None and self.last_comp.eng != eng:
                op.deps.add(self.last_comp)
            self.last_comp = op
        self.ops.append(op)
        self.by_eng[eng].append(op)
        return op

    def op(self, eng, fn, reads=(), writes=()):
        return self._add(eng, fn, reads, writes, False)

    def dma(self, eng, fn, reads=(), writes=()):
        return self._add(eng, fn, reads, writes, True)

    def barrier(self):
        tails = []
        for e in ENGS:
            comp = [o for o in self.by_eng[e] if not o.is_dma and not o.barrier]
            if comp:
                tails.append(comp[-1])
            dm = [o for o in self.by_eng[e] if o.is_dma]
            tails.extend(dm[-self.n_dma_sems:])
        for e in ENGS:
            op = Op(e, lambda eng: None, False)
            op.barrier = True
            op.seq = len(self.ops)
            op.deps = set(tails)
            self.ops.append(op)
            self.by_eng[e].append(op)
        self.last_w = {}
        self.readers = {}
        self.last_comp = None

    def finalize(self):
        for op in self.ops:
            for d in op.deps:
                if d.is_dma:
                    continue
                if d.eng == "pe" and op.eng == "pe" and not op.is_dma and not op.barrier:
                    continue
                d.signal = True
        self.n_sig = {}
        for e in ENGS:
            c = 0
            for op in self.by_eng[e]:
                if (not op.is_dma) and op.signal:
                    c += 1
                    op.sig_idx = c
            self.n_sig[e] = c
        for e in ENGS:
            k = 0
            slots = [None] * self.n_dma_sems
            counts = [0] * self.n_dma_sems
            for op in self.by_eng[e]:
                if op.is_dma:
                    s = k % self.n_dma_sems
                    prev = slots[s]
                    if prev is not None:
                        op.deps.add(prev)
                    counts[s] += 16
                    op.dma_sem = (e, s)
                    op.dma_val = counts[s]
                    slots[s] = op
                    k += 1
        for e in ENGS:
            wd = {}
            dma_waited = {}
            for op in self.by_eng[e]:
                need = {}
                for d in op.deps:
                    if d.is_dma:
                        key = d.dma_sem
                        if dma_waited.get(key, 0) < d.dma_val:
                            dma_waited[key] = d.dma_val
                            op.waits.append(("dma", key, d.dma_val))
                    else:
                        if d.eng == "pe" and op.eng == "pe" and not op.is_dma and not op.barrier:
                            continue
                        if d.sig_idx > need.get(d.eng, 0):
                            need[d.eng] = d.sig_idx
                for src, idx in need.items():
                    if wd.get(src, 0) < idx:
                        op.waits.append(("eng", src, idx))
                        wd[src] = idx

    def emit(self):
        nc = self.nc
        with contextlib.ExitStack() as st:
            sems = {}
            for e in ENGS:
                n_ep = (self.n_sig[e] + SEM_EPOCH - 1) // SEM_EPOCH
                for i in range(n_ep):
                    sems[("eng", e, i)] = st.enter_context(nc.semaphore(f"s_{e}_{i}"))
                used = sorted(set(op.dma_sem for op in self.by_eng[e] if op.is_dma))
                for key in used:
                    sems[("dma",) + key] = st.enter_context(nc.semaphore(f"d_{key[0]}_{key[1]}"))
            block = st.enter_context(nc.Block())
            engmap = {"pe": "tensor", "dve": "vector", "act": "scalar", "pool": "gpsimd", "sp": "sync"}

            def make(e):
                ops = self.by_eng[e]

                def body(engine):
                    for op in ops:
                        for w in op.waits:
                            if w[0] == "dma":
                                engine.wait_ge(sems[("dma",) + w[1]], w[2])
                            else:
                                idx = w[2]
                                ep = (idx - 1) // SEM_EPOCH
                                engine.wait_ge(sems[("eng", w[1], ep)], idx - ep * SEM_EPOCH)
                        ins = op.fn(engine)
                        if ins is None:
                            continue
                        if op.is_dma:
                            ins.then_inc(sems[("dma",) + op.dma_sem], 16)
                        elif op.signal:
                            ep = (op.sig_idx - 1) // SEM_EPOCH
                            ins.then_inc(sems[("eng", e, ep)], 1)
                return body

            for e in ENGS:
                if self.by_eng[e]:
                    getattr(block, engmap[e])(make(e))


class Arena:
    def __init__(self, tile, nbytes):
        self.tile = tile
        self.nbytes = nbytes
        self.off = 0
        self.cnt = 0

    def alloc(self, shape, dtype):
        esz = 4 if dtype == F32 else 2
        n = int(np.prod(shape))
        nb = n * esz
        self.off = (self.off + 63) // 64 * 64
        assert self.off + nb <= self.nbytes, f"arena overflow {self.off}+{nb}>{self.nbytes}"
        a = self.tile[:, self.off // 2:(self.off + nb) // 2]
        if dtype == F32:
            a = a.bitcast(F32)
        self.off += nb
        self.cnt += 1
        key = f"A{self.cnt}"
        if len(shape) == 2:
            a = a.rearrange("p (a b) -> p a b", a=shape[0])
        elif len(shape) == 3:
            a = a.rearrange("p (a b c) -> p a b c", a=shape[0], b=shape[1])
        elif len(shape) == 4:
            a = a.rearrange("p (a b c d) -> p a b c d", a=shape[0], b=shape[1], c=shape[2])
        return a, key

    def mark(self):
        return self.off

    def release(self, m):
        self.off = m


def build(NSEG, SEG, debug=False, passes=(1, 2, 3, 4)):
    T = NSEG * SEG
    NT = T // 128
    GRP = 512
    NG = T // GRP
    assert SEG % GRP == 0
    nc = bass.Bass("TRN2", target_bir_lowering=False)
    dt_in = lambda n, s, d=F32: nc.dram_tensor(n, list(s), d, kind="ExternalInput").ap()
    okind = "ExternalOutput" if debug else "Internal"
    dt_scr = lambda n, s, d: nc.dram_tensor(n, list(s), d, kind=okind).ap()

    xs = dt_in("xs", [T, D])
    mem = dt_in("mem", [NSEG, N_MEM, D])
    w_in_t = dt_in("w_in_t", [NCB * 128, D])
    w_out_t = dt_in("w_out_t", [128 * 16, D])
    w_kv_t = dt_in("w_kv_t", [128 * 16, 1024])
    rowv = dt_in("rowv", [128, 3, D])
    qkg = dt_in("qkg", [128, 2, 128])
    colv = dt_in("colv", [128, 96])
    gnv = dt_in("gnv", [128, 2, 768])
    lora_up = dt_in("lora_up", [128, 2, 768])
    cs_tab = dt_in("cs_tab", [T, 2, 64])
    consts = dt_in("consts", [128, 1024])
    flags = dt_in("flags", [128, 32])
    y_out = nc.dram_tensor("y", [T, D], F32, kind="ExternalOutput").ap()

    W1 = dt_scr("W1", [NCB * 128, D], BF16)
    WO = dt_scr("WO", [128 * 16, D], BF16)
    WKV = dt_scr("WKV", [128 * 16, 1024], BF16)
    PR = dt_scr("PR", [20, 128, T + 2], F32)
    G = dt_scr("G", [12, 128, T], BF16)
    MIX = dt_scr("MIX", [16, 128, T], BF16)
    QT = dt_scr("QT", [6, 128, T], BF16)
    KTd = dt_scr("KTd", [2, 128, T], BF16)
    Vd = dt_scr("Vd", [T, 256], BF16)
    YF = dt_scr("YF", [T, 768], F32)
    HSd = dt_scr("HSd", [T // 128, 128, 26 * 128], F32)

    with contextlib.ExitStack() as top:
        ARENA_BYTES = 200 * 1024
        arena_t = top.enter_context(nc.sbuf_tensor("arena", [128, ARENA_BYTES // 2], BF16))
        AR = Arena(arena_t, ARENA_BYTES)
        psb = [top.enter_context(nc.psum_tensor(f"psb{i}", [128, 512], F32)) for i in range(8)]
        P = Prog(nc)
        ps_rr = [0]

        import os as _os0
        _NB = int(_os0.environ.get("K_NB", "8"))

        nb_cfg = [_NB]

        def bank():
            i = ps_rr[0] % nb_cfg[0]
            ps_rr[0] += 1
            return psb[i], f"ps{i}"

        def bank2():
            if ps_rr[0] % 2:
                ps_rr[0] += 1
            i = ps_rr[0] % 8
            ps_rr[0] += 2
            return psb[i], psb[i + 1], f"ps{i}", f"ps{i + 1}"

        ev_rr = [0]

        def evac_eng():
            return "dve"

        def copy_op(eng, out, in_, reads, writes):
            if eng == "act":
                P.op("act", lambda e: e.activation(out=out, in_=in_, func=AF.Copy), reads, writes)
            else:
                P.op(eng, lambda e: e.tensor_copy(out=out, in_=in_), reads, writes)

        def MM(out, lhsT, rhs, start, stop, reads, writes):
            P.op("pe", lambda e: e.matmul(out, lhsT=lhsT, rhs=rhs, start=start, stop=stop), reads, writes)

        def TR(out, in_, ident, reads, writes):
            P.op("pe", lambda e: e.transpose(out, in_, ident), reads, writes)

        def ACT(out, in_, func, reads, writes, scale=None, bias=None, accum_out=None):
            kw = {}
            if scale is not None:
                kw["scale"] = scale
            if bias is not None:
                kw["bias"] = bias
            if accum_out is not None:
                kw["accum_out"] = accum_out
            P.op("act", lambda e: e.activation(out=out, in_=in_, func=func, **kw), reads, writes)

        def TT(eng, out, in0, in1, op, reads, writes):
            P.op(eng, lambda e: e.tensor_tensor(out=out, in0=in0, in1=in1, op=op), reads, writes)

        def TS(eng, out, in0, s1, op0, reads, writes, s2=None, op1=None):
            if op1 is None:
                P.op(eng, lambda e: e.tensor_scalar(out=out, in0=in0, scalar1=s1, scalar2=None, op0=op0), reads, writes)
            else:
                P.op(eng, lambda e: e.tensor_scalar(out=out, in0=in0, scalar1=s1, scalar2=s2, op0=op0, op1=op1), reads, writes)

        def STT(out, in0, scalar, in1, op0, op1, reads, writes):
            P.op("dve", lambda e: e.scalar_tensor_tensor(out=out, in0=in0, scalar=scalar, in1=in1, op0=op0, op1=op1), reads, writes)

        def DMA(eng, out, in_, reads, writes, slow=False):
            if slow:
                P.dma(eng, lambda e: e.dma_start(out=out, in_=in_, allow_slow_non_contiguous=True), reads, writes)
            else:
                P.dma(eng, lambda e: e.dma_start(out=out, in_=in_), reads, writes)

        def CP(eng, out, in_, reads, writes):
            if eng == "act":
                P.op("act", lambda e: e.activation(out=out, in_=in_, func=AF.Copy), reads, writes)
            else:
                P.op(eng, lambda e: e.tensor_copy(out=out, in_=in_), reads, writes)

        def MEMSET(eng, out, val, reads, writes):
            P.op(eng, lambda e: e.memset(out, val), reads, writes)

        c_f32, k_cf = AR.alloc((1024,), F32)
        c_flag, k_flag = AR.alloc((32,), F32)
        c_colv, k_colv = AR.alloc((96,), F32)
        identb, k_idb = AR.alloc((128,), BF16)
        onesb, k_onesb = AR.alloc((128,), BF16)
        c_eps_t, k_ceps = AR.alloc((4,), F32)
        DMA("sp", c_f32, consts, [], [k_cf])
        DMA("sp", c_flag, flags, [], [k_flag])
        DMA("sp", c_colv, colv, [], [k_colv])
        ident_f = c_f32[:, 0:128]
        CP("dve", identb, ident_f, [k_cf], [k_idb])
        MEMSET("dve", onesb, 1.0, [], [k_onesb])
        MEMSET("dve", c_eps_t[:, 0:1], NORM_EPS, [], [k_ceps])
        MEMSET("dve", c_eps_t[:, 1:2], GN_EPS, [k_ceps], [k_ceps])
        MEMSET("dve", c_eps_t[:, 2:3], 1e-30, [k_ceps], [k_ceps])
        c_eps = c_eps_t[:, 0:1]
        c_gneps = c_eps_t[:, 1:2]
        c_tiny = c_eps_t[:, 2:3]

        import os as _os
        _skip = _os.environ.get("K_SKIP", "")
        if "cast" not in _skip:
            for i in range(NCB):
                DMA("pool", W1[i * 128:(i + 1) * 128, :], w_in_t[i * 128:(i + 1) * 128, :], [], [f"W1_{i}"])
            for i in range(16):
                DMA("pool", WO[i * 128:(i + 1) * 128, :], w_out_t[i * 128:(i + 1) * 128, :], [], [f"WO{i}"])
            for i in range(16):
                DMA("pool", WKV[i * 128:(i + 1) * 128, :], w_kv_t[i * 128:(i + 1) * 128, :], [], [f"WKV{i}"])
        zt, k_zt = AR.alloc((20, 1), F32)
        MEMSET("pool", zt, 0.0, [], [k_zt])
        if "pad" not in _skip:
            DMA("sp", PR[:, :, 0:1].rearrange("b p o -> p b o"), zt, [k_zt], ["PRpad0"], slow=True)
            DMA("sp", PR[:, :, T + 1:T + 2].rearrange("b p o -> p b o"), zt, [k_zt], ["PRpad1"], slow=True)
        if "mixz" not in _skip:
            zb, k_zb = AR.alloc((2048,), BF16)
            MEMSET("pool", zb, 0.0, [], [k_zb])
            for cbz in range(12):
                for tz in range(0, T, 2048):
                    nz = min(2048, T - tz)
                    DMA("sp", MIX[cbz, :, tz:tz + nz], zb[:, 0:nz], [k_zb], [f"MIXz{cbz}_{tz}"])
        P.barrier()

        def rmsnorm_tile(x_ap, k_x, g_ap, k_g, h_ap, k_h, junk, k_junk, ssq, k_ssq, rstd, k_rstd):
            ACT(junk, x_ap, AF.Square, [k_x], [k_junk, k_ssq], accum_out=ssq)
            ACT(rstd, ssq, AF.Ln, [k_ssq, k_ceps], [k_rstd], scale=1.0 / D, bias=c_eps)
            ACT(rstd, rstd, AF.Exp, [k_rstd], [k_rstd], scale=-0.5)
            STT(h_ap, x_ap, rstd, g_ap, ALU.mult, ALU.mult, [k_x, k_rstd, k_g], [k_h])

        if 1 in passes:
            m1 = AR.mark()
            ng_b, k_ng = AR.alloc((D,), F32)
            qkg_b, k_qkg = AR.alloc((2, 128), F32)
            DMA("sp", ng_b, rowv[:, 0, :], [], [k_ng])
            DMA("sp", qkg_b, qkg, [], [k_qkg])
            mkT, k_mkT = AR.alloc((NSEG, 4, 256), BF16)
            mv, k_mv = AR.alloc((NSEG, 2, 512), BF16)

            mkv_mark = AR.mark()
            mng_b, k_mng = AR.alloc((D,), F32)
            DMA("sp", mng_b, rowv[:, 2, :], [], [k_mng])
            wkv_sb, k_wkv = AR.alloc((16, 1024), BF16)
            DMA("sp", wkv_sb, WKV.rearrange("(p k) n -> p k n", k=16), [f"WKV{i}" for i in range(16)], [k_wkv])
            mx, k_mx = AR.alloc((D,), F32)
            mh, k_mh = AR.alloc((D,), BF16)
            mjunk, k_mjunk = AR.alloc((D,), BF16)
            mss, k_mss = AR.alloc((2,), F32)
            mhT, k_mhT = AR.alloc((16, 256), BF16)
            for s in range(NSEG):
                for mt in range(2):
                    DMA("sp", mx, mem[s, mt * 128:(mt + 1) * 128, :], [], [k_mx])
                    if "mkvn" in _skip:
                        continue
                    rmsnorm_tile(mx, k_mx, mng_b, k_mng, mh, k_mh, mjunk, k_mjunk, mss[:, 0:1], k_mss, mss[:, 1:2], k_mss + "b")
                    for half in range(2):
                        if "mkvt" in _skip:
                            continue
                        pb, kpb = bank()
                        pbb = pb[:].bitcast(BF16)
                        for j in range(8):
                            kc = half * 8 + j
                            TR(pbb[:, j * 128:(j + 1) * 128], mh[:, kc * 128:(kc + 1) * 128], identb, [k_mh, k_idb], [kpb])
                        CP(evac_eng(), mhT[:, half * 8:(half + 1) * 8, mt * 128:(mt + 1) * 128],
                           pbb[:, 0:1024].rearrange("p (a b) -> p a b", a=8), [kpb], [k_mhT])
                if "mkvm" in _skip:
                    continue
                for hd in range(4):
                    if "mkva" in _skip:
                        continue
                    pb, kpb = bank()
                    for kc in range(16):
                        MM(pb[:, 0:256], wkv_sb[:, kc, hd * 128:(hd + 1) * 128], mhT[:, kc, :], kc == 0, kc == 15, [k_wkv, k_mhT], [kpb])
                    CP(evac_eng(), mkT[:, s, hd, :], pb[:, 0:256], [kpb], [k_mkT])
                P.barrier()
                for mt in range(2):
                    if "mkvb" in _skip:
                        continue
                    pb, kpb = bank()
                    for kc in range(16):
                        MM(pb[:, 0:512], mhT[:, kc, mt * 128:(mt + 1) * 128], wkv_sb[:, kc, 512:1024], kc == 0, kc == 15, [k_wkv, k_mhT], [kpb])
                    CP(evac_eng(), mv[:, s, mt, :], pb[:, 0:512], [kpb], [k_mv])
            P.barrier()
            AR.release(mkv_mark)

            xt = [AR.alloc((D,), F32) for _ in range(2)]
            ht = [AR.alloc((D,), BF16) for _ in range(2)]
            junk, k_junk = AR.alloc((D,), BF16)
            st_small, k_sts = AR.alloc((2, 2), F32)
            hT = [AR.alloc((16, GRP), BF16) for _ in range(2)]
            wblk = [AR.alloc((4, 16, 128), BF16) for _ in range(3)]
            stg32 = [AR.alloc((GRP,), F32) for _ in range(3)]
            stg16 = [AR.alloc((GRP,), BF16) for _ in range(3)]
            mqT, k_mqT = AR.alloc((4, GRP), BF16)
            mg, k_mg = AR.alloc((4, GRP), BF16)
            PTm = [AR.alloc((2, GRP), BF16) for _ in range(2)]
            rs_t, k_rs = AR.alloc((GRP,), F32)
            ym_t, k_ym = AR.alloc((GRP,), F32)
            qn, k_qn = AR.alloc((8, 128), F32)
            qsq, k_qsq = AR.alloc((8, 128), F32)
            qss, k_qss = AR.alloc((8,), F32)
            qrs, k_qrs = AR.alloc((8,), F32)
            rp1, k_rp1 = AR.alloc((8, 64), F32)
            rp2, k_rp2 = AR.alloc((8, 64), F32)
            qr, k_qr = AR.alloc((8, 128), BF16)
            cst = [AR.alloc((4, 2, 64), F32) for _ in range(1)]
            qTs = [AR.alloc((6, GRP), BF16) for _ in range(1)]
            kTs = [AR.alloc((2, GRP), BF16) for _ in range(1)]
            vst = [AR.alloc((4, 256), BF16) for _ in range(1)]
            wl_rr = [0]
            s32_rr = [0]
            s16_rr = [0]
            tile_ctr = [0]
            loads = [(b, min(4, NCB - b)) for b in range(0, NCB, 4)]

            for g in range(NG):
                if "main" in _skip:
                    break
                t0 = g * GRP
                seg = t0 // SEG
                hTg, k_hT = hT[g % 2]
                for ti in range(4):
                    tt = tile_ctr[0]
                    tile_ctr[0] += 1
                    x_ap, k_x = xt[tt % 2]
                    h_ap, k_h = ht[tt % 2]
                    tok = t0 + ti * 128
                    DMA("sp", x_ap, xs[tok:tok + 128, :], [], [k_x])
                    rmsnorm_tile(x_ap, k_x, ng_b, k_ng, h_ap, k_h, junk, k_junk,
                                 st_small[:, tt % 2, 0:1], k_sts + f"a{tt % 2}", st_small[:, tt % 2, 1:2], k_sts + f"b{tt % 2}")
                    for half in range(2):
                        pb, kpb = bank()
                        pbb = pb[:].bitcast(BF16)
                        for j in range(8):
                            kc = half * 8 + j
                            TR(pbb[:, j * 128:(j + 1) * 128], h_ap[:, kc * 128:(kc + 1) * 128], identb, [k_h, k_idb], [kpb])
                        CP(evac_eng(), hTg[:, half * 8:(half + 1) * 8, ti * 128:(ti + 1) * 128],
                           pbb[:, 0:1024].rearrange("p (a b) -> p a b", a=8), [kpb], [k_hT])
                cs_ap, k_cs = cst[0]
                qTs_ap, k_qTs = qTs[0]
                kTs_ap, k_kTs = kTs[0]
                vst_ap, k_vst = vst[0]
                for (b0, nb) in loads:
                    if "fm" in _skip and b0 < 40:
                        continue
                    if "tm" in _skip and b0 >= 40:
                        continue
                    w_ap, k_w = wblk[wl_rr[0] % 3]
                    wl_rr[0] += 1
                    DMA("sp", w_ap[:, 0:nb], W1[b0 * 128:(b0 + nb) * 128, :].rearrange("(b p) (k j) -> p b k j", p=128, k=16),
                        [f"W1_{j}" for j in range(b0, b0 + nb)], [k_w])
                    if b0 < 40:
                        for bi in range(nb):
                            cb = b0 + bi
                            pb, kpb = bank()
                            for kc in range(16):
                                MM(pb[:, 0:GRP], w_ap[:, bi, kc, :], hTg[:, kc, :], kc == 0, kc == 15, [k_w, k_hT], [kpb])
                            if cb < 20:
                                s_ap, k_s = stg32[s32_rr[0] % 3]
                                s32_rr[0] += 1
                                CP(evac_eng(), s_ap, pb[:, 0:GRP], [kpb], [k_s])
                                DMA("sp", PR[cb, :, 1 + t0:1 + t0 + GRP], s_ap, [k_s], [f"PR{cb}_{g}"])
                            elif cb < 32:
                                s_ap, k_s = stg16[s16_rr[0] % 3]
                                s16_rr[0] += 1
                                ACT(s_ap, pb[:, 0:GRP], AF.Silu, [kpb], [k_s])
                                DMA("sp", G[cb - 20, :, t0:t0 + GRP], s_ap, [k_s], [f"G{cb}_{g}"])
                            elif cb < 36:
                                ACT(mg[:, cb - 32, :], pb[:, 0:GRP], AF.Silu, [kpb], [k_mg])
                            else:
                                CP("dve", mqT[:, cb - 36, :], pb[:, 0:GRP], [kpb], [k_mqT])
                        if b0 == 36 and "mat" not in _skip:
                            for hd in range(4):
                                PT_ap, k_PT = PTm[hd % 2]
                                for mc in range(2):
                                    pb, kpb = bank()
                                    MM(pb[:, 0:GRP], mkT[:, seg, hd, mc * 128:(mc + 1) * 128], mqT[:, hd, :], True, True, [k_mkT, k_mqT], [kpb])
                                    ACT(PT_ap[:, mc, :], pb[:, 0:GRP], AF.Exp, [kpb], [k_PT], scale=ATT_SCALE)
                                pby, kpby = bank()
                                pbs, kpbs = bank()
                                for mc in range(2):
                                    MM(pby[:, 0:GRP], mv[:, seg, mc, hd * 128:(hd + 1) * 128], PT_ap[:, mc, :], mc == 0, mc == 1, [k_mv, k_PT], [kpby])
                                for mc in range(2):
                                    MM(pbs[:, 0:GRP], onesb, PT_ap[:, mc, :], mc == 0, mc == 1, [k_onesb, k_PT], [kpbs])
                                P.op("dve", (lambda o, i: (lambda e: e.reciprocal(out=o, in_=i)))(rs_t, pbs[:, 0:GRP]), [kpbs], [k_rs])
                                TT("dve", ym_t, pby[:, 0:GRP], rs_t, ALU.mult, [kpby, k_rs], [k_ym])
                                s_ap, k_s = stg16[s16_rr[0] % 3]
                                s16_rr[0] += 1
                                TT("pool", s_ap, ym_t, mg[:, hd, :], ALU.mult, [k_ym, k_mg], [k_s])
                                DMA("sp", MIX[12 + hd, :, t0:t0 + GRP], s_ap, [k_s], [f"MIX{12 + hd}_{g}"])
                    else:
                        if b0 == 40:
                            P.barrier()
                            DMA("sp", cs_ap, cs_tab[t0:t0 + GRP].rearrange("(t p) c f -> p t c f", p=128), [], [k_cs])
                        for ti in range(4):
                            pb, kpb = bank()
                            for kc in range(16):
                                MM(pb[:, 0:nb * 128].rearrange("p (b j) -> p b j", b=nb), hTg[:, kc, ti * 128:(ti + 1) * 128], w_ap[:, 0:nb, kc, :],
                                   kc == 0, kc == 15, [k_w, k_hT], [kpb])
                            if b0 == 48:
                                CP(evac_eng(), vst_ap[:, ti, :], pb[:, 0:256], [kpb], [k_vst])
                                continue
                            if "qk" in _skip:
                                continue
                            hoff = 0 if b0 == 40 else 4
                            pv = pb[:, 0:512].rearrange("p (h d) -> p h d", h=4)
                            ACT(qsq[:, hoff:hoff + 4, :], pv, AF.Square, [kpb], [k_qsq] + [k_qsq + f"h{hoff + i}" for i in range(4)])
                            P.op("dve", (lambda o, i: (lambda e: e.tensor_reduce(out=o, in_=i, op=ALU.add, axis=mybir.AxisListType.X)))(
                                qss[:, hoff:hoff + 4], qsq[:, hoff:hoff + 4, :]), [k_qsq], [k_qss])
                            ACT(qrs[:, hoff:hoff + 4], qss[:, hoff:hoff + 4], AF.Ln, [k_qss, k_ceps], [k_qrs], scale=1.0 / 128, bias=c_eps)
                            ACT(qrs[:, hoff:hoff + 4], qrs[:, hoff:hoff + 4], AF.Exp, [k_qrs], [k_qrs], scale=-0.5)
                            for hh in range(4):
                                h8 = hoff + hh
                                gsel = 0 if h8 < 6 else 1
                                STT(qn[:, h8, :], pv[:, hh, :], qrs[:, h8:h8 + 1], qkg_b[:, gsel, :], ALU.mult, ALU.mult, [kpb, k_qrs, k_qkg], [k_qn])
                            if "rope" in _skip:
                                continue
                            for hh in range(4):
                                h8 = hoff + hh
                                qv = qn[:, h8, :].rearrange("p (f two) -> p f two", two=2)
                                x0, x1 = qv[:, :, 0], qv[:, :, 1]
                                cosb = cs_ap[:, ti, 0, :]
                                sinb = cs_ap[:, ti, 1, :]
                                ov = qsq[:, h8, :].rearrange("p (f two) -> p f two", two=2)
                                r1 = rp1[:, h8, :]
                                r2 = rp2[:, h8, :]
                                kq = k_qsq + f"h{h8}"
                                k1 = k_rp1 + f"h{h8}"
                                k2 = k_rp2 + f"h{h8}"
                                TT("dve", r1, x0, cosb, ALU.mult, [k_qn, k_cs], [k1])
                                TT("dve", r2, x1, sinb, ALU.mult, [k_qn, k_cs], [k2])
                                TT("dve", ov[:, :, 0], r1, r2, ALU.subtract, [k1, k2, k_qss], [kq])
                                TT("dve", r1, x0, sinb, ALU.mult, [k_qn, k_cs, kq], [k1])
                                TT("dve", r2, x1, cosb, ALU.mult, [k_qn, k_cs, kq], [k2])
                                TT("dve", ov[:, :, 1], r1, r2, ALU.add, [k1, k2, kq], [kq])
                                CP("dve", qr[:, h8, :], qsq[:, h8, :], [kq], [k_qr])
                            if "notr" in _skip:
                                continue
                            if "trbar" in _skip:
                                P.barrier()
                            pbt, kpbt = bank()
                            pbtb = pbt[:].bitcast(BF16)
                            for hh in range(4):
                                TR(pbtb[:, hh * 128:(hh + 1) * 128], qr[:, hoff + hh, :], identb, [k_qr, k_idb], [kpbt])
                            if "nocp" in _skip:
                                continue
                            for hh in range(4):
                                h8 = hoff + hh
                                if h8 < 6:
                                    CP(evac_eng(), qTs_ap[:, h8, ti * 128:(ti + 1) * 128], pbtb[:, hh * 128:(hh + 1) * 128], [kpbt], [k_qTs + f"h{h8}"])
                                else:
                                    CP(evac_eng(), kTs_ap[:, h8 - 6, ti * 128:(ti + 1) * 128], pbtb[:, hh * 128:(hh + 1) * 128], [kpbt], [k_kTs + f"h{h8 - 6}"])
                if "tm" in _skip or "qk" in _skip or "rope" in _skip or "notr" in _skip or "nocp" in _skip:
                    DMA("sp", Vd[t0:t0 + GRP, :].rearrange("(t p) c -> p t c", p=128), vst_ap, [k_vst], [f"V_{g}"])
                    P.barrier()
                    continue
                if "qst" not in _skip:
                    for hq in range(6):
                        DMA("sp", QT[hq, :, t0:t0 + GRP], qTs_ap[:, hq, :], [k_qTs + f"h{hq}"], [f"QT_{g}_{hq}"])
                    for hk in range(2):
                        DMA("sp", KTd[hk, :, t0:t0 + GRP], kTs_ap[:, hk, :], [k_kTs + f"h{hk}"], [f"KT_{g}_{hk}"])
                DMA("sp", Vd[t0:t0 + GRP, :].rearrange("(t p) c -> p t c", p=128), vst_ap, [k_vst], [f"V_{g}"])
                P.barrier()
            P.barrier()
            AR.release(m1)

        if 2 in passes:
            m2 = AR.mark()
            nb_cfg[0] = 5
            C = 128
            NCH = T // C
            CPS = SEG // C
            KD = DECAY_K
            pY0, kY0 = psb[5], "ps5"
            pY1, kY1 = psb[6], "ps6"
            pZ, kZ = psb[7], "ps7"
            gnv_b, k_gnv = AR.alloc((2, 768), F32)
            DMA("sp", gnv_b, gnv, [], [k_gnv])
            lup, k_lup = AR.alloc((2, 768), F32)
            DMA("sp", lup, lora_up, [], [k_lup])
            lupb, k_lupb = AR.alloc((2, 768), BF16)
            CP("dve", lupb, lup, [k_lup], [k_lupb])
            blk1b, k_blk1 = AR.alloc((128,), BF16)
            CP("dve", blk1b, c_f32[:, 706:834], [k_cf], [k_blk1])
            bselb, k_bsel = AR.alloc((2,), BF16)
            CP("dve", bselb, c_f32[:, 704:706], [k_cf], [k_bsel])
            ones_f, k_onesf = AR.alloc((128,), F32)
            MEMSET("dve", ones_f, 1.0, [], [k_onesf])
            c0v, k_c0v = AR.alloc((20,), F32)
            TT("dve", c0v, c_colv[:, 0:20], c_colv[:, 20:40], ALU.add, [k_colv], [k_c0v])
            TS("dve", c0v, c0v, -1.0, ALU.mult, [k_c0v], [k_c0v], s2=1.0, op1=ALU.add)
            m4 = []
            mLs = []
            for d_ in range(2):
                mt_, k_mt = AR.alloc((4, 128), F32)
                s_off, i_off = (128, 256) if d_ == 0 else (384, 512)
                for q_ in range(4):
                    off = s_off if q_ % 2 == 0 else i_off
                    CP("dve", mt_[:, q_, :], c_f32[:, off:off + 128], [k_cf, k_mt], [k_mt])
                m4.append((mt_, k_mt))
                mLs.append(c_f32[:, 384:512] if d_ == 0 else c_f32[:, 128:256])
            mu_p = c_colv[:, 0:20].unsqueeze(2).to_broadcast([128, 20, 128])
            mu_n = c_colv[:, 20:40].unsqueeze(2).to_broadcast([128, 20, 128])
            c0_b = c0v.unsqueeze(2).to_broadcast([128, 20, 128])
            kk_b = c_colv[:, 64:70].unsqueeze(2).to_broadcast([128, 6, 128])
            ka_b = c_colv[:, 70:76].unsqueeze(2).to_broadcast([128, 6, 128])
            rk_b = c_colv[:, 76:82].unsqueeze(2).to_broadcast([128, 6, 128])
            flag_ap = c_flag[:, 0:1]
            praw, k_praw = AR.alloc((20, 130), F32)
            hs_bufs = [AR.alloc((26, 128), F32) for _ in range(2)]
            tmp20, k_tmp20 = AR.alloc((20, 128), F32)
            f6 = lambda: AR.alloc((6, 128), F32)
            kkraw, k_kkraw = f6()
            rn, k_rn = f6()
            sg, k_sg = f6()
            a_t, k_at = f6()
            cs, k_cs = f6()
            ex, k_ex = f6()
            eL, k_eL = f6()
            emL, k_emL = f6()
            eLx, k_eLx = f6()
            b_t, k_bt = f6()
            kp, k_kp = f6()
            t1, k_t1 = f6()
            wtot, k_wtot = AR.alloc((6,), F32)
            b6 = lambda: AR.alloc((6, 128), BF16)
            sqb, k_sqb = b6()
            aq, k_aq = AR.alloc((6, 2, 128), BF16)
            btl, k_btl = b6()
            ktl, k_ktl = b6()
            vb, k_vb = b6()
            atm, k_atm = b6()
            btm, k_btm = b6()
            ktm, k_ktm = b6()
            vtm, k_vtm = b6()
            prodb, k_prodb = b6()
            twl, k_twl = AR.alloc((128,), BF16)
            alb, k_alb = AR.alloc((128,), BF16)
            AT, k_AT = AR.alloc((12, 4, 128), BF16)
            Pp = [AR.alloc((12, 2, 128), BF16) for _ in range(2)]
            Rb = [AR.alloc((12, 128), BF16) for _ in range(2)]
            nU, k_nU = AR.alloc((12, 64), BF16)
            IXb, k_IXb = AR.alloc((6, 64), BF16)
            QhT, k_QhT = b6()
            Gp, k_Gp = AR.alloc((6, 64), F32)
            ST, k_ST = AR.alloc((6, 64), F32)
            STb, k_STb = AR.alloc((6, 64), BF16)
            ztmp, k_ztmp = AR.alloc((6, 64), F32)
            ysb, k_ysb = AR.alloc((768,), F32)
            yf_t, k_yf = AR.alloc((768,), F32)
            yc, k_yc = AR.alloc((768,), F32)
            ysq, k_ysq = AR.alloc((768,), F32)
            st12, k_st12 = AR.alloc((4, 12), F32)
            bon, k_bon = AR.alloc((12,), F32)
            gate2, k_gate2 = b6()
            mixo, k_mixo = b6()

            def exp_op(out, k_out, in_, k_in, scale):
                ACT(out, in_, AF.Exp, [k_in], [k_out], scale=scale)

            for d_ in range(2):
                MEMSET("dve", ST, 0.0, [], [k_ST])
                MEMSET("dve", STb, 0.0, [], [k_STb])
                order = list(range(NCH)) if d_ == 0 else list(range(NCH - 1, -1, -1))
                m4t, k_m4 = m4[d_]
                mL = mLs[d_]
                for ci, c in enumerate(order):
                    t0 = c * C
                    cross = (c % CPS == 0 and c > 0) if d_ == 0 else ((c + 1) % CPS == 0 and c < NCH - 1)
                    if cross:
                        TS("dve", ST, ST, flag_ap, ALU.mult, [k_ST, k_flag], [k_ST])
                        CP("dve", STb, ST, [k_ST], [k_STb])
                    hsb, k_hs = hs_bufs[ci % 2]
                    hs = hsb[:, 0:20, :]
                    kk = hsb[:, 20:26, :]
                    k_kk = k_hs + "kk"
                    r_ = hs[:, 0:6, :]
                    k_ = hs[:, 6:12, :]
                    v_ = hs[:, 12:18, :]
                    if d_ == 1:
                        DMA("sp", hsb, HSd[c].rearrange("p (a b) -> p a b", a=26), [f"HS_{c}"], [k_hs, k_kk])
                    else:
                        DMA("sp", praw, PR[:, :, t0:t0 + 130].rearrange("b p t -> p b t"), [], [k_praw])
                        if c % CPS == 0 and c > 0:
                            TS("dve", praw[:, :, 0:1], praw[:, :, 0:1], flag_ap, ALU.mult, [k_praw, k_flag], [k_praw])
                        if (c + 1) % CPS == 0 and c < NCH - 1:
                            TS("dve", praw[:, :, 129:130], praw[:, :, 129:130], flag_ap, ALU.mult, [k_praw, k_flag], [k_praw])
                        TT("dve", hs, praw[:, :, 1:129], c0_b, ALU.mult, [k_praw, k_c0v], [k_hs])
                        TT("dve", tmp20, praw[:, :, 0:128], mu_p, ALU.mult, [k_praw, k_colv], [k_tmp20])
                        TT("dve", hs, hs, tmp20, ALU.add, [k_hs, k_tmp20], [k_hs])
                        TT("dve", tmp20, praw[:, :, 2:130], mu_n, ALU.mult, [k_praw, k_colv, k_hs], [k_tmp20])
                        TT("dve", hs, hs, tmp20, ALU.add, [k_hs, k_tmp20], [k_hs])
                        r_ = hs[:, 0:6, :]
                        k_ = hs[:, 6:12, :]
                        v_ = hs[:, 12:18, :]
                        TT("dve", kkraw, k_, kk_b, ALU.mult, [k_hs, k_colv], [k_kkraw])
                        ACT(sqb, kkraw, AF.Square, [k_kkraw], [k_sqb])
                        pb, kpb = bank()
                        MM(pb[:, 0:512], blk1b, sqb[:, 0:4, :], True, True, [k_blk1, k_sqb], [kpb])
                        pb2, kpb2 = bank()
                        MM(pb2[:, 0:256], blk1b, sqb[:, 4:6, :], True, True, [k_blk1, k_sqb], [kpb2])
                        ACT(rn[:, 0:4, :], pb[:, 0:512].rearrange("p (a b) -> p a b", a=4), AF.Ln, [kpb, k_ceps], [k_rn], bias=c_tiny)
                        ACT(rn[:, 4:6, :], pb2[:, 0:256].rearrange("p (a b) -> p a b", a=2), AF.Ln, [kpb2, k_ceps, k_rn], [k_rn], bias=c_tiny)
                        ACT(rn, rn, AF.Exp, [k_rn], [k_rn], scale=-0.5)
                        TT("dve", kk, kkraw, rn, ALU.mult, [k_kkraw, k_rn], [k_kk])
                        DMA("sp", HSd[c].rearrange("p (a b) -> p a b", a=26), hsb, [k_hs, k_kk], [f"HS_{c}"])
                    ACT(twl, hs[:, 18, :], AF.Tanh, [k_hs], [k_twl])
                    CP("dve", alb, hs[:, 19, :], [k_hs], [k_alb])
                    hsl = slice(d_ * 64, (d_ + 1) * 64)
                    for which in range(2):
                        src = twl if which == 0 else alb
                        ksrc = k_twl if which == 0 else k_alb
                        dst, kdst = (sg, k_sg) if which == 0 else (a_t, k_at)
                        cbase = 40 if which == 0 else 52
                        pb, kpb = bank()
                        pb2, kpb2 = bank()
                        for blk in range(6):
                            tgt, ktgt = (pb, kpb) if blk < 4 else (pb2, kpb2)
                            cc = (blk % 4) * 128
                            MM(tgt[:, cc:cc + 128], lupb[hsl, which, blk * 128:(blk + 1) * 128], src[hsl, :], True, True, [k_lupb, ksrc], [ktgt])
                        for blk in range(6):
                            tgt, ktgt = (pb, kpb) if blk < 4 else (pb2, kpb2)
                            cc = (blk % 4) * 128
                            ACT(dst[:, blk, :], tgt[:, cc:cc + 128], AF.Sigmoid, [ktgt, k_colv, kdst], [kdst],
                                bias=c_colv[:, cbase + d_ * 6 + blk:cbase + d_ * 6 + blk + 1])
                    for blk in range(6):
                        P.op("dve", (lambda o, d1: (lambda e: e.tensor_tensor_scan(out=o, data0=ones_f, data1=d1, initial=0.0, op0=ALU.mult, op1=ALU.add)))(
                            cs[:, blk, :], sg[:, blk, :]), [k_sg, k_onesf, k_cs], [k_cs])
                    ACT(wtot.unsqueeze(2), cs[:, :, 127:128], AF.Exp, [k_cs], [k_wtot], scale=-KD)
                    if d_ == 1:
                        TT("dve", ex, sg, cs, ALU.subtract, [k_sg, k_cs], [k_ex])
                        TT("dve", cs, ex, cs[:, :, 127:128].to_broadcast([128, 6, 128]), ALU.add, [k_ex, k_cs], [k_cs])
                    TT("dve", ex, cs, sg, ALU.subtract, [k_cs, k_sg], [k_ex])
                    exp_op(eL, k_eL, cs, k_cs, -KD)
                    exp_op(emL, k_emL, cs, k_cs, KD)
                    exp_op(eLx, k_eLx, ex, k_ex, -KD)
                    TT("dve", b_t, kk, a_t, ALU.mult, [k_kk, k_at], [k_bt])
                    STT(t1, a_t, -1.0, ka_b, ALU.add, ALU.mult, [k_at, k_colv], [k_t1])
                    STT(kp, t1, 1.0, k_, ALU.add, ALU.mult, [k_t1, k_hs], [k_kp])
                    TT("dve", aq[:, :, 1, :], r_, eL, ALU.mult, [k_hs, k_eL], [k_aq])
                    TT("dve", aq[:, :, 0, :], kk, eLx, ALU.mult, [k_kk, k_eLx, k_aq], [k_aq])
                    TT("dve", btl, b_t, emL, ALU.mult, [k_bt, k_emL], [k_btl])
                    TT("dve", ktl, kp, emL, ALU.mult, [k_kp, k_emL], [k_ktl])
                    CP("dve", vb, v_, [k_hs], [k_vb])
                    for (src3, ksrc, dst3, kdst, sel) in ((aq, k_aq, atm, k_atm, 0), (btl, k_btl, btm, k_btm, None), (ktl, k_ktl, ktm, k_ktm, None), (vb, k_vb, vtm, k_vtm, None)):
                        pb, kpb = bank()
                        pbb = pb[:].bitcast(BF16)
                        for blk in range(6):
                            sin = src3[:, blk, 0, :] if sel is not None else src3[:, blk, :]
                            TR(pbb[:, blk * 128:(blk + 1) * 128], sin, identb, [ksrc, k_idb], [kpb])
                        CP("dve", dst3, pbb[:, 0:768].rearrange("p (a b) -> p a b", a=6), [kpb], [kdst])
                    def hsl_(hd):
                        return hd // 2, slice((hd % 2) * 64, (hd % 2) * 64 + 64)
                    P0, k_P0 = Pp[0]
                    R0, k_R0 = Rb[0]
                    for hd in range(12):
                        blk, hp = hsl_(hd)
                        pA, kpA = bank()
                        MM(pA[:, 0:256].rearrange("p (a b) -> p a b", a=2), btl[hp, blk, :], aq[hp, blk, :, :], True, True, [k_btl, k_aq], [kpA])
                        MM(pA[:, 256:512].rearrange("p (a b) -> p a b", a=2), ktl[hp, blk, :], aq[hp, blk, :, :], True, True, [k_ktl, k_aq], [kpA])
                        TT("dve", AT[:, hd, :, :], pA[:, 0:512].rearrange("p (a b) -> p a b", a=4), m4t, ALU.mult, [kpA, k_m4], [k_AT + f"{hd}"])
                        STT(P0[:, hd, 1, :], pA[:, 0:128], -1.0, m4t[:, 0, :], ALU.mult, ALU.mult, [kpA, k_m4], [k_P0 + f"t{hd}"])
                        pL, kpL = bank()
                        MM(pL[:, 0:128], aq[hp, blk, 0, :], btl[hp, blk, :], True, True, [k_aq, k_btl], [kpL])
                        STT(P0[:, hd, 0, :], pL[:, 0:128], -1.0, mL, ALU.mult, ALU.mult, [kpL, k_cf], [k_P0 + f"n{hd}"])
                    for hd in range(12):
                        blk, hp = hsl_(hd)
                        pR, kpR = bank()
                        MM(pR[:, 0:64], AT[:, hd, 2, :], vtm[:, blk, hp], True, True, [k_AT + f"{hd}", k_vtm], [kpR])
                        CP("dve", R0[:, hd, 0:64], atm[:, blk, hp], [k_atm], [k_R0 + f"a{hd}"])
                        CP("dve", R0[:, hd, 64:128], pR[:, 0:64], [kpR], [k_R0 + f"b{hd}"])
                    for lv in range(7):
                        Pc, k_Pc = Pp[lv % 2]
                        Pn_, k_Pn = Pp[(lv + 1) % 2]
                        Rc, k_Rc = Rb[lv % 2]
                        Rn, k_Rn = Rb[(lv + 1) % 2]
                        for hd in range(12):
                            kP = [k_Pc + f"t{hd}", k_Pc + f"n{hd}"]
                            kR = [k_Rc + f"a{hd}", k_Rc + f"b{hd}"]
                            pD, kpD = bank()
                            MM(pD[:, 0:128], Pc[:, hd, 1, :], Rc[:, hd, :], True, True, kP + kR, [kpD])
                            if lv < 6:
                                MM(pD[:, 128:256], Pc[:, hd, 1, :], Pc[:, hd, 0, :], True, True, kP, [kpD])
                                MM(pD[:, 256:384], Pc[:, hd, 0, :], Pc[:, hd, 1, :], True, True, kP, [kpD])
                            TT("dve", Rn[:, hd, :], pD[:, 0:128], Rc[:, hd, :], ALU.add, [kpD] + kR, [k_Rn + f"a{hd}", k_Rn + f"b{hd}"])
                            if lv < 6:
                                CP("dve", Pn_[:, hd, :, :], pD[:, 128:384].rearrange("p (a b) -> p a b", a=2), [kpD], [k_Pn + f"n{hd}", k_Pn + f"t{hd}"])
                    Rf, k_Rf = Rb[1]
                    for hd in range(12):
                        blk, hp = hsl_(hd)
                        kRf = [k_Rf + f"a{hd}", k_Rf + f"b{hd}"]
                        TS("dve", nU[:, hd, :], Rf[:, hd, 64:128], -1.0, ALU.mult, kRf, [k_nU + f"{hd}"])
                        pE, kpE = bank()
                        MM(pE[hp, 0:64], Rf[:, hd, 0:64], btm[:, blk, hp], True, True, kRf + [k_btm], [kpE])
                        MM(pE[hp, 128:256], Rf[:, hd, 0:64], AT[:, hd, 1, :], True, True, kRf + [k_AT + f"{hd}"], [kpE])
                        MM(pE[hp, 64:128], ktm[:, blk, hp], vtm[:, blk, hp], True, False, [k_ktm, k_vtm], [kpE])
                        MM(pE[hp, 64:128], btm[:, blk, hp], nU[:, hd, :], False, True, [k_btm, k_nU + f"{hd}"], [kpE])
                        TT("dve", IXb[hp, blk, :], c_f32[hp, 640:704], pE[hp, 0:64], ALU.subtract, [k_cf, kpE], [k_IXb + f"{hd}"])
                        TT("dve", QhT[hp, blk, :], aq[hp, blk, 1, :], pE[hp, 128:256], ALU.subtract, [k_aq, kpE], [k_QhT + f"{hd}"])
                        CP("dve", Gp[hp, blk, :], pE[hp, 64:128], [kpE], [k_Gp + f"{hd}"])
                    for hd in range(12):
                        blk, hp = hsl_(hd)
                        pY, kY = (pY0, kY0) if hd < 8 else (pY1, kY1)
                        yc0 = (hd % 8) * 64
                        MM(pY[:, yc0:yc0 + 64], AT[:, hd, 3, :], vtm[:, blk, hp], True, False, [k_AT + f"{hd}", k_vtm], [kY])
                        MM(pY[:, yc0:yc0 + 64], AT[:, hd, 1, :], nU[:, hd, :], False, False, [k_AT + f"{hd}", k_nU + f"{hd}"], [kY])
                        MM(pY[:, yc0:yc0 + 64], QhT[hp, blk, :], STb[hp, blk, :], False, True, [k_QhT + f"{hd}", k_STb], [kY])
                        MM(pZ[hp, blk * 64:(blk + 1) * 64], IXb[hp, blk, :], STb[hp, blk, :], True, True, [k_IXb + f"{hd}", k_STb], [kZ])
                    kGp_all = [k_Gp + f"{i}" for i in range(12)]
                    TT("dve", ztmp, pZ[:, 0:384].rearrange("p (a b) -> p a b", a=6), Gp, ALU.add, [kZ] + kGp_all, [k_ztmp])
                    TT("dve", ST, ztmp, wtot.unsqueeze(2).to_broadcast([128, 6, 64]), ALU.mult, [k_ztmp, k_wtot], [k_ST])
                    CP("dve", STb, ST, [k_ST], [k_STb])
                    CP("dve", ysb[:, 0:512], pY0[:, 0:512], [kY0], [k_ysb + "0"])
                    CP("dve", ysb[:, 512:768], pY1[:, 0:256], [kY1], [k_ysb + "1"])
                    k_ysb_all = [k_ysb + "0", k_ysb + "1"]
                    if d_ == 0:
                        DMA("sp", YF[t0:t0 + C, :], ysb, k_ysb_all, [f"YF_{c}"])
                        continue
                    DMA("sp", yf_t, YF[t0:t0 + C, :], [f"YF_{c}"], [k_yf])
                    DMA("sp", gate2, G[0:6, :, t0:t0 + C].rearrange("b p t -> p b t"), [], [k_gate2])
                    TT("dve", ysb, ysb, yf_t, ALU.add, k_ysb_all + [k_yf], k_ysb_all)
                    y3 = ysb.rearrange("p (h n) -> p h n", h=12)
                    yc3 = yc.rearrange("p (h n) -> p h n", h=12)
                    ysq3 = ysq.rearrange("p (h n) -> p h n", h=12)
                    mu = st12[:, 0, :]
                    var = st12[:, 1, :]
                    rstd = st12[:, 2, :]
                    P.op("dve", (lambda o, i: (lambda e: e.tensor_reduce(out=o, in_=i, op=ALU.add, axis=mybir.AxisListType.X)))(mu, y3), k_ysb_all, [k_st12 + "m"])
                    TS("dve", mu, mu, 1.0 / 64, ALU.mult, [k_st12 + "m"], [k_st12 + "m"])
                    TT("dve", yc3, y3, mu.unsqueeze(2).to_broadcast([128, 12, 64]), ALU.subtract, k_ysb_all + [k_st12 + "m"], [k_yc])
                    TT("dve", ysq, yc, yc, ALU.mult, [k_yc], [k_ysq])
                    P.op("dve", (lambda o, i: (lambda e: e.tensor_reduce(out=o, in_=i, op=ALU.add, axis=mybir.AxisListType.X)))(var, ysq3), [k_ysq], [k_st12 + "v"])
                    ACT(rstd, var, AF.Ln, [k_st12 + "v", k_ceps], [k_st12 + "r"], scale=1.0 / 64, bias=c_gneps)
                    ACT(rstd, rstd, AF.Exp, [k_st12 + "r"], [k_st12 + "r"], scale=-0.5)
                    TT("dve", yc3, yc3, rstd.unsqueeze(2).to_broadcast([128, 12, 64]), ALU.mult, [k_yc, k_st12 + "r"], [k_yc])
                    TT("dve", yc, yc, gnv_b[:, 0, :], ALU.mult, [k_yc, k_gnv], [k_yc])
                    TT("dve", yc, yc, gnv_b[:, 1, :], ALU.add, [k_yc, k_gnv], [k_yc])
                    TT("dve", t1, r_, k_, ALU.mult, [k_hs], [k_t1])
                    TT("dve", prodb, t1, rk_b, ALU.mult, [k_t1, k_colv], [k_prodb])
                    pB, kpB = bank()
                    for blk in range(6):
                        MM(pB[:, blk * 2:(blk + 1) * 2], prodb[:, blk, :], bselb, True, True, [k_prodb, k_bsel], [kpB])
                    CP("dve", bon, pB[:, 0:12], [kpB], [k_bon])
                    TT("dve", ysq3, vtm.rearrange("p b (h n) -> p (b h) n", h=2), bon.unsqueeze(2).to_broadcast([128, 12, 64]), ALU.mult, [k_vtm, k_bon], [k_ysq])
                    TT("dve", yc, yc, ysq, ALU.add, [k_yc, k_ysq], [k_yc])
                    pT0, kpT0 = bank()
                    pT1, kpT1 = bank()
                    for blk in range(6):
                        tgt, ktgt = (pT0, kpT0) if blk < 4 else (pT1, kpT1)
                        cc = (blk % 4) * 128
                        TR(tgt[:, cc:cc + 128], yc[:, blk * 128:(blk + 1) * 128], ident_f, [k_yc, k_cf], [ktgt])
                    TT("dve", mixo[:, 0:4, :], pT0[:, 0:512].rearrange("p (a b) -> p a b", a=4), gate2[:, 0:4, :], ALU.mult, [kpT0, k_gate2], [k_mixo + "0"])
                    TT("dve", mixo[:, 4:6, :], pT1[:, 0:256].rearrange("p (a b) -> p a b", a=2), gate2[:, 4:6, :], ALU.mult, [kpT1, k_gate2], [k_mixo + "1"])
                    DMA("sp", MIX[0:6, :, t0:t0 + C].rearrange("b p t -> p b t"), mixo, [k_mixo + "0", k_mixo + "1"], [f"MIXr_{c}"])
            P.barrier()
            nb_cfg[0] = 8
            AR.release(m2)

        if 3 in passes:
            m3 = AR.mark()
            nb_cfg[0] = 6
            KT_sb, k_KT = AR.alloc((2, T), BF16)
            V_sb, k_V = AR.alloc((NT, 256), BF16)
            for hk in range(2):
                for tq in range(0, T, 2048):
                    nq = min(2048, T - tq)
                    DMA("sp", KT_sb[:, hk, tq:tq + nq], KTd[hk, :, tq:tq + nq], [], [k_KT + f"_{hk}_{tq}"])
            k_KT_all = [k_KT + f"_{hk}_{tq}" for hk in range(2) for tq in range(0, T, 2048)]
            for tq in range(0, NT, 16):
                nq = min(16, NT - tq)
                DMA("sp", V_sb[:, tq:tq + nq, :], Vd[tq * 128:(tq + nq) * 128, :].rearrange("(t p) c -> p t c", p=128), [], [k_V + f"_{tq}"])
            k_V_all = [k_V + f"_{tq}" for tq in range(0, NT, 16)]
            qT3, k_qT3 = AR.alloc((6, GRP), BF16)
            g3, k_g3 = AR.alloc((6, GRP), BF16)
            PT3 = [AR.alloc((GRP,), BF16) for _ in range(2)]
            rs3, k_rs3 = AR.alloc((GRP,), F32)
            y3, k_y3 = AR.alloc((GRP,), F32)
            o3 = [AR.alloc((GRP,), BF16) for _ in range(2)]
            pO, kpO = psb[6], "ps6"
            pS, kpS = psb[7], "ps7"
            for qg in range(NG):
                t0 = qg * GRP
                qseg = t0 // SEG
                for hq in range(6):
                    DMA("sp", qT3[:, hq, :], QT[hq, :, t0:t0 + GRP], [], [k_qT3 + f"h{hq}"])
                    DMA("sp", g3[:, hq, :], G[6 + hq, :, t0:t0 + GRP], [], [k_g3 + f"h{hq}"])
                for hq in range(6):
                    kvh = hq // 3
                    for kt in range(NT):
                        kseg = (kt * 128) // SEG
                        PT_ap, k_PT = PT3[kt % 2]
                        pb, kpb = bank()
                        MM(pb[:, 0:GRP], KT_sb[:, kvh, kt * 128:(kt + 1) * 128], qT3[:, hq, :], True, True, k_KT_all + [k_qT3 + f"h{hq}"], [kpb])
                        ACT(PT_ap, pb[:, 0:GRP], AF.Exp, [kpb, k_flag], [k_PT], scale=ATT_SCALE,
                            bias=c_flag[:, 8 + qseg * NSEG + kseg:8 + qseg * NSEG + kseg + 1])
                        MM(pO[:, 0:GRP], V_sb[:, kt, kvh * 128:(kvh + 1) * 128], PT_ap, kt == 0, kt == NT - 1, k_V_all + [k_PT], [kpO])
                        MM(pS[:, 0:GRP], onesb, PT_ap, kt == 0, kt == NT - 1, [k_onesb, k_PT], [kpS])
                    P.op("dve", (lambda o, i: (lambda e: e.reciprocal(out=o, in_=i)))(rs3, pS[:, 0:GRP]), [kpS], [k_rs3])
                    TT("dve", y3, pO[:, 0:GRP], rs3, ALU.mult, [kpO, k_rs3], [k_y3])
                    o_ap, k_o = o3[hq % 2]
                    TT("dve", o_ap, y3, g3[:, hq, :], ALU.mult, [k_y3, k_g3 + f"h{hq}"], [k_o])
                    DMA("sp", MIX[6 + hq, :, t0:t0 + GRP], o_ap, [k_o], [f"MIXa{hq}_{qg}"])
            P.barrier()
            nb_cfg[0] = 8
            AR.release(m3)

        if 4 in passes:
            m4 = AR.mark()
            fg_b, k_fg = AR.alloc((D,), F32)
            DMA("sp", fg_b, rowv[:, 1, :], [], [k_fg])
            wo_sb, k_wo = AR.alloc((16, D), BF16)
            for i in range(4):
                DMA("sp", wo_sb[:, i * 4:(i + 1) * 4, :], WO.rearrange("(p k) n -> p k n", k=16)[:, i * 4:(i + 1) * 4, :], [], [k_wo + f"_{i}"])
            k_wo_all = [k_wo + f"_{i}" for i in range(4)]
            mixt = [AR.alloc((16, 128), BF16) for _ in range(2)]
            x4 = [AR.alloc((D,), F32) for _ in range(2)]
            r4 = [AR.alloc((D,), F32) for _ in range(2)]
            y4 = [AR.alloc((D,), F32) for _ in range(2)]
            junk4, k_junk4 = AR.alloc((D,), BF16)
            st4, k_st4 = AR.alloc((2, 2), F32)
            for tt in range(NT):
                tok = tt * 128
                m_ap, k_m = mixt[tt % 2]
                x_ap, k_x = x4[tt % 2]
                r_ap, k_r = r4[tt % 2]
                y_ap, k_y = y4[tt % 2]
                DMA("sp", m_ap, MIX[:, :, tok:tok + 128].rearrange("c p t -> p c t"), [], [k_m])
                DMA("sp", x_ap, xs[tok:tok + 128, :], [], [k_x])
                for ng in range(4):
                    pb, kpb = bank()
                    for kc in range(16):
                        MM(pb[:, 0:512], m_ap[:, kc, :], wo_sb[:, kc, ng * 512:(ng + 1) * 512], kc == 0, kc == 15, [k_m] + k_wo_all, [kpb])
                    TT("dve", r_ap[:, ng * 512:(ng + 1) * 512], pb[:, 0:512], x_ap[:, ng * 512:(ng + 1) * 512], ALU.add, [kpb, k_x], [k_r + f"_{ng}"])
                kr_all = [k_r + f"_{ng}" for ng in range(4)]
                ssq = st4[:, tt % 2, 0:1]
                rstd = st4[:, tt % 2, 1:2]
                ks1 = k_st4 + f"a{tt % 2}"
                ks2 = k_st4 + f"b{tt % 2}"
                ACT(junk4, r_ap, AF.Square, kr_all, [k_junk4, ks1], accum_out=ssq)
                ACT(rstd, ssq, AF.Ln, [ks1, k_ceps], [ks2], scale=1.0 / D, bias=c_eps)
                ACT(rstd, rstd, AF.Exp, [ks2], [ks2], scale=-0.5)
                STT(y_ap, r_ap, rstd, fg_b, ALU.mult, ALU.mult, kr_all + [ks2, k_fg], [k_y])
                DMA("sp", y_out[tok:tok + 128, :], y_ap, [k_y], [f"y_{tt}"])
            P.barrier()
            AR.release(m4)

        P.barrier()
        P.finalize()
        P.emit()
    return nc


def _rope_tab(nseg, seg, carry):
    if carry:
        pos = np.arange(nseg * seg)
    else:
        pos = np.tile(np.arange(seg), nseg)
    row = (pos // 64).astype(np.float32)
    col = (pos % 64).astype(np.float32)
    freqs = (np.float32(10000.0) ** (-np.arange(0, 64, 2, dtype=np.float32) / np.float32(64))).astype(np.float32)
    ang = np.concatenate([row[:, None] * freqs, col[:, None] * freqs], axis=-1).astype(np.float32)
    return np.stack([np.cos(ang), np.sin(ang)], axis=1).astype(np.float32)


def _consts():
    c = np.zeros((128, 1024), np.float32)
    r = np.arange(128)[:, None]
    q = np.arange(128)[None, :]
    c[:, 0:128] = (r == q)
    c[:, 128:256] = (q > r)
    c[:, 256:384] = (q >= r)
    c[:, 384:512] = (q < r)
    c[:, 512:640] = (q <= r)
    c[:, 640:704] = (np.arange(64)[None, :] == (r % 64))
    c[:, 704:706] = (np.arange(2)[None, :] == (r // 64))
    c[:, 706:834] = ((r // 64) == (q // 64))
    return c


def shared_inputs(norm_g, w_in, mu_prev, mu_next, w0, w_up, a0, a_up, k_k, k_a, r_k, gn_g, gn_b,
                  q_norm_g, k_norm_g, mem_norm_g, w_mem_kv, w_out, final_g):
    f = lambda a: np.ascontiguousarray(np.asarray(a, dtype=np.float32))
    w_in = f(w_in)[0]
    wt = w_in.reshape(16, 128, NCB, 128)[:, :, CB_PERM, :]
    w_in_t = np.ascontiguousarray(wt.transpose(2, 1, 0, 3)).reshape(NCB * 128, D)
    w_out_t = np.ascontiguousarray(f(w_out)[0].reshape(16, 128, D).transpose(1, 0, 2)).reshape(128 * 16, D)
    w_kv_t = np.ascontiguousarray(f(w_mem_kv)[0].reshape(16, 128, 1024).transpose(1, 0, 2)).reshape(128 * 16, 1024)
    rowv = np.ascontiguousarray(np.broadcast_to(np.stack([f(norm_g)[0], f(final_g), f(mem_norm_g)[0]])[None], (128, 3, D)))
    qkg = np.ascontiguousarray(np.broadcast_to(np.stack([f(q_norm_g)[0], f(k_norm_g)[0]])[None], (128, 2, 128)))
    gnv = np.ascontiguousarray(np.broadcast_to(np.stack([f(gn_g)[0], f(gn_b)[0]])[None], (128, 2, 768)))
    colv = np.zeros((128, 96), np.float32)
    colv[:, 0:20] = f(mu_prev)[0].reshape(20, 128).T
    colv[:, 20:40] = f(mu_next)[0].reshape(20, 128).T
    colv[:, 40:52] = f(w0)[0].reshape(12, 128).T
    colv[:, 52:64] = f(a0)[0].reshape(12, 128).T
    colv[:, 64:70] = f(k_k)[0].reshape(6, 128).T
    colv[:, 70:76] = f(k_a)[0].reshape(6, 128).T
    colv[:, 76:82] = f(r_k)[0].reshape(6, 128).T
    lora_up = np.ascontiguousarray(np.stack([f(w_up)[0].reshape(128, 768), f(a_up)[0].reshape(128, 768)], axis=1))
    return dict(w_in_t=w_in_t, w_out_t=w_out_t, w_kv_t=w_kv_t, rowv=rowv, qkg=qkg, gnv=gnv, colv=colv,
                lora_up=lora_up, consts=_consts())


def core_inputs(shared, x_core, mem_core, nseg, seg, carry):
    flags = np.zeros((128, 32), np.float32)
    flags[:, 0] = 1.0 if carry else 0.0
    for qs in range(nseg):
        for ks in range(nseg):
            flags[:, 8 + qs * nseg + ks] = 0.0 if (carry or qs == ks) else NEG
    d = dict(shared)
    d["xs"] = np.ascontiguousarray(x_core, dtype=np.float32)
    d["mem"] = np.ascontiguousarray(mem_core, dtype=np.float32)
    d["cs_tab"] = _rope_tab(nseg, seg, carry)
    d["flags"] = flags
    return d


_NC_CACHE = {}


def kernel(x_prompt, x_sample, mem_prompt, mem_sample, norm_g, w_in, mu_prev, mu_next, w0, w_up, a0, a_up,
           k_k, k_a, r_k, gn_g, gn_b, q_norm_g, k_norm_g, mem_norm_g, w_mem_kv, w_out, final_g):
    NSEG, SEG = 4, 2048
    x_prompt = np.asarray(x_prompt, dtype=np.float32)
    x_sample = np.asarray(x_sample, dtype=np.float32)
    mem_prompt = np.asarray(mem_prompt, dtype=np.float32)
    mem_sample = np.asarray(mem_sample, dtype=np.float32)
    shared = shared_inputs(norm_g, w_in, mu_prev, mu_next, w0, w_up, a0, a_up, k_k, k_a, r_k, gn_g, gn_b,
                           q_norm_g, k_norm_g, mem_norm_g, w_mem_kv, w_out, final_g)
    in_maps = []
    for c in range(4):
        in_maps.append(core_inputs(shared, x_prompt[c], np.broadcast_to(mem_prompt[c][None], (NSEG, N_MEM, D)), NSEG, SEG, True))
    for c in range(4):
        in_maps.append(core_inputs(shared, x_sample[4 * c:4 * c + 4].reshape(NSEG * SEG, D), mem_sample[4 * c:4 * c + 4], NSEG, SEG, False))
    key = (NSEG, SEG)
    if key not in _NC_CACHE:
        _NC_CACHE[key] = build(NSEG, SEG)
    nc = _NC_CACHE[key]
    res = run_bass_kernel_spmd(nc, in_maps, core_ids=list(range(8)))
    yp = np.stack([np.asarray(res.results[c]["y"], dtype=np.float32) for c in range(4)])
    ysm = np.concatenate([np.asarray(res.results[4 + c]["y"], dtype=np.float32).reshape(4, SEG, D) for c in range(4)], axis=0)
    return (yp, ysm)
```

```python
import contextlib
import numpy as np
import ml_dtypes
import concourse.bass as bass
import concourse.mybir as mybir
from concourse.bass_utils import run_bass_kernel_spmd

F32 = mybir.dt.float32
BF16 = mybir.dt.bfloat16
ALU = mybir.AluOpType
AF = mybir.ActivationFunctionType

D = 2048
IN_W = 6400
NCB = 50
N_MEM = 256
NORM_EPS = 1e-6
GN_EPS = 64e-5
DECAY_K = float(np.exp(-0.5))
ATT_SCALE = 128 ** -0.5
NEG = -30000.0

CB_PERM = list(range(0, 20)) + list(range(20, 26)) + list(range(36, 42)) + list(range(46, 50)) + \
    list(range(42, 46)) + list(range(26, 32)) + [32, 33] + [34, 35]

ENGS = ("pe", "dve", "act", "pool", "sp")
SEM_EPOCH = 20000
import os as _osg
SERIAL = int(_osg.environ.get("K_SERIAL", "0"))


class Op:
    __slots__ = ("eng", "fn", "deps", "is_dma", "seq", "signal", "sig_idx", "dma_sem", "dma_val", "waits", "barrier")

    def __init__(self, eng, fn, is_dma):
        self.eng = eng
        self.fn = fn
        self.is_dma = is_dma
        self.deps = set()
        self.signal = False
        self.sig_idx = 0
        self.dma_sem = None
        self.dma_val = 0
        self.waits = []
        self.barrier = False


class Prog:
    def __init__(self, nc, n_dma_sems=8):
        self.nc = nc
        self.ops = []
        self.by_eng = {e: [] for e in ENGS}
        self.last_w = {}
        self.readers = {}
        self.n_dma_sems = n_dma_sems
        self.last_comp = None

    def _add(self, eng, fn, reads, writes, is_dma):
        op = Op(eng, fn, is_dma)
        op.seq = len(self.ops)
        for k in reads:
            w = self.last_w.get(k)
            if w is not None:
                op.deps.add(w)
        for k in writes:
            w = self.last_w.get(k)
            if w is not None:
                op.deps.add(w)
            rl = self.readers.get(k)
            if rl:
                op.deps.update(rl)
        for k in reads:
            self.readers.setdefault(k, []).append(op)
        for k in writes:
            self.last_w[k] = op
            self.readers[k] = []
        op.deps.discard(op)
        if SERIAL == 1 and self.ops and not self.ops[-1].barrier:
            op.deps.add(self.ops[-1])
        elif SERIAL == 2 and not is_dma:
            if self.last_comp is not None:
                op.deps.add(self.last_comp)
            self.last_comp = op
        elif SERIAL == 3 and not is_dma and eng != "pe":
            if self.last_comp is not None and self.last_comp.eng != eng:
                op.deps.add(self.last_comp)
            self.last_comp = op
        self.ops.append(op)
        self.by_eng[eng].append(op)
        return op

    def op(self, eng, fn, reads=(), writes=()):
        return self._add(eng, fn, reads, writes, False)

    def dma(self, eng, fn, reads=(), writes=()):
        return self._add(eng, fn, reads, writes, True)

    def barrier(self):
        tails = []
        for e in ENGS:
            comp = [o for o in self.by_eng[e] if not o.is_dma and not o.barrier]
            if comp:
                tails.append(comp[-1])
            dm = [o for o in self.by_eng[e] if o.is_dma]
            tails.extend(dm[-self.n_dma_sems:])
        for e in ENGS:
            op = Op(e, lambda eng: None, False)
            op.barrier = True
            op.seq = len(self.ops)
            op.deps = set(tails)
            self.ops.append(op)
            self.by_eng[e].append(op)
        self.last_w = {}
        self.readers = {}
        self.last_comp = None

    def finalize(self):
        for op in self.ops:
            for d in op.deps:
                if d.is_dma:
                    continue
                if d.eng == "pe" and op.eng == "pe" and not op.is_dma and not op.barrier:
                    continue
                d.signal = True
        self.n_sig = {}
        for e in ENGS:
            c = 0
            for op in self.by_eng[e]:
                if (not op.is_dma) and op.signal:
                    c += 1
                    op.sig_idx = c
            self.n_sig[e] = c
        for e in ENGS:
            k = 0
            slots = [None] * self.n_dma_sems
            counts = [0] * self.n_dma_sems
            for op in self.by_eng[e]:
                if op.is_dma:
                    s = k % self.n_dma_sems
                    prev = slots[s]
                    if prev is not None:
                        op.deps.add(prev)
                    counts[s] += 16
                    op.dma_sem = (e, s)
                    op.dma_val = counts[s]
                    slots[s] = op
                    k += 1
        for e in ENGS:
            wd = {}
            dma_waited = {}
            for op in self.by_eng[e]:
                need = {}
                for d in op.deps:
                    if d.is_dma:
                        key = d.dma_sem
                        if dma_waited.get(key, 0) < d.dma_val:
                            dma_waited[key] = d.dma_val
                            op.waits.append(("dma", key, d.dma_val))
                    else:
                        if d.eng == "pe" and op.eng == "pe" and not op.is_dma and not op.barrier:
                            continue
                        if d.sig_idx > need.get(d.eng, 0):
                            need[d.eng] = d.sig_idx
                for src, idx in need.items():
                    if wd.get(src, 0) < idx:
                        op.waits.append(("eng", src, idx))
                        wd[src] = idx

    def emit(self):
        nc = self.nc
        with contextlib.ExitStack() as st:
            sems = {}
            for e in ENGS:
                n_ep = (self.n_sig[e] + SEM_EPOCH - 1) // SEM_EPOCH
                for i in range(n_ep):
                    sems[("eng", e, i)] = st.enter_context(nc.semaphore(f"s_{e}_{i}"))
                used = sorted(set(op.dma_sem for op in self.by_eng[e] if op.is_dma))
                for key in used:
                    sems[("dma",) + key] = st.enter_context(nc.semaphore(f"d_{key[0]}_{key[1]}"))
            block = st.enter_context(nc.Block())
            engmap = {"pe": "tensor", "dve": "vector", "act": "scalar", "pool": "gpsimd", "sp": "sync"}

            def make(e):
                ops = self.by_eng[e]

                def body(engine):
                    for op in ops:
                        for w in op.waits:
                            if w[0] == "dma":
                                engine.wait_ge(sems[("dma",) + w[1]], w[2])
                            else:
                                idx = w[2]
                                ep = (idx - 1) // SEM_EPOCH
                                engine.wait_ge(sems[("eng", w[1], ep)], idx - ep * SEM_EPOCH)
                        ins = op.fn(engine)
                        if ins is None:
                            continue
                        if op.is_dma:
                            ins.then_inc(sems[("dma",) + op.dma_sem], 16)
                        elif op.signal:
                            ep = (op.sig_idx - 1) // SEM_EPOCH
                            ins.then_inc(sems[("eng", e, ep)], 1)
                return body

            for e in ENGS:
                if self.by_eng[e]:
                    getattr(block, engmap[e])(make(e))


class Arena:
    def __init__(self, tile, nbytes):
        self.tile = tile
        self.nbytes = nbytes
        self.off = 0
        self.cnt = 0

    def alloc(self, shape, dtype):
        esz = 4 if dtype == F32 else 2
        n = int(np.prod(shape))
        nb = n * esz
        self.off = (self.off + 63) // 64 * 64
        assert self.off + nb <= self.nbytes, f"arena overflow {self.off}+{nb}>{self.nbytes}"
        a = self.tile[:, self.off // 2:(self.off + nb) // 2]
        if dtype == F32:
            a = a.bitcast(F32)
        self.off += nb
        self.cnt += 1
        key = f"A{self.cnt}"
        if len(shape) == 2:
            a = a.rearrange("p (a b) -> p a b", a=shape[0])
        elif len(shape) == 3:
            a = a.rearrange("p (a b c) -> p a b c", a=shape[0], b=shape[1])
        elif len(shape) == 4:
            a = a.rearrange("p (a b c d) -> p a b c d", a=shape[0], b=shape[1], c=shape[2])
        return a, key

    def mark(self):
        return self.off

    def release(self, m):
        self.off = m


def build(NSEG, SEG, debug=False, passes=(1, 2, 3, 4)):
    T = NSEG * SEG
    NT = T // 128
    GRP = 512
    NG = T // GRP
    assert SEG % GRP == 0
    nc = bass.Bass("TRN2", target_bir_lowering=False)
    dt_in = lambda n, s, d=F32: nc.dram_tensor(n, list(s), d, kind="ExternalInput").ap()
    okind = "ExternalOutput" if debug else "Internal"
    dt_scr = lambda n, s, d: nc.dram_tensor(n, list(s), d, kind=okind).ap()

    xs = dt_in("xs", [T, D])
    mem = dt_in("mem", [NSEG, N_MEM, D])
    w_in_t = dt_in("w_in_t", [NCB * 128, D])
    w_out_t = dt_in("w_out_t", [128 * 16, D])
    w_kv_t = dt_in("w_kv_t", [128 * 16, 1024])
    rowv = dt_in("rowv", [128, 3, D])
    qkg = dt_in("qkg", [128, 2, 128])
    colv = dt_in("colv", [128, 96])
    gnv = dt_in("gnv", [128, 2, 768])
    lora_up = dt_in("lora_up", [128, 2, 768])
    cs_tab = dt_in("cs_tab", [T, 2, 64])
    consts = dt_in("consts", [128, 1024])
    flags = dt_in("flags", [128, 32])
    y_out = nc.dram_tensor("y", [T, D], F32, kind="ExternalOutput").ap()

    W1 = dt_scr("W1", [NCB * 128, D], BF16)
    WO = dt_scr("WO", [128 * 16, D], BF16)
    WKV = dt_scr("WKV", [128 * 16, 1024], BF16)
    PR = dt_scr("PR", [20, 128, T + 2], F32)
    G = dt_scr("G", [12, 128, T], BF16)
    MIX = dt_scr("MIX", [16, 128, T], BF16)
    QT = dt_scr("QT", [6, 128, T], BF16)
    KTd = dt_scr("KTd", [2, 128, T], BF16)
    Vd = dt_scr("Vd", [T, 256], BF16)
    YF = dt_scr("YF", [T, 768], F32)
    HSd = dt_scr("HSd", [T // 128, 128, 26 * 128], F32)

    with contextlib.ExitStack() as top:
        ARENA_BYTES = 200 * 1024
        arena_t = top.enter_context(nc.sbuf_tensor("arena", [128, ARENA_BYTES // 2], BF16))
        AR = Arena(arena_t, ARENA_BYTES)
        psb = [top.enter_context(nc.psum_tensor(f"psb{i}", [128, 512], F32)) for i in range(8)]
        P = Prog(nc)
        ps_rr = [0]

        import os as _os0
        _NB = int(_os0.environ.get("K_NB", "8"))

        nb_cfg = [_NB]

        def bank():
            i = ps_rr[0] % nb_cfg[0]
            ps_rr[0] += 1
            return psb[i], f"ps{i}"

        def bank2():
            if ps_rr[0] % 2:
                ps_rr[0] += 1
            i = ps_rr[0] % 8
            ps_rr[0] += 2
            return psb[i], psb[i + 1], f"ps{i}", f"ps{i + 1}"

        ev_rr = [0]

        def evac_eng():
            return "dve"

        def copy_op(eng, out, in_, reads, writes):
            if eng == "act":
                P.op("act", lambda e: e.activation(out=out, in_=in_, func=AF.Copy), reads, writes)
            else:
                P.op(eng, lambda e: e.tensor_copy(out=out, in_=in_), reads, writes)

        def MM(out, lhsT, rhs, start, stop, reads, writes):
            P.op("pe", lambda e: e.matmul(out, lhsT=lhsT, rhs=rhs, start=start, stop=stop), reads, writes)

        def TR(out, in_, ident, reads, writes):
            P.op("pe", lambda e: e.transpose(out, in_, ident), reads, writes)

        def ACT(out, in_, func, reads, writes, scale=None, bias=None, accum_out=None):
            kw = {}
            if scale is not None:
                kw["scale"] = scale
            if bias is not None:
                kw["bias"] = bias
            if accum_out is not None:
                kw["accum_out"] = accum_out
            P.op("act", lambda e: e.activation(out=out, in_=in_, func=func, **kw), reads, writes)

        def TT(eng, out, in0, in1, op, reads, writes):
            P.op(eng, lambda e: e.tensor_tensor(out=out, in0=in0, in1=in1, op=op), reads, writes)

        def TS(eng, out, in0, s1, op0, reads, writes, s2=None, op1=None):
            if op1 is None:
                P.op(eng, lambda e: e.tensor_scalar(out=out, in0=in0, scalar1=s1, scalar2=None, op0=op0), reads, writes)
            else:
                P.op(eng, lambda e: e.tensor_scalar(out=out, in0=in0, scalar1=s1, scalar2=s2, op0=op0, op1=op1), reads, writes)

        def STT(out, in0, scalar, in1, op0, op1, reads, writes):
            P.op("dve", lambda e: e.scalar_tensor_tensor(out=out, in0=in0, scalar=scalar, in1=in1, op0=op0, op1=op1), reads, writes)

        def DMA(eng, out, in_, reads, writes, slow=False):
            if slow:
                P.dma(eng, lambda e: e.dma_start(out=out, in_=in_, allow_slow_non_contiguous=True), reads, writes)
            else:
                P.dma(eng, lambda e: e.dma_start(out=out, in_=in_), reads, writes)

        def CP(eng, out, in_, reads, writes):
            if eng == "act":
                P.op("act", lambda e: e.activation(out=out, in_=in_, func=AF.Copy), reads, writes)
            else:
                P.op(eng, lambda e: e.tensor_copy(out=out, in_=in_), reads, writes)

        def MEMSET(eng, out, val, reads, writes):
            P.op(eng, lambda e: e.memset(out, val), reads, writes)

        c_f32, k_cf = AR.alloc((1024,), F32)
        c_flag, k_flag = AR.alloc((32,), F32)
        c_colv, k_colv = AR.alloc((96,), F32)
        identb, k_idb = AR.alloc((128,), BF16)
        onesb, k_onesb = AR.alloc((128,), BF16)
        c_eps_t, k_ceps = AR.alloc((4,), F32)
        DMA("sp", c_f32, consts, [], [k_cf])
        DMA("sp", c_flag, flags, [], [k_flag])
        DMA("sp", c_colv, colv, [], [k_colv])
        ident_f = c_f32[:, 0:128]
        CP("dve", identb, ident_f, [k_cf], [k_idb])
        MEMSET("dve", onesb, 1.0, [], [k_onesb])
        MEMSET("dve", c_eps_t[:, 0:1], NORM_EPS, [], [k_ceps])
        MEMSET("dve", c_eps_t[:, 1:2], GN_EPS, [k_ceps], [k_ceps])
        MEMSET("dve", c_eps_t[:, 2:3], 1e-30, [k_ceps], [k_ceps])
        c_eps = c_eps_t[:, 0:1]
        c_gneps = c_eps_t[:, 1:2]
        c_tiny = c_eps_t[:, 2:3]

        import os as _os
        _skip = _os.environ.get("K_SKIP", "")
        if "cast" not in _skip:
            for i in range(NCB):
                DMA("pool", W1[i * 128:(i + 1) * 128, :], w_in_t[i * 128:(i + 1) * 128, :], [], [f"W1_{i}"])
            for i in range(16):
                DMA("pool", WO[i * 128:(i + 1) * 128, :], w_out_t[i * 128:(i + 1) * 128, :], [], [f"WO{i}"])
            for i in range(16):
                DMA("pool", WKV[i * 128:(i + 1) * 128, :], w_kv_t[i * 128:(i + 1) * 128, :], [], [f"WKV{i}"])
        zt, k_zt = AR.alloc((20, 1), F32)
        MEMSET("pool", zt, 0.0, [], [k_zt])
        if "pad" not in _skip:
            DMA("sp", PR[:, :, 0:1].rearrange("b p o -> p b o"), zt, [k_zt], ["PRpad0"], slow=True)
            DMA("sp", PR[:, :, T + 1:T + 2].rearrange("b p o -> p b o"), zt, [k_zt], ["PRpad1"], slow=True)
        if "mixz" not in _skip:
            zb, k_zb = AR.alloc((2048,), BF16)
            MEMSET("pool", zb, 0.0, [], [k_zb])
            for cbz in range(12):
                for tz in range(0, T, 2048):
                    nz = min(2048, T - tz)
                    DMA("sp", MIX[cbz, :, tz:tz + nz], zb[:, 0:nz], [k_zb], [f"MIXz{cbz}_{tz}"])
        P.barrier()

        def rmsnorm_tile(x_ap, k_x, g_ap, k_g, h_ap, k_h, junk, k_junk, ssq, k_ssq, rstd, k_rstd):
            ACT(junk, x_ap, AF.Square, [k_x], [k_junk, k_ssq], accum_out=ssq)
            ACT(rstd, ssq, AF.Ln, [k_ssq, k_ceps], [k_rstd], scale=1.0 / D, bias=c_eps)
            ACT(rstd, rstd, AF.Exp, [k_rstd], [k_rstd], scale=-0.5)
            STT(h_ap, x_ap, rstd, g_ap, ALU.mult, ALU.mult, [k_x, k_rstd, k_g], [k_h])

        if 1 in passes:
            m1 = AR.mark()
            ng_b, k_ng = AR.alloc((D,), F32)
            qkg_b, k_qkg = AR.alloc((2, 128), F32)
            DMA("sp", ng_b, rowv[:, 0, :], [], [k_ng])
            DMA("sp", qkg_b, qkg, [], [k_qkg])
            mkT, k_mkT = AR.alloc((NSEG, 4, 256), BF16)
            mv, k_mv = AR.alloc((NSEG, 2, 512), BF16)

            mkv_mark = AR.mark()
            mng_b, k_mng = AR.alloc((D,), F32)
            DMA("sp", mng_b, rowv[:, 2, :], [], [k_mng])
            wkv_sb, k_wkv = AR.alloc((16, 1024), BF16)
            DMA("sp", wkv_sb, WKV.rearrange("(p k) n -> p k n", k=16), [f"WKV{i}" for i in range(16)], [k_wkv])
            mx, k_mx = AR.alloc((D,), F32)
            mh, k_mh = AR.alloc((D,), BF16)
            mjunk, k_mjunk = AR.alloc((D,), BF16)
            mss, k_mss = AR.alloc((2,), F32)
            mhT, k_mhT = AR.alloc((16, 256), BF16)
            for s in range(NSEG):
                for mt in range(2):
                    DMA("sp", mx, mem[s, mt * 128:(mt + 1) * 128, :], [], [k_mx])
                    if "mkvn" in _skip:
                        continue
                    rmsnorm_tile(mx, k_mx, mng_b, k_mng, mh, k_mh, mjunk, k_mjunk, mss[:, 0:1], k_mss, mss[:, 1:2], k_mss + "b")
                    for half in range(2):
                        if "mkvt" in _skip:
                            continue
                        pb, kpb = bank()
                        pbb = pb[:].bitcast(BF16)
                        for j in range(8):
                            kc = half * 8 + j
                            TR(pbb[:, j * 128:(j + 1) * 128], mh[:, kc * 128:(kc + 1) * 128], identb, [k_mh, k_idb], [kpb])
                        CP(evac_eng(), mhT[:, half * 8:(half + 1) * 8, mt * 128:(mt + 1) * 128],
                           pbb[:, 0:1024].rearrange("p (a b) -> p a b", a=8), [kpb], [k_mhT])
                if "mkvm" in _skip:
                    continue
                for hd in range(4):
                    if "mkva" in _skip:
                        continue
                    pb, kpb = bank()
                    for kc in range(16):
                        MM(pb[:, 0:256], wkv_sb[:, kc, hd * 128:(hd + 1) * 128], mhT[:, kc, :], kc == 0, kc == 15, [k_wkv, k_mhT], [kpb])
                    CP(evac_eng(), mkT[:, s, hd, :], pb[:, 0:256], [kpb], [k_mkT])
                P.barrier()
                for mt in range(2):
                    if "mkvb" in _skip:
                        continue
                    pb, kpb = bank()
                    for kc in range(16):
                        MM(pb[:, 0:512], mhT[:, kc, mt * 128:(mt + 1) * 128], wkv_sb[:, kc, 512:1024], kc == 0, kc == 15, [k_wkv, k_mhT], [kpb])
                    CP(evac_eng(), mv[:, s, mt, :], pb[:, 0:512], [kpb], [k_mv])
            P.barrier()
            AR.release(mkv_mark)

            xt = [AR.alloc((D,), F32) for _ in range(2)]
            ht = [AR.alloc((D,), BF16) for _ in range(2)]
            junk, k_junk = AR.alloc((D,), BF16)
            st_small, k_sts = AR.alloc((2, 2), F32)
            hT = [AR.alloc((16, GRP), BF16) for _ in range(2)]
            wblk = [AR.alloc((4, 16, 128), BF16) for _ in range(3)]
            stg32 = [AR.alloc((GRP,), F32) for _ in range(3)]
            stg16 = [AR.alloc((GRP,), BF16) for _ in range(3)]
            mqT, k_mqT = AR.alloc((4, GRP), BF16)
            mg, k_mg = AR.alloc((4, GRP), BF16)
            PTm = [AR.alloc((2, GRP), BF16) for _ in range(2)]
            rs_t, k_rs = AR.alloc((GRP,), F32)
            ym_t, k_ym = AR.alloc((GRP,), F32)
            qn, k_qn = AR.alloc((8, 128), F32)
            qsq, k_qsq = AR.alloc((8, 128), F32)
            qss, k_qss = AR.alloc((8,), F32)
            qrs, k_qrs = AR.alloc((8,), F32)
            rp1, k_rp1 = AR.alloc((8, 64), F32)
            rp2, k_rp2 = AR.alloc((8, 64), F32)
            qr, k_qr = AR.alloc((8, 128), BF16)
            cst = [AR.alloc((4, 2, 64), F32) for _ in range(1)]
            qTs = [AR.alloc((6, GRP), BF16) for _ in range(1)]
            kTs = [AR.alloc((2, GRP), BF16) for _ in range(1)]
            vst = [AR.alloc((4, 256), BF16) for _ in range(1)]
            wl_rr = [0]
            s32_rr = [0]
            s16_rr = [0]
            tile_ctr = [0]
            loads = [(b, min(4, NCB - b)) for b in range(0, NCB, 4)]

            for g in range(NG):
                if "main" in _skip:
                    break
                t0 = g * GRP
                seg = t0 // SEG
                hTg, k_hT = hT[g % 2]
                for ti in range(4):
                    tt = tile_ctr[0]
                    tile_ctr[0] += 1
                    x_ap, k_x = xt[tt % 2]
                    h_ap, k_h = ht[tt % 2]
                    tok = t0 + ti * 128
                    DMA("sp", x_ap, xs[tok:tok + 128, :], [], [k_x])
                    rmsnorm_tile(x_ap, k_x, ng_b, k_ng, h_ap, k_h, junk, k_junk,
                                 st_small[:, tt % 2, 0:1], k_sts + f"a{tt % 2}", st_small[:, tt % 2, 1:2], k_sts + f"b{tt % 2}")
                    for half in range(2):
                        pb, kpb = bank()
                        pbb = pb[:].bitcast(BF16)
                        for j in range(8):
                            kc = half * 8 + j
                            TR(pbb[:, j * 128:(j + 1) * 128], h_ap[:, kc * 128:(kc + 1) * 128], identb, [k_h, k_idb], [kpb])
                        CP(evac_eng(), hTg[:, half * 8:(half + 1) * 8, ti * 128:(ti + 1) * 128],
                           pbb[:, 0:1024].rearrange("p (a b) -> p a b", a=8), [kpb], [k_hT])
                cs_ap, k_cs = cst[0]
                qTs_ap, k_qTs = qTs[0]
                kTs_ap, k_kTs = kTs[0]
                vst_ap, k_vst = vst[0]
                for (b0, nb) in loads:
                    if "fm" in _skip and b0 < 40:
                        continue
                    if "tm" in _skip and b0 >= 40:
                        continue
                    w_ap, k_w = wblk[wl_rr[0] % 3]
                    wl_rr[0] += 1
                    DMA("sp", w_ap[:, 0:nb], W1[b0 * 128:(b0 + nb) * 128, :].rearrange("(b p) (k j) -> p b k j", p=128, k=16),
                        [f"W1_{j}" for j in range(b0, b0 + nb)], [k_w])
                    if b0 < 40:
                        for bi in range(nb):
                            cb = b0 + bi
                            pb, kpb = bank()
                            for kc in range(16):
                                MM(pb[:, 0:GRP], w_ap[:, bi, kc, :], hTg[:, kc, :], kc == 0, kc == 15, [k_w, k_hT], [kpb])
                            if cb < 20:
                                s_ap, k_s = stg32[s32_rr[0] % 3]
                                s32_rr[0] += 1
                                CP(evac_eng(), s_ap, pb[:, 0:GRP], [kpb], [k_s])
                                DMA("sp", PR[cb, :, 1 + t0:1 + t0 + GRP], s_ap, [k_s], [f"PR{cb}_{g}"])
                            elif cb < 32:
                                s_ap, k_s = stg16[s16_rr[0] % 3]
                                s16_rr[0] += 1
                                ACT(s_ap, pb[:, 0:GRP], AF.Silu, [kpb], [k_s])
                                DMA("sp", G[cb - 20, :, t0:t0 + GRP], s_ap, [k_s], [f"G{cb}_{g}"])
                            elif cb < 36:
                                ACT(mg[:, cb - 32, :], pb[:, 0:GRP], AF.Silu, [kpb], [k_mg])
                            else:
                                CP("dve", mqT[:, cb - 36, :], pb[:, 0:GRP], [kpb], [k_mqT])
                        if b0 == 36 and "mat" not in _skip:
                            for hd in range(4):
                                PT_ap, k_PT = PTm[hd % 2]
                                for mc in range(2):
                                    pb, kpb = bank()
                                    MM(pb[:, 0:GRP], mkT[:, seg, hd, mc * 128:(mc + 1) * 128], mqT[:, hd, :], True, True, [k_mkT, k_mqT], [kpb])
                                    ACT(PT_ap[:, mc, :], pb[:, 0:GRP], AF.Exp, [kpb], [k_PT], scale=ATT_SCALE)
                                pby, kpby = bank()
                                pbs, kpbs = bank()
                                for mc in range(2):
                                    MM(pby[:, 0:GRP], mv[:, seg, mc, hd * 128:(hd + 1) * 128], PT_ap[:, mc, :], mc == 0, mc == 1, [k_mv, k_PT], [kpby])
                                for mc in range(2):
                                    MM(pbs[:, 0:GRP], onesb, PT_ap[:, mc, :], mc == 0, mc == 1, [k_onesb, k_PT], [kpbs])
                                P.op("dve", (lambda o, i: (lambda e: e.reciprocal(out=o, in_=i)))(rs_t, pbs[:, 0:GRP]), [kpbs], [k_rs])
                                TT("dve", ym_t, pby[:, 0:GRP], rs_t, ALU.mult, [kpby, k_rs], [k_ym])
                                s_ap, k_s = stg16[s16_rr[0] % 3]
                                s16_rr[0] += 1
                                TT("pool", s_ap, ym_t, mg[:, hd, :], ALU.mult, [k_ym, k_mg], [k_s])
                                DMA("sp", MIX[12 + hd, :, t0:t0 + GRP], s_ap, [k_s], [f"MIX{12 + hd}_{g}"])
                    else:
                        if b0 == 40:
                            P.barrier()
                            DMA("sp", cs_ap, cs_tab[t0:t0 + GRP].rearrange("(t p) c f -> p t c f", p=128), [], [k_cs])
                        for ti in range(4):
                            pb, kpb = bank()
                            for kc in range(16):
                                MM(pb[:, 0:nb * 128].rearrange("p (b j) -> p b j", b=nb), hTg[:, kc, ti * 128:(ti + 1) * 128], w_ap[:, 0:nb, kc, :],
                                   kc == 0, kc == 15, [k_w, k_hT], [kpb])
                            if b0 == 48:
                                CP(evac_eng(), vst_ap[:, ti, :], pb[:, 0:256], [kpb], [k_vst])
                                continue
                            if "qk" in _skip:
                                continue
                            hoff = 0 if b0 == 40 else 4
                            pv = pb[:, 0:512].rearrange("p (h d) -> p h d", h=4)
                            ACT(qsq[:, hoff:hoff + 4, :], pv, AF.Square, [kpb], [k_qsq] + [k_qsq + f"h{hoff + i}" for i in range(4)])
                            P.op("dve", (lambda o, i: (lambda e: e.tensor_reduce(out=o, in_=i, op=ALU.add, axis=mybir.AxisListType.X)))(
                                qss[:, hoff:hoff + 4], qsq[:, hoff:hoff + 4, :]), [k_qsq], [k_qss])
                            ACT(qrs[:, hoff:hoff + 4], qss[:, hoff:hoff + 4], AF.Ln, [k_qss, k_ceps], [k_qrs], scale=1.0 / 128, bias=c_eps)
                            ACT(qrs[:, hoff:hoff + 4], qrs[:, hoff:hoff + 4], AF.Exp, [k_qrs], [k_qrs], scale=-0.5)
                            for hh in range(4):
                                h8 = hoff + hh
                                gsel = 0 if h8 < 6 else 1
                                STT(qn[:, h8, :], pv[:, hh, :], qrs[:, h8:h8 + 1], qkg_b[:, gsel, :], ALU.mult, ALU.mult, [kpb, k_qrs, k_qkg], [k_qn])
                            if "rope" in _skip:
                                continue
                            for hh in range(4):
                                h8 = hoff + hh
                                qv = qn[:, h8, :].rearrange("p (f two) -> p f two", two=2)
                                x0, x1 = qv[:, :, 0], qv[:, :, 1]
                                cosb = cs_ap[:, ti, 0, :]
                                sinb = cs_ap[:, ti, 1, :]
                                ov = qsq[:, h8, :].rearrange("p (f two) -> p f two", two=2)
                                r1 = rp1[:, h8, :]
                                r2 = rp2[:, h8, :]
                                kq = k_qsq + f"h{h8}"
                                k1 = k_rp1 + f"h{h8}"
                                k2 = k_rp2 + f"h{h8}"
                                TT("dve", r1, x0, cosb, ALU.mult, [k_qn, k_cs], [k1])
                                TT("dve", r2, x1, sinb, ALU.mult, [k_qn, k_cs], [k2])
                                TT("dve", ov[:, :, 0], r1, r2, ALU.subtract, [k1, k2, k_qss], [kq])
                                TT("dve", r1, x0, sinb, ALU.mult, [k_qn, k_cs, kq], [k1])
                                TT("dve", r2, x1, cosb, ALU.mult, [k_qn, k_cs, kq], [k2])
                                TT("dve", ov[:, :, 1], r1, r2, ALU.add, [k1, k2, kq], [kq])
                                CP("dve", qr[:, h8, :], qsq[:, h8, :], [kq], [k_qr])
                            if "notr" in _skip:
                                continue
                            if "trbar" in _skip:
                                P.barrier()
                            pbt, kpbt = bank()
                            pbtb = pbt[:].bitcast(BF16)
                            for hh in range(4):
                                TR(pbtb[:, hh * 128:(hh + 1) * 128], qr[:, hoff + hh, :], identb, [k_qr, k_idb], [kpbt])
                            if "nocp" in _skip:
                                continue
                            for hh in range(4):
                                h8 = hoff + hh
                                if h8 < 6:
                                    CP(evac_eng(), qTs_ap[:, h8, ti * 128:(ti + 1) * 128], pbtb[:, hh * 128:(hh + 1) * 128], [kpbt], [k_qTs + f"h{h8}"])
                                else:
                                    CP(evac_eng(), kTs_ap[:, h8 - 6, ti * 128:(ti + 1) * 128], pbtb[:, hh * 128:(hh + 1) * 128], [kpbt], [k_kTs + f"h{h8 - 6}"])
                if "tm" in _skip or "qk" in _skip or "rope" in _skip or "notr" in _skip or "nocp" in _skip:
                    DMA("sp", Vd[t0:t0 + GRP, :].rearrange("(t p) c -> p t c", p=128), vst_ap, [k_vst], [f"V_{g}"])
                    P.barrier()
                    continue
                if "qst" not in _skip:
                    for hq in range(6):
                        DMA("sp", QT[hq, :, t0:t0 + GRP], qTs_ap[:, hq, :], [k_qTs + f"h{hq}"], [f"QT_{g}_{hq}"])
                    for hk in range(2):
                        DMA("sp", KTd[hk, :, t0:t0 + GRP], kTs_ap[:, hk, :], [k_kTs + f"h{hk}"], [f"KT_{g}_{hk}"])
                DMA("sp", Vd[t0:t0 + GRP, :].rearrange("(t p) c -> p t c", p=128), vst_ap, [k_vst], [f"V_{g}"])
                P.barrier()
            P.barrier()
            AR.release(m1)

        if 2 in passes:
            m2 = AR.mark()
            nb_cfg[0] = 5
            C = 128
            NCH = T // C
            CPS = SEG // C
            KD = DECAY_K
            pY0, kY0 = psb[5], "ps5"
            pY1, kY1 = psb[6], "ps6"
            pZ, kZ = psb[7], "ps7"
            gnv_b, k_gnv = AR.alloc((2, 768), F32)
            DMA("sp", gnv_b, gnv, [], [k_gnv])
            lup, k_lup = AR.alloc((2, 768), F32)
            DMA("sp", lup, lora_up, [], [k_lup])
            lupb, k_lupb = AR.alloc((2, 768), BF16)
            CP("dve", lupb, lup, [k_lup], [k_lupb])
            blk1b, k_blk1 = AR.alloc((128,), BF16)
            CP("dve", blk1b, c_f32[:, 706:834], [k_cf], [k_blk1])
            bselb, k_bsel = AR.alloc((2,), BF16)
            CP("dve", bselb, c_f32[:, 704:706], [k_cf], [k_bsel])
            ones_f, k_onesf = AR.alloc((128,), F32)
            MEMSET("dve", ones_f, 1.0, [], [k_onesf])
            c0v, k_c0v = AR.alloc((20,), F32)
            TT("dve", c0v, c_colv[:, 0:20], c_colv[:, 20:40], ALU.add, [k_colv], [k_c0v])
            TS("dve", c0v, c0v, -1.0, ALU.mult, [k_c0v], [k_c0v], s2=1.0, op1=ALU.add)
            m4 = []
            mLs = []
            for d_ in range(2):
                mt_, k_mt = AR.alloc((4, 128), F32)
                s_off, i_off = (128, 256) if d_ == 0 else (384, 512)
                for q_ in range(4):
                    off = s_off if q_ % 2 == 0 else i_off
                    CP("dve", mt_[:, q_, :], c_f32[:, off:off + 128], [k_cf, k_mt], [k_mt])
                m4.append((mt_, k_mt))
                mLs.append(c_f32[:, 384:512] if d_ == 0 else c_f32[:, 128:256])
            mu_p = c_colv[:, 0:20].unsqueeze(2).to_broadcast([128, 20, 128])
            mu_n = c_colv[:, 20:40].unsqueeze(2).to_broadcast([128, 20, 128])
            c0_b = c0v.unsqueeze(2).to_broadcast([128, 20, 128])
            kk_b = c_colv[:, 64:70].unsqueeze(2).to_broadcast([128, 6, 128])
            ka_b = c_colv[:, 70:76].unsqueeze(2).to_broadcast([128, 6, 128])
            rk_b = c_colv[:, 76:82].unsqueeze(2).to_broadcast([128, 6, 128])
            flag_ap = c_flag[:, 0:1]
            praw, k_praw = AR.alloc((20, 130), F32)
            hs_bufs = [AR.alloc((26, 128), F32) for _ in range(2)]
            tmp20, k_tmp20 = AR.alloc((20, 128), F32)
            f6 = lambda: AR.alloc((6, 128), F32)
            kkraw, k_kkraw = f6()
            rn, k_rn = f6()
            sg, k_sg = f6()
            a_t, k_at = f6()
            cs, k_cs = f6()
            ex, k_ex = f6()
            eL, k_eL = f6()
            emL, k_emL = f6()
            eLx, k_eLx = f6()
            b_t, k_bt = f6()
            kp, k_kp = f6()
            t1, k_t1 = f6()
            wtot, k_wtot = AR.alloc((6,), F32)
            jn, k_jn = AR.alloc((1,), F32)
            b6 = lambda: AR.alloc((6, 128), BF16)
            sqb, k_sqb = b6()
            aq, k_aq = AR.alloc((6, 2, 128), BF16)
            btl, k_btl = b6()
            ktl, k_ktl = b6()
            vb, k_vb = b6()
            atm, k_atm = b6()
            btm, k_btm = b6()
            ktm, k_ktm = b6()
            vtm, k_vtm = b6()
            prodb, k_prodb = b6()
            twl, k_twl = AR.alloc((128,), BF16)
            alb, k_alb = AR.alloc((128,), BF16)
            AT, k_AT = AR.alloc((12, 4, 128), BF16)
            Pp = [AR.alloc((12, 2, 128), BF16) for _ in range(2)]
            Rb = [AR.alloc((12, 128), BF16) for _ in range(2)]
            nU, k_nU = AR.alloc((12, 64), BF16)
            IXb, k_IXb = AR.alloc((6, 64), BF16)
            QhT, k_QhT = b6()
            Gp, k_Gp = AR.alloc((6, 64), F32)
            ST, k_ST = AR.alloc((6, 64), F32)
            STb, k_STb = AR.alloc((6, 64), BF16)
            ztmp, k_ztmp = AR.alloc((6, 64), F32)
            ysb, k_ysb = AR.alloc((768,), F32)
            yf_t, k_yf = AR.alloc((768,), F32)
            yc, k_yc = AR.alloc((768,), F32)
            ysq, k_ysq = AR.alloc((768,), F32)
            st12, k_st12 = AR.alloc((4, 12), F32)
            bon, k_bon = AR.alloc((12,), F32)
            gate2, k_gate2 = b6()
            mixo, k_mixo = b6()

            def exp_op(out, k_out, in_, k_in, scale):
                ACT(out, in_, AF.Exp, [k_in], [k_out], scale=scale)

            for d_ in range(2):
                MEMSET("dve", ST, 0.0, [], [k_ST])
                MEMSET("dve", STb, 0.0, [], [k_STb])
                order = list(range(NCH)) if d_ == 0 else list(range(NCH - 1, -1, -1))
                m4t, k_m4 = m4[d_]
                mL = mLs[d_]
                for ci, c in enumerate(order):
                    t0 = c * C
                    cross = (c % CPS == 0 and c > 0) if d_ == 0 else ((c + 1) % CPS == 0 and c < NCH - 1)
                    if cross:
                        TS("dve", ST, ST, flag_ap, ALU.mult, [k_ST, k_flag], [k_ST])
                        CP("dve", STb, ST, [k_ST], [k_STb])
                    hsb, k_hs = hs_bufs[ci % 2]
                    hs = hsb[:, 0:20, :]
                    kk = hsb[:, 20:26, :]
                    k_kk = k_hs + "kk"
                    r_ = hs[:, 0:6, :]
                    k_ = hs[:, 6:12, :]
                    v_ = hs[:, 12:18, :]
                    if d_ == 1:
                        DMA("sp", hsb, HSd[c].rearrange("p (a b) -> p a b", a=26), [f"HS_{c}"], [k_hs, k_kk])
                    else:
                        DMA("sp", praw, PR[:, :, t0:t0 + 130].rearrange("b p t -> p b t"), [], [k_praw])
                        if c % CPS == 0 and c > 0:
                            TS("dve", praw[:, :, 0:1], praw[:, :, 0:1], flag_ap, ALU.mult, [k_praw, k_flag], [k_praw])
                        if (c + 1) % CPS == 0 and c < NCH - 1:
                            TS("dve", praw[:, :, 129:130], praw[:, :, 129:130], flag_ap, ALU.mult, [k_praw, k_flag], [k_praw])
                        for sb_ in range(20):
                            TS("dve", hs[:, sb_, :], praw[:, sb_, 1:129], c0v[:, sb_:sb_ + 1], ALU.mult, [k_praw, k_c0v],
                               [k_hs + f"s{sb_}"] + ([k_hs] if sb_ == 0 else []))
                        for sb_ in range(20):
                            khb = k_hs + f"s{sb_}"
                            STT(hs[:, sb_, :], praw[:, sb_, 0:128], c_colv[:, sb_:sb_ + 1], hs[:, sb_, :], ALU.mult, ALU.add, [k_praw, k_colv, khb], [khb])
                        for sb_ in range(20):
                            khb = k_hs + f"s{sb_}"
                            STT(hs[:, sb_, :], praw[:, sb_, 2:130], c_colv[:, 20 + sb_:21 + sb_], hs[:, sb_, :], ALU.mult, ALU.add, [k_praw, k_colv, khb], [khb])
                        MEMSET("dve", jn, 0.0, [k_hs + f"s{i}" for i in range(20)], [k_hs, k_jn])
                        r_ = hs[:, 0:6, :]
                        k_ = hs[:, 6:12, :]
                        v_ = hs[:, 12:18, :]
                        TT("dve", kkraw, k_, kk_b, ALU.mult, [k_hs, k_colv], [k_kkraw])
                        ACT(sqb, kkraw, AF.Square, [k_kkraw], [k_sqb])
                        pb, kpb = bank()
                        MM(pb[:, 0:512], blk1b, sqb[:, 0:4, :], True, True, [k_blk1, k_sqb], [kpb])
                        pb2, kpb2 = bank()
                        MM(pb2[:, 0:256], blk1b, sqb[:, 4:6, :], True, True, [k_blk1, k_sqb], [kpb2])
                        ACT(rn[:, 0:4, :], pb[:, 0:512].rearrange("p (a b) -> p a b", a=4), AF.Ln, [kpb, k_ceps], [k_rn], bias=c_tiny)
                        ACT(rn[:, 4:6, :], pb2[:, 0:256].rearrange("p (a b) -> p a b", a=2), AF.Ln, [kpb2, k_ceps, k_rn], [k_rn], bias=c_tiny)
                        ACT(rn, rn, AF.Exp, [k_rn], [k_rn], scale=-0.5)
                        TT("dve", kk, kkraw, rn, ALU.mult, [k_kkraw, k_rn], [k_kk])
                        DMA("sp", HSd[c].rearrange("p (a b) -> p a b", a=26), hsb, [k_hs, k_kk], [f"HS_{c}"])
                    ACT(twl, hs[:, 18, :], AF.Tanh, [k_hs], [k_twl])
                    CP("dve", alb, hs[:, 19, :], [k_hs], [k_alb])
                    hsl = slice(d_ * 64, (d_ + 1) * 64)
                    for which in range(2):
                        src = twl if which == 0 else alb
                        ksrc = k_twl if which == 0 else k_alb
                        dst, kdst = (sg, k_sg) if which == 0 else (a_t, k_at)
                        cbase = 40 if which == 0 else 52
                        pb, kpb = bank()
                        pb2, kpb2 = bank()
                        for blk in range(6):
                            tgt, ktgt = (pb, kpb) if blk < 4 else (pb2, kpb2)
                            cc = (blk % 4) * 128
                            MM(tgt[:, cc:cc + 128], lupb[hsl, which, blk * 128:(blk + 1) * 128], src[hsl, :], True, True, [k_lupb, ksrc], [ktgt])
                        for blk in range(6):
                            tgt, ktgt = (pb, kpb) if blk < 4 else (pb2, kpb2)
                            cc = (blk % 4) * 128
                            ACT(dst[:, blk, :], tgt[:, cc:cc + 128], AF.Sigmoid, [ktgt, k_colv, kdst], [kdst],
                                bias=c_colv[:, cbase + d_ * 6 + blk:cbase + d_ * 6 + blk + 1])
                    for blk in range(6):
                        P.op("dve", (lambda o, d1: (lambda e: e.tensor_tensor_scan(out=o, data0=ones_f, data1=d1, initial=0.0, op0=ALU.mult, op1=ALU.add)))(
                            cs[:, blk, :], sg[:, blk, :]), [k_sg, k_onesf, k_cs], [k_cs])
                    ACT(wtot.unsqueeze(2), cs[:, :, 127:128], AF.Exp, [k_cs], [k_wtot], scale=-KD)
                    if d_ == 1:
                        TT("dve", ex, sg, cs, ALU.subtract, [k_sg, k_cs], [k_ex])
                        TT("dve", cs, ex, cs[:, :, 127:128].to_broadcast([128, 6, 128]), ALU.add, [k_ex, k_cs], [k_cs])
                    TT("dve", ex, cs, sg, ALU.subtract, [k_cs, k_sg], [k_ex])
                    exp_op(eL, k_eL, cs, k_cs, -KD)
                    exp_op(emL, k_emL, cs, k_cs, KD)
                    exp_op(eLx, k_eLx, ex, k_ex, -KD)
                    TT("dve", b_t, kk, a_t, ALU.mult, [k_kk, k_at], [k_bt])
                    STT(t1, a_t, -1.0, ka_b, ALU.add, ALU.mult, [k_at, k_colv], [k_t1])
                    STT(kp, t1, 1.0, k_, ALU.add, ALU.mult, [k_t1, k_hs], [k_kp])
                    TT("dve", aq[:, :, 1, :], r_, eL, ALU.mult, [k_hs, k_eL], [k_aq])
                    TT("dve", aq[:, :, 0, :], kk, eLx, ALU.mult, [k_kk, k_eLx, k_aq], [k_aq])
                    TT("dve", btl, b_t, emL, ALU.mult, [k_bt, k_emL], [k_btl])
                    TT("dve", ktl, kp, emL, ALU.mult, [k_kp, k_emL], [k_ktl])
                    CP("dve", vb, v_, [k_hs], [k_vb])
                    for (src3, ksrc, dst3, kdst, sel) in ((aq, k_aq, atm, k_atm, 0), (btl, k_btl, btm, k_btm, None), (ktl, k_ktl, ktm, k_ktm, None), (vb, k_vb, vtm, k_vtm, None)):
                        pb, kpb = bank()
                        pbb = pb[:].bitcast(BF16)
                        for blk in range(6):
                            sin = src3[:, blk, 0, :] if sel is not None else src3[:, blk, :]
                            TR(pbb[:, blk * 128:(blk + 1) * 128], sin, identb, [ksrc, k_idb], [kpb])
                        CP("dve", dst3, pbb[:, 0:768].rearrange("p (a b) -> p a b", a=6), [kpb], [kdst])
                    def hsl_(hd):
                        return hd // 2, slice((hd % 2) * 64, (hd % 2) * 64 + 64)
                    P0, k_P0 = Pp[0]
                    R0, k_R0 = Rb[0]
                    for hd in range(12):
                        blk, hp = hsl_(hd)
                        pA, kpA = bank()
                        MM(pA[:, 0:256].rearrange("p (a b) -> p a b", a=2), btl[hp, blk, :], aq[hp, blk, :, :], True, True, [k_btl, k_aq], [kpA])
                        MM(pA[:, 256:512].rearrange("p (a b) -> p a b", a=2), ktl[hp, blk, :], aq[hp, blk, :, :], True, True, [k_ktl, k_aq], [kpA])
                        TT("dve", AT[:, hd, :, :], pA[:, 0:512].rearrange("p (a b) -> p a b", a=4), m4t, ALU.mult, [kpA, k_m4], [k_AT + f"{hd}"])
                        STT(P0[:, hd, 1, :], pA[:, 0:128], -1.0, m4t[:, 0, :], ALU.mult, ALU.mult, [kpA, k_m4], [k_P0 + f"t{hd}"])
                        pL, kpL = bank()
                        MM(pL[:, 0:128], aq[hp, blk, 0, :], btl[hp, blk, :], True, True, [k_aq, k_btl], [kpL])
                        STT(P0[:, hd, 0, :], pL[:, 0:128], -1.0, mL, ALU.mult, ALU.mult, [kpL, k_cf], [k_P0 + f"n{hd}"])
                    for hd in range(12):
                        blk, hp = hsl_(hd)
                        pR, kpR = bank()
                        MM(pR[:, 0:64], AT[:, hd, 2, :], vtm[:, blk, hp], True, True, [k_AT + f"{hd}", k_vtm], [kpR])
                        CP("dve", R0[:, hd, 0:64], atm[:, blk, hp], [k_atm], [k_R0 + f"a{hd}"])
                        CP("dve", R0[:, hd, 64:128], pR[:, 0:64], [kpR], [k_R0 + f"b{hd}"])
                    for lv in range(7):
                        Pc, k_Pc = Pp[lv % 2]
                        Pn_, k_Pn = Pp[(lv + 1) % 2]
                        Rc, k_Rc = Rb[lv % 2]
                        Rn, k_Rn = Rb[(lv + 1) % 2]
                        for hd in range(12):
                            kP = [k_Pc + f"t{hd}", k_Pc + f"n{hd}"]
                            kR = [k_Rc + f"a{hd}", k_Rc + f"b{hd}"]
                            pD, kpD = bank()
                            MM(pD[:, 0:128], Pc[:, hd, 1, :], Rc[:, hd, :], True, True, kP + kR, [kpD])
                            if lv < 6:
                                MM(pD[:, 128:256], Pc[:, hd, 1, :], Pc[:, hd, 0, :], True, True, kP, [kpD])
                                MM(pD[:, 256:384], Pc[:, hd, 0, :], Pc[:, hd, 1, :], True, True, kP, [kpD])
                            TT("dve", Rn[:, hd, :], pD[:, 0:128], Rc[:, hd, :], ALU.add, [kpD] + kR, [k_Rn + f"a{hd}", k_Rn + f"b{hd}"])
                            if lv < 6:
                                CP("dve", Pn_[:, hd, :, :], pD[:, 128:384].rearrange("p (a b) -> p a b", a=2), [kpD], [k_Pn + f"n{hd}", k_Pn + f"t{hd}"])
                    Rf, k_Rf = Rb[1]
                    for hd in range(12):
                        blk, hp = hsl_(hd)
                        kRf = [k_Rf + f"a{hd}", k_Rf + f"b{hd}"]
                        TS("dve", nU[:, hd, :], Rf[:, hd, 64:128], -1.0, ALU.mult, kRf, [k_nU + f"{hd}"])
                        pE, kpE = bank()
                        MM(pE[hp, 0:64], Rf[:, hd, 0:64], btm[:, blk, hp], True, True, kRf + [k_btm], [kpE])
                        MM(pE[hp, 128:256], Rf[:, hd, 0:64], AT[:, hd, 1, :], True, True, kRf + [k_AT + f"{hd}"], [kpE])
                        MM(pE[hp, 64:128], ktm[:, blk, hp], vtm[:, blk, hp], True, False, [k_ktm, k_vtm], [kpE])
                        MM(pE[hp, 64:128], btm[:, blk, hp], nU[:, hd, :], False, True, [k_btm, k_nU + f"{hd}"], [kpE])
                        TT("dve", IXb[hp, blk, :], c_f32[hp, 640:704], pE[hp, 0:64], ALU.subtract, [k_cf, kpE], [k_IXb + f"{hd}"])
                        TT("dve", QhT[hp, blk, :], aq[hp, blk, 1, :], pE[hp, 128:256], ALU.subtract, [k_aq, kpE], [k_QhT + f"{hd}"])
                        CP("dve", Gp[hp, blk, :], pE[hp, 64:128], [kpE], [k_Gp + f"{hd}"])
                    for hd in range(12):
                        blk, hp = hsl_(hd)
                        pY, kY = (pY0, kY0) if hd < 8 else (pY1, kY1)
                        yc0 = (hd % 8) * 64
                        MM(pY[:, yc0:yc0 + 64], AT[:, hd, 3, :], vtm[:, blk, hp], True, False, [k_AT + f"{hd}", k_vtm], [kY])
                        MM(pY[:, yc0:yc0 + 64], AT[:, hd, 1, :], nU[:, hd, :], False, False, [k_AT + f"{hd}", k_nU + f"{hd}"], [kY])
                        MM(pY[:, yc0:yc0 + 64], QhT[hp, blk, :], STb[hp, blk, :], False, True, [k_QhT + f"{hd}", k_STb], [kY])
                        MM(pZ[hp, blk * 64:(blk + 1) * 64], IXb[hp, blk, :], STb[hp, blk, :], True, True, [k_IXb + f"{hd}", k_STb], [kZ])
                    kGp_all = [k_Gp + f"{i}" for i in range(12)]
                    TT("dve", ztmp, pZ[:, 0:384].rearrange("p (a b) -> p a b", a=6), Gp, ALU.add, [kZ] + kGp_all, [k_ztmp])
                    TT("dve", ST, ztmp, wtot.unsqueeze(2).to_broadcast([128, 6, 64]), ALU.mult, [k_ztmp, k_wtot], [k_ST])
                    CP("dve", STb, ST, [k_ST], [k_STb])
                    CP("dve", ysb[:, 0:512], pY0[:, 0:512], [kY0], [k_ysb + "0"])
                    CP("dve", ysb[:, 512:768], pY1[:, 0:256], [kY1], [k_ysb + "1"])
                    k_ysb_all = [k_ysb + "0", k_ysb + "1"]
                    if d_ == 0:
                        DMA("sp", YF[t0:t0 + C, :], ysb, k_ysb_all, [f"YF_{c}"])
                        continue
                    DMA("sp", yf_t, YF[t0:t0 + C, :], [f"YF_{c}"], [k_yf])
                    DMA("sp", gate2, G[0:6, :, t0:t0 + C].rearrange("b p t -> p b t"), [], [k_gate2])
                    TT("dve", ysb, ysb, yf_t, ALU.add, k_ysb_all + [k_yf], k_ysb_all)
                    y3 = ysb.rearrange("p (h n) -> p h n", h=12)
                    yc3 = yc.rearrange("p (h n) -> p h n", h=12)
                    ysq3 = ysq.rearrange("p (h n) -> p h n", h=12)
                    mu = st12[:, 0, :]
                    var = st12[:, 1, :]
                    rstd = st12[:, 2, :]
                    P.op("dve", (lambda o, i: (lambda e: e.tensor_reduce(out=o, in_=i, op=ALU.add, axis=mybir.AxisListType.X)))(mu, y3), k_ysb_all, [k_st12 + "m"])
                    TS("dve", mu, mu, 1.0 / 64, ALU.mult, [k_st12 + "m"], [k_st12 + "m"])
                    TT("dve", yc3, y3, mu.unsqueeze(2).to_broadcast([128, 12, 64]), ALU.subtract, k_ysb_all + [k_st12 + "m"], [k_yc])
                    TT("dve", ysq, yc, yc, ALU.mult, [k_yc], [k_ysq])
                    P.op("dve", (lambda o, i: (lambda e: e.tensor_reduce(out=o, in_=i, op=ALU.add, axis=mybir.AxisListType.X)))(var, ysq3), [k_ysq], [k_st12 + "v"])
                    ACT(rstd, var, AF.Ln, [k_st12 + "v", k_ceps], [k_st12 + "r"], scale=1.0 / 64, bias=c_gneps)
                    ACT(rstd, rstd, AF.Exp, [k_st12 + "r"], [k_st12 + "r"], scale=-0.5)
                    TT("dve", yc3, yc3, rstd.unsqueeze(2).to_broadcast([128, 12, 64]), ALU.mult, [k_yc, k_st12 + "r"], [k_yc])
                    TT("dve", yc, yc, gnv_b[:, 0, :], ALU.mult, [k_yc, k_gnv], [k_yc])
                    TT("dve", yc, yc, gnv_b[:, 1, :], ALU.add, [k_yc, k_gnv], [k_yc])
                    TT("dve", t1, r_, k_, ALU.mult, [k_hs], [k_t1])
                    TT("dve", prodb, t1, rk_b, ALU.mult, [k_t1, k_colv], [k_prodb])
                    pB, kpB = bank()
                    for blk in range(6):
                        MM(pB[:, blk * 2:(blk + 1) * 2], prodb[:, blk, :], bselb, True, True, [k_prodb, k_bsel], [kpB])
                    CP("dve", bon, pB[:, 0:12], [kpB], [k_bon])
                    TT("dve", ysq3, vtm.rearrange("p b (h n) -> p (b h) n", h=2), bon.unsqueeze(2).to_broadcast([128, 12, 64]), ALU.mult, [k_vtm, k_bon], [k_ysq])
                    TT("dve", yc, yc, ysq, ALU.add, [k_yc, k_ysq], [k_yc])
                    pT0, kpT0 = bank()
                    pT1, kpT1 = bank()
                    for blk in range(6):
                        tgt, ktgt = (pT0, kpT0) if blk < 4 else (pT1, kpT1)
                        cc = (blk % 4) * 128
                        TR(tgt[:, cc:cc + 128], yc[:, blk * 128:(blk + 1) * 128], ident_f, [k_yc, k_cf], [ktgt])
                    TT("dve", mixo[:, 0:4, :], pT0[:, 0:512].rearrange("p (a b) -> p a b", a=4), gate2[:, 0:4, :], ALU.mult, [kpT0, k_gate2], [k_mixo + "0"])
                    TT("dve", mixo[:, 4:6, :], pT1[:, 0:256].rearrange("p (a b) -> p a b", a=2), gate2[:, 4:6, :], ALU.mult, [kpT1, k_gate2], [k_mixo + "1"])
                    DMA("sp", MIX[0:6, :, t0:t0 + C].rearrange("b p t -> p b t"), mixo, [k_mixo + "0", k_mixo + "1"], [f"MIXr_{c}"])
            P.barrier()
            nb_cfg[0] = 8
            AR.release(m2)

        if 3 in passes:
            m3 = AR.mark()
            nb_cfg[0] = 6
            KT_sb, k_KT = AR.alloc((2, T), BF16)
            V_sb, k_V = AR.alloc((NT, 256), BF16)
            for hk in range(2):
                for tq in range(0, T, 2048):
                    nq = min(2048, T - tq)
                    DMA("sp", KT_sb[:, hk, tq:tq + nq], KTd[hk, :, tq:tq + nq], [], [k_KT + f"_{hk}_{tq}"])
            k_KT_all = [k_KT + f"_{hk}_{tq}" for hk in range(2) for tq in range(0, T, 2048)]
            for tq in range(0, NT, 16):
                nq = min(16, NT - tq)
                DMA("sp", V_sb[:, tq:tq + nq, :], Vd[tq * 128:(tq + nq) * 128, :].rearrange("(t p) c -> p t c", p=128), [], [k_V + f"_{tq}"])
            k_V_all = [k_V + f"_{tq}" for tq in range(0, NT, 16)]
            qT3, k_qT3 = AR.alloc((6, GRP), BF16)
            g3, k_g3 = AR.alloc((6, GRP), BF16)
            PT3 = [AR.alloc((GRP,), BF16) for _ in range(2)]
            rs3, k_rs3 = AR.alloc((GRP,), F32)
            y3, k_y3 = AR.alloc((GRP,), F32)
            o3 = [AR.alloc((GRP,), BF16) for _ in range(2)]
            pO, kpO = psb[6], "ps6"
            pS, kpS = psb[7], "ps7"
            for qg in range(NG):
                t0 = qg * GRP
                qseg = t0 // SEG
                for hq in range(6):
                    DMA("sp", qT3[:, hq, :], QT[hq, :, t0:t0 + GRP], [], [k_qT3 + f"h{hq}"])
                    DMA("sp", g3[:, hq, :], G[6 + hq, :, t0:t0 + GRP], [], [k_g3 + f"h{hq}"])
                for hq in range(6):
                    kvh = hq // 3
                    for kt in range(NT):
                        kseg = (kt * 128) // SEG
                        PT_ap, k_PT = PT3[kt % 2]
                        pb, kpb = bank()
                        MM(pb[:, 0:GRP], KT_sb[:, kvh, kt * 128:(kt + 1) * 128], qT3[:, hq, :], True, True, k_KT_all + [k_qT3 + f"h{hq}"], [kpb])
                        ACT(PT_ap, pb[:, 0:GRP], AF.Exp, [kpb, k_flag], [k_PT], scale=ATT_SCALE,
                            bias=c_flag[:, 8 + qseg * NSEG + kseg:8 + qseg * NSEG + kseg + 1])
                        MM(pO[:, 0:GRP], V_sb[:, kt, kvh * 128:(kvh + 1) * 128], PT_ap, kt == 0, kt == NT - 1, k_V_all + [k_PT], [kpO])
                        MM(pS[:, 0:GRP], onesb, PT_ap, kt == 0, kt == NT - 1, [k_onesb, k_PT], [kpS])
                    P.op("dve", (lambda o, i: (lambda e: e.reciprocal(out=o, in_=i)))(rs3, pS[:, 0:GRP]), [kpS], [k_rs3])
                    TT("dve", y3, pO[:, 0:GRP], rs3, ALU.mult, [kpO, k_rs3], [k_y3])
                    o_ap, k_o = o3[hq % 2]
                    TT("dve", o_ap, y3, g3[:, hq, :], ALU.mult, [k_y3, k_g3 + f"h{hq}"], [k_o])
                    DMA("sp", MIX[6 + hq, :, t0:t0 + GRP], o_ap, [k_o], [f"MIXa{hq}_{qg}"])
            P.barrier()
            nb_cfg[0] = 8
            AR.release(m3)

        if 4 in passes:
            m4 = AR.mark()
            fg_b, k_fg = AR.alloc((D,), F32)
            DMA("sp", fg_b, rowv[:, 1, :], [], [k_fg])
            wo_sb, k_wo = AR.alloc((16, D), BF16)
            for i in range(4):
                DMA("sp", wo_sb[:, i * 4:(i + 1) * 4, :], WO.rearrange("(p k) n -> p k n", k=16)[:, i * 4:(i + 1) * 4, :], [], [k_wo + f"_{i}"])
            k_wo_all = [k_wo + f"_{i}" for i in range(4)]
            mixt = [AR.alloc((16, 128), BF16) for _ in range(2)]
            x4 = [AR.alloc((D,), F32) for _ in range(2)]
            r4 = [AR.alloc((D,), F32) for _ in range(2)]
            y4 = [AR.alloc((D,), F32) for _ in range(2)]
            junk4, k_junk4 = AR.alloc((D,), BF16)
            st4, k_st4 = AR.alloc((2, 2), F32)
            for tt in range(NT):
                tok = tt * 128
                m_ap, k_m = mixt[tt % 2]
                x_ap, k_x = x4[tt % 2]
                r_ap, k_r = r4[tt % 2]
                y_ap, k_y = y4[tt % 2]
                DMA("sp", m_ap, MIX[:, :, tok:tok + 128].rearrange("c p t -> p c t"), [], [k_m])
                DMA("sp", x_ap, xs[tok:tok + 128, :], [], [k_x])
                for ng in range(4):
                    pb, kpb = bank()
                    for kc in range(16):
                        MM(pb[:, 0:512], m_ap[:, kc, :], wo_sb[:, kc, ng * 512:(ng + 1) * 512], kc == 0, kc == 15, [k_m] + k_wo_all, [kpb])
                    TT("dve", r_ap[:, ng * 512:(ng + 1) * 512], pb[:, 0:512], x_ap[:, ng * 512:(ng + 1) * 512], ALU.add, [kpb, k_x], [k_r + f"_{ng}"])
                kr_all = [k_r + f"_{ng}" for ng in range(4)]
                ssq = st4[:, tt % 2, 0:1]
                rstd = st4[:, tt % 2, 1:2]
                ks1 = k_st4 + f"a{tt % 2}"
                ks2 = k_st4 + f"b{tt % 2}"
                ACT(junk4, r_ap, AF.Square, kr_all, [k_junk4, ks1], accum_out=ssq)
                ACT(rstd, ssq, AF.Ln, [ks1, k_ceps], [ks2], scale=1.0 / D, bias=c_eps)
                ACT(rstd, rstd, AF.Exp, [ks2], [ks2], scale=-0.5)
                STT(y_ap, r_ap, rstd, fg_b, ALU.mult, ALU.mult, kr_all + [ks2, k_fg], [k_y])
                DMA("sp", y_out[tok:tok + 128, :], y_ap, [k_y], [f"y_{tt}"])
            P.barrier()
            AR.release(m4)

        P.barrier()
        P.finalize()
        P.emit()
    return nc


def _rope_tab(nseg, seg, carry):
    if carry:
        pos = np.arange(nseg * seg)
    else:
        pos = np.tile(np.arange(seg), nseg)
    row = (pos // 64).astype(np.float32)
    col = (pos % 64).astype(np.float32)
    freqs = (np.float32(10000.0) ** (-np.arange(0, 64, 2, dtype=np.float32) / np.float32(64))).astype(np.float32)
    ang = np.concatenate([row[:, None] * freqs, col[:, None] * freqs], axis=-1).astype(np.float32)
    return np.stack([np.cos(ang), np.sin(ang)], axis=1).astype(np.float32)


def _consts():
    c = np.zeros((128, 1024), np.float32)
    r = np.arange(128)[:, None]
    q = np.arange(128)[None, :]
    c[:, 0:128] = (r == q)
    c[:, 128:256] = (q > r)
    c[:, 256:384] = (q >= r)
    c[:, 384:512] = (q < r)
    c[:, 512:640] = (q <= r)
    c[:, 640:704] = (np.arange(64)[None, :] == (r % 64))
    c[:, 704:706] = (np.arange(2)[None, :] == (r // 64))
    c[:, 706:834] = ((r // 64) == (q // 64))
    return c


def shared_inputs(norm_g, w_in, mu_prev, mu_next, w0, w_up, a0, a_up, k_k, k_a, r_k, gn_g, gn_b,
                  q_norm_g, k_norm_g, mem_norm_g, w_mem_kv, w_out, final_g):
    f = lambda a: np.ascontiguousarray(np.asarray(a, dtype=np.float32))
    w_in = f(w_in)[0]
    wt = w_in.reshape(16, 128, NCB, 128)[:, :, CB_PERM, :]
    w_in_t = np.ascontiguousarray(wt.transpose(2, 1, 0, 3)).reshape(NCB * 128, D)
    w_out_t = np.ascontiguousarray(f(w_out)[0].reshape(16, 128, D).transpose(1, 0, 2)).reshape(128 * 16, D)
    w_kv_t = np.ascontiguousarray(f(w_mem_kv)[0].reshape(16, 128, 1024).transpose(1, 0, 2)).reshape(128 * 16, 1024)
    rowv = np.ascontiguousarray(np.broadcast_to(np.stack([f(norm_g)[0], f(final_g), f(mem_norm_g)[0]])[None], (128, 3, D)))
    qkg = np.ascontiguousarray(np.broadcast_to(np.stack([f(q_norm_g)[0], f(k_norm_g)[0]])[None], (128, 2, 128)))
    gnv = np.ascontiguousarray(np.broadcast_to(np.stack([f(gn_g)[0], f(gn_b)[0]])[None], (128, 2, 768)))
    colv = np.zeros((128, 96), np.float32)
    colv[:, 0:20] = f(mu_prev)[0].reshape(20, 128).T
    colv[:, 20:40] = f(mu_next)[0].reshape(20, 128).T
    colv[:, 40:52] = f(w0)[0].reshape(12, 128).T
    colv[:, 52:64] = f(a0)[0].reshape(12, 128).T
    colv[:, 64:70] = f(k_k)[0].reshape(6, 128).T
    colv[:, 70:76] = f(k_a)[0].reshape(6, 128).T
    colv[:, 76:82] = f(r_k)[0].reshape(6, 128).T
    lora_up = np.ascontiguousarray(np.stack([f(w_up)[0].reshape(128, 768), f(a_up)[0].reshape(128, 768)], axis=1))
    return dict(w_in_t=w_in_t, w_out_t=w_out_t, w_kv_t=w_kv_t, rowv=rowv, qkg=qkg, gnv=gnv, colv=colv,
                lora_up=lora_up, consts=_consts())


def core_inputs(shared, x_core, mem_core, nseg, seg, carry):
    flags = np.zeros((128, 32), np.float32)
    flags[:, 0] = 1.0 if carry else 0.0
    for qs in range(nseg):
        for ks in range(nseg):
            flags[:, 8 + qs * nseg + ks] = 0.0 if (carry or qs == ks) else NEG
    d = dict(shared)
    d["xs"] = np.ascontiguousarray(x_core, dtype=np.float32)
    d["mem"] = np.ascontiguousarray(mem_core, dtype=np.float32)
    d["cs_tab"] = _rope_tab(nseg, seg, carry)
    d["flags"] = flags
    return d


_NC_CACHE = {}


def kernel(x_prompt, x_sample, mem_prompt, mem_sample, norm_g, w_in, mu_prev, mu_next, w0, w_up, a0, a_up,
           k_k, k_a, r_k, gn_g, gn_b, q_norm_g, k_norm_g, mem_norm_g, w_mem_kv, w_out, final_g):
    NSEG, SEG = 4, 2048
    x_prompt = np.asarray(x_prompt, dtype=np.float32)
    x_sample = np.asarray(x_sample, dtype=np.float32)
    mem_prompt = np.asarray(mem_prompt, dtype=np.float32)
    mem_sample = np.asarray(mem_sample, dtype=np.float32)
    shared = shared_inputs(norm_g, w_in, mu_prev, mu_next, w0, w_up, a0, a_up, k_k, k_a, r_k, gn_g, gn_b,
                           q_norm_g, k_norm_g, mem_norm_g, w_mem_kv, w_out, final_g)
    in_maps = []
    for c in range(4):
        in_maps.append(core_inputs(shared, x_prompt[c], np.broadcast_to(mem_prompt[c][None], (NSEG, N_MEM, D)), NSEG, SEG, True))
    for c in range(4):
        in_maps.append(core_inputs(shared, x_sample[4 * c:4 * c + 4].reshape(NSEG * SEG, D), mem_sample[4 * c:4 * c + 4], NSEG, SEG, False))
    key = (NSEG, SEG)
    if key not in _NC_CACHE:
        _NC_CACHE[key] = build(NSEG, SEG)
    nc = _NC_CACHE[key]
    res = run_bass_kernel_spmd(nc, in_maps, core_ids=list(range(8)))
    yp = np.stack([np.asarray(res.results[c]["y"], dtype=np.float32) for c in range(4)])
    ysm = np.concatenate([np.asarray(res.results[4 + c]["y"], dtype=np.float32).reshape(4, SEG, D) for c in range(4)], axis=0)
    return (yp, ysm)
```
